# Optimizing a Trainium2 kernel written in Bass

```python
import math
import numpy as np
import jax
import jax.numpy as jnp
from jax import lax


D_MODEL = 2048
BATCH = 8
SEQ = 2048
DEPTH = 2

MEM_LEN = 256
MIX_WIDTH = D_MODEL
MIX_GROUP_WIDTH = MIX_WIDTH // 4
N_MIXERS = 4
HEAD_DIM = 64
Q_BLOCK = 128
ROPE_THETA = 10000.0
LN_EPS = 1e-5
RMS_EPS = 1e-6
SSM_CH = 16
SSM_GROUPS = MIX_GROUP_WIDTH // SSM_CH
SSM_STATE = 64
NSA_HEADS = MIX_GROUP_WIDTH // HEAD_DIM
NSA_KV_HEADS = 2
NSA_REP = NSA_HEADS // NSA_KV_HEADS
CMP_LEN = 32
CMP_STRIDE = 16
CMP_HIDDEN = 128
SEL_BLOCK = 64
SEL_TOPN = 8
NSA_WINDOW = 512
SB_HEADS = MIX_GROUP_WIDTH // HEAD_DIM
DIL_HEADS = MIX_GROUP_WIDTH // HEAD_DIM
DIL_CONFIGS = ((128, 1), (512, 4), (2048, 16))
XA_HEADS = 4
XA_HEAD_DIM = 128
XA_WIDTH = XA_HEADS * XA_HEAD_DIM
FFN_HIDDEN = -(-8 * D_MODEL // (3 * 256)) * 256
DEEPNORM_ALPHA = (2 * DEPTH) ** 0.25
DEEPNORM_BETA = (8 * DEPTH) ** -0.25
SSM_WIDTH = SSM_GROUPS * SSM_CH
NSA_Q_WIDTH = NSA_HEADS * HEAD_DIM
NSA_KV_WIDTH = 6 * NSA_KV_HEADS * HEAD_DIM
NSA_GATE_WIDTH = 3 * NSA_HEADS
SB_QKV_WIDTH = 3 * SB_HEADS * HEAD_DIM
DIL_QKV_WIDTH = 3 * DIL_HEADS * HEAD_DIM
IN_WIDTHS = (SSM_WIDTH, NSA_Q_WIDTH, NSA_KV_WIDTH, NSA_GATE_WIDTH, SB_QKV_WIDTH, DIL_QKV_WIDTH)
IN_WIDTH = SSM_WIDTH + NSA_Q_WIDTH + NSA_KV_WIDTH + NSA_GATE_WIDTH + SB_QKV_WIDTH + DIL_QKV_WIDTH

kernel_name = 'hybrid_s5_nsa_stickbreak_dilated_deepnorm'

F32 = jnp.float32


def layer_norm(x, g, b):
    xf = x.astype(F32)
    mu = jnp.mean(xf, -1, keepdims=True)
    xc = xf - mu
    var = jnp.mean(xc * xc, -1, keepdims=True)
    return (xc * lax.rsqrt(var + LN_EPS) * g + b).astype(x.dtype)


def rms_norm(x, g):
    xf = x.astype(F32)
    return (xf * lax.rsqrt(jnp.mean(xf * xf, -1, keepdims=True) + RMS_EPS) * g).astype(x.dtype)


def rope(x, pos):
    half = x.shape[-1] // 2
    inv_freq = ROPE_THETA ** (-jnp.arange(half, dtype=F32) / half)
    ang = pos.astype(F32)[:, None] * inv_freq[None, :]
    cos = jnp.cos(ang)[:, None, :]
    sin = jnp.sin(ang)[:, None, :]
    x1 = x[..., :half].astype(F32)
    x2 = x[..., half:].astype(F32)
    return jnp.concatenate([x1 * cos - x2 * sin, x2 * cos + x1 * sin], -1).astype(x.dtype)


def masked_softmax(s, mask, axis):
    s = jnp.where(mask, s, -jnp.inf)
    m = jnp.max(s, axis=axis, keepdims=True)
    m = jnp.where(jnp.isfinite(m), m, 0.0)
    p = jnp.where(mask, jnp.exp(s - m), 0.0)
    den = jnp.sum(p, axis=axis, keepdims=True)
    return p / jnp.maximum(den, 1e-30), m + jnp.log(den)


def _s5_combine(e1, e2):
    a1r, a1i, b1r, b1i = e1
    a2r, a2i, b2r, b2i = e2
    return (a2r * a1r - a2i * a1i, a2r * a1i + a2i * a1r,
            a2r * b1r - a2i * b1i + b2r, a2r * b1i + a2i * b1r + b2i)


def s5_mixer(u, lam_re, lam_im, log_dt, b_re, b_im, c_re, c_im, d_skip, w_glu, b_glu):
    Bsz, S, _ = u.shape
    uf = u.astype(F32).reshape(Bsz, S, SSM_GROUPS, SSM_CH)
    lr = jnp.minimum(lam_re.astype(F32), -1e-4)
    li = lam_im.astype(F32)
    dt = jnp.exp(log_dt.astype(F32))[:, None]
    mag = jnp.exp(lr * dt)
    a_re = mag * jnp.cos(li * dt)
    a_im = mag * jnp.sin(li * dt)
    den = lr * lr + li * li
    z_re = ((a_re - 1.0) * lr + a_im * li) / den
    z_im = (a_im * lr - (a_re - 1.0) * li) / den
    bb_re = z_re[..., None] * b_re - z_im[..., None] * b_im
    bb_im = z_re[..., None] * b_im + z_im[..., None] * b_re
    bu_re = jnp.einsum('bsgc,gpc->bsgp', uf, bb_re)
    bu_im = jnp.einsum('bsgc,gpc->bsgp', uf, bb_im)
    shape = bu_re.shape
    elems = (jnp.broadcast_to(a_re, shape), jnp.broadcast_to(a_im, shape), bu_re, bu_im)
    _, _, x_re, x_im = lax.associative_scan(_s5_combine, elems, axis=1)
    y = (jnp.einsum('bsgp,gcp->bsgc', x_re, c_re) - jnp.einsum('bsgp,gcp->bsgc', x_im, c_im)
         + d_skip * uf)
    g = jax.nn.gelu(y.reshape(Bsz, S, SSM_WIDTH))
    return g * jax.nn.sigmoid(g @ w_glu + b_glu)


def nsa_mixer(q, k_cmp, v_cmp, k_slc, v_slc, k_win, v_win, gates, cmp_pe, cmp_w1, cmp_w2):
    Bsz, S = q.shape[:2]
    G, R, dh = NSA_KV_HEADS, NSA_REP, HEAD_DIM
    scale = dh ** -0.5
    n_cmp = (S - CMP_LEN) // CMP_STRIDE + 1
    blk_idx = jnp.arange(n_cmp)[:, None] * CMP_STRIDE + jnp.arange(CMP_LEN)[None, :]
    cmp_end = jnp.arange(n_cmp) * CMP_STRIDE + CMP_LEN - 1

    def compress(t, j):
        blocks = t[:, blk_idx] + cmp_pe[j][:, None, :]
        flat = blocks.transpose(0, 1, 3, 2, 4).reshape(Bsz, n_cmp, G, CMP_LEN * dh)
        return jax.nn.gelu(flat @ cmp_w1[j]) @ cmp_w2[j]

    kc = rope(compress(k_cmp, 0), cmp_end)
    vc = compress(v_cmp, 1)

    n_sel = S // SEL_BLOCK
    top_n = min(SEL_TOPN, n_sel)
    ci = np.arange(n_cmp)[:, None] * CMP_STRIDE
    sj = np.arange(n_sel)[None, :] * SEL_BLOCK
    overlap = np.clip(np.minimum(ci + CMP_LEN, sj + SEL_BLOCK) - np.maximum(ci, sj), 0, None) / CMP_LEN
    overlap = jnp.asarray(overlap, dtype=F32)
    ks_blocks = k_slc.reshape(Bsz, n_sel, SEL_BLOCK, G, dh).transpose(0, 3, 1, 2, 4)
    vs_blocks = v_slc.reshape(Bsz, n_sel, SEL_BLOCK, G, dh).transpose(0, 3, 1, 2, 4)
    pad = jnp.zeros((Bsz, NSA_WINDOW, G, dh), k_win.dtype)
    kw_pad = jnp.concatenate([pad, k_win], 1)
    vw_pad = jnp.concatenate([pad, v_win], 1)
    n_qb = S // Q_BLOCK
    qb = q.reshape(Bsz, n_qb, Q_BLOCK, G, R, dh).transpose(1, 0, 2, 3, 4, 5)
    gb = gates.reshape(Bsz, n_qb, Q_BLOCK, G, R, 3).transpose(1, 0, 2, 3, 4, 5)
    b_ix = jnp.arange(Bsz)[:, None, None, None]
    g_ix = jnp.arange(G)[None, :, None, None]
    sel_j = jnp.arange(n_sel)

    def block(args):
        qi, g_blk, blk = args
        t = blk * Q_BLOCK + jnp.arange(Q_BLOCK)
        s_c = jnp.einsum('bqgrd,bcgd->bgrqc', qi, kc).astype(F32) * scale
        p_c, _ = masked_softmax(s_c, cmp_end[None, :] <= t[:, None], -1)
        o_c = jnp.einsum('bgrqc,bcgd->bqgrd', p_c.astype(vc.dtype), vc)
        imp = jnp.einsum('bgrqc,cn->bgqn', p_c, overlap)
        cur = (t // SEL_BLOCK)[:, None]
        imp = jnp.where(sel_j[None, :] * SEL_BLOCK > t[:, None], -jnp.inf, imp)
        forced = (sel_j[None, :] == 0) | (sel_j[None, :] == cur) | (sel_j[None, :] == cur - 1)
        imp = jnp.where(forced, jnp.inf, imp)
        top_val, top_idx = lax.top_k(imp, top_n)
        ks = ks_blocks[b_ix, g_ix, top_idx]
        vs = vs_blocks[b_ix, g_ix, top_idx]
        s_s = jnp.einsum('bqgrd,bgqnkd->bgrqnk', qi, ks).astype(F32) * scale
        key_pos = top_idx[..., None] * SEL_BLOCK + jnp.arange(SEL_BLOCK)
        m_s = (top_val > -jnp.inf)[..., None] & (key_pos <= t[:, None, None])
        p_s, _ = masked_softmax(s_s, m_s[:, :, None], (-2, -1))
        o_s = jnp.einsum('bgrqnk,bgqnkd->bqgrd', p_s.astype(vs.dtype), vs)
        kw = lax.dynamic_slice_in_dim(kw_pad, blk * Q_BLOCK, Q_BLOCK + NSA_WINDOW, axis=1)
        vw = lax.dynamic_slice_in_dim(vw_pad, blk * Q_BLOCK, Q_BLOCK + NSA_WINDOW, axis=1)
        s_w = jnp.einsum('bqgrd,bkgd->bgrqk', qi, kw).astype(F32) * scale
        kpos = blk * Q_BLOCK - NSA_WINDOW + jnp.arange(Q_BLOCK + NSA_WINDOW)
        diff = t[:, None] - kpos[None, :]
        m_w = (kpos[None, :] >= 0) & (diff >= 0) & (diff < NSA_WINDOW)
        p_w, _ = masked_softmax(s_w, m_w, -1)
        o_w = jnp.einsum('bgrqk,bkgd->bqgrd', p_w.astype(vw.dtype), vw)
        return g_blk[..., 0:1] * o_c + g_blk[..., 1:2] * o_s + g_blk[..., 2:3] * o_w

    out = lax.map(block, (qb, gb, jnp.arange(n_qb)))
    return out.transpose(1, 0, 2, 3, 4, 5).reshape(Bsz, S, NSA_HEADS * dh)


def stick_breaking_mixer(q, k, v):
    Bsz, S, H, dh = q.shape
    n_qb = S // Q_BLOCK
    qb = q.reshape(Bsz, n_qb, Q_BLOCK, H, dh).transpose(1, 0, 2, 3, 4)
    kpos = jnp.arange(S)

    def block(args):
        qi, blk = args
        t = blk * Q_BLOCK + jnp.arange(Q_BLOCK)
        z = jnp.einsum('bqhd,bkhd->bhqk', qi, k).astype(F32) * dh ** -0.5
        mask = kpos[None, :] < t[:, None]
        log_fail = jnp.where(mask, jax.nn.log_sigmoid(-z), 0.0)
        after = lax.cumsum(log_fail, axis=3, reverse=True) - log_fail
        w = jnp.where(mask, jnp.exp(jax.nn.log_sigmoid(z) + after), 0.0)
        return jnp.einsum('bhqk,bkhd->bqhd', w.astype(v.dtype), v)

    out = lax.map(block, (qb, jnp.arange(n_qb)))
    return out.transpose(1, 0, 2, 3, 4).reshape(Bsz, S, H * dh)


def dilated_branch(q, k, v, window, dil):
    Bsz, S, H, dh = q.shape
    L = S // dil
    wd = window // dil
    qlen = Q_BLOCK // dil
    n_b = L // qlen

    def sub(a):
        return a.reshape(Bsz, L, dil, H, dh)

    qs = sub(q).reshape(Bsz, n_b, qlen, dil, H, dh).transpose(1, 0, 2, 3, 4, 5)
    pad = jnp.zeros((Bsz, wd, dil, H, dh), k.dtype)
    ks = jnp.concatenate([pad, sub(k)], 1)
    vs = jnp.concatenate([pad, sub(v)], 1)
    i = jnp.arange(qlen)[:, None]
    j = jnp.arange(qlen + wd)[None, :]
    band = (j >= i) & (j <= i + wd)

    def block(args):
        qi, blk = args
        kb = lax.dynamic_slice_in_dim(ks, blk * qlen, qlen + wd, axis=1)
        vb = lax.dynamic_slice_in_dim(vs, blk * qlen, qlen + wd, axis=1)
        s = jnp.einsum('bqrhd,bkrhd->brhqk', qi, kb).astype(F32) * dh ** -0.5
        valid = band & (blk * qlen + j - wd >= 0)
        p, lse = masked_softmax(s, valid, -1)
        o = jnp.einsum('brhqk,bkrhd->bqrhd', p.astype(vb.dtype), vb)
        return o, lse[..., 0]

    o, lse = lax.map(block, (qs, jnp.arange(n_b)))
    o = o.transpose(1, 0, 2, 3, 4, 5).reshape(Bsz, S, H, dh)
    lse = lse.transpose(1, 0, 4, 2, 3).reshape(Bsz, S, H)
    return o, lse


def dilated_mixer(q, k, v):
    Bsz, S, H, dh = q.shape
    outs, lses = [], []
    for window, dil in DIL_CONFIGS:
        o, lse = dilated_branch(q, k, v, window, dil)
        outs.append(o.astype(F32))
        lses.append(lse)
    wts = jax.nn.softmax(jnp.stack(lses, 0), axis=0)
    out = jnp.einsum('cbsh,cbshd->bshd', wts, jnp.stack(outs, 0))
    return out.reshape(Bsz, S, H * dh)


def hybrid_mixer(h, w_in, lam_re, lam_im, log_dt, b_re, b_im, c_re, c_im, d_skip, w_glu, b_glu,
                 cmp_pe, cmp_w1, cmp_w2, norm_g, w_out):
    Bsz, S, _ = h.shape
    proj = h @ w_in
    offs = np.cumsum(IN_WIDTHS)[:-1].tolist()
    u, nq, nkv, ngate, sb_qkv, dil_qkv = jnp.split(proj, offs, axis=-1)
    pos = jnp.arange(S)
    y_a = s5_mixer(u, lam_re, lam_im, log_dt, b_re, b_im, c_re, c_im, d_skip, w_glu, b_glu)
    q = rope(nq.reshape(Bsz, S, NSA_HEADS, HEAD_DIM), pos)
    kv = nkv.reshape(Bsz, S, 6, NSA_KV_HEADS, HEAD_DIM)
    gates = jax.nn.sigmoid(ngate.astype(F32)).reshape(Bsz, S, NSA_HEADS, 3)
    y_b = nsa_mixer(q, kv[:, :, 0], kv[:, :, 1], rope(kv[:, :, 2], pos), kv[:, :, 3],
                    rope(kv[:, :, 4], pos), kv[:, :, 5], gates, cmp_pe, cmp_w1, cmp_w2)
    sqkv = sb_qkv.reshape(Bsz, S, 3, SB_HEADS, HEAD_DIM)
    y_c = stick_breaking_mixer(sqkv[:, :, 0], sqkv[:, :, 1], sqkv[:, :, 2])
    dqkv = dil_qkv.reshape(Bsz, S, 3, DIL_HEADS, HEAD_DIM)
    y_d = dilated_mixer(rope(dqkv[:, :, 0], pos), rope(dqkv[:, :, 1], pos), dqkv[:, :, 2])
    ys = jnp.stack([y_a.astype(F32), y_b.astype(F32), y_c.astype(F32), y_d.astype(F32)], axis=2)
    ys = rms_norm(ys, norm_g)
    return ys.reshape(Bsz, S, MIX_WIDTH).astype(h.dtype) @ w_out


def memory_cross_attention(h, mem, wq, wkv, wo):
    Bsz, S, _ = h.shape
    M = mem.shape[1]
    q = (h @ wq).reshape(Bsz, S, XA_HEADS, XA_HEAD_DIM)
    kv = (mem @ wkv).reshape(Bsz, M, 2, XA_HEADS, XA_HEAD_DIM)
    s = jnp.einsum('bshd,bmhd->bhsm', q, kv[:, :, 0]).astype(F32) * XA_HEAD_DIM ** -0.5
    p = jax.nn.softmax(s, axis=-1)
    o = jnp.einsum('bhsm,bmhd->bshd', p.astype(kv.dtype), kv[:, :, 1])
    return o.reshape(Bsz, S, XA_WIDTH) @ wo


def swiglu_ffn(h, wg, wu, wd):
    return (jax.nn.silu(h @ wg) * (h @ wu)) @ wd


def setup_inputs(seed: int = 0) -> dict:
    key = jax.random.key(seed)
    keys = iter(jax.random.split(key, 40))

    def nrm(shape, scale):
        return jax.random.normal(next(keys), shape, F32) * scale

    L, D = DEPTH, D_MODEL
    G, P, C = SSM_GROUPS, SSM_STATE, SSM_CH
    return {
        'x': nrm((BATCH, SEQ, D), 1.0),
        'mem': nrm((BATCH, MEM_LEN, D), 1.0),
        'ln_in_g': 1.0 + nrm((D,), 0.01),
        'ln_in_b': nrm((D,), 0.01),
        'w_in': nrm((L, D, IN_WIDTH), D ** -0.5),
        's5_lambda_re': -0.5 + nrm((L, G, P), 0.01),
        's5_lambda_im': math.pi * jnp.arange(P, dtype=F32) + nrm((L, G, P), 0.01),
        's5_log_dt': jax.random.uniform(next(keys), (L, G), F32, math.log(1e-3), math.log(1e-1)),
        's5_b_re': nrm((L, G, P, C), (2 * C) ** -0.5),
        's5_b_im': nrm((L, G, P, C), (2 * C) ** -0.5),
        's5_c_re': nrm((L, G, C, P), (2 * P) ** -0.5),
        's5_c_im': nrm((L, G, C, P), (2 * P) ** -0.5),
        's5_d': nrm((L, G, C), 1.0),
        's5_w_glu': nrm((L, SSM_WIDTH, SSM_WIDTH), SSM_WIDTH ** -0.5),
        's5_b_glu': nrm((L, SSM_WIDTH), 0.01),
        'nsa_cmp_pe': nrm((L, 2, CMP_LEN, HEAD_DIM), 0.02),
        'nsa_cmp_w1': nrm((L, 2, CMP_LEN * HEAD_DIM, CMP_HIDDEN), (CMP_LEN * HEAD_DIM) ** -0.5),
        'nsa_cmp_w2': nrm((L, 2, CMP_HIDDEN, HEAD_DIM), CMP_HIDDEN ** -0.5),
        'mix_norm_g': 1.0 + nrm((L, N_MIXERS, MIX_GROUP_WIDTH), 0.01),
        'w_out': nrm((L, MIX_WIDTH, D), MIX_WIDTH ** -0.5 * DEEPNORM_BETA),
        'ln1_g': 1.0 + nrm((L, D), 0.01),
        'ln1_b': nrm((L, D), 0.01),
        'xa_wq': nrm((L, D, XA_WIDTH), D ** -0.5),
        'xa_wkv': nrm((L, D, 2 * XA_WIDTH), D ** -0.5),
        'xa_wo': nrm((L, XA_WIDTH, D), XA_WIDTH ** -0.5 * DEEPNORM_BETA),
        'ln2_g': 1.0 + nrm((L, D), 0.01),
        'ln2_b': nrm((L, D), 0.01),
        'ffn_w_gate': nrm((L, D, FFN_HIDDEN), D ** -0.5),
        'ffn_w_up': nrm((L, D, FFN_HIDDEN), D ** -0.5),
        'ffn_w_down': nrm((L, FFN_HIDDEN, D), FFN_HIDDEN ** -0.5 * DEEPNORM_BETA),
        'ln3_g': 1.0 + nrm((L, D), 0.01),
        'ln3_b': nrm((L, D), 0.01),
    }


def reference(x, mem, ln_in_g, ln_in_b, w_in, s5_lambda_re, s5_lambda_im, s5_log_dt, s5_b_re, s5_b_im,
              s5_c_re, s5_c_im, s5_d, s5_w_glu, s5_b_glu, nsa_cmp_pe, nsa_cmp_w1, nsa_cmp_w2, mix_norm_g,
              w_out, ln1_g, ln1_b, xa_wq, xa_wkv, xa_wo, ln2_g, ln2_b, ffn_w_gate, ffn_w_up, ffn_w_down,
              ln3_g, ln3_b):
    h = layer_norm(x, ln_in_g, ln_in_b)
    for l in range(DEPTH):
        f = hybrid_mixer(h, w_in[l], s5_lambda_re[l], s5_lambda_im[l], s5_log_dt[l], s5_b_re[l], s5_b_im[l],
                         s5_c_re[l], s5_c_im[l], s5_d[l], s5_w_glu[l], s5_b_glu[l], nsa_cmp_pe[l],
                         nsa_cmp_w1[l], nsa_cmp_w2[l], mix_norm_g[l], w_out[l])
        h = layer_norm(DEEPNORM_ALPHA * h + f, ln1_g[l], ln1_b[l])
        f = memory_cross_attention(h, mem, xa_wq[l], xa_wkv[l], xa_wo[l])
        h = layer_norm(DEEPNORM_ALPHA * h + f, ln2_g[l], ln2_b[l])
        f = swiglu_ffn(h, ffn_w_gate[l], ffn_w_up[l], ffn_w_down[l])
        h = layer_norm(DEEPNORM_ALPHA * h + f, ln3_g[l], ln3_b[l])
    return h
```

```python
import math
from contextlib import ExitStack

import numpy as np
import ml_dtypes
import concourse.bass as bass
import concourse.mybir as mybir
from concourse.bass_utils import run_bass_kernel_spmd

F32 = mybir.dt.float32
BF16 = mybir.dt.bfloat16
AF = mybir.ActivationFunctionType
ALU = mybir.AluOpType
AX = mybir.AxisListType


class Res:
    __slots__ = ("name", "w", "r", "excl")

    def __init__(self, name=""):
        self.name = name
        self.excl = False
        self.w = None
        self.r = []


class KB:
    N_DMA_SEMS = 24

    def __init__(self, nc):
        self.nc = nc
        self.es = ExitStack()
        self.eng = {"pe": nc.tensor, "dve": nc.vector, "act": nc.scalar, "pool": nc.gpsimd, "sp": nc.sync}
        self.sem = {}
        self.cnt = {}
        self.semh = {}
        for e in self.eng:
            h = self.es.enter_context(nc.semaphore("s_" + e))
            self.sem[e] = h
            self.cnt[e] = 0
            self.semh[("c", e)] = h
        self.dsem = {}
        self.dval = {}
        self.dnext = {}
        for q in ("sp", "pool", "act"):
            self.dsem[q] = []
            for i in range(self.N_DMA_SEMS if q != "act" else 8):
                h = self.es.enter_context(nc.semaphore("d_%s_%d" % (q, i)))
                self.dsem[q].append(h)
                self.semh[("d", q, i)] = h
                self.dval[(q, i)] = 0
            self.dnext[q] = 0
        self.known = {e: {} for e in self.eng}
        self.ninstr = 0
        for h in self.semh.values():
            nc.gpsimd.sem_clear(h)
        nc.all_engine_barrier()

    def _wait(self, e, ev):
        if ev is None:
            return
        key, val = ev
        if key == ("c", e) and e == "pe":
            return
        k = self.known[e]
        if k.get(key, 0) >= val:
            return
        self.eng[e].wait_ge(self.semh[key], val)
        k[key] = val

    def _deps(self, e, reads, writes):
        for r in reads:
            self._wait(e, r.w)
        for w in writes:
            self._wait(e, w.w)
            for ev in w.r:
                self._wait(e, ev)

    def _commit(self, ev, reads, writes):
        for r in reads:
            if r in writes:
                continue
            r.r.append(ev)
            if len(r.r) > 12:
                d = {}
                for k, v in r.r:
                    d[k] = max(d.get(k, 0), v)
                r.r = list(d.items())
        for w in writes:
            w.w = ev
            w.r = []

    def op(self, e, fn, reads=(), writes=()):
        self._deps(e, reads, writes)
        ins = fn(self.eng[e])
        self.cnt[e] += 1
        ins.then_inc(self.sem[e], 1)
        ev = (("c", e), self.cnt[e])
        self._commit(ev, reads, writes)
        self.ninstr += 1
        return ev

    def dma(self, q, out, in_, reads=(), writes=(), **kw):
        i = self.dnext[q]
        self.dnext[q] = (i + 1) % len(self.dsem[q])
        key = ("d", q, i)
        if self.dval[(q, i)] > 0:
            self._wait(q, (key, self.dval[(q, i)]))
        self._deps(q, reads, writes)
        kw.setdefault("allow_slow_non_contiguous", True)
        ins = self.eng[q].dma_start(out=out, in_=in_, **kw)
        self.dval[(q, i)] += 16
        ins.then_inc(self.semh[key], 16)
        ev = (key, self.dval[(q, i)])
        self._commit(ev, reads, writes)
        self.ninstr += 1
        return ev

    def finish(self, final_res):
        for r in final_res:
            self._wait("sp", r.w)
        for q in self.dsem:
            for i in range(len(self.dsem[q])):
                if self.dval[(q, i)] > 0:
                    self._wait("sp", (("d", q, i), self.dval[(q, i)]))
        for e in ("pe", "dve", "act", "pool"):
            if self.cnt[e] > 0:
                self._wait("sp", (("c", e), self.cnt[e]))
        self.nc.all_engine_barrier()
        for h in self.semh.values():
            self.nc.gpsimd.sem_clear(h)
        self.nc.all_engine_barrier()

    def close(self):
        self.es.close()

    def barrier(self):
        evs = []
        for e in ("pe", "dve", "act", "pool", "sp"):
            if self.cnt[e] > 0:
                evs.append((("c", e), self.cnt[e]))
        for q in self.dsem:
            for i in range(len(self.dsem[q])):
                if self.dval[(q, i)] > 0:
                    evs.append((("d", q, i), self.dval[(q, i)]))
        for e in ("pe", "dve", "act", "pool", "sp"):
            for ev in evs:
                if ev[0] == ("c", e):
                    continue
                self._wait(e, ev)


def _res(x):
    return x.r if hasattr(x, "r") and not isinstance(x, Res) else x


_orig_op = KB.op
_orig_dma = KB.dma


def _op(self, e, fn, reads=(), writes=()):
    reads = [_res(x) for x in reads]
    writes = [_res(x) for x in writes]
    for r in reads:
        if r.excl and r not in writes:
            writes.append(r)
    return _orig_op(self, e, fn, reads, writes)


def _dma(self, q, out, in_, reads=(), writes=(), **kw):
    return _orig_dma(self, q, out, in_, [_res(x) for x in reads], [_res(x) for x in writes], **kw)


KB.op = _op
KB.dma = _dma


def pipeline(steps, nst):
    n = len(steps)
    for i in range(n + nst - 1):
        for s_ in range(nst):
            k = i - s_
            if 0 <= k < n and steps[k][s_] is not None:
                steps[k][s_]()


class TL:
    def __init__(self, h, name=""):
        self.h = h
        self.r = Res(name)

    def __getitem__(self, k):
        return self.h[k]


D = 2048
S = 2048
NT = S // 128
KC = D // 128
DEPTH = 2
MEM = 256
FFN = 5632
ALPHA = (2 * DEPTH) ** 0.25
IN_W = 4888
O_U, O_NQ, O_NKV, O_GATE, O_SB, O_DIL = 0, 512, 1024, 1792, 1816, 3352
FM_SRC = [128 * i for i in range(14)] + [1816 + 128 * j for j in range(8)] + [3352 + 128 * j for j in range(8)]
FM_ROPE = set([4, 5, 6, 7, 10, 12] + list(range(22, 30)))
FM_U, FM_NQ, FM_KCMP, FM_VCMP, FM_KSLC, FM_KWIN, FM_SBQ, FM_SBK, FM_DQ, FM_DK = 0, 4, 8, 9, 10, 12, 14, 18, 22, 26
TM_SRC = [(1408, 128), (1664, 128), (2840, 512), (4376, 512), (1792, 24)]
TM_VSLC, TM_VWIN, TM_SBV, TM_DV = 0, 128, 256, 768
TM_W = 1280

WNAMES = ["w_in", "s5_w_glu", "nsa_cmp_w1", "nsa_cmp_w2", "w_out", "xa_wq", "xa_wkv", "xa_wo",
          "ffn_w_gate", "ffn_w_up", "ffn_w_down"]
IN_SHAPES = {
    "x": (S, D), "mem": (MEM, D), "ln_in_g": (D,), "ln_in_b": (D,), "w_in": (DEPTH, D, IN_W),
    "s5_lambda_re": (DEPTH, 32, 64), "s5_lambda_im": (DEPTH, 32, 64), "s5_log_dt": (DEPTH, 32),
    "s5_b_re": (DEPTH, 32, 64, 16), "s5_b_im": (DEPTH, 32, 64, 16), "s5_c_re": (DEPTH, 32, 16, 64),
    "s5_c_im": (DEPTH, 32, 16, 64), "s5_d": (DEPTH, 32, 16), "s5_w_glu": (DEPTH, 512, 512),
    "s5_b_glu": (DEPTH, 512), "nsa_cmp_pe": (DEPTH, 2, 32, 64), "nsa_cmp_w1": (DEPTH, 2, 2048, 128),
    "nsa_cmp_w2": (DEPTH, 2, 128, 64), "mix_norm_g": (DEPTH, 4, 512), "w_out": (DEPTH, D, D),
    "ln1_g": (DEPTH, D), "ln1_b": (DEPTH, D), "xa_wq": (DEPTH, D, 512), "xa_wkv": (DEPTH, D, 1024),
    "xa_wo": (DEPTH, 512, D), "ln2_g": (DEPTH, D), "ln2_b": (DEPTH, D), "ffn_w_gate": (DEPTH, D, FFN),
    "ffn_w_up": (DEPTH, D, FFN), "ffn_w_down": (DEPTH, FFN, D), "ln3_g": (DEPTH, D), "ln3_b": (DEPTH, D),
}


def host_consts():
    c = {}
    c["c_ident"] = np.eye(128, dtype=np.float32)
    half = 32
    inv = (10000.0 ** (-np.arange(half, dtype=np.float32) / half)).astype(np.float32)
    pos = np.arange(S, dtype=np.float32)
    ang = pos[None, :] * inv[:, None]
    cos = np.cos(ang).astype(np.float32)
    sin = np.sin(ang).astype(np.float32)
    cos64 = np.concatenate([cos, cos], 0)
    sin64 = np.concatenate([-sin, sin], 0)
    c["c_cos"] = np.concatenate([cos64, cos64], 0)
    c["c_sin"] = np.concatenate([sin64, sin64], 0)
    sw = np.zeros((128, 128), np.float32)
    for m in range(128):
        k = m + 32 if (m % 64) < 32 else m - 32
        sw[k, m] = 1.0
    c["c_swap"] = sw
    k = np.arange(128)[:, None, None]
    i = np.arange(4)[None, :, None]
    q = np.arange(512)[None, None, :]
    c["c_maskS"] = ((128 * i + k) < q).astype(np.float32)
    c["c_maskC"] = ((128 * i + k) <= q).astype(np.float32)
    c["c_maskCn"] = 1.0 - c["c_maskC"]
    jj = np.arange(128)[:, None]
    kk = np.arange(128)[None, :]
    c["c_negU"] = -(jj >= kk).astype(np.float32)
    c["c_negOnes"] = -np.ones((128, 128), np.float32)
    c["c_ones"] = np.ones((128, 128), np.float32)
    c["c_maskD"] = np.stack([(jj >= kk), (jj <= kk)], 1).astype(np.float32)
    cc = np.arange(128)[:, None]
    tt_ = np.arange(S)[None, :]
    c["c_maskCmp"] = ((16 * cc + 31 <= tt_) & (cc < 127)).astype(np.float32)
    ci = np.arange(128)[:, None] * 16
    sj = np.arange(32)[None, :] * 64
    ov = np.clip(np.minimum(ci + 32, sj + 64) - np.maximum(ci, sj), 0, None) / 32.0
    ov[127] = 0
    c["c_overlap"] = ov.astype(np.float32)
    t = np.arange(S)[:, None]
    n = np.arange(32)[None, :]
    cur = t // 64
    invalid = n * 64 > t
    forced = ((n == 0) | (n == cur) | (n == cur - 1)) & ~invalid
    c["c_selMul"] = (~(invalid | forced)).astype(np.float32)
    c["c_selAdd"] = np.where(forced, 1e4, np.where(invalid, -1e4, 0.0)).astype(np.float32)
    eb = np.zeros((32, 16, 128), np.float32)
    for kb_ in range(16):
        for k in range(128):
            eb[2 * kb_ + k // 64, kb_, k] = 32768.0
    c["c_Ebig"] = eb
    c["c_iota"] = np.tile(np.arange(S, dtype=np.float32)[None, :], (128, 1))
    pp = np.arange(128)
    c["c_mask2"] = np.stack([((pp // 16) % 2 == 0), ((pp // 16) % 2 == 1)], 1).astype(np.float32)
    m3 = ((pp[:, None] // 64) == ((pp[None, :] // 16) % 2)).astype(np.float32)
    c["c_mask2z"] = c["c_mask2"] * (pp[:, None] >= 96)
    c["c_mask3"] = m3
    c["c_mask3n"] = -m3
    c["c_cos_cmp"] = np.ascontiguousarray(c["c_cos"][:, 31::16][:, :127])
    c["c_sin_cmp"] = np.ascontiguousarray(c["c_sin"][:, 31::16][:, :127])
    return c


class Prog:
    def __init__(self, ext_in=(), ext_out=()):
        self.nc = bass.Bass("TRN2", target_bir_lowering=False)
        self.kb = KB(self.nc)
        self.ext_in = set(ext_in)
        self.ext_out = set(ext_out)
        self.d = {}
        self.kinds = {}
        self.uid = 0
        self.es = self.kb.es

    def dram(self, name, shape, dtype, kind=None):
        if name in self.d:
            return self.d[name]
        if kind is None:
            kind = "ExternalInput" if name in self.ext_in else ("ExternalOutput" if name in self.ext_out else "Internal")
        t = TL(self.nc.dram_tensor(name, list(shape), dtype, kind=kind).ap(), name)
        self.d[name] = t
        self.kinds[name] = kind
        return t

    def sb(self, es, name, shape, dtype):
        self.uid += 1
        name = "%s_u%d" % (name, self.uid)
        return TL(es.enter_context(self.nc.sbuf_tensor(name, list(shape), dtype)), name)

    def ps(self, es, name, shape, dtype=F32):
        self.uid += 1
        name = "%s_u%d" % (name, self.uid)
        t = TL(es.enter_context(self.nc.psum_tensor(name, list(shape), dtype)), name)
        t.r.excl = True
        return t


def setup_globals(p):
    kb = p.kb
    p.hTd = p.dram("hT_stream", (D, S), BF16)
    p.ident = p.sb(p.es, "ident", [128, 128], BF16)
    cid = p.dram("c_ident", (128, 128), F32, kind="ExternalInput")
    kb.dma("pool", p.ident[:], cid[:, :], writes=[p.ident])
    p.h = p.dram("h_stream", (S, D), F32)
    p.hres = [Res("h%d" % i) for i in range(NT)]


def ln_tiles(p, es, pfx):
    w = {}
    w["st"] = p.sb(es, pfx + "st", [128, 24], F32)
    w["mv"] = p.sb(es, pfx + "mv", [128, 2], F32)
    w["hbs"] = [p.sb(es, pfx + "hb%d" % i, [128, D], BF16) for i in range(2)]
    w["pT"] = p.ps(es, pfx + "pT", [128, D], BF16)
    w["g"] = p.sb(es, pfx + "g", [128, D], F32)
    w["b"] = p.sb(es, pfx + "b", [128, D], F32)
    w["hTs"] = [p.sb(es, pfx + "hTs%d" % i, [128, KC, 128], BF16) for i in range(2)]
    return w


def ln_load_gb(p, w, g_ap, b_ap):
    p.kb.dma("sp", w["g"][:], g_ap.partition_broadcast(128), writes=[w["g"]])
    p.kb.dma("sp", w["b"][:], b_ap.partition_broadcast(128), writes=[w["b"]])


def emit_ln_a(p, w, v, tt, eps, out_dram=None, out_res=None, write_hT=True, h_store=True):
    kb = p.kb
    st, mv, g, b = w["st"], w["mv"], w["g"], w["b"]
    hb = w["hbs"][tt % 2]
    for c in range(4):
        kb.op("dve", lambda e: e.bn_stats(out=st[:, 6 * c:6 * c + 6], in_=v[:, 512 * c:512 * c + 512]), reads=[v], writes=[st])
    kb.op("dve", lambda e: e.bn_aggr(out=mv[:], in_=st[:]), reads=[st], writes=[mv])
    kb.op("act", lambda e: e.activation(out=mv[:, 1:2], in_=mv[:, 1:2], func=AF.Sqrt, bias=eps), reads=[mv], writes=[mv])
    kb.op("dve", lambda e: e.reciprocal(out=mv[:, 1:2], in_=mv[:, 1:2]), reads=[mv], writes=[mv])
    kb.op("dve", lambda e: e.tensor_scalar(out=v[:], in0=v[:], scalar1=mv[:, 0:1], scalar2=mv[:, 1:2], op0=ALU.subtract, op1=ALU.mult), reads=[v, mv], writes=[v])
    kb.op("pool", lambda e: e.tensor_tensor(out=v[:], in0=v[:], in1=g[:], op=ALU.mult), reads=[v, g], writes=[v])
    kb.op("pool", lambda e: e.tensor_tensor(out=v[:], in0=v[:], in1=b[:], op=ALU.add), reads=[v, b], writes=[v])
    if h_store:
        kb.dma("sp", p.h[tt * 128:(tt + 1) * 128, :], v[:], reads=[v], writes=[p.hres[tt]])
    if out_dram is not None:
        kb.dma("sp", out_dram[tt * 128:(tt + 1) * 128, :], v[:], reads=[v], writes=[out_res])
    if write_hT:
        kb.op("act", lambda e: e.activation(out=hb[:], in_=v[:], func=AF.Copy), reads=[v], writes=[hb])


def emit_ln_b(p, w, tt):
    kb = p.kb
    hb, pT = w["hbs"][tt % 2], w["pT"]
    for c in range(KC):
        kb.op("pe", lambda e: e.transpose(pT[:, 128 * c:128 * c + 128], hb[:, 128 * c:128 * c + 128], p.ident[:]), reads=[hb, p.ident], writes=[pT])
    hs = w["hTs"][tt % 2]
    kb.op("dve", lambda e: e.tensor_copy(out=hs[:], in_=pT[:].rearrange("p (c t) -> p c t", c=KC)), reads=[pT], writes=[hs])
    kb.dma("sp", p.hTd[:, tt * 128:(tt + 1) * 128].rearrange("(c p) t -> p c t", p=128), hs[:], reads=[hs], writes=[p.hTd])


def emit_ln(p, w, v, tt, eps, out_dram=None, out_res=None, write_hT=True, h_store=True):
    emit_ln_a(p, w, v, tt, eps, out_dram, out_res, write_hT, h_store)
    if write_hT:
        emit_ln_b(p, w, tt)


def load_hT(p, es, name="hT"):
    t = p.sb(es, name, [128, KC, S], BF16)
    for c in range(0, KC, 4):
        p.kb.dma("sp", t[:, c:c + 4, :], p.hTd[c * 128:(c + 4) * 128, :].rearrange("(c p) t -> p c t", p=128), reads=[p.hTd], writes=[t])
    return t


def convert_weights(p, names=WNAMES, layers=range(DEPTH)):
    kb = p.kb
    if not hasattr(p, "wb"):
        p.wb = {}
    for l in layers:
        for n in names:
            shp = IN_SHAPES[n][1:]
            src = p.dram(n, IN_SHAPES[n], F32, kind="ExternalInput")
            dst = p.dram("bf_%s_%d" % (n, l), shp, BF16)
            p.wb[(n, l)] = dst
            s_l = src[l]
            d_l = dst.h
            if len(shp) == 3:
                s_l = s_l.rearrange("a b c -> (a b) c")
                d_l = d_l.rearrange("a b c -> (a b) c")
            rows, cols = s_l.shape
            step = max(1, (1024 * 1024) // cols)
            r0 = 0
            while r0 < rows:
                r1 = min(rows, r0 + step)
                kb.dma("pool", d_l[r0:r1, :], s_l[r0:r1, :], writes=[dst])
                r0 = r1


def phase_ln0(p):
    kb = p.kb
    x = p.dram("x", (S, D), F32, kind="ExternalInput")
    g = p.dram("ln_in_g", (D,), F32, kind="ExternalInput")
    b = p.dram("ln_in_b", (D,), F32, kind="ExternalInput")
    with ExitStack() as es:
        w = ln_tiles(p, es, "l0")
        ln_load_gb(p, w, g.h, b.h)
        vs = [p.sb(es, "l0v%d" % i, [128, D], F32) for i in range(2)]
        for tt in range(NT):
            v = vs[tt % 2]
            kb.dma("sp", v[:], x[tt * 128:(tt + 1) * 128, :], writes=[v])
            emit_ln(p, w, v, tt, 1e-5)
        kb.barrier()


def phase_inproj(p, l, do_fm=True, do_tm=True, max_groups=99):
    kb = p.kb
    wb = p.wb[("w_in", l)]
    fm = p.dram("fm", (30 * 128, S), BF16)
    tm = p.dram("tm", (S, TM_W), BF16)
    gates = p.dram("gates", (S, 24), F32)
    ccos = p.dram("c_cos", (128, S), F32, kind="ExternalInput")
    csin = p.dram("c_sin", (128, S), F32, kind="ExternalInput")
    cswap = p.dram("c_swap", (128, 128), F32, kind="ExternalInput")
    with ExitStack() as es:
        p.hT = load_hT(p, es)
        cos = p.sb(es, "a_cos", [128, S], F32)
        sin = p.sb(es, "a_sin", [128, S], F32)
        swp = p.sb(es, "a_swp", [128, 128], BF16)
        kb.dma("sp", cos[:], ccos[:, :], writes=[cos])
        kb.dma("sp", sin[:], csin[:, :], writes=[sin])
        kb.dma("pool", swp[:], cswap[:, :], writes=[swp])
        wts = [p.sb(es, "a_w%d" % i, [128, KC, 512], BF16) for i in range(2)]
        pacc = [p.ps(es, "a_pacc%d" % i, [128, 512]) for i in range(2)]
        psw = [p.ps(es, "a_psw%d" % i, [128, 512]) for i in range(2)]
        xb = [p.sb(es, "a_xb%d" % i, [128, 512], BF16) for i in range(2)]
        t1 = [p.sb(es, "a_t1%d" % i, [128, 512], F32) for i in range(2)]
        t2 = [p.sb(es, "a_t2%d" % i, [128, 512], F32) for i in range(2)]
        outs = [p.sb(es, "a_out%d" % i, [128, S], BF16) for i in range(2)]
        groups = []
        i = 0
        while i < len(FM_SRC):
            j = i
            while j + 1 < len(FM_SRC) and j + 1 - i < 4 and FM_SRC[j + 1] == FM_SRC[j] + 128:
                j += 1
            groups.append((i, j - i + 1))
            i = j + 1
        steps = []
        for gi, (c0, n) in enumerate(groups if do_fm else []):
            if gi >= max_groups:
                break
            for ci in range(n):
                for tg in range(4):
                    def mk(gi=gi, c0=c0, n=n, ci=ci, tg=tg, cnt=len(steps)):
                        wt = wts[gi % 2]
                        src0 = FM_SRC[c0]
                        ch = c0 + ci
                        ot = outs[ch % 2]
                        pa = pacc[cnt % 2]
                        osl = ot[:, 512 * tg:512 * tg + 512]
                        rope = ch in FM_ROPE
                        x_b, ps2, a1, a2 = xb[cnt % 2], psw[cnt % 2], t1[cnt % 2], t2[cnt % 2]

                        def fin():
                            if tg == 3:
                                kb.dma("sp", fm[ch * 128:(ch + 1) * 128, :], ot[:], reads=[ot], writes=[fm])

                        def s1():
                            if ci == 0 and tg == 0:
                                kb.dma("sp", wt[:, :, 0:128 * n], wb[:, src0:src0 + 128 * n].rearrange("(c p) n -> p c n", p=128), reads=[wb], writes=[wt])
                            for k in range(KC):
                                kb.op("pe", lambda e: e.matmul(pa[:], wt[:, k, 128 * ci:128 * ci + 128], p.hT[:, k, 512 * tg:512 * tg + 512], start=(k == 0), stop=(k == KC - 1)), reads=[wt, p.hT], writes=[pa])
                            if rope:
                                kb.op("act", lambda e: e.activation(out=x_b[:], in_=pa[:], func=AF.Copy), reads=[pa], writes=[x_b])
                            else:
                                if cnt % 2 == 0:
                                    kb.op("act", lambda e: e.activation(out=osl, in_=pa[:], func=AF.Copy), reads=[pa], writes=[ot])
                                else:
                                    kb.op("dve", lambda e: e.tensor_copy(out=osl, in_=pa[:]), reads=[pa], writes=[ot])
                                fin()

                        def s2():
                            kb.op("pe", lambda e: e.matmul(ps2[:], swp[:], x_b[:], start=True, stop=True), reads=[swp, x_b], writes=[ps2])
                            kb.op("dve", lambda e: e.tensor_tensor(out=a1[:], in0=pa[:], in1=cos[:, 512 * tg:512 * tg + 512], op=ALU.mult), reads=[pa, cos], writes=[a1])
                            kb.op("dve", lambda e: e.tensor_tensor(out=a2[:], in0=ps2[:], in1=sin[:, 512 * tg:512 * tg + 512], op=ALU.mult), reads=[ps2, sin], writes=[a2])
                            kb.op("dve", lambda e: e.tensor_tensor(out=osl, in0=a1[:], in1=a2[:], op=ALU.add), reads=[a1, a2], writes=[ot])
                            fin()
                        return (s1, s2 if rope else None)
                    steps.append(mk())
        pipeline(steps, 2)
        kb.barrier()
    with ExitStack() as es:
        p.hT = load_hT(p, es)
        wv = p.sb(es, "a_wv", [128, KC, TM_W + 24], BF16)
        off = 0
        for (s0, wd) in TM_SRC:
            kb.dma("sp", wv[:, :, off:off + wd], wb[:, s0:s0 + wd].rearrange("(c p) n -> p c n", p=128), reads=[wb], writes=[wv])
            off += wd
        pt = [[p.ps(es, "a_pt%d_%d" % (i, j), [128, 512]) for j in range(4)] for i in range(2)]
        ot = [p.sb(es, "a_ot%d" % i, [128, TM_W], BF16) for i in range(2)]
        gt = [p.sb(es, "a_gt%d" % i, [128, 24], F32) for i in range(2)]
        segs = [(0, 256, 0), (256, 512, 1), (768, 512, 2)]
        for tt in range(NT if do_tm else 0):
            pp = pt[tt % 2]
            o = ot[tt % 2]
            g_ = gt[tt % 2]
            for k in range(KC):
                lhsT = p.hT[:, k, 128 * tt:128 * tt + 128]
                for (c0, wd, pi) in segs:
                    kb.op("pe", lambda e: e.matmul(pp[pi][:, 0:wd], lhsT, wv[:, k, c0:c0 + wd], start=(k == 0), stop=(k == KC - 1)), reads=[wv, p.hT], writes=[pp[pi]])
                kb.op("pe", lambda e: e.matmul(pp[3][:, 0:24], lhsT, wv[:, k, TM_W:TM_W + 24], start=(k == 0), stop=(k == KC - 1)), reads=[wv, p.hT], writes=[pp[3]])
            kb.op("act", lambda e: e.activation(out=o[:, 0:256], in_=pp[0][:, 0:256], func=AF.Copy), reads=[pp[0]], writes=[o])
            kb.op("act", lambda e: e.activation(out=g_[:], in_=pp[3][:, 0:24], func=AF.Sigmoid), reads=[pp[3]], writes=[g_])
            kb.op("dve", lambda e: e.tensor_copy(out=o[:, 256:768], in_=pp[1][:]), reads=[pp[1]], writes=[o])
            kb.op("dve", lambda e: e.tensor_copy(out=o[:, 768:1280], in_=pp[2][:]), reads=[pp[2]], writes=[o])
            kb.dma("sp", tm[tt * 128:(tt + 1) * 128, :], o[:], reads=[o], writes=[tm])
            kb.dma("sp", gates[tt * 128:(tt + 1) * 128, :], g_[:], reads=[g_], writes=[gates])
        kb.barrier()


def go_tiles(p, es, pfx, l, grp):
    w = {}
    w["ss"] = p.sb(es, pfx + "ss", [128, 2], F32)
    w["junk"] = p.sb(es, pfx + "junk", [128, 512], F32)
    w["yb"] = p.sb(es, pfx + "yb", [128, 512], BF16)
    w["pt"] = p.ps(es, pfx + "pt", [128, 1024], BF16)
    w["yt"] = [p.sb(es, pfx + "yt%d" % i, [128, 4, 128], BF16) for i in range(2)]
    w["gain"] = p.sb(es, pfx + "gain", [128, 512], F32)
    g = p.dram("mix_norm_g", IN_SHAPES["mix_norm_g"], F32, kind="ExternalInput")
    p.kb.dma("sp", w["gain"][:], g[l, grp, :].partition_broadcast(128), writes=[w["gain"]])
    w["cnt"] = 0
    return w


def emit_group_out(p, w, y_ap, y_tl, grp, tt):
    kb = p.kb
    ysT = p.dram("ysT", (D, S), BF16)
    ss, junk, yb, pt, gain = w["ss"], w["junk"], w["yb"], w["pt"], w["gain"]
    yt = w["yt"][w["cnt"] % 2]
    w["cnt"] += 1
    kb.op("act", lambda e: e.activation(out=junk[:], in_=y_ap, func=AF.Square, accum_out=ss[:, 0:1]), reads=[y_tl], writes=[junk, ss])
    kb.op("act", lambda e: e.activation(out=ss[:, 1:2], in_=ss[:, 0:1], func=AF.Sqrt, bias=1e-6, scale=1.0 / 512), reads=[ss], writes=[ss])
    kb.op("dve", lambda e: e.reciprocal(out=ss[:, 1:2], in_=ss[:, 1:2]), reads=[ss], writes=[ss])
    kb.op("dve", lambda e: e.scalar_tensor_tensor(out=yb[:], in0=y_ap, scalar=ss[:, 1:2], in1=gain[:], op0=ALU.mult, op1=ALU.mult), reads=[y_tl, ss, gain], writes=[yb])
    for c in range(4):
        kb.op("pe", lambda e: e.transpose(pt[:, 128 * c:128 * c + 128], yb[:, 128 * c:128 * c + 128], p.ident[:]), reads=[yb, p.ident], writes=[pt])
    kb.op("dve", lambda e: e.tensor_copy(out=yt[:], in_=pt[:, 0:512].rearrange("p (c t) -> p c t", c=4)), reads=[pt], writes=[yt])
    kb.dma("sp", ysT[grp * 512:(grp + 1) * 512, tt * 128:(tt + 1) * 128].rearrange("(c p) t -> p c t", p=128), yt[:], reads=[yt], writes=[ysT])


def load_const_bf16(p, es, name, shape):
    src = p.dram(name, shape, F32, kind="ExternalInput")
    t = p.sb(es, "k_" + name, list(shape), BF16)
    p.kb.dma("pool", t[:], src.h, writes=[t])
    return t


def phase_sb(p, l, heads=range(8), groups=range(4)):
    kb = p.kb
    fm = p.dram("fm", (30 * 128, S), BF16)
    tm = p.dram("tm", (S, TM_W), BF16)
    with ExitStack() as es:
        maskS = load_const_bf16(p, es, "c_maskS", (128, 4, 512))
        negU = load_const_bf16(p, es, "c_negU", (128, 128))
        negO = load_const_bf16(p, es, "c_negOnes", (128, 128))
        V = p.sb(es, "sb_V", [128, NT, 512], BF16)
        kb.dma("sp", V[:], tm[:, TM_SBV:TM_SBV + 512].rearrange("(t p) f -> p t f", p=128), reads=[tm], writes=[V])
        yc = p.sb(es, "sb_yc", [128, NT, 512], F32)
        kb.op("pool", lambda e: e.memset(yc[:], 0.0), writes=[yc])
        qTs = [p.sb(es, "sb_q%d" % i, [128, S], BF16) for i in range(2)]
        kTs = [p.sb(es, "sb_k%d" % i, [128, S], BF16) for i in range(2)]
        psA = [p.ps(es, "sb_pA%d" % i, [128, 512]) for i in range(2)]
        psB = [p.ps(es, "sb_pB%d" % i, [128, 512]) for i in range(2)]
        psO = [p.ps(es, "sb_pO%d" % i, [128, 512]) for i in range(2)]
        e_t = [p.sb(es, "sb_e%d" % i, [128, 512], F32) for i in range(2)]
        sp_t = [p.sb(es, "sb_sp%d" % i, [128, 512], BF16) for i in range(2)]
        w_t = [p.sb(es, "sb_w%d" % i, [128, 512], BF16) for i in range(2)]
        Ssum = [p.sb(es, "sb_S%d" % i, [128, 512], BF16) for i in range(2)]
        sp3 = sp_t + [p.sb(es, "sb_sp2", [128, 512], BF16)]
        w3 = w_t + [p.sb(es, "sb_w2", [128, 512], BF16)]
        steps = []
        hg = 0
        loaded = None
        for h in heads:
            hp = h // 2
            base = 64 * (h % 2)
            qT, kT = qTs[hp % 2], kTs[hp % 2]
            need_load = loaded != hp
            loaded = hp
            for G in groups:
                Ss = Ssum[hg % 2]
                hg += 1
                for kb_ in range(4 * G + 3, -1, -1):
                    def mk(h=h, hp=hp, base=base, qT=qT, kT=kT, G=G, kb_=kb_, Ss=Ss, idx=len(steps), load=need_load):
                        i = kb_ - 4 * G
                        diag = i >= 0
                        first = kb_ == 4 * G + 3
                        pa, pb, po = psA[idx % 2], psB[idx % 2], psO[idx % 2]
                        et, st_, wt = e_t[idx % 2], sp3[idx % 3], w3[idx % 3]
                        qs = qT[base:base + 64, 512 * G:512 * G + 512]
                        ks = kT[base:base + 64, 128 * kb_:128 * kb_ + 128]

                        def s1():
                            if load:
                                kb.dma("sp", qT[:], fm[(FM_SBQ + hp) * 128:(FM_SBQ + hp + 1) * 128, :], reads=[fm], writes=[qT])
                                kb.dma("sp", kT[:], fm[(FM_SBK + hp) * 128:(FM_SBK + hp + 1) * 128, :], reads=[fm], writes=[kT])
                                kb.op("act", lambda e: e.mul(qT[:], qT[:], 0.125), reads=[qT], writes=[qT])
                            kb.op("pe", lambda e: e.matmul(pa[:], ks, qs, start=True, stop=True), reads=[kT, qT], writes=[pa])
                            kb.op("act", lambda e: e.activation(out=et[:], in_=pa[:], func=AF.Exp), reads=[pa], writes=[et])
                            kb.op("act", lambda e: e.activation(out=st_[:], in_=et[:], func=AF.Ln, bias=1.0), reads=[et], writes=[st_])
                            if diag:
                                kb.op("pool", lambda e: e.tensor_tensor(out=st_[:], in0=st_[:], in1=maskS[:, i, :], op=ALU.mult), reads=[st_, maskS], writes=[st_])

                        def s2():
                            kb.op("pe", lambda e: e.matmul(pb[:], ks, qs, start=True, stop=False), reads=[kT, qT], writes=[pb])
                            kb.op("pe", lambda e: e.matmul(pb[:], negU[:], st_[:], start=False, stop=first), reads=[negU, st_], writes=[pb])
                            if not first:
                                kb.op("pe", lambda e: e.matmul(pb[:], negO[:], Ss[:], start=False, stop=True), reads=[negO, Ss], writes=[pb])
                            if first:
                                kb.op("dve", lambda e: e.tensor_copy(out=Ss[:], in_=st_[:]), reads=[st_], writes=[Ss])
                            elif kb_ > 0:
                                kb.op("dve", lambda e: e.tensor_tensor(out=Ss[:], in0=Ss[:], in1=st_[:], op=ALU.add), reads=[Ss, st_], writes=[Ss])
                            kb.op("act", lambda e: e.activation(out=wt[:], in_=pb[:], func=AF.Exp), reads=[pb], writes=[wt])
                            if diag:
                                kb.op("pool", lambda e: e.tensor_tensor(out=wt[:], in0=wt[:], in1=maskS[:, i, :], op=ALU.mult), reads=[wt, maskS], writes=[wt])

                        def s3():
                            j0 = max(i, 0)
                            for j in range(j0, 4):
                                kb.op("pe", lambda e: e.matmul(po[:, 64 * j:64 * j + 64], wt[:, 128 * j:128 * j + 128], V[:, kb_, 64 * h:64 * h + 64], start=True, stop=True), reads=[wt, V], writes=[po])
                            ysl = yc[:, 4 * G + j0:4 * G + 4, 64 * h:64 * h + 64]
                            kb.op("dve", lambda e: e.tensor_tensor(out=ysl, in0=ysl, in1=po[:, 64 * j0:256].rearrange("p (j d) -> p j d", d=64), op=ALU.add), reads=[yc, po], writes=[yc])
                        return (s1, s2, s3)
                    steps.append(mk())
                    need_load = False
        pipeline(steps, 3)
        dbg = p.d.get("dbg_yc")
        if dbg is not None:
            kb.dma("sp", dbg.h.rearrange("(t p) f -> p t f", p=128), yc[:], reads=[yc], writes=[dbg])
        gw = go_tiles(p, es, "sbgo", l, 2)
        for tt in range(NT):
            emit_group_out(p, gw, yc[:, tt, :], yc, 2, tt)
        kb.barrier()


DIL_CFG = ((1, 2048), (4, 512), (16, 128))


def phase_dil(p, l, branches=range(3), heads=range(8), max_pairs=999):
    kb = p.kb
    fm = p.dram("fm", (30 * 128, S), BF16)
    tm = p.dram("tm", (S, TM_W), BF16)
    dacc = [p.dram("dil_acc%d" % c, (S, 520), F32) for c in range(3)]
    with ExitStack() as es:
        maskD = load_const_bf16(p, es, "c_maskD", (128, 2, 128))
        qT = p.sb(es, "dl_q", [128, 4, S], BF16)
        kT = p.sb(es, "dl_k", [128, 4, S], BF16)
        kb.dma("sp", qT[:], fm[FM_DQ * 128:(FM_DQ + 4) * 128, :].rearrange("(c p) t -> p c t", p=128), reads=[fm], writes=[qT])
        kb.dma("sp", kT[:], fm[FM_DK * 128:(FM_DK + 4) * 128, :].rearrange("(c p) t -> p c t", p=128), reads=[fm], writes=[kT])
        kb.op("act", lambda e: e.mul(qT[:], qT[:], 0.125), reads=[qT], writes=[qT])
        Vp = [p.sb(es, "dl_v%d" % i, [128, 8, 65], BF16) for i in range(3)]
        for v in Vp:
            kb.op("pool", lambda e: e.memset(v[:, :, 64:65], 1.0), writes=[v])
        psS = [p.ps(es, "dl_pS%d" % i, [128, 512]) for i in range(3)]
        psO = [p.ps(es, "dl_pO%d" % i, [128, 512]) for i in range(2)]
        pt = [p.sb(es, "dl_pt%d" % i, [128, 256], BF16) for i in range(3)]
        Oall = [p.sb(es, "dl_O%d" % i, [128, 520], F32) for i in range(2)]
        vcnt = 0
        npair = 0
        steps = []
        for c in branches:
            dil, L = DIL_CFG[c]
            vsrc = tm[:, TM_DV:TM_DV + 512].rearrange("(l r) f -> r l f", r=dil)
            dst = dacc[c].h.rearrange("(l r) f -> r l f", r=dil)
            for r in range(dil):
                vt = {}
                for b in range(L // 128):
                    if npair >= max_pairs:
                        break
                    npair += 1
                    kbs = [x for x in (b - 1, b) if x >= 0]
                    loads = []
                    for x in kbs:
                        if x not in vt:
                            vt[x] = Vp[vcnt % 3]
                            vcnt += 1
                            loads.append((vt[x], vsrc[r, 128 * x:128 * x + 128, :].rearrange("p (h d) -> p h d", d=64)))
                    oa = Oall[npair % 2]
                    vts = [vt[x] for x in kbs]
                    hl = list(heads)
                    for h in hl:
                        def mk(c=c, dil=dil, r=r, b=b, kbs=kbs, vts=vts, oa=oa, h=h, idx=len(steps), loads=(loads if h == hl[0] else []), lasth=(h == hl[-1]), dst=dst):
                            hp, base = h // 2, 64 * (h % 2)
                            qv = qT[base:base + 64, hp, :].rearrange("p (l r) -> p r l", r=dil)[:, r, 128 * b:128 * b + 128]
                            kvw = kT[base:base + 64, hp, :].rearrange("p (l r) -> p r l", r=dil)
                            ps, pp = psS[idx % 3], pt[idx % 3]
                            n = len(kbs)
                            po = psO[(h // 4) % 2]
                            o0 = 65 * (h % 4)

                            def s1():
                                for (t, src) in loads:
                                    kb.dma("sp", t[:, :, 0:64], src, reads=[tm], writes=[t])
                                for ix, x in enumerate(kbs):
                                    kb.op("pe", lambda e: e.matmul(ps[:, 128 * ix:128 * ix + 128], kvw[:, r, 128 * x:128 * x + 128], qv, start=True, stop=True), reads=[kT, qT], writes=[ps])
                                kb.op("act", lambda e: e.activation(out=pp[:, 0:128 * n], in_=ps[:, 0:128 * n], func=AF.Exp), reads=[ps], writes=[pp])
                                m0 = 2 - n
                                kb.op("pool", lambda e: e.tensor_tensor(out=pp[:, 0:128 * n], in0=pp[:, 0:128 * n], in1=maskD[:, m0:2, :].rearrange("p a q -> p (a q)"), op=ALU.mult), reads=[pp, maskD], writes=[pp])

                            def s2():
                                for ix, x in enumerate(kbs):
                                    kb.op("pe", lambda e: e.matmul(po[:, o0:o0 + 65], pp[:, 128 * ix:128 * ix + 128], vts[ix][:, h, :], start=(ix == 0), stop=(ix == n - 1)), reads=[pp, vts[ix]], writes=[po])
                                if h % 4 == 3 or lasth:
                                    hh = h // 4
                                    kb.op("dve", lambda e: e.tensor_copy(out=oa[:, 260 * hh:260 * hh + 260], in_=po[:, 0:260]), reads=[po], writes=[oa])
                                if lasth:
                                    kb.dma("sp", dst[r, 128 * b:128 * b + 128, :], oa[:], reads=[oa], writes=[dacc[c]])
                            return (s1, s2)
                        steps.append(mk())
                    for x in list(vt):
                        if x < b:
                            del vt[x]
        pipeline(steps, 2)
        kb.barrier()
    with ExitStack() as es:
        gw = go_tiles(p, es, "dlgo", l, 3)
        acc = [[p.sb(es, "dl_a%d_%d" % (i, c), [128, 8, 65], F32) for c in range(3)] for i in range(2)]
        rd = p.sb(es, "dl_rd", [128, 8], F32)
        yd = [p.sb(es, "dl_y%d" % i, [128, 512], F32) for i in range(2)]
        for tt in range(NT):
            a = acc[tt % 2]
            y = yd[tt % 2]
            for c in range(3):
                kb.dma("sp", a[c][:], dacc[c][128 * tt:128 * tt + 128, :].rearrange("p (h d) -> p h d", d=65), reads=[dacc[c]], writes=[a[c]])
            kb.op("dve", lambda e: e.tensor_tensor(out=a[0][:], in0=a[0][:], in1=a[1][:], op=ALU.add), reads=[a[0], a[1]], writes=[a[0]])
            kb.op("dve", lambda e: e.tensor_tensor(out=a[0][:], in0=a[0][:], in1=a[2][:], op=ALU.add), reads=[a[0], a[2]], writes=[a[0]])
            kb.op("dve", lambda e: e.reciprocal(out=rd[:], in_=a[0][:, :, 64]), reads=[a[0]], writes=[rd])
            kb.op("dve", lambda e: e.tensor_tensor(out=y[:].rearrange("p (h d) -> p h d", d=64), in0=a[0][:, :, 0:64], in1=rd[:].unsqueeze(2).to_broadcast([128, 8, 64]), op=ALU.mult), reads=[a[0], rd], writes=[y])
            dbg = p.d.get("dbg_yd")
            if dbg is not None:
                kb.dma("sp", dbg[128 * tt:128 * tt + 128, :], y[:], reads=[y], writes=[dbg])
            emit_group_out(p, gw, y[:], y, 3, tt)
        kb.barrier()


def phase_nsa(p, l, heads=range(8), do_slc=True, do_win=True):
    kb = p.kb
    fm = p.dram("fm", (30 * 128, S), BF16)
    tm = p.dram("tm", (S, TM_W), BF16)
    gates = p.dram("gates", (S, 24), F32)
    w1 = p.wb[("nsa_cmp_w1", l)]
    w2 = p.wb[("nsa_cmp_w2", l)]
    pe = p.dram("nsa_cmp_pe", IN_SHAPES["nsa_cmp_pe"], F32, kind="ExternalInput")
    with ExitStack() as es:
        swp = load_const_bf16(p, es, "c_swap", (128, 128))
        maskC = load_const_bf16(p, es, "c_maskC", (128, 4, 512))
        maskCn = load_const_bf16(p, es, "c_maskCn", (128, 4, 512))
        maskCmp = load_const_bf16(p, es, "c_maskCmp", (128, S))
        Ebig = load_const_bf16(p, es, "c_Ebig", (32, 16, 128))
        ccos = p.sb(es, "ns_cos", [128, 127], F32)
        csin = p.sb(es, "ns_sin", [128, 127], F32)
        kb.dma("sp", ccos[:], p.dram("c_cos_cmp", (128, 127), F32, kind="ExternalInput")[:, :], writes=[ccos])
        kb.dma("sp", csin[:], p.dram("c_sin_cmp", (128, 127), F32, kind="ExternalInput")[:, :], writes=[csin])
        selMul = p.sb(es, "ns_selMul", [128, NT, 32], F32)
        selAdd = p.sb(es, "ns_selAdd", [128, NT, 32], F32)
        kb.dma("sp", selMul[:], p.dram("c_selMul", (S, 32), F32, kind="ExternalInput").h.rearrange("(t p) n -> p t n", p=128), writes=[selMul])
        kb.dma("sp", selAdd[:], p.dram("c_selAdd", (S, 32), F32, kind="ExternalInput").h.rearrange("(t p) n -> p t n", p=128), writes=[selAdd])
        qT = p.sb(es, "ns_q", [128, 4, S], BF16)
        kb.dma("sp", qT[:], fm[FM_NQ * 128:(FM_NQ + 4) * 128, :].rearrange("(c p) t -> p c t", p=128), reads=[fm], writes=[qT])
        kb.op("act", lambda e: e.mul(qT[:], qT[:], 0.125), reads=[qT], writes=[qT])
        gt = p.sb(es, "ns_gt", [128, NT, 24], F32)
        kb.dma("sp", gt[:], gates.h.rearrange("(t p) f -> p t f", p=128), reads=[gates], writes=[gt])
        yb = p.sb(es, "ns_yb", [128, NT, 512], F32)
        kb.op("pool", lambda e: e.memset(yb[:], 0.0), writes=[yb])
        impacc = p.sb(es, "ns_imp", [128, NT, 2, 32], F32)
        kb.op("pool", lambda e: e.memset(impacc[:], 0.0), writes=[impacc])
        kdup = {}
        for nm, ch in (("slc", FM_KSLC), ("win", FM_KWIN)):
            for g in range(2):
                t = p.sb(es, "ns_k%s%d" % (nm, g), [128, S], BF16)
                for half in range(2):
                    kb.dma("sp", t[64 * half:64 * half + 64, :], fm[ch * 128 + 64 * g:ch * 128 + 64 * g + 64, :], reads=[fm], writes=[t])
                kdup[(nm, g)] = t
        Vs = {}
        for nm, off in (("slc", TM_VSLC), ("win", TM_VWIN)):
            t = p.sb(es, "ns_v" + nm, [128, NT, 2, 65], BF16)
            kb.op("pool", lambda e: e.memset(t[:, :, :, 64:65], 1.0), writes=[t])
            for g in range(2):
                kb.dma("sp", t[:, :, g, 0:64], tm[:, off + 64 * g:off + 64 * g + 64].rearrange("(t p) d -> p t d", p=128), reads=[tm], writes=[t])
            Vs[nm] = t
        pS = [p.ps(es, "ns_pS%d" % i, [128, 512]) for i in range(2)]
        pO = [p.ps(es, "ns_pO%d" % i, [128, 512]) for i in range(2)]
        pX = [p.ps(es, "ns_pX%d" % i, [128, 512]) for i in range(2)]
        pXb = p.ps(es, "ns_pXb", [128, 1024], BF16)
        kcT = [p.sb(es, "ns_kc%d" % g, [128, 128], BF16) for g in range(2)]
        Rg = [p.sb(es, "ns_R%d" % g, [128, 97], BF16) for g in range(2)]
        ovl = load_const_bf16(p, es, "c_overlap", (128, 32))
        with ExitStack() as es2:
            tT = p.sb(es2, "ns_tT", [128, S], BF16)
            W1 = p.sb(es2, "ns_W1", [128, 32, 128], BF16)
            W2d = p.sb(es2, "ns_W2d", [128, 2, 64], BF16)
            pet = p.sb(es2, "ns_pe", [128, 32], F32)
            xl = [p.sb(es2, "ns_xl%d" % i, [128, 128], BF16) for i in range(4)]
            h1 = p.sb(es2, "ns_h1", [128, 128], BF16)
            xb = p.sb(es2, "ns_xb", [128, 128], BF16)
            a1 = p.sb(es2, "ns_a1", [128, 128], F32)
            a2 = p.sb(es2, "ns_a2", [128, 128], F32)
            xc = 0
            for j in range(2):
                kb.dma("sp", tT[:], fm[(FM_KCMP + j) * 128:(FM_KCMP + j + 1) * 128, :], reads=[fm], writes=[tT])
                for half in range(2):
                    kb.dma("sp", W1[64 * half:64 * half + 64, :, :], w1[j].rearrange("(l d) n -> d l n", d=64), reads=[w1], writes=[W1])
                    kb.dma("sp", pet[64 * half:64 * half + 64, :], pe[l, j].rearrange("l d -> d l"), writes=[pet], allow_slow_non_contiguous=True)
                    kb.dma("sp", W2d[:, half, :], w2[j], reads=[w2], writes=[W2d])
                for g in range(2):
                    base = 64 * g
                    ph = pX[0]
                    for ll in range(32):
                        x_ = xl[xc % 4]
                        xc += 1
                        src = tT[base:base + 64, :].rearrange("p (i s) -> p s i", s=16)
                        sh, lo = ll // 16, ll % 16
                        kb.op("dve", lambda e: e.tensor_scalar(out=x_[base:base + 64, 0:127], in0=src[:, lo, sh:sh + 127], scalar1=pet[base:base + 64, ll:ll + 1], scalar2=None, op0=ALU.add), reads=[tT, pet], writes=[x_])
                        kb.op("pe", lambda e: e.matmul(ph[:, 0:127], W1[base:base + 64, ll, :], x_[base:base + 64, 0:127], start=(ll == 0), stop=(ll == 31)), reads=[W1, x_], writes=[ph])
                    kb.op("act", lambda e: e.activation(out=h1[:, 0:127], in_=ph[:, 0:127], func=AF.Gelu_apprx_tanh), reads=[ph], writes=[h1])
                    if j == 0:
                        pk = pX[1]
                        kb.op("pe", lambda e: e.matmul(pk[:, 0:127], W2d[:].rearrange("p a d -> p (a d)"), h1[:, 0:127], start=True, stop=True), reads=[W2d, h1], writes=[pk])
                        kb.op("act", lambda e: e.activation(out=xb[:, 0:127], in_=pk[:, 0:127], func=AF.Copy), reads=[pk], writes=[xb])
                        kb.op("dve", lambda e: e.tensor_tensor(out=a1[:, 0:127], in0=pk[:, 0:127], in1=ccos[:], op=ALU.mult), reads=[pk, ccos], writes=[a1])
                        pk2 = pO[0]
                        kb.op("pe", lambda e: e.matmul(pk2[:, 0:127], swp[:], xb[:, 0:127], start=True, stop=True), reads=[swp, xb], writes=[pk2])
                        kb.op("dve", lambda e: e.tensor_tensor(out=a2[:, 0:127], in0=pk2[:, 0:127], in1=csin[:], op=ALU.mult), reads=[pk2, csin], writes=[a2])
                        kb.op("dve", lambda e: e.tensor_tensor(out=kcT[g][:, 0:127], in0=a1[:, 0:127], in1=a2[:, 0:127], op=ALU.add), reads=[a1, a2], writes=[kcT[g]])
                    else:
                        pv = pX[1]
                        kb.op("pe", lambda e: e.matmul(pv[0:127, 0:64], h1[:, 0:127], W2d[:, 0, :], start=True, stop=True), reads=[W2d, h1], writes=[pv])
                        kb.op("pool", lambda e: e.memset(Rg[g][:, 64:65], 1.0), writes=[Rg[g]])
                        kb.op("act", lambda e: e.activation(out=Rg[g][0:127, 0:64], in_=pv[0:127, 0:64], func=AF.Copy), reads=[pv], writes=[Rg[g]])
                        kb.op("dve", lambda e: e.tensor_copy(out=Rg[g][:, 65:97], in_=ovl[:]), reads=[ovl], writes=[Rg[g]])
        pc = [p.sb(es, "ns_pc%d" % i, [128, 512], BF16) for i in range(2)]
        rd = [p.sb(es, "ns_rd%d" % i, [128, 4], F32) for i in range(2)]
        steps = []
        for h in heads:
            for G in range(4):
                def mk(h=h, G=G, idx=len(steps)):
                    g, hp, base = h // 4, h // 2, 64 * (h % 2)
                    ps, po, pc_, rd_ = pS[idx % 2], pO[idx % 2], pc[idx % 2], rd[idx % 2]

                    def s1():
                        kb.op("pe", lambda e: e.matmul(ps[0:127, :], kcT[g][base:base + 64, 0:127], qT[base:base + 64, hp, 512 * G:512 * G + 512], start=True, stop=True), reads=[kcT[g], qT], writes=[ps])
                        kb.op("act", lambda e: e.activation(out=pc_[0:127, :], in_=ps[0:127, :], func=AF.Exp), reads=[ps], writes=[pc_])
                        kb.op("pool", lambda e: e.tensor_tensor(out=pc_[0:127, :], in0=pc_[0:127, :], in1=maskCmp[0:127, 512 * G:512 * G + 512], op=ALU.mult), reads=[pc_, maskCmp], writes=[pc_])

                    def s2():
                        for j in range(4):
                            kb.op("pe", lambda e: e.matmul(po[:, 97 * j:97 * j + 97], pc_[0:127, 128 * j:128 * j + 128], Rg[g][0:127, :], start=True, stop=True), reads=[pc_, Rg[g]], writes=[po])
                        pov = po[:, 0:388].rearrange("p (j f) -> p j f", f=97)
                        kb.op("dve", lambda e: e.tensor_scalar(out=rd_[:], in0=pov[:, :, 64], scalar1=1e-30, scalar2=None, op0=ALU.max), reads=[po], writes=[rd_])
                        kb.op("dve", lambda e: e.reciprocal(out=rd_[:], in_=rd_[:]), reads=[rd_], writes=[rd_])
                        for j in range(4):
                            tt = 4 * G + j
                            kb.op("dve", lambda e: e.tensor_scalar(out=yb[:, tt, 64 * h:64 * h + 64], in0=po[:, 97 * j:97 * j + 64], scalar1=rd_[:, j:j + 1], scalar2=gt[:, tt, 3 * h:3 * h + 1], op0=ALU.mult, op1=ALU.mult), reads=[po, rd_, gt], writes=[yb])
                            kb.op("dve", lambda e: e.scalar_tensor_tensor(out=impacc[:, tt, g, :], in0=po[:, 97 * j + 65:97 * j + 97], scalar=rd_[:, j:j + 1], in1=impacc[:, tt, g, :], op0=ALU.mult, op1=ALU.add), reads=[po, rd_, impacc], writes=[impacc])
                    return (s1, s2)
                steps.append(mk())
        pipeline(steps, 2)
        dbg = p.d.get("dbg_imp")
        if dbg is not None:
            kb.dma("sp", dbg.h.rearrange("(t p) g n -> p t g n", p=128), impacc[:], reads=[impacc], writes=[dbg])
        selT = [p.sb(es, "ns_selT%d" % g, [32, S], BF16) for g in range(2)]
        m8 = p.sb(es, "ns_m8", [128, 8], F32)
        selb = p.sb(es, "ns_selb", [128, 4, 32], BF16)
        for g in range(2):
            kb.op("dve", lambda e: e.tensor_tensor(out=impacc[:, :, g, :], in0=impacc[:, :, g, :], in1=selMul[:], op=ALU.mult), reads=[impacc, selMul], writes=[impacc])
            kb.op("dve", lambda e: e.tensor_tensor(out=impacc[:, :, g, :], in0=impacc[:, :, g, :], in1=selAdd[:], op=ALU.add), reads=[impacc, selAdd], writes=[impacc])
            for G in range(4):
                for j in range(4):
                    tt = 4 * G + j
                    kb.op("dve", lambda e: e.max(out=m8[:], in_=impacc[:, tt, g, :]), reads=[impacc], writes=[m8])
                    kb.op("dve", lambda e: e.tensor_scalar(out=selb[:, j, :], in0=impacc[:, tt, g, :], scalar1=m8[:, 7:8], scalar2=1.0, op0=ALU.is_ge, op1=ALU.subtract), reads=[impacc, m8], writes=[selb])
                for j in range(4):
                    kb.op("pe", lambda e: e.transpose(pXb[0:32, 128 * j:128 * j + 128], selb[:, j, :], p.ident[:]), reads=[selb, p.ident], writes=[pXb])
                kb.op("act", lambda e: e.activation(out=selT[g][:, 512 * G:512 * G + 512], in_=pXb[0:32, 0:512], func=AF.Copy), reads=[pXb], writes=[selT[g]])
        Pt = [p.sb(es, "ns_P%d" % i, [128, 512], BF16) for i in range(2)]
        Oacc = [p.sb(es, "ns_Oa%d" % i, [128, 4, 65], F32) for i in range(2)]
        rdg = [p.sb(es, "ns_rdg%d" % i, [128, 4], F32) for i in range(2)]
        Pt = Pt + [p.sb(es, "ns_P2", [128, 512], BF16)]
        oc = 0
        branches = ([("slc", 1)] if do_slc else []) + ([("win", 2)] if do_win else [])
        steps = []
        for h in heads:
            for (nm, gi) in branches:
                for G in range(4):
                    oa, rg_ = Oacc[oc % 2], rdg[oc % 2]
                    oc += 1
                    kbs = list(range(0, 4 * G + 4) if nm == "slc" else range(max(0, 4 * G - 4), 4 * G + 4))
                    for kb_ in kbs:
                        def mk(h=h, nm=nm, gi=gi, G=G, kb_=kb_, oa=oa, rg_=rg_, idx=len(steps), firstk=(kb_ == kbs[0]), lastk=(kb_ == kbs[-1])):
                            g, hp, base = h // 4, h // 2, 64 * (h % 2)
                            kd, V = kdup[(nm, g)], Vs[nm]
                            qs = qT[base:base + 64, hp, 512 * G:512 * G + 512]
                            i = kb_ - 4 * G
                            ps, po, P_ = pS[idx % 2], pO[idx % 2], Pt[idx % 3]
                            ks = kd[base:base + 64, 128 * kb_:128 * kb_ + 128]
                            if i >= 0:
                                js = range(i, 4)
                            elif nm == "win":
                                js = range(0, i + 5)
                            else:
                                js = range(4)

                            def s1():
                                if firstk:
                                    kb.op("pool", lambda e: e.memset(oa[:], 0.0), writes=[oa])
                                if nm == "slc":
                                    kb.op("pe", lambda e: e.matmul(ps[:], ks, qs, start=True, stop=False), reads=[kd, qT], writes=[ps])
                                    kb.op("pe", lambda e: e.matmul(ps[:], Ebig[:, kb_, :], selT[g][:, 512 * G:512 * G + 512], start=False, stop=True), reads=[Ebig, selT[g]], writes=[ps])
                                else:
                                    kb.op("pe", lambda e: e.matmul(ps[:], ks, qs, start=True, stop=True), reads=[kd, qT], writes=[ps])
                                kb.op("act", lambda e: e.activation(out=P_[:], in_=ps[:], func=AF.Exp), reads=[ps], writes=[P_])
                                if i >= 0:
                                    kb.op("pool", lambda e: e.tensor_tensor(out=P_[:], in0=P_[:], in1=maskC[:, i, :], op=ALU.mult), reads=[P_, maskC], writes=[P_])
                                elif nm == "win":
                                    kb.op("pool", lambda e: e.tensor_tensor(out=P_[:], in0=P_[:], in1=maskCn[:, i + 4, :], op=ALU.mult), reads=[P_, maskCn], writes=[P_])

                            def s2():
                                for j in js:
                                    kb.op("pe", lambda e: e.matmul(po[:, 65 * j:65 * j + 65], P_[:, 128 * j:128 * j + 128], V[:, kb_, g, :], start=True, stop=True), reads=[P_, V], writes=[po])
                                j0, j1 = js[0], js[-1] + 1
                                kb.op("dve", lambda e: e.tensor_tensor(out=oa[:, j0:j1, :], in0=oa[:, j0:j1, :], in1=po[:, 65 * j0:65 * j1].rearrange("p (j f) -> p j f", f=65), op=ALU.add), reads=[oa, po], writes=[oa])
                                if lastk:
                                    kb.op("dve", lambda e: e.reciprocal(out=rg_[:], in_=oa[:, :, 64]), reads=[oa], writes=[rg_])
                                    kb.op("dve", lambda e: e.tensor_tensor(out=rg_[:], in0=rg_[:], in1=gt[:, 4 * G:4 * G + 4, 3 * h + gi], op=ALU.mult), reads=[rg_, gt], writes=[rg_])
                                    for j in range(4):
                                        tt = 4 * G + j
                                        kb.op("dve", lambda e: e.scalar_tensor_tensor(out=yb[:, tt, 64 * h:64 * h + 64], in0=oa[:, j, 0:64], scalar=rg_[:, j:j + 1], in1=yb[:, tt, 64 * h:64 * h + 64], op0=ALU.mult, op1=ALU.add), reads=[oa, rg_, yb], writes=[yb])
                            return (s1, s2)
                        steps.append(mk())
        pipeline(steps, 2)
        dbg = p.d.get("dbg_yb")
        if dbg is not None:
            kb.dma("sp", dbg.h.rearrange("(t p) f -> p t f", p=128), yb[:], reads=[yb], writes=[dbg])
        gw = go_tiles(p, es, "nsgo", l, 1)
        for tt in range(NT):
            emit_group_out(p, gw, yb[:, tt, :], yb, 1, tt)
        kb.barrier()


def phase_s5(p, l, chunks=range(16)):
    kb = p.kb
    PI = math.pi
    fm = p.dram("fm", (30 * 128, S), BF16)
    ysT = p.dram("ysT", (D, S), BF16)
    inp = lambda n: p.dram(n, IN_SHAPES[n], F32, kind="ExternalInput")
    lam_re, lam_im, log_dt = inp("s5_lambda_re"), inp("s5_lambda_im"), inp("s5_log_dt")
    b_re, b_im, c_re, c_im = inp("s5_b_re"), inp("s5_b_im"), inp("s5_c_re"), inp("s5_c_im")
    d_skip, b_glu, gain = inp("s5_d"), inp("s5_b_glu"), inp("mix_norm_g")
    wglu = p.wb[("s5_w_glu", l)]
    with ExitStack() as es:
        uT = p.sb(es, "s5_uT", [128, 4, S], BF16)
        kb.dma("sp", uT[:], fm[FM_U * 128:(FM_U + 4) * 128, :].rearrange("(c p) t -> p c t", p=128), reads=[fm], writes=[uT])
        yT = p.sb(es, "s5_yT", [128, 4, S], F32)
        dT = p.sb(es, "s5_dT", [128, 4], F32)
        kb.dma("sp", dT[:], d_skip[l].rearrange("(q j) c -> (j c) q", j=8), writes=[dT], allow_slow_non_contiguous=True)
        BT = [p.sb(es, "s5_BT%d" % i, [128, 4, 2, 64], BF16) for i in range(2)]
        BTz = [p.sb(es, "s5_BTz%d" % i, [128, 4, 2, 64], BF16) for i in range(2)]
        CT = [p.sb(es, "s5_CT%d" % i, [128, 4, 128], BF16) for i in range(2)]
        CTz = [p.sb(es, "s5_CTz%d" % i, [128, 4, 64], BF16) for i in range(2)]
        for q in range(4):
            kb.op("dve", lambda e: e.tensor_scalar(out=yT[:, q, :], in0=uT[:, q, :], scalar1=dT[:, q:q + 1], scalar2=None, op0=ALU.mult), reads=[uT, dT], writes=[yT])
        r_p = p.sb(es, "s5_rp", [128, 16], F32)
        th_p = p.sb(es, "s5_thp", [128, 16], F32)
        with ExitStack() as es2:
            sb2 = lambda n, shp, dt=F32: p.sb(es2, "s5p_" + n, shp, dt)
            pX = p.ps(es2, "s5p_pX", [128, 1024], BF16)
            mask2 = sb2("mask2", [128, 2])
            kb.dma("sp", mask2[:], p.dram("c_mask2", (128, 2), F32, kind="ExternalInput")[:, :], writes=[mask2])
            mask2z = sb2("mask2z", [128, 2])
            kb.dma("sp", mask2z[:], p.dram("c_mask2z", (128, 2), F32, kind="ExternalInput")[:, :], writes=[mask2z])
            mask3 = [sb2("mask3%d" % i, [128, 128]) for i in range(2)]
            kb.dma("sp", mask3[0][:], p.dram("c_mask3", (128, 128), F32, kind="ExternalInput")[:, :], writes=[mask3[0]])
            kb.dma("sp", mask3[1][:], p.dram("c_mask3n", (128, 128), F32, kind="ExternalInput")[:, :], writes=[mask3[1]])

            def prep(P_, G_, lr_src, li_src, dt_loads, pfx):
                t = {k: sb2(pfx + k, [P_, G_]) for k in ("lr", "li", "dt", "mag", "ang", "tmp")}
                kb.dma("sp", t["lr"][:], lr_src, writes=[t["lr"]], allow_slow_non_contiguous=True)
                kb.dma("sp", t["li"][:], li_src, writes=[t["li"]], allow_slow_non_contiguous=True)
                for (dst_sl, src) in dt_loads:
                    kb.dma("sp", t["dt"][dst_sl, :], src, writes=[t["dt"]])
                kb.op("dve", lambda e: e.tensor_scalar(out=t["lr"][:], in0=t["lr"][:], scalar1=-1e-4, scalar2=None, op0=ALU.min), reads=[t["lr"]], writes=[t["lr"]])
                kb.op("act", lambda e: e.activation(out=t["dt"][:], in_=t["dt"][:], func=AF.Exp), reads=[t["dt"]], writes=[t["dt"]])
                kb.op("dve", lambda e: e.tensor_tensor(out=t["tmp"][:], in0=t["lr"][:], in1=t["dt"][:], op=ALU.mult), reads=[t["lr"], t["dt"]], writes=[t["tmp"]])
                kb.op("act", lambda e: e.activation(out=t["mag"][:], in_=t["tmp"][:], func=AF.Exp), reads=[t["tmp"]], writes=[t["mag"]])
                kb.op("dve", lambda e: e.tensor_tensor(out=t["ang"][:], in0=t["li"][:], in1=t["dt"][:], op=ALU.mult), reads=[t["li"], t["dt"]], writes=[t["ang"]])
                return t

            tp = prep(128, 16,
                      lam_re[l].rearrange("g s -> (g s)").rearrange("(c q) -> q c", q=128),
                      lam_im[l].rearrange("g s -> (g s)").rearrange("(c q) -> q c", q=128),
                      [(slice(64 * two, 64 * two + 64), log_dt[l].rearrange("(c two) -> two c", two=2)[two].partition_broadcast(64)) for two in range(2)], "p")
            kb.op("dve", lambda e: e.tensor_copy(out=r_p[:], in_=tp["mag"][:]), reads=[tp["mag"]], writes=[r_p])
            kb.op("dve", lambda e: e.tensor_copy(out=th_p[:], in_=tp["ang"][:]), reads=[tp["ang"]], writes=[th_p])
            ts = prep(64, 32, lam_re[l].rearrange("g s -> s g"), lam_im[l].rearrange("g s -> s g"),
                      [(slice(0, 64), log_dt[l].partition_broadcast(64))], "s")
            sn, cs = sb2("sn", [64, 32]), sb2("cs", [64, 32])
            ki0 = sb2("ki0", [64, 32], mybir.dt.int32)
            kb.op("dve", lambda e: e.tensor_scalar(out=ki0[:], in0=ts["ang"][:], scalar1=1.0 / (2 * PI), scalar2=None, op0=ALU.mult), reads=[ts["ang"]], writes=[ki0])
            kb.op("dve", lambda e: e.tensor_copy(out=sn[:], in_=ki0[:]), reads=[ki0], writes=[sn])
            kb.op("dve", lambda e: e.scalar_tensor_tensor(out=sn[:], in0=sn[:], scalar=-2 * PI, in1=ts["ang"][:], op0=ALU.mult, op1=ALU.add), reads=[sn, ts["ang"]], writes=[sn])
            kb.op("dve", lambda e: e.tensor_scalar(out=sn[:], in0=sn[:], scalar1=-PI, scalar2=PI, op0=ALU.max, op1=ALU.min), reads=[sn], writes=[sn])
            kb.op("act", lambda e: e.activation(out=cs[:], in_=sn[:], func=AF.Abs), reads=[sn], writes=[cs])
            kb.op("act", lambda e: e.activation(out=sn[:], in_=sn[:], func=AF.Sin), reads=[sn], writes=[sn])
            kb.op("act", lambda e: e.activation(out=cs[:], in_=cs[:], func=AF.Sin, scale=-1.0, bias=0.5 * PI), reads=[cs], writes=[cs])
            are, aim, den, zre, zim, t1, t2 = [sb2(n, [64, 32]) for n in ("are", "aim", "den", "zre", "zim", "t1", "t2")]
            tt_ = lambda o, a, b, op: kb.op("dve", lambda e: e.tensor_tensor(out=o[:], in0=a[:], in1=b[:], op=op), reads=[a, b], writes=[o])
            tt_(are, ts["mag"], cs, ALU.mult)
            tt_(aim, ts["mag"], sn, ALU.mult)
            kb.op("dve", lambda e: e.tensor_scalar(out=are[:], in0=are[:], scalar1=-1.0, scalar2=None, op0=ALU.add), reads=[are], writes=[are])
            tt_(t1, ts["lr"], ts["lr"], ALU.mult)
            tt_(t2, ts["li"], ts["li"], ALU.mult)
            tt_(den, t1, t2, ALU.add)
            kb.op("dve", lambda e: e.reciprocal(out=den[:], in_=den[:]), reads=[den], writes=[den])
            tt_(t1, are, ts["lr"], ALU.mult)
            tt_(t2, aim, ts["li"], ALU.mult)
            tt_(zre, t1, t2, ALU.add)
            tt_(zre, zre, den, ALU.mult)
            tt_(t1, aim, ts["lr"], ALU.mult)
            tt_(t2, are, ts["li"], ALU.mult)
            tt_(zim, t1, t2, ALU.subtract)
            tt_(zim, zim, den, ALU.mult)
            Bs = [sb2("Bs%d" % i, [64, 32, 16]) for i in range(2)]
            kb.dma("sp", Bs[0][:], b_re[l].rearrange("g s c -> s g c"), writes=[Bs[0]])
            kb.dma("sp", Bs[1][:], b_im[l].rearrange("g s c -> s g c"), writes=[Bs[1]])
            m1, m2 = sb2("m1", [64, 32, 16]), sb2("m2", [64, 32, 16])
            bb = [sb2("bb%d" % i, [64, 32, 16], BF16) for i in range(2)]
            zb = lambda z: z[:].unsqueeze(2).to_broadcast([64, 32, 16])
            kb.op("dve", lambda e: e.tensor_tensor(out=m1[:], in0=Bs[0][:], in1=zb(zre), op=ALU.mult), reads=[Bs[0], zre], writes=[m1])
            kb.op("dve", lambda e: e.tensor_tensor(out=m2[:], in0=Bs[1][:], in1=zb(zim), op=ALU.mult), reads=[Bs[1], zim], writes=[m2])
            kb.op("dve", lambda e: e.tensor_tensor(out=bb[0][:], in0=m1[:], in1=m2[:], op=ALU.subtract), reads=[m1, m2], writes=[bb[0]])
            kb.op("dve", lambda e: e.tensor_tensor(out=m1[:], in0=Bs[1][:], in1=zb(zre), op=ALU.mult), reads=[Bs[1], zre], writes=[m1])
            kb.op("dve", lambda e: e.tensor_tensor(out=m2[:], in0=Bs[0][:], in1=zb(zim), op=ALU.mult), reads=[Bs[0], zim], writes=[m2])
            kb.op("dve", lambda e: e.tensor_tensor(out=bb[1][:], in0=m1[:], in1=m2[:], op=ALU.add), reads=[m1, m2], writes=[bb[1]])
            for ri in range(2):
                for q in range(4):
                    kb.op("pe", lambda e: e.transpose(pX[:, 0:64], bb[ri][:, 8 * q:8 * q + 8, :].rearrange("s j c -> s (j c)"), p.ident[0:64, 0:64]), reads=[bb[ri], p.ident], writes=[pX])
                    for two in range(2):
                        kb.op("dve", lambda e: e.tensor_scalar(out=BT[ri][:, q, two, :], in0=pX[:, 0:64], scalar1=mask2[:, two:two + 1], scalar2=None, op0=ALU.mult), reads=[pX, mask2], writes=[BT[ri]])
                        kb.op("dve", lambda e: e.tensor_scalar(out=BTz[ri][:, q, two, :], in0=pX[:, 0:64], scalar1=mask2z[:, two:two + 1], scalar2=None, op0=ALU.mult), reads=[pX, mask2z], writes=[BTz[ri]])
            Cn = sb2("Cn", [128, 4, 64])
            Cd = sb2("Cd", [128, 4, 2, 64], BF16)
            for ri, csrc in enumerate((c_re, c_im)):
                kb.dma("sp", Cn[:], csrc[l].rearrange("(q j) c s -> (j c) q s", j=8), writes=[Cn])
                for two in range(2):
                    kb.op("dve", lambda e: e.tensor_copy(out=Cd[:, :, two, :], in_=Cn[:]), reads=[Cn], writes=[Cd])
                for q in range(4):
                    kb.op("pe", lambda e: e.transpose(pX[:, 0:128], Cd[:, q, :, :].rearrange("p a s -> p (a s)"), p.ident[:]), reads=[Cd, p.ident], writes=[pX])
                    kb.op("dve", lambda e: e.tensor_tensor(out=CT[ri][:, q, :], in0=pX[:, 0:128], in1=mask3[ri][:], op=ALU.mult), reads=[pX, mask3[ri]], writes=[CT[ri]])
                    kb.op("pool", lambda e: e.memset(CTz[ri][:, q, 0:32], 0.0), writes=[CTz[ri]])
                    kb.op("dve", lambda e: e.tensor_copy(out=CTz[ri][:, q, 32:64], in_=CT[ri][:, q, 96:128]), reads=[CT[ri]], writes=[CTz[ri]])
            kb.barrier()
        with ExitStack() as es2:
            sb2 = lambda n, shp, dt=F32: p.sb(es2, "s5m_" + n, shp, dt)
            iota = sb2("iota", [128, S])
            kb.dma("sp", iota[:], p.dram("c_iota", (128, S), F32, kind="ExternalInput")[:, :], writes=[iota])
            HS = S // 2
            wri, wii, wi, wr = [sb2(n, [128, HS]) for n in ("wri", "wii", "wi", "wr")]
            phs = [sb2("ph%d" % i, [128, HS]) for i in range(2)]
            sns = [sb2("sn%d" % i, [128, HS]) for i in range(2)]
            css = [sb2("cs%d" % i, [128, HS]) for i in range(2)]
            xre, xim = sb2("xre", [128, HS], BF16), sb2("xim", [128, HS], BF16)
            ki = sb2("ki", [128, HS], mybir.dt.int32)
            tmp = [sb2("t%d" % i, [128, 512]) for i in range(4)]
            car = sb2("car", [128, 2])
            psBr = [p.ps(es2, "s5_pBr%d" % i, [128, 512]) for i in range(2)]
            psBi = [p.ps(es2, "s5_pBi%d" % i, [128, 512]) for i in range(2)]
            psY = [p.ps(es2, "s5_pY%d" % i, [128, 512]) for i in range(2)]
            steps = []
            for ch in chunks:
                for half in range(2):
                    def mk(ch=ch, half=half, idx=len(steps)):
                        q, rb = ch // 4, 32 * (ch % 4)
                        rows = slice(rb, rb + 32) if rb < 96 else slice(64, 128)
                        t0 = half * HS
                        ph, sn, cs = phs[idx % 2], sns[idx % 2], css[idx % 2]

                        def s1():
                            kb.op("dve", lambda e: e.tensor_scalar(out=ph[:], in0=iota[:, t0:t0 + HS], scalar1=th_p[:, ch:ch + 1], scalar2=None, op0=ALU.mult), reads=[iota, th_p], writes=[ph])
                            kb.op("dve", lambda e: e.tensor_scalar(out=ki[:], in0=ph[:], scalar1=1.0 / (2 * PI), scalar2=None, op0=ALU.mult), reads=[ph], writes=[ki])
                            kb.op("dve", lambda e: e.tensor_copy(out=sn[:], in_=ki[:]), reads=[ki], writes=[sn])
                            kb.op("dve", lambda e: e.scalar_tensor_tensor(out=sn[:], in0=sn[:], scalar=-2 * PI, in1=ph[:], op0=ALU.mult, op1=ALU.add), reads=[sn, ph], writes=[sn])
                            kb.op("dve", lambda e: e.tensor_scalar(out=sn[:], in0=sn[:], scalar1=-PI, scalar2=PI, op0=ALU.max, op1=ALU.min), reads=[sn], writes=[sn])
                            kb.op("act", lambda e: e.activation(out=cs[:], in_=sn[:], func=AF.Abs), reads=[sn], writes=[cs])
                            kb.op("act", lambda e: e.activation(out=sn[:], in_=sn[:], func=AF.Sin), reads=[sn], writes=[sn])
                            kb.op("act", lambda e: e.activation(out=cs[:], in_=cs[:], func=AF.Sin, scale=-1.0, bias=0.5 * PI), reads=[cs], writes=[cs])

                        def s2():
                            for tg in range(HS // 512):
                                tk = slice(t0 + 512 * tg, t0 + 512 * tg + 512)
                                lk = slice(512 * tg, 512 * tg + 512)
                                pr, pi_ = psBr[tg % 2], psBi[tg % 2]
                                Bm = BTz if rb == 96 else BT
                                kb.op("pe", lambda e: e.matmul(pr[:], Bm[0][rows, q, :, :].rearrange("p a s -> p (a s)"), uT[rows, q, tk], start=True, stop=True), reads=[Bm[0], uT], writes=[pr])
                                kb.op("pe", lambda e: e.matmul(pi_[:], Bm[1][rows, q, :, :].rearrange("p a s -> p (a s)"), uT[rows, q, tk], start=True, stop=True), reads=[Bm[1], uT], writes=[pi_])
                                a, b, c, d = tmp
                                kb.op("dve", lambda e: e.tensor_tensor(out=a[:], in0=pr[:], in1=cs[:, lk], op=ALU.mult), reads=[pr, cs], writes=[a])
                                kb.op("dve", lambda e: e.tensor_tensor(out=b[:], in0=pi_[:], in1=sn[:, lk], op=ALU.mult), reads=[pi_, sn], writes=[b])
                                kb.op("pool", lambda e: e.tensor_tensor(out=wri[:, lk], in0=a[:], in1=b[:], op=ALU.add), reads=[a, b], writes=[wri])
                                kb.op("dve", lambda e: e.tensor_tensor(out=c[:], in0=pi_[:], in1=cs[:, lk], op=ALU.mult), reads=[pi_, cs], writes=[c])
                                kb.op("dve", lambda e: e.tensor_tensor(out=d[:], in0=pr[:], in1=sn[:, lk], op=ALU.mult), reads=[pr, sn], writes=[d])
                                kb.op("pool", lambda e: e.tensor_tensor(out=wii[:, lk], in0=c[:], in1=d[:], op=ALU.subtract), reads=[c, d], writes=[wii])
                            rbc = r_p[:, ch:ch + 1].to_broadcast([128, HS])
                            ini_r = 0.0 if half == 0 else car[:, 0:1]
                            ini_i = 0.0 if half == 0 else car[:, 1:2]
                            kb.op("dve", lambda e: e.tensor_tensor_scan(out=wr[:], data0=rbc, data1=wri[:], initial=ini_r, op0=ALU.mult, op1=ALU.add), reads=[r_p, wri, car], writes=[wr])
                            kb.op("dve", lambda e: e.tensor_tensor_scan(out=wi[:], data0=rbc, data1=wii[:], initial=ini_i, op0=ALU.mult, op1=ALU.add), reads=[r_p, wii, car], writes=[wi])
                            if half == 0:
                                kb.op("act", lambda e: e.activation(out=car[:, 0:1], in_=wr[:, HS - 1:HS], func=AF.Copy), reads=[wr], writes=[car])
                                kb.op("act", lambda e: e.activation(out=car[:, 1:2], in_=wi[:, HS - 1:HS], func=AF.Copy), reads=[wi], writes=[car])
                            kb.op("dve", lambda e: e.tensor_tensor(out=wri[:], in0=wr[:], in1=cs[:], op=ALU.mult), reads=[wr, cs], writes=[wri])
                            kb.op("pool", lambda e: e.tensor_tensor(out=wii[:], in0=wi[:], in1=sn[:], op=ALU.mult), reads=[wi, sn], writes=[wii])
                            kb.op("dve", lambda e: e.tensor_tensor(out=xre[:], in0=wri[:], in1=wii[:], op=ALU.subtract), reads=[wri, wii], writes=[xre])
                            kb.op("pool", lambda e: e.tensor_tensor(out=wri[:], in0=wi[:], in1=cs[:], op=ALU.mult), reads=[wi, cs], writes=[wri])
                            kb.op("dve", lambda e: e.tensor_tensor(out=wii[:], in0=wr[:], in1=sn[:], op=ALU.mult), reads=[wr, sn], writes=[wii])
                            kb.op("pool", lambda e: e.tensor_tensor(out=xim[:], in0=wri[:], in1=wii[:], op=ALU.add), reads=[wri, wii], writes=[xim])
                            for tg in range(HS // 512):
                                tk = slice(t0 + 512 * tg, t0 + 512 * tg + 512)
                                lk = slice(512 * tg, 512 * tg + 512)
                                py = psY[tg % 2]
                                c0 = CTz[0][:, q, :] if rb == 96 else CT[0][:, q, rb:rb + 32]
                                c1 = CTz[1][:, q, :] if rb == 96 else CT[1][:, q, rb:rb + 32]
                                kb.op("pe", lambda e: e.matmul(py[rows, :], c0, xre[:, lk], start=True, stop=False), reads=[CT[0], CTz[0], xre], writes=[py])
                                kb.op("pe", lambda e: e.matmul(py[rows, :], c1, xim[:, lk], start=False, stop=True), reads=[CT[1], CTz[1], xim], writes=[py])
                                kb.op("dve", lambda e: e.tensor_tensor(out=yT[rows, q, tk], in0=yT[rows, q, tk], in1=py[rows, :], op=ALU.add), reads=[yT, py], writes=[yT])
                        return (s1, s2)
                    steps.append(mk())
            pipeline(steps, 2)
            kb.barrier()
        dbg = p.d.get("dbg_s5y")
        if dbg is not None:
            kb.dma("sp", dbg.h.rearrange("(q p) t -> p q t", p=128), yT[:], reads=[yT], writes=[dbg])
        with ExitStack() as es2:
            sb2 = lambda n, shp, dt=F32: p.sb(es2, "s5g_" + n, shp, dt)
            gT = sb2("gT", [128, 4, S], BF16)
            Wg = sb2("Wg", [128, 4, 512], BF16)
            kb.dma("sp", Wg[:], wglu.h.rearrange("(k p) n -> p k n", p=128), reads=[wglu], writes=[Wg])
            bg = sb2("bg", [128, 4])
            kb.dma("sp", bg[:], b_glu[l].rearrange("(q p) -> p q", p=128), writes=[bg], allow_slow_non_contiguous=True)
            gn = sb2("gn", [128, 4])
            kb.dma("sp", gn[:], gain[l, 0].rearrange("(q p) -> p q", p=128), writes=[gn], allow_slow_non_contiguous=True)
            ones = load_const_bf16(p, es2, "c_ones", (128, 128))
            sig = [sb2("sig%d" % i, [128, 512]) for i in range(2)]
            sq = sb2("sq", [128, 4, 512], BF16)
            rs = sb2("rs", [128, 512])
            yo = [sb2("yo%d" % i, [128, 512], BF16) for i in range(2)]
            pG = [p.ps(es2, "s5_pG%d" % i, [128, 512]) for i in range(2)]
            pN = p.ps(es2, "s5_pN", [128, 512])
            for q in range(4):
                kb.op("act", lambda e: e.activation(out=yT[:, q, :], in_=yT[:, q, :], func=AF.Gelu_apprx_tanh), reads=[yT], writes=[yT])
                kb.op("dve", lambda e: e.tensor_copy(out=gT[:, q, :], in_=yT[:, q, :]), reads=[yT], writes=[gT])
            cnt = 0
            for tg in range(4):
                tk = slice(512 * tg, 512 * tg + 512)
                for nq in range(4):
                    pg, sg = pG[cnt % 2], sig[cnt % 2]
                    cnt += 1
                    for kq in range(4):
                        kb.op("pe", lambda e: e.matmul(pg[:], Wg[:, kq, 128 * nq:128 * nq + 128], gT[:, kq, tk], start=(kq == 0), stop=(kq == 3)), reads=[Wg, gT], writes=[pg])
                    kb.op("act", lambda e: e.activation(out=sg[:], in_=pg[:], func=AF.Sigmoid, bias=bg[:, nq:nq + 1]), reads=[pg, bg], writes=[sg])
                    kb.op("dve", lambda e: e.tensor_tensor(out=yT[:, nq, tk], in0=yT[:, nq, tk], in1=sg[:], op=ALU.mult), reads=[yT, sg], writes=[yT])
                    kb.op("act", lambda e: e.activation(out=sq[:, nq, :], in_=yT[:, nq, tk], func=AF.Square), reads=[yT], writes=[sq])
                for nq in range(4):
                    kb.op("pe", lambda e: e.matmul(pN[:], ones[:], sq[:, nq, :], start=(nq == 0), stop=(nq == 3)), reads=[ones, sq], writes=[pN])
                kb.op("act", lambda e: e.activation(out=rs[:], in_=pN[:], func=AF.Sqrt, bias=1e-6, scale=1.0 / 512), reads=[pN], writes=[rs])
                kb.op("dve", lambda e: e.reciprocal(out=rs[:], in_=rs[:]), reads=[rs], writes=[rs])
                for nq in range(4):
                    y_ = yo[cnt % 2]
                    cnt += 1
                    kb.op("dve", lambda e: e.scalar_tensor_tensor(out=y_[:], in0=yT[:, nq, tk], scalar=gn[:, nq:nq + 1], in1=rs[:], op0=ALU.mult, op1=ALU.mult), reads=[yT, gn, rs], writes=[y_])
                    kb.dma("sp", ysT[128 * nq:128 * nq + 128, tk], y_[:], reads=[y_], writes=[ysT])
            dbg = p.d.get("dbg_s5ya")
            if dbg is not None:
                kb.dma("sp", dbg.h.rearrange("(q p) t -> p q t", p=128), yT[:], reads=[yT], writes=[dbg])
            kb.barrier()


def resid_ln(p, lnw, hold, pfs, tt, out_dram=None, out_res=None):
    kb = p.kb
    for cg in range(4):
        kb.op("dve", lambda e: e.scalar_tensor_tensor(out=hold[:, 512 * cg:512 * cg + 512], in0=hold[:, 512 * cg:512 * cg + 512], scalar=ALPHA, in1=pfs[cg][:], op0=ALU.mult, op1=ALU.add), reads=[hold, pfs[cg]], writes=[hold])
    emit_ln(p, lnw, hold, tt, 1e-5, out_dram=out_dram, out_res=out_res)


def phase_outproj(p, l):
    kb = p.kb
    W = p.wb[("w_out", l)]
    ysT = p.dram("ysT", (D, S), BF16)
    g = p.dram("ln1_g", IN_SHAPES["ln1_g"], F32, kind="ExternalInput")
    b = p.dram("ln1_b", IN_SHAPES["ln1_b"], F32, kind="ExternalInput")
    with ExitStack() as es:
        Wt = p.sb(es, "op_W", [128, KC, D], BF16)
        for c in range(0, KC, 4):
            kb.dma("sp", Wt[:, c:c + 4, :], W[c * 128:(c + 4) * 128, :].rearrange("(c p) n -> p c n", p=128), reads=[W], writes=[Wt])
        lnw = ln_tiles(p, es, "op_ln")
        ln_load_gb(p, lnw, g[l], b[l])
        yst = [p.sb(es, "op_ys%d" % i, [128, KC, 128], BF16) for i in range(2)]
        hold = [p.sb(es, "op_h%d" % i, [128, D], F32) for i in range(2)]
        pf = [p.ps(es, "op_pf%d" % i, [128, 512]) for i in range(4)]
        hold.append(p.sb(es, "op_h2", [128, D], F32))
        steps = []
        for tt in range(NT):
            def mk(tt=tt):
                ys, ho = yst[tt % 2], hold[tt % 3]

                def s1():
                    kb.dma("sp", ys[:], ysT[:, tt * 128:(tt + 1) * 128].rearrange("(c p) t -> p c t", p=128), reads=[ysT], writes=[ys])
                    kb.dma("sp", ho[:], p.h[tt * 128:(tt + 1) * 128, :], reads=[p.hres[tt]], writes=[ho])
                    for cg in range(4):
                        for k in range(KC):
                            kb.op("pe", lambda e: e.matmul(pf[cg][:], ys[:, k, :], Wt[:, k, 512 * cg:512 * cg + 512], start=(k == 0), stop=(k == KC - 1)), reads=[ys, Wt], writes=[pf[cg]])
                        kb.op("dve", lambda e: e.scalar_tensor_tensor(out=ho[:, 512 * cg:512 * cg + 512], in0=ho[:, 512 * cg:512 * cg + 512], scalar=ALPHA, in1=pf[cg][:], op0=ALU.mult, op1=ALU.add), reads=[ho, pf[cg]], writes=[ho])
                    emit_ln_a(p, lnw, ho, tt, 1e-5)

                def s2():
                    emit_ln_b(p, lnw, tt)
                return (s1, s2)
            steps.append(mk())
        pipeline(steps, 2)
        kb.barrier()


def phase_xattn(p, l):
    kb = p.kb
    wq, wkv, wo = p.wb[("xa_wq", l)], p.wb[("xa_wkv", l)], p.wb[("xa_wo", l)]
    mem = p.dram("mem", (MEM, D), F32, kind="ExternalInput")
    g = p.dram("ln2_g", IN_SHAPES["ln2_g"], F32, kind="ExternalInput")
    b = p.dram("ln2_b", IN_SHAPES["ln2_b"], F32, kind="ExternalInput")
    with ExitStack() as es0:
        o_all = p.sb(es0, "xa_o", [128, NT, 512], BF16)
        with ExitStack() as es:
            hT = load_hT(p, es)
            Wq = p.sb(es, "xa_Wq", [128, KC, 512], BF16)
            Wkv = p.sb(es, "xa_Wkv", [128, KC, 1024], BF16)
            kb.dma("sp", Wq[:], wq.h.rearrange("(c p) n -> p c n", p=128), reads=[wq], writes=[Wq])
            kb.dma("sp", Wkv[:], wkv.h.rearrange("(c p) n -> p c n", p=128), reads=[wkv], writes=[Wkv])
            memT = p.sb(es, "xa_memT", [128, KC, MEM], BF16)
            kT = p.sb(es, "xa_kT", [128, 4, MEM], BF16)
            Vx = p.sb(es, "xa_V", [128, 2, 4, 129], BF16)
            kb.op("pool", lambda e: e.memset(Vx[:, :, :, 128:129], 1.0), writes=[Vx])
            pA = [p.ps(es, "xa_pA%d" % i, [128, 512]) for i in range(2)]
            pS = [p.ps(es, "xa_pS%d" % i, [128, 512]) for i in range(2)]
            pO = [p.ps(es, "xa_pO%d" % i, [128, 512]) for i in range(2)]
            pTb = p.ps(es, "xa_pTb", [128, 2048], BF16)
            with ExitStack() as es2:
                mf = p.sb(es2, "xa_mf", [128, D], F32)
                mb_ = p.sb(es2, "xa_mb", [128, D], BF16)
                for mt in range(2):
                    kb.dma("sp", mf[:], mem[mt * 128:(mt + 1) * 128, :], writes=[mf])
                    kb.op("act", lambda e: e.activation(out=mb_[:], in_=mf[:], func=AF.Copy), reads=[mf], writes=[mb_])
                    for c in range(KC):
                        kb.op("pe", lambda e: e.transpose(pTb[:, 128 * c:128 * c + 128], mb_[:, 128 * c:128 * c + 128], p.ident[:]), reads=[mb_, p.ident], writes=[pTb])
                    kb.op("dve", lambda e: e.tensor_copy(out=memT[:, :, 128 * mt:128 * mt + 128], in_=pTb[:].rearrange("p (c t) -> p c t", c=KC)), reads=[pTb], writes=[memT])
            for h in range(4):
                pa = pA[h % 2]
                for k in range(KC):
                    kb.op("pe", lambda e: e.matmul(pa[:, 0:MEM], Wkv[:, k, 128 * h:128 * h + 128], memT[:, k, :], start=(k == 0), stop=(k == KC - 1)), reads=[Wkv, memT], writes=[pa])
                kb.op("act", lambda e: e.activation(out=kT[:, h, :], in_=pa[:, 0:MEM], func=AF.Copy), reads=[pa], writes=[kT])
            for mt in range(2):
                pa = pA[mt % 2]
                for k in range(KC):
                    kb.op("pe", lambda e: e.matmul(pa[:], memT[:, k, 128 * mt:128 * mt + 128], Wkv[:, k, 512:1024], start=(k == 0), stop=(k == KC - 1)), reads=[Wkv, memT], writes=[pa])
                kb.op("act", lambda e: e.activation(out=Vx[:, mt, :, 0:128], in_=pa[:].rearrange("p (h d) -> p h d", d=128), func=AF.Copy), reads=[pa], writes=[Vx])
            qs = [p.sb(es, "xa_qs%d" % i, [128, 512], BF16) for i in range(2)]
            Pm = [[p.sb(es, "xa_P%d_%d" % (i, m), [128, 512], BF16) for m in range(2)] for i in range(2)]
            rd = p.sb(es, "xa_rd", [128, 2], F32)
            qs.append(p.sb(es, "xa_qs2", [128, 512], BF16))
            Pm.append([p.sb(es, "xa_P2_%d" % m, [128, 512], BF16) for m in range(2)])
            rds = [rd, p.sb(es, "xa_rd2", [128, 2], F32)]
            steps = []
            for G in range(4):
                for h in range(4):
                    def mk(G=G, h=h, cnt=len(steps)):
                        pa, q_, P_ = pA[cnt % 2], qs[cnt % 3], Pm[cnt % 3]

                        def s1():
                            for k in range(KC):
                                kb.op("pe", lambda e: e.matmul(pa[:], Wq[:, k, 128 * h:128 * h + 128], hT[:, k, 512 * G:512 * G + 512], start=(k == 0), stop=(k == KC - 1)), reads=[Wq, hT], writes=[pa])
                            kb.op("act", lambda e: e.mul(q_[:], pa[:], 128 ** -0.5), reads=[pa], writes=[q_])

                        def s2():
                            for m in range(2):
                                kb.op("pe", lambda e: e.matmul(pS[m][:], kT[:, h, 128 * m:128 * m + 128], q_[:], start=True, stop=True), reads=[kT, q_], writes=[pS[m]])
                                kb.op("act", lambda e: e.activation(out=P_[m][:], in_=pS[m][:], func=AF.Exp), reads=[pS[m]], writes=[P_[m]])

                        def s3():
                            for jj in range(2):
                                po = pO[jj]
                                rd_ = rds[jj]
                                for j2 in range(2):
                                    j = 2 * jj + j2
                                    for m in range(2):
                                        kb.op("pe", lambda e: e.matmul(po[:, 129 * j2:129 * j2 + 129], P_[m][:, 128 * j:128 * j + 128], Vx[:, m, h, :], start=(m == 0), stop=(m == 1)), reads=[P_[m], Vx], writes=[po])
                                kb.op("dve", lambda e: e.reciprocal(out=rd_[:], in_=po[:, 0:258].rearrange("p (j f) -> p j f", f=129)[:, :, 128]), reads=[po], writes=[rd_])
                                for j2 in range(2):
                                    j = 2 * jj + j2
                                    kb.op("dve", lambda e: e.tensor_scalar(out=o_all[:, 4 * G + j, 128 * h:128 * h + 128], in0=po[:, 129 * j2:129 * j2 + 128], scalar1=rd_[:, j2:j2 + 1], scalar2=None, op0=ALU.mult), reads=[po, rd_], writes=[o_all])
                        return (s1, s2, s3)
                    steps.append(mk())
            pipeline(steps, 3)
            kb.barrier()
        with ExitStack() as es:
            Wo = p.sb(es, "xa_Wo", [128, 4, D], BF16)
            kb.dma("sp", Wo[:], wo.h.rearrange("(c p) n -> p c n", p=128), reads=[wo], writes=[Wo])
            lnw = ln_tiles(p, es, "xa_ln")
            ln_load_gb(p, lnw, g[l], b[l])
            hold = [p.sb(es, "xa_h%d" % i, [128, D], F32) for i in range(2)]
            pf = [p.ps(es, "xa_pf%d" % i, [128, 512]) for i in range(4)]
            pT2 = p.ps(es, "xa_pT2", [128, 1024], BF16)
            oT = [p.sb(es, "xa_oT%d" % i, [128, 4, 128], BF16) for i in range(2)]
            hold.append(p.sb(es, "xa_h2", [128, D], F32))
            steps = []
            for tt in range(NT):
                def mk(tt=tt):
                    ho, o_ = hold[tt % 3], oT[tt % 2]

                    def s1():
                        kb.dma("sp", ho[:], p.h[tt * 128:(tt + 1) * 128, :], reads=[p.hres[tt]], writes=[ho])
                        for c in range(4):
                            kb.op("pe", lambda e: e.transpose(pT2[:, 128 * c:128 * c + 128], o_all[:, tt, 128 * c:128 * c + 128], p.ident[:]), reads=[o_all, p.ident], writes=[pT2])
                        kb.op("act", lambda e: e.activation(out=o_[:], in_=pT2[:, 0:512].rearrange("p (c t) -> p c t", c=4), func=AF.Copy), reads=[pT2], writes=[o_])

                    def s2():
                        for cg in range(4):
                            for c in range(4):
                                kb.op("pe", lambda e: e.matmul(pf[cg][:], o_[:, c, :], Wo[:, c, 512 * cg:512 * cg + 512], start=(c == 0), stop=(c == 3)), reads=[o_, Wo], writes=[pf[cg]])
                            kb.op("dve", lambda e: e.scalar_tensor_tensor(out=ho[:, 512 * cg:512 * cg + 512], in0=ho[:, 512 * cg:512 * cg + 512], scalar=ALPHA, in1=pf[cg][:], op0=ALU.mult, op1=ALU.add), reads=[ho, pf[cg]], writes=[ho])
                        emit_ln_a(p, lnw, ho, tt, 1e-5)

                    def s3():
                        emit_ln_b(p, lnw, tt)
                    return (s1, s2, s3)
                steps.append(mk())
            pipeline(steps, 3)
            kb.barrier()


def phase_ffn(p, l, final_out=None):
    kb = p.kb
    wg, wu, wd = p.wb[("ffn_w_gate", l)], p.wb[("ffn_w_up", l)], p.wb[("ffn_w_down", l)]
    g = p.dram("ln3_g", IN_SHAPES["ln3_g"], F32, kind="ExternalInput")
    b = p.dram("ln3_b", IN_SHAPES["ln3_b"], F32, kind="ExternalInput")
    NCH = FFN // 128
    with ExitStack() as es:
        lnw = ln_tiles(p, es, "ff_ln")
        ln_load_gb(p, lnw, g[l], b[l])
        hTg = p.sb(es, "ff_hT", [128, KC, 512], BF16)
        a_t = p.sb(es, "ff_a", [128, NCH, 512], BF16)
        Wg_t = [p.sb(es, "ff_Wg%d" % i, [128, KC, 256], BF16) for i in range(2)]
        Wu_t = [p.sb(es, "ff_Wu%d" % i, [128, KC, 256], BF16) for i in range(2)]
        Wd_t = [p.sb(es, "ff_Wd%d" % i, [128, 11, 512], BF16) for i in range(2)]
        s_t = [p.sb(es, "ff_s%d" % i, [128, 512], F32) for i in range(2)]
        v4 = p.sb(es, "ff_v4", [128, 4, D], F32)
        pg = [p.ps(es, "ff_pg%d" % i, [128, 512]) for i in range(2)]
        pu = [p.ps(es, "ff_pu%d" % i, [128, 512]) for i in range(2)]
        pf = [p.ps(es, "ff_pf%d" % i, [128, 512]) for i in range(2)]
        lnw["hbs"] = lnw["hbs"] + [p.sb(es, "ff_hb%d" % i, [128, D], BF16) for i in range(2, 4)]
        lnw["hTs"] = lnw["hTs"] + [p.sb(es, "ff_hTs%d" % i, [128, KC, 128], BF16) for i in range(2, 4)]
        wl = 0
        dl = 0
        cnt = 0
        accs = [pf[0], pf[1], pg[0], pg[1]]
        pending = []

        class _W:
            def __init__(self, j):
                self.j = j

            def __getitem__(self, k):
                if k == "hbs":
                    return [lnw["hbs"][self.j]] * 2
                if k == "hTs":
                    return [lnw["hTs"][self.j]] * 2
                return lnw[k]

        for tg in range(4):
            kb.dma("sp", hTg[:], p.hTd[:, 512 * tg:512 * tg + 512].rearrange("(c p) t -> p c t", p=128), reads=[p.hTd], writes=[hTg])
            for c2 in range(NCH // 2):
                Wg_, Wu_ = Wg_t[wl % 2], Wu_t[wl % 2]
                wl += 1
                kb.dma("sp", Wg_[:], wg[:, 256 * c2:256 * c2 + 256].rearrange("(c p) n -> p c n", p=128), reads=[wg], writes=[Wg_])
                kb.dma("sp", Wu_[:], wu[:, 256 * c2:256 * c2 + 256].rearrange("(c p) n -> p c n", p=128), reads=[wu], writes=[Wu_])
                for ci in range(2):
                    ch = 2 * c2 + ci
                    pg_, pu_, st_ = pg[cnt % 2], pu[cnt % 2], s_t[cnt % 2]
                    cnt += 1
                    for k in range(KC):
                        kb.op("pe", lambda e: e.matmul(pg_[:], Wg_[:, k, 128 * ci:128 * ci + 128], hTg[:, k, :], start=(k == 0), stop=(k == KC - 1)), reads=[Wg_, hTg], writes=[pg_])
                    for k in range(KC):
                        kb.op("pe", lambda e: e.matmul(pu_[:], Wu_[:, k, 128 * ci:128 * ci + 128], hTg[:, k, :], start=(k == 0), stop=(k == KC - 1)), reads=[Wu_, hTg], writes=[pu_])
                    kb.op("act", lambda e: e.activation(out=st_[:], in_=pg_[:], func=AF.Silu), reads=[pg_], writes=[st_])
                    kb.op("dve", lambda e: e.tensor_tensor(out=a_t[:, ch, :], in0=st_[:], in1=pu_[:], op=ALU.mult), reads=[st_, pu_], writes=[a_t])
                if c2 == 1:
                    for f in pending:
                        f()
                    pending = []
                    for j in range(4):
                        tt = 4 * tg + j
                        kb.dma("sp", v4[:, j, :], p.h[tt * 128:(tt + 1) * 128, :], reads=[p.hres[tt]], writes=[v4])
            for cg in range(4):
                for pc in range(4):
                    Wd_ = Wd_t[dl % 2]
                    dl += 1
                    kb.dma("sp", Wd_[:], wd[pc * 11 * 128:(pc + 1) * 11 * 128, 512 * cg:512 * cg + 512].rearrange("(c p) n -> p c n", p=128), reads=[wd], writes=[Wd_])
                    for j in range(4):
                        for c in range(11):
                            ch = pc * 11 + c
                            kb.op("pe", lambda e: e.matmul(accs[j][:], a_t[:, ch, 128 * j:128 * j + 128], Wd_[:, c, :], start=(ch == 0), stop=(ch == NCH - 1)), reads=[a_t, Wd_], writes=[accs[j]])
                for j in range(4):
                    kb.op("dve", lambda e: e.scalar_tensor_tensor(out=v4[:, j, 512 * cg:512 * cg + 512], in0=v4[:, j, 512 * cg:512 * cg + 512], scalar=ALPHA, in1=accs[j][:], op0=ALU.mult, op1=ALU.add), reads=[v4, accs[j]], writes=[v4])
            fin = final_out is not None
            for j in range(4):
                tt = 4 * tg + j
                emit_ln_a(p, _W(j), _V4(v4, j), tt, 1e-5, out_dram=(final_out.h if fin else None), out_res=final_out, write_hT=(not fin), h_store=(not fin))
                if not fin:
                    pending.append(lambda j=j, tt=tt: emit_ln_b(p, _W(j), tt))
        for f in pending:
            f()
        kb.barrier()


class _V4:
    def __init__(self, t, j):
        self.t, self.j, self.r = t, j, t.r

    def __getitem__(self, k):
        if isinstance(k, tuple):
            return self.t.h[(k[0], self.j) + tuple(k[1:])]
        return self.t.h[k, self.j]


def build_full(depth=DEPTH):
    p = Prog(ext_out=["out"])
    setup_globals(p)
    out = p.dram("out", (S, D), F32)
    convert_weights(p, names=["w_in"], layers=[0])
    phase_ln0(p)
    for l in range(depth):
        convert_weights(p, names=["s5_w_glu", "nsa_cmp_w1", "nsa_cmp_w2"], layers=[l])
        phase_inproj(p, l)
        convert_weights(p, names=["w_out", "xa_wq", "xa_wkv", "xa_wo", "ffn_w_gate"], layers=[l])
        phase_s5(p, l)
        convert_weights(p, names=["ffn_w_up"], layers=[l])
        if l + 1 < depth:
            convert_weights(p, names=["w_in"], layers=[l + 1])
        phase_nsa(p, l)
        convert_weights(p, names=["ffn_w_down"], layers=[l])
        phase_sb(p, l)
        phase_dil(p, l)
        phase_outproj(p, l)
        phase_xattn(p, l)
        phase_ffn(p, l, final_out=(out if l == depth - 1 else None))
    p.kb.finish([out.r])
    p.kb.close()
    return p


def kernel(**inputs):
    p = build_full()
    consts = host_consts()
    n = 8
    in_maps = []
    for b in range(n):
        m = {}
        for name, kind in p.kinds.items():
            if kind != "ExternalInput":
                continue
            if name in consts:
                m[name] = consts[name]
            elif name in ("x", "mem"):
                m[name] = np.ascontiguousarray(np.asarray(inputs[name])[b], dtype=np.float32)
            else:
                m[name] = np.ascontiguousarray(np.asarray(inputs[name]), dtype=np.float32)
        in_maps.append(m)
    res = run_bass_kernel_spmd(p.nc, in_maps, core_ids=list(range(n)))
    return np.stack([np.asarray(res.results[b]["out"], dtype=np.float32) for b in range(n)], axis=0)
```

```python
import math
from contextlib import ExitStack

import numpy as np
import ml_dtypes
import concourse.bass as bass
import concourse.mybir as mybir
from concourse.bass_utils import run_bass_kernel_spmd

F32 = mybir.dt.float32
BF16 = mybir.dt.bfloat16
AF = mybir.ActivationFunctionType
ALU = mybir.AluOpType
AX = mybir.AxisListType


class Res:
    __slots__ = ("name", "w", "r", "excl")

    def __init__(self, name=""):
        self.name = name
        self.excl = False
        self.w = None
        self.r = []


class KB:
    N_DMA_SEMS = 24

    def __init__(self, nc):
        self.nc = nc
        self.es = ExitStack()
        self.eng = {"pe": nc.tensor, "dve": nc.vector, "act": nc.scalar, "pool": nc.gpsimd, "sp": nc.sync}
        self.sem = {}
        self.cnt = {}
        self.semh = {}
        for e in self.eng:
            h = self.es.enter_context(nc.semaphore("s_" + e))
            self.sem[e] = h
            self.cnt[e] = 0
            self.semh[("c", e)] = h
        self.dsem = {}
        self.dval = {}
        self.dnext = {}
        for q in ("sp", "pool", "act"):
            self.dsem[q] = []
            for i in range(self.N_DMA_SEMS):
                h = self.es.enter_context(nc.semaphore("d_%s_%d" % (q, i)))
                self.dsem[q].append(h)
                self.semh[("d", q, i)] = h
                self.dval[(q, i)] = 0
            self.dnext[q] = 0
        self.known = {e: {} for e in self.eng}
        self.ninstr = 0
        for h in self.semh.values():
            nc.gpsimd.sem_clear(h)
        nc.all_engine_barrier()

    def _wait(self, e, ev):
        if ev is None:
            return
        key, val = ev
        if key == ("c", e) and e == "pe":
            return
        k = self.known[e]
        if k.get(key, 0) >= val:
            return
        self.eng[e].wait_ge(self.semh[key], val)
        k[key] = val

    def _deps(self, e, reads, writes):
        for r in reads:
            self._wait(e, r.w)
        for w in writes:
            self._wait(e, w.w)
            for ev in w.r:
                self._wait(e, ev)

    def _commit(self, ev, reads, writes):
        for r in reads:
            if r in writes:
                continue
            r.r.append(ev)
            if len(r.r) > 12:
                d = {}
                for k, v in r.r:
                    d[k] = max(d.get(k, 0), v)
                r.r = list(d.items())
        for w in writes:
            w.w = ev
            w.r = []

    def op(self, e, fn, reads=(), writes=()):
        self._deps(e, reads, writes)
        ins = fn(self.eng[e])
        self.cnt[e] += 1
        ins.then_inc(self.sem[e], 1)
        ev = (("c", e), self.cnt[e])
        self._commit(ev, reads, writes)
        self.ninstr += 1
        return ev

    def dma(self, q, out, in_, reads=(), writes=(), **kw):
        i = self.dnext[q]
        self.dnext[q] = (i + 1) % len(self.dsem[q])
        key = ("d", q, i)
        if self.dval[(q, i)] > 0:
            self._wait(q, (key, self.dval[(q, i)]))
        self._deps(q, reads, writes)
        kw.setdefault("allow_slow_non_contiguous", True)
        ins = self.eng[q].dma_start(out=out, in_=in_, **kw)
        self.dval[(q, i)] += 16
        ins.then_inc(self.semh[key], 16)
        ev = (key, self.dval[(q, i)])
        self._commit(ev, reads, writes)
        self.ninstr += 1
        return ev

    def finish(self, final_res):
        for r in final_res:
            self._wait("sp", r.w)
        for q in self.dsem:
            for i in range(len(self.dsem[q])):
                if self.dval[(q, i)] > 0:
                    self._wait("sp", (("d", q, i), self.dval[(q, i)]))
        for e in ("pe", "dve", "act", "pool"):
            if self.cnt[e] > 0:
                self._wait("sp", (("c", e), self.cnt[e]))
        self.nc.all_engine_barrier()
        for h in self.semh.values():
            self.nc.gpsimd.sem_clear(h)
        self.nc.all_engine_barrier()

    def close(self):
        self.es.close()

    def barrier(self):
        evs = []
        for e in ("pe", "dve", "act", "pool", "sp"):
            if self.cnt[e] > 0:
                evs.append((("c", e), self.cnt[e]))
        for q in self.dsem:
            for i in range(len(self.dsem[q])):
                if self.dval[(q, i)] > 0:
                    evs.append((("d", q, i), self.dval[(q, i)]))
        for e in ("pe", "dve", "act", "pool", "sp"):
            for ev in evs:
                if ev[0] == ("c", e):
                    continue
                self._wait(e, ev)


def _res(x):
    return x.r if hasattr(x, "r") and not isinstance(x, Res) else x


_orig_op = KB.op
_orig_dma = KB.dma


def _op(self, e, fn, reads=(), writes=()):
    reads = [_res(x) for x in reads]
    writes = [_res(x) for x in writes]
    for r in reads:
        if r.excl and r not in writes:
            writes.append(r)
    return _orig_op(self, e, fn, reads, writes)


def _dma(self, q, out, in_, reads=(), writes=(), **kw):
    return _orig_dma(self, q, out, in_, [_res(x) for x in reads], [_res(x) for x in writes], **kw)


KB.op = _op
KB.dma = _dma


def pipeline(steps, nst):
    n = len(steps)
    for i in range(n + nst - 1):
        for s_ in range(nst):
            k = i - s_
            if 0 <= k < n and steps[k][s_] is not None:
                steps[k][s_]()


class TL:
    def __init__(self, h, name=""):
        self.h = h
        self.r = Res(name)

    def __getitem__(self, k):
        return self.h[k]


D = 2048
S = 2048
NT = S // 128
KC = D // 128
DEPTH = 2
MEM = 256
FFN = 5632
ALPHA = (2 * DEPTH) ** 0.25
IN_W = 4888
O_U, O_NQ, O_NKV, O_GATE, O_SB, O_DIL = 0, 512, 1024, 1792, 1816, 3352
FM_SRC = [128 * i for i in range(14)] + [1816 + 128 * j for j in range(8)] + [3352 + 128 * j for j in range(8)]
FM_ROPE = set([4, 5, 6, 7, 10, 12] + list(range(22, 30)))
FM_U, FM_NQ, FM_KCMP, FM_VCMP, FM_KSLC, FM_KWIN, FM_SBQ, FM_SBK, FM_DQ, FM_DK = 0, 4, 8, 9, 10, 12, 14, 18, 22, 26
TM_SRC = [(1408, 128), (1664, 128), (2840, 512), (4376, 512), (1792, 24)]
TM_VSLC, TM_VWIN, TM_SBV, TM_DV = 0, 128, 256, 768
TM_W = 1280

WNAMES = ["w_in", "s5_w_glu", "nsa_cmp_w1", "nsa_cmp_w2", "w_out", "xa_wq", "xa_wkv", "xa_wo",
          "ffn_w_gate", "ffn_w_up", "ffn_w_down"]
IN_SHAPES = {
    "x": (S, D), "mem": (MEM, D), "ln_in_g": (D,), "ln_in_b": (D,), "w_in": (DEPTH, D, IN_W),
    "s5_lambda_re": (DEPTH, 32, 64), "s5_lambda_im": (DEPTH, 32, 64), "s5_log_dt": (DEPTH, 32),
    "s5_b_re": (DEPTH, 32, 64, 16), "s5_b_im": (DEPTH, 32, 64, 16), "s5_c_re": (DEPTH, 32, 16, 64),
    "s5_c_im": (DEPTH, 32, 16, 64), "s5_d": (DEPTH, 32, 16), "s5_w_glu": (DEPTH, 512, 512),
    "s5_b_glu": (DEPTH, 512), "nsa_cmp_pe": (DEPTH, 2, 32, 64), "nsa_cmp_w1": (DEPTH, 2, 2048, 128),
    "nsa_cmp_w2": (DEPTH, 2, 128, 64), "mix_norm_g": (DEPTH, 4, 512), "w_out": (DEPTH, D, D),
    "ln1_g": (DEPTH, D), "ln1_b": (DEPTH, D), "xa_wq": (DEPTH, D, 512), "xa_wkv": (DEPTH, D, 1024),
    "xa_wo": (DEPTH, 512, D), "ln2_g": (DEPTH, D), "ln2_b": (DEPTH, D), "ffn_w_gate": (DEPTH, D, FFN),
    "ffn_w_up": (DEPTH, D, FFN), "ffn_w_down": (DEPTH, FFN, D), "ln3_g": (DEPTH, D), "ln3_b": (DEPTH, D),
}


def host_consts():
    c = {}
    c["c_ident"] = np.eye(128, dtype=np.float32)
    half = 32
    inv = (10000.0 ** (-np.arange(half, dtype=np.float32) / half)).astype(np.float32)
    pos = np.arange(S, dtype=np.float32)
    ang = pos[None, :] * inv[:, None]
    cos = np.cos(ang).astype(np.float32)
    sin = np.sin(ang).astype(np.float32)
    cos64 = np.concatenate([cos, cos], 0)
    sin64 = np.concatenate([-sin, sin], 0)
    c["c_cos"] = np.concatenate([cos64, cos64], 0)
    c["c_sin"] = np.concatenate([sin64, sin64], 0)
    sw = np.zeros((128, 128), np.float32)
    for m in range(128):
        k = m + 32 if (m % 64) < 32 else m - 32
        sw[k, m] = 1.0
    c["c_swap"] = sw
    k = np.arange(128)[:, None, None]
    i = np.arange(4)[None, :, None]
    q = np.arange(512)[None, None, :]
    c["c_maskS"] = ((128 * i + k) < q).astype(np.float32)
    c["c_maskC"] = ((128 * i + k) <= q).astype(np.float32)
    c["c_maskCn"] = 1.0 - c["c_maskC"]
    jj = np.arange(128)[:, None]
    kk = np.arange(128)[None, :]
    c["c_negU"] = -(jj >= kk).astype(np.float32)
    c["c_negOnes"] = -np.ones((128, 128), np.float32)
    c["c_ones"] = np.ones((128, 128), np.float32)
    c["c_maskD"] = np.stack([(jj >= kk), (jj <= kk)], 1).astype(np.float32)
    cc = np.arange(128)[:, None]
    tt_ = np.arange(S)[None, :]
    c["c_maskCmp"] = ((16 * cc + 31 <= tt_) & (cc < 127)).astype(np.float32)
    ci = np.arange(128)[:, None] * 16
    sj = np.arange(32)[None, :] * 64
    ov = np.clip(np.minimum(ci + 32, sj + 64) - np.maximum(ci, sj), 0, None) / 32.0
    ov[127] = 0
    c["c_overlap"] = ov.astype(np.float32)
    t = np.arange(S)[:, None]
    n = np.arange(32)[None, :]
    cur = t // 64
    invalid = n * 64 > t
    forced = ((n == 0) | (n == cur) | (n == cur - 1)) & ~invalid
    c["c_selMul"] = (~(invalid | forced)).astype(np.float32)
    c["c_selAdd"] = np.where(forced, 1e4, np.where(invalid, -1e4, 0.0)).astype(np.float32)
    eb = np.zeros((32, 16, 128), np.float32)
    for kb_ in range(16):
        for k in range(128):
            eb[2 * kb_ + k // 64, kb_, k] = 32768.0
    c["c_Ebig"] = eb
    c["c_iota"] = np.tile(np.arange(S, dtype=np.float32)[None, :], (128, 1))
    pp = np.arange(128)
    c["c_mask2"] = np.stack([((pp // 16) % 2 == 0), ((pp // 16) % 2 == 1)], 1).astype(np.float32)
    m3 = ((pp[:, None] // 64) == ((pp[None, :] // 16) % 2)).astype(np.float32)
    c["c_mask2z"] = c["c_mask2"] * (pp[:, None] >= 96)
    c["c_mask3"] = m3
    c["c_mask3n"] = -m3
    c["c_cos_cmp"] = np.ascontiguousarray(c["c_cos"][:, 31::16][:, :127])
    c["c_sin_cmp"] = np.ascontiguousarray(c["c_sin"][:, 31::16][:, :127])
    return c


class Prog:
    def __init__(self, ext_in=(), ext_out=()):
        self.nc = bass.Bass("TRN2", target_bir_lowering=False)
        self.kb = KB(self.nc)
        self.ext_in = set(ext_in)
        self.ext_out = set(ext_out)
        self.d = {}
        self.kinds = {}
        self.uid = 0
        self.es = self.kb.es

    def dram(self, name, shape, dtype, kind=None):
        if name in self.d:
            return self.d[name]
        if kind is None:
            kind = "ExternalInput" if name in self.ext_in else ("ExternalOutput" if name in self.ext_out else "Internal")
        t = TL(self.nc.dram_tensor(name, list(shape), dtype, kind=kind).ap(), name)
        self.d[name] = t
        self.kinds[name] = kind
        return t

    def sb(self, es, name, shape, dtype):
        self.uid += 1
        name = "%s_u%d" % (name, self.uid)
        return TL(es.enter_context(self.nc.sbuf_tensor(name, list(shape), dtype)), name)

    def ps(self, es, name, shape, dtype=F32):
        self.uid += 1
        name = "%s_u%d" % (name, self.uid)
        t = TL(es.enter_context(self.nc.psum_tensor(name, list(shape), dtype)), name)
        t.r.excl = True
        return t


def setup_globals(p):
    kb = p.kb
    p.hTd = p.dram("hT_stream", (D, S), BF16)
    p.ident = p.sb(p.es, "ident", [128, 128], BF16)
    cid = p.dram("c_ident", (128, 128), F32, kind="ExternalInput")
    kb.dma("pool", p.ident[:], cid[:, :], writes=[p.ident])
    p.h = p.dram("h_stream", (S, D), F32)
    p.hres = [Res("h%d" % i) for i in range(NT)]


def ln_tiles(p, es, pfx):
    w = {}
    w["st"] = p.sb(es, pfx + "st", [128, 24], F32)
    w["mv"] = p.sb(es, pfx + "mv", [128, 2], F32)
    w["hbs"] = [p.sb(es, pfx + "hb%d" % i, [128, D], BF16) for i in range(2)]
    w["pT"] = p.ps(es, pfx + "pT", [128, D], BF16)
    w["g"] = p.sb(es, pfx + "g", [128, D], F32)
    w["b"] = p.sb(es, pfx + "b", [128, D], F32)
    w["hTs"] = [p.sb(es, pfx + "hTs%d" % i, [128, KC, 128], BF16) for i in range(2)]
    return w


def ln_load_gb(p, w, g_ap, b_ap):
    p.kb.dma("sp", w["g"][:], g_ap.partition_broadcast(128), writes=[w["g"]])
    p.kb.dma("sp", w["b"][:], b_ap.partition_broadcast(128), writes=[w["b"]])


def emit_ln_a(p, w, v, tt, eps, out_dram=None, out_res=None, write_hT=True, h_store=True):
    kb = p.kb
    st, mv, g, b = w["st"], w["mv"], w["g"], w["b"]
    hb = w["hbs"][tt % 2]
    for c in range(4):
        kb.op("dve", lambda e: e.bn_stats(out=st[:, 6 * c:6 * c + 6], in_=v[:, 512 * c:512 * c + 512]), reads=[v], writes=[st])
    kb.op("dve", lambda e: e.bn_aggr(out=mv[:], in_=st[:]), reads=[st], writes=[mv])
    kb.op("act", lambda e: e.activation(out=mv[:, 1:2], in_=mv[:, 1:2], func=AF.Sqrt, bias=eps), reads=[mv], writes=[mv])
    kb.op("dve", lambda e: e.reciprocal(out=mv[:, 1:2], in_=mv[:, 1:2]), reads=[mv], writes=[mv])
    kb.op("dve", lambda e: e.tensor_scalar(out=v[:], in0=v[:], scalar1=mv[:, 0:1], scalar2=mv[:, 1:2], op0=ALU.subtract, op1=ALU.mult), reads=[v, mv], writes=[v])
    kb.op("pool", lambda e: e.tensor_tensor(out=v[:], in0=v[:], in1=g[:], op=ALU.mult), reads=[v, g], writes=[v])
    kb.op("pool", lambda e: e.tensor_tensor(out=v[:], in0=v[:], in1=b[:], op=ALU.add), reads=[v, b], writes=[v])
    if h_store:
        kb.dma("pool", p.h[tt * 128:(tt + 1) * 128, :], v[:], reads=[v], writes=[p.hres[tt]])
    if out_dram is not None:
        kb.dma("pool", out_dram[tt * 128:(tt + 1) * 128, :], v[:], reads=[v], writes=[out_res])
    if write_hT:
        kb.op("act", lambda e: e.activation(out=hb[:], in_=v[:], func=AF.Copy), reads=[v], writes=[hb])


def emit_ln_b(p, w, tt):
    kb = p.kb
    hb, pT = w["hbs"][tt % 2], w["pT"]
    for c in range(KC):
        kb.op("pe", lambda e: e.transpose(pT[:, 128 * c:128 * c + 128], hb[:, 128 * c:128 * c + 128], p.ident[:]), reads=[hb, p.ident], writes=[pT])
    hs = w["hTs"][tt % 2]
    kb.op("act", lambda e: e.activation(out=hs[:], in_=pT[:].rearrange("p (c t) -> p c t", c=KC), func=AF.Copy), reads=[pT], writes=[hs])
    kb.dma("act", p.hTd[:, tt * 128:(tt + 1) * 128].rearrange("(c p) t -> p c t", p=128), hs[:], reads=[hs], writes=[p.hTd])


def emit_ln(p, w, v, tt, eps, out_dram=None, out_res=None, write_hT=True, h_store=True):
    emit_ln_a(p, w, v, tt, eps, out_dram, out_res, write_hT, h_store)
    if write_hT:
        emit_ln_b(p, w, tt)


def load_hT(p, es, name="hT"):
    t = p.sb(es, name, [128, KC, S], BF16)
    for c in range(0, KC, 4):
        p.kb.dma("sp", t[:, c:c + 4, :], p.hTd[c * 128:(c + 4) * 128, :].rearrange("(c p) t -> p c t", p=128), reads=[p.hTd], writes=[t])
    return t


def convert_weights(p, names=WNAMES, layers=range(DEPTH)):
    kb = p.kb
    if not hasattr(p, "wb"):
        p.wb = {}
    for l in layers:
        for n in names:
            shp = IN_SHAPES[n][1:]
            src = p.dram(n, IN_SHAPES[n], F32, kind="ExternalInput")
            dst = p.dram("bf_%s_%d" % (n, l), shp, BF16)
            p.wb[(n, l)] = dst
            s_l = src[l]
            d_l = dst.h
            if len(shp) == 3:
                s_l = s_l.rearrange("a b c -> (a b) c")
                d_l = d_l.rearrange("a b c -> (a b) c")
            rows, cols = s_l.shape
            step = max(1, (1024 * 1024) // cols)
            r0 = 0
            while r0 < rows:
                r1 = min(rows, r0 + step)
                kb.dma("pool", d_l[r0:r1, :], s_l[r0:r1, :], writes=[dst])
                r0 = r1


def phase_ln0(p):
    kb = p.kb
    x = p.dram("x", (S, D), F32, kind="ExternalInput")
    g = p.dram("ln_in_g", (D,), F32, kind="ExternalInput")
    b = p.dram("ln_in_b", (D,), F32, kind="ExternalInput")
    with ExitStack() as es:
        w = ln_tiles(p, es, "l0")
        ln_load_gb(p, w, g.h, b.h)
        vs = [p.sb(es, "l0v%d" % i, [128, D], F32) for i in range(2)]
        for tt in range(NT):
            v = vs[tt % 2]
            kb.dma("sp", v[:], x[tt * 128:(tt + 1) * 128, :], writes=[v])
            emit_ln(p, w, v, tt, 1e-5)
        kb.barrier()


def phase_inproj(p, l, do_fm=True, do_tm=True, max_groups=99):
    kb = p.kb
    wb = p.wb[("w_in", l)]
    fm = p.dram("fm", (30 * 128, S), BF16)
    tm = p.dram("tm", (S, TM_W), BF16)
    gates = p.dram("gates", (S, 24), F32)
    ccos = p.dram("c_cos", (128, S), F32, kind="ExternalInput")
    csin = p.dram("c_sin", (128, S), F32, kind="ExternalInput")
    cswap = p.dram("c_swap", (128, 128), F32, kind="ExternalInput")
    with ExitStack() as es:
        p.hT = load_hT(p, es)
        cos = p.sb(es, "a_cos", [128, S], F32)
        sin = p.sb(es, "a_sin", [128, S], F32)
        swp = p.sb(es, "a_swp", [128, 128], BF16)
        kb.dma("sp", cos[:], ccos[:, :], writes=[cos])
        kb.dma("sp", sin[:], csin[:, :], writes=[sin])
        kb.dma("pool", swp[:], cswap[:, :], writes=[swp])
        wts = [p.sb(es, "a_w%d" % i, [128, KC, 512], BF16) for i in range(2)]
        pacc = [p.ps(es, "a_pacc%d" % i, [128, 512]) for i in range(2)]
        psw = [p.ps(es, "a_psw%d" % i, [128, 512]) for i in range(2)]
        xb = [p.sb(es, "a_xb%d" % i, [128, 512], BF16) for i in range(2)]
        t1 = [p.sb(es, "a_t1%d" % i, [128, 512], F32) for i in range(2)]
        t2 = [p.sb(es, "a_t2%d" % i, [128, 512], F32) for i in range(2)]
        outs = [p.sb(es, "a_out%d" % i, [128, S], BF16) for i in range(2)]
        groups = []
        i = 0
        while i < len(FM_SRC):
            j = i
            while j + 1 < len(FM_SRC) and j + 1 - i < 4 and FM_SRC[j + 1] == FM_SRC[j] + 128:
                j += 1
            groups.append((i, j - i + 1))
            i = j + 1
        steps = []
        for gi, (c0, n) in enumerate(groups if do_fm else []):
            if gi >= max_groups:
                break
            for ci in range(n):
                for tg in range(4):
                    def mk(gi=gi, c0=c0, n=n, ci=ci, tg=tg, cnt=len(steps)):
                        wt = wts[gi % 2]
                        src0 = FM_SRC[c0]
                        ch = c0 + ci
                        ot = outs[ch % 2]
                        pa = pacc[cnt % 2]
                        osl = ot[:, 512 * tg:512 * tg + 512]
                        rope = ch in FM_ROPE
                        x_b, ps2, a1, a2 = xb[cnt % 2], psw[cnt % 2], t1[cnt % 2], t2[cnt % 2]

                        def fin():
                            if tg == 3:
                                kb.dma("pool", fm[ch * 128:(ch + 1) * 128, :], ot[:], reads=[ot], writes=[fm])

                        def s1():
                            if ci == 0 and tg == 0:
                                kb.dma("sp", wt[:, :, 0:128 * n], wb[:, src0:src0 + 128 * n].rearrange("(c p) n -> p c n", p=128), reads=[wb], writes=[wt])
                            for k in range(KC):
                                kb.op("pe", lambda e: e.matmul(pa[:], wt[:, k, 128 * ci:128 * ci + 128], p.hT[:, k, 512 * tg:512 * tg + 512], start=(k == 0), stop=(k == KC - 1)), reads=[wt, p.hT], writes=[pa])
                            if rope:
                                kb.op("act", lambda e: e.activation(out=x_b[:], in_=pa[:], func=AF.Copy), reads=[pa], writes=[x_b])
                            else:
                                if cnt % 2 == 0:
                                    kb.op("act", lambda e: e.activation(out=osl, in_=pa[:], func=AF.Copy), reads=[pa], writes=[ot])
                                else:
                                    kb.op("dve", lambda e: e.tensor_copy(out=osl, in_=pa[:]), reads=[pa], writes=[ot])
                                fin()

                        def s2():
                            kb.op("pe", lambda e: e.matmul(ps2[:], swp[:], x_b[:], start=True, stop=True), reads=[swp, x_b], writes=[ps2])
                            kb.op("dve", lambda e: e.tensor_tensor(out=a1[:], in0=pa[:], in1=cos[:, 512 * tg:512 * tg + 512], op=ALU.mult), reads=[pa, cos], writes=[a1])
                            kb.op("dve", lambda e: e.tensor_tensor(out=a2[:], in0=ps2[:], in1=sin[:, 512 * tg:512 * tg + 512], op=ALU.mult), reads=[ps2, sin], writes=[a2])
                            kb.op("dve", lambda e: e.tensor_tensor(out=osl, in0=a1[:], in1=a2[:], op=ALU.add), reads=[a1, a2], writes=[ot])
                            fin()
                        return (s1, s2 if rope else None)
                    steps.append(mk())
        pipeline(steps, 2)
        kb.barrier()
    with ExitStack() as es:
        p.hT = load_hT(p, es)
        wv = p.sb(es, "a_wv", [128, KC, TM_W + 24], BF16)
        off = 0
        for (s0, wd) in TM_SRC:
            kb.dma("sp", wv[:, :, off:off + wd], wb[:, s0:s0 + wd].rearrange("(c p) n -> p c n", p=128), reads=[wb], writes=[wv])
            off += wd
        pt = [[p.ps(es, "a_pt%d_%d" % (i, j), [128, 512]) for j in range(4)] for i in range(2)]
        ot = [p.sb(es, "a_ot%d" % i, [128, TM_W], BF16) for i in range(2)]
        gt = [p.sb(es, "a_gt%d" % i, [128, 24], F32) for i in range(2)]
        segs = [(0, 256, 0), (256, 512, 1), (768, 512, 2)]
        for tt in range(NT if do_tm else 0):
            pp = pt[tt % 2]
            o = ot[tt % 2]
            g_ = gt[tt % 2]
            for k in range(KC):
                lhsT = p.hT[:, k, 128 * tt:128 * tt + 128]
                for (c0, wd, pi) in segs:
                    kb.op("pe", lambda e: e.matmul(pp[pi][:, 0:wd], lhsT, wv[:, k, c0:c0 + wd], start=(k == 0), stop=(k == KC - 1)), reads=[wv, p.hT], writes=[pp[pi]])
                kb.op("pe", lambda e: e.matmul(pp[3][:, 0:24], lhsT, wv[:, k, TM_W:TM_W + 24], start=(k == 0), stop=(k == KC - 1)), reads=[wv, p.hT], writes=[pp[3]])
            kb.op("act", lambda e: e.activation(out=o[:, 0:256], in_=pp[0][:, 0:256], func=AF.Copy), reads=[pp[0]], writes=[o])
            kb.op("act", lambda e: e.activation(out=g_[:], in_=pp[3][:, 0:24], func=AF.Sigmoid), reads=[pp[3]], writes=[g_])
            kb.op("dve", lambda e: e.tensor_copy(out=o[:, 256:768], in_=pp[1][:]), reads=[pp[1]], writes=[o])
            kb.op("dve", lambda e: e.tensor_copy(out=o[:, 768:1280], in_=pp[2][:]), reads=[pp[2]], writes=[o])
            kb.dma("pool", tm[tt * 128:(tt + 1) * 128, :], o[:], reads=[o], writes=[tm])
            kb.dma("pool", gates[tt * 128:(tt + 1) * 128, :], g_[:], reads=[g_], writes=[gates])
        kb.barrier()


def go_tiles(p, es, pfx, l, grp):
    w = {}
    w["ss"] = p.sb(es, pfx + "ss", [128, 2], F32)
    w["junk"] = p.sb(es, pfx + "junk", [128, 512], F32)
    w["yb"] = p.sb(es, pfx + "yb", [128, 512], BF16)
    w["pt"] = p.ps(es, pfx + "pt", [128, 1024], BF16)
    w["yt"] = [p.sb(es, pfx + "yt%d" % i, [128, 4, 128], BF16) for i in range(2)]
    w["gain"] = p.sb(es, pfx + "gain", [128, 512], F32)
    g = p.dram("mix_norm_g", IN_SHAPES["mix_norm_g"], F32, kind="ExternalInput")
    p.kb.dma("sp", w["gain"][:], g[l, grp, :].partition_broadcast(128), writes=[w["gain"]])
    w["cnt"] = 0
    return w


def emit_group_out(p, w, y_ap, y_tl, grp, tt):
    kb = p.kb
    ysT = p.dram("ysT", (D, S), BF16)
    ss, junk, yb, pt, gain = w["ss"], w["junk"], w["yb"], w["pt"], w["gain"]
    yt = w["yt"][w["cnt"] % 2]
    w["cnt"] += 1
    kb.op("act", lambda e: e.activation(out=junk[:], in_=y_ap, func=AF.Square, accum_out=ss[:, 0:1]), reads=[y_tl], writes=[junk, ss])
    kb.op("act", lambda e: e.activation(out=ss[:, 1:2], in_=ss[:, 0:1], func=AF.Sqrt, bias=1e-6, scale=1.0 / 512), reads=[ss], writes=[ss])
    kb.op("dve", lambda e: e.reciprocal(out=ss[:, 1:2], in_=ss[:, 1:2]), reads=[ss], writes=[ss])
    kb.op("dve", lambda e: e.scalar_tensor_tensor(out=yb[:], in0=y_ap, scalar=ss[:, 1:2], in1=gain[:], op0=ALU.mult, op1=ALU.mult), reads=[y_tl, ss, gain], writes=[yb])
    for c in range(4):
        kb.op("pe", lambda e: e.transpose(pt[:, 128 * c:128 * c + 128], yb[:, 128 * c:128 * c + 128], p.ident[:]), reads=[yb, p.ident], writes=[pt])
    kb.op("act", lambda e: e.activation(out=yt[:], in_=pt[:, 0:512].rearrange("p (c t) -> p c t", c=4), func=AF.Copy), reads=[pt], writes=[yt])
    kb.dma("act", ysT[grp * 512:(grp + 1) * 512, tt * 128:(tt + 1) * 128].rearrange("(c p) t -> p c t", p=128), yt[:], reads=[yt], writes=[ysT])


def load_const_bf16(p, es, name, shape):
    src = p.dram(name, shape, F32, kind="ExternalInput")
    t = p.sb(es, "k_" + name, list(shape), BF16)
    p.kb.dma("pool", t[:], src.h, writes=[t])
    return t


def phase_sb(p, l, heads=range(8), groups=range(4), after_setup=None):
    kb = p.kb
    fm = p.dram("fm", (30 * 128, S), BF16)
    tm = p.dram("tm", (S, TM_W), BF16)
    with ExitStack() as es:
        maskS = load_const_bf16(p, es, "c_maskS", (128, 4, 512))
        negU = load_const_bf16(p, es, "c_negU", (128, 128))
        negO = load_const_bf16(p, es, "c_negOnes", (128, 128))
        V = p.sb(es, "sb_V", [128, NT, 512], BF16)
        kb.dma("sp", V[:], tm[:, TM_SBV:TM_SBV + 512].rearrange("(t p) f -> p t f", p=128), reads=[tm], writes=[V])
        yc = p.sb(es, "sb_yc", [128, NT, 512], F32)
        kb.op("pool", lambda e: e.memset(yc[:], 0.0), writes=[yc])
        qTs = [p.sb(es, "sb_q%d" % i, [128, S], BF16) for i in range(2)]
        kTs = [p.sb(es, "sb_k%d" % i, [128, S], BF16) for i in range(2)]
        psA = [p.ps(es, "sb_pA%d" % i, [128, 512]) for i in range(2)]
        psB = [p.ps(es, "sb_pB%d" % i, [128, 512]) for i in range(2)]
        psO = [p.ps(es, "sb_pO%d" % i, [128, 512]) for i in range(2)]
        e_t = [p.sb(es, "sb_e%d" % i, [128, 512], F32) for i in range(2)]
        sp_t = [p.sb(es, "sb_sp%d" % i, [128, 512], BF16) for i in range(2)]
        w_t = [p.sb(es, "sb_w%d" % i, [128, 512], BF16) for i in range(2)]
        Ssum = [p.sb(es, "sb_S%d" % i, [128, 512], BF16) for i in range(2)]
        if after_setup is not None:
            after_setup()
        sp3 = sp_t + [p.sb(es, "sb_sp2", [128, 512], BF16)]
        w3 = w_t + [p.sb(es, "sb_w2", [128, 512], BF16)]
        steps = []
        hg = 0
        loaded = None
        for h in heads:
            hp = h // 2
            base = 64 * (h % 2)
            qT, kT = qTs[hp % 2], kTs[hp % 2]
            need_load = loaded != hp
            loaded = hp
            for G in groups:
                Ss = Ssum[hg % 2]
                hg += 1
                for kb_ in range(4 * G + 3, -1, -1):
                    def mk(h=h, hp=hp, base=base, qT=qT, kT=kT, G=G, kb_=kb_, Ss=Ss, idx=len(steps), load=need_load):
                        i = kb_ - 4 * G
                        diag = i >= 0
                        first = kb_ == 4 * G + 3
                        pa, pb, po = psA[idx % 2], psB[idx % 2], psO[idx % 2]
                        et, st_, wt = e_t[idx % 2], sp3[idx % 3], w3[idx % 3]
                        qs = qT[base:base + 64, 512 * G:512 * G + 512]
                        ks = kT[base:base + 64, 128 * kb_:128 * kb_ + 128]

                        def s1():
                            if load:
                                kb.dma("sp", qT[:], fm[(FM_SBQ + hp) * 128:(FM_SBQ + hp + 1) * 128, :], reads=[fm], writes=[qT])
                                kb.dma("sp", kT[:], fm[(FM_SBK + hp) * 128:(FM_SBK + hp + 1) * 128, :], reads=[fm], writes=[kT])
                                kb.op("act", lambda e: e.mul(qT[:], qT[:], 0.125), reads=[qT], writes=[qT])
                            kb.op("pe", lambda e: e.matmul(pa[:], ks, qs, start=True, stop=True), reads=[kT, qT], writes=[pa])
                            kb.op("act", lambda e: e.activation(out=et[:], in_=pa[:], func=AF.Exp), reads=[pa], writes=[et])
                            kb.op("act", lambda e: e.activation(out=st_[:], in_=et[:], func=AF.Ln, bias=1.0), reads=[et], writes=[st_])
                            if diag:
                                kb.op("pool", lambda e: e.tensor_tensor(out=st_[:], in0=st_[:], in1=maskS[:, i, :], op=ALU.mult), reads=[st_, maskS], writes=[st_])

                        def s2():
                            kb.op("pe", lambda e: e.matmul(pb[:], ks, qs, start=True, stop=False), reads=[kT, qT], writes=[pb])
                            kb.op("pe", lambda e: e.matmul(pb[:], negU[:], st_[:], start=False, stop=first), reads=[negU, st_], writes=[pb])
                            if not first:
                                kb.op("pe", lambda e: e.matmul(pb[:], negO[:], Ss[:], start=False, stop=True), reads=[negO, Ss], writes=[pb])
                            if first:
                                kb.op("dve", lambda e: e.tensor_copy(out=Ss[:], in_=st_[:]), reads=[st_], writes=[Ss])
                            elif kb_ > 0:
                                kb.op("dve", lambda e: e.tensor_tensor(out=Ss[:], in0=Ss[:], in1=st_[:], op=ALU.add), reads=[Ss, st_], writes=[Ss])
                            kb.op("act", lambda e: e.activation(out=wt[:], in_=pb[:], func=AF.Exp), reads=[pb], writes=[wt])
                            if diag:
                                kb.op("pool", lambda e: e.tensor_tensor(out=wt[:], in0=wt[:], in1=maskS[:, i, :], op=ALU.mult), reads=[wt, maskS], writes=[wt])

                        def s3():
                            j0 = max(i, 0)
                            for j in range(j0, 4):
                                kb.op("pe", lambda e: e.matmul(po[:, 64 * j:64 * j + 64], wt[:, 128 * j:128 * j + 128], V[:, kb_, 64 * h:64 * h + 64], start=True, stop=True), reads=[wt, V], writes=[po])
                            ysl = yc[:, 4 * G + j0:4 * G + 4, 64 * h:64 * h + 64]
                            kb.op("dve", lambda e: e.tensor_tensor(out=ysl, in0=ysl, in1=po[:, 64 * j0:256].rearrange("p (j d) -> p j d", d=64), op=ALU.add), reads=[yc, po], writes=[yc])
                        return (s1, s2, s3)
                    steps.append(mk())
                    need_load = False
        pipeline(steps, 3)
        dbg = p.d.get("dbg_yc")
        if dbg is not None:
            kb.dma("sp", dbg.h.rearrange("(t p) f -> p t f", p=128), yc[:], reads=[yc], writes=[dbg])
        gw = go_tiles(p, es, "sbgo", l, 2)
        for tt in range(NT):
            emit_group_out(p, gw, yc[:, tt, :], yc, 2, tt)
        kb.barrier()


DIL_CFG = ((1, 2048), (4, 512), (16, 128))


def phase_dil(p, l, branches=range(3), heads=range(8), max_pairs=999):
    kb = p.kb
    fm = p.dram("fm", (30 * 128, S), BF16)
    tm = p.dram("tm", (S, TM_W), BF16)
    dacc = [p.dram("dil_acc%d" % c, (S, 520), F32) for c in range(3)]
    with ExitStack() as es:
        maskD = load_const_bf16(p, es, "c_maskD", (128, 2, 128))
        qT = p.sb(es, "dl_q", [128, 4, S], BF16)
        kT = p.sb(es, "dl_k", [128, 4, S], BF16)
        kb.dma("sp", qT[:], fm[FM_DQ * 128:(FM_DQ + 4) * 128, :].rearrange("(c p) t -> p c t", p=128), reads=[fm], writes=[qT])
        kb.dma("sp", kT[:], fm[FM_DK * 128:(FM_DK + 4) * 128, :].rearrange("(c p) t -> p c t", p=128), reads=[fm], writes=[kT])
        kb.op("act", lambda e: e.mul(qT[:], qT[:], 0.125), reads=[qT], writes=[qT])
        Vp = [p.sb(es, "dl_v%d" % i, [128, 8, 65], BF16) for i in range(3)]
        for v in Vp:
            kb.op("pool", lambda e: e.memset(v[:, :, 64:65], 1.0), writes=[v])
        psS = [p.ps(es, "dl_pS%d" % i, [128, 512]) for i in range(3)]
        psO = [p.ps(es, "dl_pO%d" % i, [128, 512]) for i in range(2)]
        pt = [p.sb(es, "dl_pt%d" % i, [128, 256], BF16) for i in range(3)]
        Oall = [p.sb(es, "dl_O%d" % i, [128, 520], F32) for i in range(2)]
        vcnt = 0
        npair = 0
        steps = []
        for c in branches:
            dil, L = DIL_CFG[c]
            vsrc = tm[:, TM_DV:TM_DV + 512].rearrange("(l r) f -> r l f", r=dil)
            dst = dacc[c].h.rearrange("(l r) f -> r l f", r=dil)
            for r in range(dil):
                vt = {}
                for b in range(L // 128):
                    if npair >= max_pairs:
                        break
                    npair += 1
                    kbs = [x for x in (b - 1, b) if x >= 0]
                    loads = []
                    for x in kbs:
                        if x not in vt:
                            vt[x] = Vp[vcnt % 3]
                            vcnt += 1
                            loads.append((vt[x], vsrc[r, 128 * x:128 * x + 128, :].rearrange("p (h d) -> p h d", d=64)))
                    oa = Oall[npair % 2]
                    vts = [vt[x] for x in kbs]
                    hl = list(heads)
                    for h in hl:
                        def mk(c=c, dil=dil, r=r, b=b, kbs=kbs, vts=vts, oa=oa, h=h, idx=len(steps), loads=(loads if h == hl[0] else []), lasth=(h == hl[-1]), dst=dst):
                            hp, base = h // 2, 64 * (h % 2)
                            qv = qT[base:base + 64, hp, :].rearrange("p (l r) -> p r l", r=dil)[:, r, 128 * b:128 * b + 128]
                            kvw = kT[base:base + 64, hp, :].rearrange("p (l r) -> p r l", r=dil)
                            ps, pp = psS[idx % 3], pt[idx % 3]
                            n = len(kbs)
                            po = psO[(h // 4) % 2]
                            o0 = 65 * (h % 4)

                            def s1():
                                for (t, src) in loads:
                                    kb.dma("sp", t[:, :, 0:64], src, reads=[tm], writes=[t])
                                for ix, x in enumerate(kbs):
                                    kb.op("pe", lambda e: e.matmul(ps[:, 128 * ix:128 * ix + 128], kvw[:, r, 128 * x:128 * x + 128], qv, start=True, stop=True), reads=[kT, qT], writes=[ps])
                                kb.op("act", lambda e: e.activation(out=pp[:, 0:128 * n], in_=ps[:, 0:128 * n], func=AF.Exp), reads=[ps], writes=[pp])
                                m0 = 2 - n
                                kb.op("pool", lambda e: e.tensor_tensor(out=pp[:, 0:128 * n], in0=pp[:, 0:128 * n], in1=maskD[:, m0:2, :].rearrange("p a q -> p (a q)"), op=ALU.mult), reads=[pp, maskD], writes=[pp])

                            def s2():
                                for ix, x in enumerate(kbs):
                                    kb.op("pe", lambda e: e.matmul(po[:, o0:o0 + 65], pp[:, 128 * ix:128 * ix + 128], vts[ix][:, h, :], start=(ix == 0), stop=(ix == n - 1)), reads=[pp, vts[ix]], writes=[po])
                                if h % 4 == 3 or lasth:
                                    hh = h // 4
                                    kb.op("dve", lambda e: e.tensor_copy(out=oa[:, 260 * hh:260 * hh + 260], in_=po[:, 0:260]), reads=[po], writes=[oa])
                                if lasth:
                                    kb.dma("pool", dst[r, 128 * b:128 * b + 128, :], oa[:], reads=[oa], writes=[dacc[c]])
                            return (s1, s2)
                        steps.append(mk())
                    for x in list(vt):
                        if x < b:
                            del vt[x]
        pipeline(steps, 2)
        kb.barrier()
    with ExitStack() as es:
        gw = go_tiles(p, es, "dlgo", l, 3)
        acc = [[p.sb(es, "dl_a%d_%d" % (i, c), [128, 8, 65], F32) for c in range(3)] for i in range(2)]
        rd = p.sb(es, "dl_rd", [128, 8], F32)
        yd = [p.sb(es, "dl_y%d" % i, [128, 512], F32) for i in range(2)]
        for tt in range(NT):
            a = acc[tt % 2]
            y = yd[tt % 2]
            for c in range(3):
                kb.dma("sp", a[c][:], dacc[c][128 * tt:128 * tt + 128, :].rearrange("p (h d) -> p h d", d=65), reads=[dacc[c]], writes=[a[c]])
            kb.op("dve", lambda e: e.tensor_tensor(out=a[0][:], in0=a[0][:], in1=a[1][:], op=ALU.add), reads=[a[0], a[1]], writes=[a[0]])
            kb.op("dve", lambda e: e.tensor_tensor(out=a[0][:], in0=a[0][:], in1=a[2][:], op=ALU.add), reads=[a[0], a[2]], writes=[a[0]])
            kb.op("dve", lambda e: e.reciprocal(out=rd[:], in_=a[0][:, :, 64]), reads=[a[0]], writes=[rd])
            kb.op("dve", lambda e: e.tensor_tensor(out=y[:].rearrange("p (h d) -> p h d", d=64), in0=a[0][:, :, 0:64], in1=rd[:].unsqueeze(2).to_broadcast([128, 8, 64]), op=ALU.mult), reads=[a[0], rd], writes=[y])
            dbg = p.d.get("dbg_yd")
            if dbg is not None:
                kb.dma("sp", dbg[128 * tt:128 * tt + 128, :], y[:], reads=[y], writes=[dbg])
            emit_group_out(p, gw, y[:], y, 3, tt)
        kb.barrier()


def phase_nsa(p, l, heads=range(8), do_slc=True, do_win=True, after_setup=None):
    kb = p.kb
    fm = p.dram("fm", (30 * 128, S), BF16)
    tm = p.dram("tm", (S, TM_W), BF16)
    gates = p.dram("gates", (S, 24), F32)
    w1 = p.wb[("nsa_cmp_w1", l)]
    w2 = p.wb[("nsa_cmp_w2", l)]
    pe = p.dram("nsa_cmp_pe", IN_SHAPES["nsa_cmp_pe"], F32, kind="ExternalInput")
    with ExitStack() as es:
        swp = load_const_bf16(p, es, "c_swap", (128, 128))
        maskC = load_const_bf16(p, es, "c_maskC", (128, 4, 512))
        maskCn = load_const_bf16(p, es, "c_maskCn", (128, 4, 512))
        maskCmp = load_const_bf16(p, es, "c_maskCmp", (128, S))
        Ebig = load_const_bf16(p, es, "c_Ebig", (32, 16, 128))
        ccos = p.sb(es, "ns_cos", [128, 127], F32)
        csin = p.sb(es, "ns_sin", [128, 127], F32)
        kb.dma("sp", ccos[:], p.dram("c_cos_cmp", (128, 127), F32, kind="ExternalInput")[:, :], writes=[ccos])
        kb.dma("sp", csin[:], p.dram("c_sin_cmp", (128, 127), F32, kind="ExternalInput")[:, :], writes=[csin])
        selMul = p.sb(es, "ns_selMul", [128, NT, 32], F32)
        selAdd = p.sb(es, "ns_selAdd", [128, NT, 32], F32)
        kb.dma("sp", selMul[:], p.dram("c_selMul", (S, 32), F32, kind="ExternalInput").h.rearrange("(t p) n -> p t n", p=128), writes=[selMul])
        kb.dma("sp", selAdd[:], p.dram("c_selAdd", (S, 32), F32, kind="ExternalInput").h.rearrange("(t p) n -> p t n", p=128), writes=[selAdd])
        qT = p.sb(es, "ns_q", [128, 4, S], BF16)
        kb.dma("sp", qT[:], fm[FM_NQ * 128:(FM_NQ + 4) * 128, :].rearrange("(c p) t -> p c t", p=128), reads=[fm], writes=[qT])
        kb.op("act", lambda e: e.mul(qT[:], qT[:], 0.125), reads=[qT], writes=[qT])
        gt = p.sb(es, "ns_gt", [128, NT, 24], F32)
        kb.dma("sp", gt[:], gates.h.rearrange("(t p) f -> p t f", p=128), reads=[gates], writes=[gt])
        yb = p.sb(es, "ns_yb", [128, NT, 512], F32)
        kb.op("pool", lambda e: e.memset(yb[:], 0.0), writes=[yb])
        impacc = p.sb(es, "ns_imp", [128, NT, 2, 32], F32)
        kb.op("pool", lambda e: e.memset(impacc[:], 0.0), writes=[impacc])
        kdup = {}
        for nm, ch in (("slc", FM_KSLC), ("win", FM_KWIN)):
            for g in range(2):
                t = p.sb(es, "ns_k%s%d" % (nm, g), [128, S], BF16)
                for half in range(2):
                    kb.dma("sp", t[64 * half:64 * half + 64, :], fm[ch * 128 + 64 * g:ch * 128 + 64 * g + 64, :], reads=[fm], writes=[t])
                kdup[(nm, g)] = t
        Vs = {}
        for nm, off in (("slc", TM_VSLC), ("win", TM_VWIN)):
            t = p.sb(es, "ns_v" + nm, [128, NT, 2, 65], BF16)
            kb.op("pool", lambda e: e.memset(t[:, :, :, 64:65], 1.0), writes=[t])
            for g in range(2):
                kb.dma("sp", t[:, :, g, 0:64], tm[:, off + 64 * g:off + 64 * g + 64].rearrange("(t p) d -> p t d", p=128), reads=[tm], writes=[t])
            Vs[nm] = t
        pS = [p.ps(es, "ns_pS%d" % i, [128, 512]) for i in range(2)]
        pO = [p.ps(es, "ns_pO%d" % i, [128, 512]) for i in range(2)]
        pX = [p.ps(es, "ns_pX%d" % i, [128, 512]) for i in range(2)]
        pXb = p.ps(es, "ns_pXb", [128, 1024], BF16)
        kcT = [p.sb(es, "ns_kc%d" % g, [128, 128], BF16) for g in range(2)]
        Rg = [p.sb(es, "ns_R%d" % g, [128, 97], BF16) for g in range(2)]
        ovl = load_const_bf16(p, es, "c_overlap", (128, 32))
        with ExitStack() as es2:
            tT = p.sb(es2, "ns_tT", [128, S], BF16)
            W1 = p.sb(es2, "ns_W1", [128, 32, 128], BF16)
            W2d = p.sb(es2, "ns_W2d", [128, 2, 64], BF16)
            pet = p.sb(es2, "ns_pe", [128, 32], F32)
            xl = [p.sb(es2, "ns_xl%d" % i, [128, 128], BF16) for i in range(4)]
            h1 = p.sb(es2, "ns_h1", [128, 128], BF16)
            xb = p.sb(es2, "ns_xb", [128, 128], BF16)
            a1 = p.sb(es2, "ns_a1", [128, 128], F32)
            a2 = p.sb(es2, "ns_a2", [128, 128], F32)
            xc = 0
            for j in range(2):
                kb.dma("sp", tT[:], fm[(FM_KCMP + j) * 128:(FM_KCMP + j + 1) * 128, :], reads=[fm], writes=[tT])
                for half in range(2):
                    kb.dma("sp", W1[64 * half:64 * half + 64, :, :], w1[j].rearrange("(l d) n -> d l n", d=64), reads=[w1], writes=[W1])
                    kb.dma("sp", pet[64 * half:64 * half + 64, :], pe[l, j].rearrange("l d -> d l"), writes=[pet], allow_slow_non_contiguous=True)
                    kb.dma("sp", W2d[:, half, :], w2[j], reads=[w2], writes=[W2d])
                for g in range(2):
                    base = 64 * g
                    ph = pX[0]
                    for ll in range(32):
                        x_ = xl[xc % 4]
                        xc += 1
                        src = tT[base:base + 64, :].rearrange("p (i s) -> p s i", s=16)
                        sh, lo = ll // 16, ll % 16
                        kb.op("dve", lambda e: e.tensor_scalar(out=x_[base:base + 64, 0:127], in0=src[:, lo, sh:sh + 127], scalar1=pet[base:base + 64, ll:ll + 1], scalar2=None, op0=ALU.add), reads=[tT, pet], writes=[x_])
                        kb.op("pe", lambda e: e.matmul(ph[:, 0:127], W1[base:base + 64, ll, :], x_[base:base + 64, 0:127], start=(ll == 0), stop=(ll == 31)), reads=[W1, x_], writes=[ph])
                    kb.op("act", lambda e: e.activation(out=h1[:, 0:127], in_=ph[:, 0:127], func=AF.Gelu_apprx_tanh), reads=[ph], writes=[h1])
                    if j == 0:
                        pk = pX[1]
                        kb.op("pe", lambda e: e.matmul(pk[:, 0:127], W2d[:].rearrange("p a d -> p (a d)"), h1[:, 0:127], start=True, stop=True), reads=[W2d, h1], writes=[pk])
                        kb.op("act", lambda e: e.activation(out=xb[:, 0:127], in_=pk[:, 0:127], func=AF.Copy), reads=[pk], writes=[xb])
                        kb.op("dve", lambda e: e.tensor_tensor(out=a1[:, 0:127], in0=pk[:, 0:127], in1=ccos[:], op=ALU.mult), reads=[pk, ccos], writes=[a1])
                        pk2 = pO[0]
                        kb.op("pe", lambda e: e.matmul(pk2[:, 0:127], swp[:], xb[:, 0:127], start=True, stop=True), reads=[swp, xb], writes=[pk2])
                        kb.op("dve", lambda e: e.tensor_tensor(out=a2[:, 0:127], in0=pk2[:, 0:127], in1=csin[:], op=ALU.mult), reads=[pk2, csin], writes=[a2])
                        kb.op("dve", lambda e: e.tensor_tensor(out=kcT[g][:, 0:127], in0=a1[:, 0:127], in1=a2[:, 0:127], op=ALU.add), reads=[a1, a2], writes=[kcT[g]])
                    else:
                        pv = pX[1]
                        kb.op("pe", lambda e: e.matmul(pv[0:127, 0:64], h1[:, 0:127], W2d[:, 0, :], start=True, stop=True), reads=[W2d, h1], writes=[pv])
                        kb.op("pool", lambda e: e.memset(Rg[g][:, 64:65], 1.0), writes=[Rg[g]])
                        kb.op("act", lambda e: e.activation(out=Rg[g][0:127, 0:64], in_=pv[0:127, 0:64], func=AF.Copy), reads=[pv], writes=[Rg[g]])
                        kb.op("dve", lambda e: e.tensor_copy(out=Rg[g][:, 65:97], in_=ovl[:]), reads=[ovl], writes=[Rg[g]])
        if after_setup is not None:
            after_setup()
        pc = [p.sb(es, "ns_pc%d" % i, [128, 512], BF16) for i in range(2)]
        rd = [p.sb(es, "ns_rd%d" % i, [128, 4], F32) for i in range(2)]
        steps = []
        for h in heads:
            for G in range(4):
                def mk(h=h, G=G, idx=len(steps)):
                    g, hp, base = h // 4, h // 2, 64 * (h % 2)
                    ps, po, pc_, rd_ = pS[idx % 2], pO[idx % 2], pc[idx % 2], rd[idx % 2]

                    def s1():
                        kb.op("pe", lambda e: e.matmul(ps[0:127, :], kcT[g][base:base + 64, 0:127], qT[base:base + 64, hp, 512 * G:512 * G + 512], start=True, stop=True), reads=[kcT[g], qT], writes=[ps])
                        kb.op("act", lambda e: e.activation(out=pc_[0:127, :], in_=ps[0:127, :], func=AF.Exp), reads=[ps], writes=[pc_])
                        kb.op("pool", lambda e: e.tensor_tensor(out=pc_[0:127, :], in0=pc_[0:127, :], in1=maskCmp[0:127, 512 * G:512 * G + 512], op=ALU.mult), reads=[pc_, maskCmp], writes=[pc_])

                    def s2():
                        for j in range(4):
                            kb.op("pe", lambda e: e.matmul(po[:, 97 * j:97 * j + 97], pc_[0:127, 128 * j:128 * j + 128], Rg[g][0:127, :], start=True, stop=True), reads=[pc_, Rg[g]], writes=[po])
                        pov = po[:, 0:388].rearrange("p (j f) -> p j f", f=97)
                        kb.op("dve", lambda e: e.tensor_scalar(out=rd_[:], in0=pov[:, :, 64], scalar1=1e-30, scalar2=None, op0=ALU.max), reads=[po], writes=[rd_])
                        kb.op("dve", lambda e: e.reciprocal(out=rd_[:], in_=rd_[:]), reads=[rd_], writes=[rd_])
                        for j in range(4):
                            tt = 4 * G + j
                            kb.op("dve", lambda e: e.tensor_scalar(out=yb[:, tt, 64 * h:64 * h + 64], in0=po[:, 97 * j:97 * j + 64], scalar1=rd_[:, j:j + 1], scalar2=gt[:, tt, 3 * h:3 * h + 1], op0=ALU.mult, op1=ALU.mult), reads=[po, rd_, gt], writes=[yb])
                            kb.op("dve", lambda e: e.scalar_tensor_tensor(out=impacc[:, tt, g, :], in0=po[:, 97 * j + 65:97 * j + 97], scalar=rd_[:, j:j + 1], in1=impacc[:, tt, g, :], op0=ALU.mult, op1=ALU.add), reads=[po, rd_, impacc], writes=[impacc])
                    return (s1, s2)
                steps.append(mk())
        pipeline(steps, 2)
        dbg = p.d.get("dbg_imp")
        if dbg is not None:
            kb.dma("sp", dbg.h.rearrange("(t p) g n -> p t g n", p=128), impacc[:], reads=[impacc], writes=[dbg])
        selT = [p.sb(es, "ns_selT%d" % g, [32, S], BF16) for g in range(2)]
        m8 = p.sb(es, "ns_m8", [128, 8], F32)
        selb = p.sb(es, "ns_selb", [128, 4, 32], BF16)
        for g in range(2):
            kb.op("dve", lambda e: e.tensor_tensor(out=impacc[:, :, g, :], in0=impacc[:, :, g, :], in1=selMul[:], op=ALU.mult), reads=[impacc, selMul], writes=[impacc])
            kb.op("dve", lambda e: e.tensor_tensor(out=impacc[:, :, g, :], in0=impacc[:, :, g, :], in1=selAdd[:], op=ALU.add), reads=[impacc, selAdd], writes=[impacc])
            for G in range(4):
                for j in range(4):
                    tt = 4 * G + j
                    kb.op("dve", lambda e: e.max(out=m8[:], in_=impacc[:, tt, g, :]), reads=[impacc], writes=[m8])
                    kb.op("dve", lambda e: e.tensor_scalar(out=selb[:, j, :], in0=impacc[:, tt, g, :], scalar1=m8[:, 7:8], scalar2=1.0, op0=ALU.is_ge, op1=ALU.subtract), reads=[impacc, m8], writes=[selb])
                for j in range(4):
                    kb.op("pe", lambda e: e.transpose(pXb[0:32, 128 * j:128 * j + 128], selb[:, j, :], p.ident[:]), reads=[selb, p.ident], writes=[pXb])
                kb.op("act", lambda e: e.activation(out=selT[g][:, 512 * G:512 * G + 512], in_=pXb[0:32, 0:512], func=AF.Copy), reads=[pXb], writes=[selT[g]])
        Pt = [p.sb(es, "ns_P%d" % i, [128, 512], BF16) for i in range(2)]
        Oacc = [p.sb(es, "ns_Oa%d" % i, [128, 4, 65], F32) for i in range(2)]
        rdg = [p.sb(es, "ns_rdg%d" % i, [128, 4], F32) for i in range(2)]
        Pt = Pt + [p.sb(es, "ns_P2", [128, 512], BF16)]
        oc = 0
        branches = ([("slc", 1)] if do_slc else []) + ([("win", 2)] if do_win else [])
        steps = []
        for h in heads:
            for (nm, gi) in branches:
                for G in range(4):
                    oa, rg_ = Oacc[oc % 2], rdg[oc % 2]
                    oc += 1
                    kbs = list(range(0, 4 * G + 4) if nm == "slc" else range(max(0, 4 * G - 4), 4 * G + 4))
                    for kb_ in kbs:
                        def mk(h=h, nm=nm, gi=gi, G=G, kb_=kb_, oa=oa, rg_=rg_, idx=len(steps), firstk=(kb_ == kbs[0]), lastk=(kb_ == kbs[-1])):
                            g, hp, base = h // 4, h // 2, 64 * (h % 2)
                            kd, V = kdup[(nm, g)], Vs[nm]
                            qs = qT[base:base + 64, hp, 512 * G:512 * G + 512]
                            i = kb_ - 4 * G
                            ps, po, P_ = pS[idx % 2], pO[idx % 2], Pt[idx % 3]
                            ks = kd[base:base + 64, 128 * kb_:128 * kb_ + 128]
                            if i >= 0:
                                js = range(i, 4)
                            elif nm == "win":
                                js = range(0, i + 5)
                            else:
                                js = range(4)

                            def s1():
                                if firstk:
                                    kb.op("pool", lambda e: e.memset(oa[:], 0.0), writes=[oa])
                                if nm == "slc":
                                    kb.op("pe", lambda e: e.matmul(ps[:], ks, qs, start=True, stop=False), reads=[kd, qT], writes=[ps])
                                    kb.op("pe", lambda e: e.matmul(ps[:], Ebig[:, kb_, :], selT[g][:, 512 * G:512 * G + 512], start=False, stop=True), reads=[Ebig, selT[g]], writes=[ps])
                                else:
                                    kb.op("pe", lambda e: e.matmul(ps[:], ks, qs, start=True, stop=True), reads=[kd, qT], writes=[ps])
                                kb.op("act", lambda e: e.activation(out=P_[:], in_=ps[:], func=AF.Exp), reads=[ps], writes=[P_])
                                if i >= 0:
                                    kb.op("pool", lambda e: e.tensor_tensor(out=P_[:], in0=P_[:], in1=maskC[:, i, :], op=ALU.mult), reads=[P_, maskC], writes=[P_])
                                elif nm == "win":
                                    kb.op("pool", lambda e: e.tensor_tensor(out=P_[:], in0=P_[:], in1=maskCn[:, i + 4, :], op=ALU.mult), reads=[P_, maskCn], writes=[P_])

                            def s2():
                                for j in js:
                                    kb.op("pe", lambda e: e.matmul(po[:, 65 * j:65 * j + 65], P_[:, 128 * j:128 * j + 128], V[:, kb_, g, :], start=True, stop=True), reads=[P_, V], writes=[po])
                                j0, j1 = js[0], js[-1] + 1
                                kb.op("dve", lambda e: e.tensor_tensor(out=oa[:, j0:j1, :], in0=oa[:, j0:j1, :], in1=po[:, 65 * j0:65 * j1].rearrange("p (j f) -> p j f", f=65), op=ALU.add), reads=[oa, po], writes=[oa])
                                if lastk:
                                    kb.op("dve", lambda e: e.reciprocal(out=rg_[:], in_=oa[:, :, 64]), reads=[oa], writes=[rg_])
                                    kb.op("dve", lambda e: e.tensor_tensor(out=rg_[:], in0=rg_[:], in1=gt[:, 4 * G:4 * G + 4, 3 * h + gi], op=ALU.mult), reads=[rg_, gt], writes=[rg_])
                                    for j in range(4):
                                        tt = 4 * G + j
                                        kb.op("dve", lambda e: e.scalar_tensor_tensor(out=yb[:, tt, 64 * h:64 * h + 64], in0=oa[:, j, 0:64], scalar=rg_[:, j:j + 1], in1=yb[:, tt, 64 * h:64 * h + 64], op0=ALU.mult, op1=ALU.add), reads=[oa, rg_, yb], writes=[yb])
                            return (s1, s2)
                        steps.append(mk())
        pipeline(steps, 2)
        dbg = p.d.get("dbg_yb")
        if dbg is not None:
            kb.dma("sp", dbg.h.rearrange("(t p) f -> p t f", p=128), yb[:], reads=[yb], writes=[dbg])
        gw = go_tiles(p, es, "nsgo", l, 1)
        for tt in range(NT):
            emit_group_out(p, gw, yb[:, tt, :], yb, 1, tt)
        kb.barrier()


def phase_s5(p, l, chunks=range(16), after_setup=None):
    kb = p.kb
    PI = math.pi
    fm = p.dram("fm", (30 * 128, S), BF16)
    ysT = p.dram("ysT", (D, S), BF16)
    inp = lambda n: p.dram(n, IN_SHAPES[n], F32, kind="ExternalInput")
    lam_re, lam_im, log_dt = inp("s5_lambda_re"), inp("s5_lambda_im"), inp("s5_log_dt")
    b_re, b_im, c_re, c_im = inp("s5_b_re"), inp("s5_b_im"), inp("s5_c_re"), inp("s5_c_im")
    d_skip, b_glu, gain = inp("s5_d"), inp("s5_b_glu"), inp("mix_norm_g")
    wglu = p.wb[("s5_w_glu", l)]
    with ExitStack() as es:
        uT = p.sb(es, "s5_uT", [128, 4, S], BF16)
        kb.dma("sp", uT[:], fm[FM_U * 128:(FM_U + 4) * 128, :].rearrange("(c p) t -> p c t", p=128), reads=[fm], writes=[uT])
        yT = p.sb(es, "s5_yT", [128, 4, S], F32)
        dT = p.sb(es, "s5_dT", [128, 4], F32)
        kb.dma("sp", dT[:], d_skip[l].rearrange("(q j) c -> (j c) q", j=8), writes=[dT], allow_slow_non_contiguous=True)
        BT = [p.sb(es, "s5_BT%d" % i, [128, 4, 2, 64], BF16) for i in range(2)]
        BTz = [p.sb(es, "s5_BTz%d" % i, [128, 4, 2, 64], BF16) for i in range(2)]
        CT = [p.sb(es, "s5_CT%d" % i, [128, 4, 128], BF16) for i in range(2)]
        CTz = [p.sb(es, "s5_CTz%d" % i, [128, 4, 64], BF16) for i in range(2)]
        for q in range(4):
            kb.op("dve", lambda e: e.tensor_scalar(out=yT[:, q, :], in0=uT[:, q, :], scalar1=dT[:, q:q + 1], scalar2=None, op0=ALU.mult), reads=[uT, dT], writes=[yT])
        r_p = p.sb(es, "s5_rp", [128, 16], F32)
        th_p = p.sb(es, "s5_thp", [128, 16], F32)
        with ExitStack() as es2:
            sb2 = lambda n, shp, dt=F32: p.sb(es2, "s5p_" + n, shp, dt)
            pX = p.ps(es2, "s5p_pX", [128, 1024], BF16)
            mask2 = sb2("mask2", [128, 2])
            kb.dma("sp", mask2[:], p.dram("c_mask2", (128, 2), F32, kind="ExternalInput")[:, :], writes=[mask2])
            mask2z = sb2("mask2z", [128, 2])
            kb.dma("sp", mask2z[:], p.dram("c_mask2z", (128, 2), F32, kind="ExternalInput")[:, :], writes=[mask2z])
            mask3 = [sb2("mask3%d" % i, [128, 128]) for i in range(2)]
            kb.dma("sp", mask3[0][:], p.dram("c_mask3", (128, 128), F32, kind="ExternalInput")[:, :], writes=[mask3[0]])
            kb.dma("sp", mask3[1][:], p.dram("c_mask3n", (128, 128), F32, kind="ExternalInput")[:, :], writes=[mask3[1]])

            def prep(P_, G_, lr_src, li_src, dt_loads, pfx):
                t = {k: sb2(pfx + k, [P_, G_]) for k in ("lr", "li", "dt", "mag", "ang", "tmp")}
                kb.dma("sp", t["lr"][:], lr_src, writes=[t["lr"]], allow_slow_non_contiguous=True)
                kb.dma("sp", t["li"][:], li_src, writes=[t["li"]], allow_slow_non_contiguous=True)
                for (dst_sl, src) in dt_loads:
                    kb.dma("sp", t["dt"][dst_sl, :], src, writes=[t["dt"]])
                kb.op("dve", lambda e: e.tensor_scalar(out=t["lr"][:], in0=t["lr"][:], scalar1=-1e-4, scalar2=None, op0=ALU.min), reads=[t["lr"]], writes=[t["lr"]])
                kb.op("act", lambda e: e.activation(out=t["dt"][:], in_=t["dt"][:], func=AF.Exp), reads=[t["dt"]], writes=[t["dt"]])
                kb.op("dve", lambda e: e.tensor_tensor(out=t["tmp"][:], in0=t["lr"][:], in1=t["dt"][:], op=ALU.mult), reads=[t["lr"], t["dt"]], writes=[t["tmp"]])
                kb.op("act", lambda e: e.activation(out=t["mag"][:], in_=t["tmp"][:], func=AF.Exp), reads=[t["tmp"]], writes=[t["mag"]])
                kb.op("dve", lambda e: e.tensor_tensor(out=t["ang"][:], in0=t["li"][:], in1=t["dt"][:], op=ALU.mult), reads=[t["li"], t["dt"]], writes=[t["ang"]])
                return t

            tp = prep(128, 16,
                      lam_re[l].rearrange("g s -> (g s)").rearrange("(c q) -> q c", q=128),
                      lam_im[l].rearrange("g s -> (g s)").rearrange("(c q) -> q c", q=128),
                      [(slice(64 * two, 64 * two + 64), log_dt[l].rearrange("(c two) -> two c", two=2)[two].partition_broadcast(64)) for two in range(2)], "p")
            kb.op("dve", lambda e: e.tensor_copy(out=r_p[:], in_=tp["mag"][:]), reads=[tp["mag"]], writes=[r_p])
            kb.op("dve", lambda e: e.tensor_copy(out=th_p[:], in_=tp["ang"][:]), reads=[tp["ang"]], writes=[th_p])
            ts = prep(64, 32, lam_re[l].rearrange("g s -> s g"), lam_im[l].rearrange("g s -> s g"),
                      [(slice(0, 64), log_dt[l].partition_broadcast(64))], "s")
            sn, cs = sb2("sn", [64, 32]), sb2("cs", [64, 32])
            ki0 = sb2("ki0", [64, 32], mybir.dt.int32)
            kb.op("dve", lambda e: e.tensor_scalar(out=ki0[:], in0=ts["ang"][:], scalar1=1.0 / (2 * PI), scalar2=None, op0=ALU.mult), reads=[ts["ang"]], writes=[ki0])
            kb.op("dve", lambda e: e.tensor_copy(out=sn[:], in_=ki0[:]), reads=[ki0], writes=[sn])
            kb.op("dve", lambda e: e.scalar_tensor_tensor(out=sn[:], in0=sn[:], scalar=-2 * PI, in1=ts["ang"][:], op0=ALU.mult, op1=ALU.add), reads=[sn, ts["ang"]], writes=[sn])
            kb.op("dve", lambda e: e.tensor_scalar(out=sn[:], in0=sn[:], scalar1=-PI, scalar2=PI, op0=ALU.max, op1=ALU.min), reads=[sn], writes=[sn])
            kb.op("act", lambda e: e.activation(out=cs[:], in_=sn[:], func=AF.Abs), reads=[sn], writes=[cs])
            kb.op("act", lambda e: e.activation(out=sn[:], in_=sn[:], func=AF.Sin), reads=[sn], writes=[sn])
            kb.op("act", lambda e: e.activation(out=cs[:], in_=cs[:], func=AF.Sin, scale=-1.0, bias=0.5 * PI), reads=[cs], writes=[cs])
            are, aim, den, zre, zim, t1, t2 = [sb2(n, [64, 32]) for n in ("are", "aim", "den", "zre", "zim", "t1", "t2")]
            tt_ = lambda o, a, b, op: kb.op("dve", lambda e: e.tensor_tensor(out=o[:], in0=a[:], in1=b[:], op=op), reads=[a, b], writes=[o])
            tt_(are, ts["mag"], cs, ALU.mult)
            tt_(aim, ts["mag"], sn, ALU.mult)
            kb.op("dve", lambda e: e.tensor_scalar(out=are[:], in0=are[:], scalar1=-1.0, scalar2=None, op0=ALU.add), reads=[are], writes=[are])
            tt_(t1, ts["lr"], ts["lr"], ALU.mult)
            tt_(t2, ts["li"], ts["li"], ALU.mult)
            tt_(den, t1, t2, ALU.add)
            kb.op("dve", lambda e: e.reciprocal(out=den[:], in_=den[:]), reads=[den], writes=[den])
            tt_(t1, are, ts["lr"], ALU.mult)
            tt_(t2, aim, ts["li"], ALU.mult)
            tt_(zre, t1, t2, ALU.add)
            tt_(zre, zre, den, ALU.mult)
            tt_(t1, aim, ts["lr"], ALU.mult)
            tt_(t2, are, ts["li"], ALU.mult)
            tt_(zim, t1, t2, ALU.subtract)
            tt_(zim, zim, den, ALU.mult)
            Bs = [sb2("Bs%d" % i, [64, 32, 16]) for i in range(2)]
            kb.dma("sp", Bs[0][:], b_re[l].rearrange("g s c -> s g c"), writes=[Bs[0]])
            kb.dma("sp", Bs[1][:], b_im[l].rearrange("g s c -> s g c"), writes=[Bs[1]])
            m1, m2 = sb2("m1", [64, 32, 16]), sb2("m2", [64, 32, 16])
            bb = [sb2("bb%d" % i, [64, 32, 16], BF16) for i in range(2)]
            zb = lambda z: z[:].unsqueeze(2).to_broadcast([64, 32, 16])
            kb.op("dve", lambda e: e.tensor_tensor(out=m1[:], in0=Bs[0][:], in1=zb(zre), op=ALU.mult), reads=[Bs[0], zre], writes=[m1])
            kb.op("dve", lambda e: e.tensor_tensor(out=m2[:], in0=Bs[1][:], in1=zb(zim), op=ALU.mult), reads=[Bs[1], zim], writes=[m2])
            kb.op("dve", lambda e: e.tensor_tensor(out=bb[0][:], in0=m1[:], in1=m2[:], op=ALU.subtract), reads=[m1, m2], writes=[bb[0]])
            kb.op("dve", lambda e: e.tensor_tensor(out=m1[:], in0=Bs[1][:], in1=zb(zre), op=ALU.mult), reads=[Bs[1], zre], writes=[m1])
            kb.op("dve", lambda e: e.tensor_tensor(out=m2[:], in0=Bs[0][:], in1=zb(zim), op=ALU.mult), reads=[Bs[0], zim], writes=[m2])
            kb.op("dve", lambda e: e.tensor_tensor(out=bb[1][:], in0=m1[:], in1=m2[:], op=ALU.add), reads=[m1, m2], writes=[bb[1]])
            for ri in range(2):
                for q in range(4):
                    kb.op("pe", lambda e: e.transpose(pX[:, 0:64], bb[ri][:, 8 * q:8 * q + 8, :].rearrange("s j c -> s (j c)"), p.ident[0:64, 0:64]), reads=[bb[ri], p.ident], writes=[pX])
                    for two in range(2):
                        kb.op("dve", lambda e: e.tensor_scalar(out=BT[ri][:, q, two, :], in0=pX[:, 0:64], scalar1=mask2[:, two:two + 1], scalar2=None, op0=ALU.mult), reads=[pX, mask2], writes=[BT[ri]])
                        kb.op("dve", lambda e: e.tensor_scalar(out=BTz[ri][:, q, two, :], in0=pX[:, 0:64], scalar1=mask2z[:, two:two + 1], scalar2=None, op0=ALU.mult), reads=[pX, mask2z], writes=[BTz[ri]])
            Cn = sb2("Cn", [128, 4, 64])
            Cd = sb2("Cd", [128, 4, 2, 64], BF16)
            for ri, csrc in enumerate((c_re, c_im)):
                kb.dma("sp", Cn[:], csrc[l].rearrange("(q j) c s -> (j c) q s", j=8), writes=[Cn])
                for two in range(2):
                    kb.op("dve", lambda e: e.tensor_copy(out=Cd[:, :, two, :], in_=Cn[:]), reads=[Cn], writes=[Cd])
                for q in range(4):
                    kb.op("pe", lambda e: e.transpose(pX[:, 0:128], Cd[:, q, :, :].rearrange("p a s -> p (a s)"), p.ident[:]), reads=[Cd, p.ident], writes=[pX])
                    kb.op("dve", lambda e: e.tensor_tensor(out=CT[ri][:, q, :], in0=pX[:, 0:128], in1=mask3[ri][:], op=ALU.mult), reads=[pX, mask3[ri]], writes=[CT[ri]])
                    kb.op("pool", lambda e: e.memset(CTz[ri][:, q, 0:32], 0.0), writes=[CTz[ri]])
                    kb.op("dve", lambda e: e.tensor_copy(out=CTz[ri][:, q, 32:64], in_=CT[ri][:, q, 96:128]), reads=[CT[ri]], writes=[CTz[ri]])
            kb.barrier()
        with ExitStack() as es2:
            sb2 = lambda n, shp, dt=F32: p.sb(es2, "s5m_" + n, shp, dt)
            iota = sb2("iota", [128, S])
            kb.dma("sp", iota[:], p.dram("c_iota", (128, S), F32, kind="ExternalInput")[:, :], writes=[iota])
            if after_setup is not None:
                after_setup()
            HS = S // 2
            wri, wii, wi, wr = [sb2(n, [128, HS]) for n in ("wri", "wii", "wi", "wr")]
            phs = [sb2("ph%d" % i, [128, HS]) for i in range(2)]
            sns = [sb2("sn%d" % i, [128, HS]) for i in range(2)]
            css = [sb2("cs%d" % i, [128, HS]) for i in range(2)]
            xre, xim = sb2("xre", [128, HS], BF16), sb2("xim", [128, HS], BF16)
            ki = sb2("ki", [128, HS], mybir.dt.int32)
            tmp = [sb2("t%d" % i, [128, 512]) for i in range(4)]
            car = sb2("car", [128, 2])
            psBr = [p.ps(es2, "s5_pBr%d" % i, [128, 512]) for i in range(2)]
            psBi = [p.ps(es2, "s5_pBi%d" % i, [128, 512]) for i in range(2)]
            psY = [p.ps(es2, "s5_pY%d" % i, [128, 512]) for i in range(2)]
            steps = []
            for ch in chunks:
                for half in range(2):
                    def mk(ch=ch, half=half, idx=len(steps)):
                        q, rb = ch // 4, 32 * (ch % 4)
                        rows = slice(rb, rb + 32) if rb < 96 else slice(64, 128)
                        t0 = half * HS
                        ph, sn, cs = phs[idx % 2], sns[idx % 2], css[idx % 2]

                        def s1():
                            kb.op("dve", lambda e: e.tensor_scalar(out=ph[:], in0=iota[:, t0:t0 + HS], scalar1=th_p[:, ch:ch + 1], scalar2=None, op0=ALU.mult), reads=[iota, th_p], writes=[ph])
                            kb.op("dve", lambda e: e.tensor_scalar(out=ki[:], in0=ph[:], scalar1=1.0 / (2 * PI), scalar2=None, op0=ALU.mult), reads=[ph], writes=[ki])
                            kb.op("dve", lambda e: e.tensor_copy(out=sn[:], in_=ki[:]), reads=[ki], writes=[sn])
                            kb.op("dve", lambda e: e.scalar_tensor_tensor(out=sn[:], in0=sn[:], scalar=-2 * PI, in1=ph[:], op0=ALU.mult, op1=ALU.add), reads=[sn, ph], writes=[sn])
                            kb.op("dve", lambda e: e.tensor_scalar(out=sn[:], in0=sn[:], scalar1=-PI, scalar2=PI, op0=ALU.max, op1=ALU.min), reads=[sn], writes=[sn])
                            kb.op("act", lambda e: e.activation(out=cs[:], in_=sn[:], func=AF.Abs), reads=[sn], writes=[cs])
                            kb.op("act", lambda e: e.activation(out=sn[:], in_=sn[:], func=AF.Sin), reads=[sn], writes=[sn])
                            kb.op("act", lambda e: e.activation(out=cs[:], in_=cs[:], func=AF.Sin, scale=-1.0, bias=0.5 * PI), reads=[cs], writes=[cs])

                        def s2():
                            for tg in range(HS // 512):
                                tk = slice(t0 + 512 * tg, t0 + 512 * tg + 512)
                                lk = slice(512 * tg, 512 * tg + 512)
                                pr, pi_ = psBr[tg % 2], psBi[tg % 2]
                                Bm = BTz if rb == 96 else BT
                                kb.op("pe", lambda e: e.matmul(pr[:], Bm[0][rows, q, :, :].rearrange("p a s -> p (a s)"), uT[rows, q, tk], start=True, stop=True), reads=[Bm[0], uT], writes=[pr])
                                kb.op("pe", lambda e: e.matmul(pi_[:], Bm[1][rows, q, :, :].rearrange("p a s -> p (a s)"), uT[rows, q, tk], start=True, stop=True), reads=[Bm[1], uT], writes=[pi_])
                                a, b, c, d = tmp
                                kb.op("dve", lambda e: e.tensor_tensor(out=a[:], in0=pr[:], in1=cs[:, lk], op=ALU.mult), reads=[pr, cs], writes=[a])
                                kb.op("dve", lambda e: e.tensor_tensor(out=b[:], in0=pi_[:], in1=sn[:, lk], op=ALU.mult), reads=[pi_, sn], writes=[b])
                                kb.op("pool", lambda e: e.tensor_tensor(out=wri[:, lk], in0=a[:], in1=b[:], op=ALU.add), reads=[a, b], writes=[wri])
                                kb.op("dve", lambda e: e.tensor_tensor(out=c[:], in0=pi_[:], in1=cs[:, lk], op=ALU.mult), reads=[pi_, cs], writes=[c])
                                kb.op("dve", lambda e: e.tensor_tensor(out=d[:], in0=pr[:], in1=sn[:, lk], op=ALU.mult), reads=[pr, sn], writes=[d])
                                kb.op("pool", lambda e: e.tensor_tensor(out=wii[:, lk], in0=c[:], in1=d[:], op=ALU.subtract), reads=[c, d], writes=[wii])
                            rbc = r_p[:, ch:ch + 1].to_broadcast([128, HS])
                            ini_r = 0.0 if half == 0 else car[:, 0:1]
                            ini_i = 0.0 if half == 0 else car[:, 1:2]
                            kb.op("dve", lambda e: e.tensor_tensor_scan(out=wr[:], data0=rbc, data1=wri[:], initial=ini_r, op0=ALU.mult, op1=ALU.add), reads=[r_p, wri, car], writes=[wr])
                            kb.op("dve", lambda e: e.tensor_tensor_scan(out=wi[:], data0=rbc, data1=wii[:], initial=ini_i, op0=ALU.mult, op1=ALU.add), reads=[r_p, wii, car], writes=[wi])
                            if half == 0:
                                kb.op("act", lambda e: e.activation(out=car[:, 0:1], in_=wr[:, HS - 1:HS], func=AF.Copy), reads=[wr], writes=[car])
                                kb.op("act", lambda e: e.activation(out=car[:, 1:2], in_=wi[:, HS - 1:HS], func=AF.Copy), reads=[wi], writes=[car])
                            kb.op("dve", lambda e: e.tensor_tensor(out=wri[:], in0=wr[:], in1=cs[:], op=ALU.mult), reads=[wr, cs], writes=[wri])
                            kb.op("pool", lambda e: e.tensor_tensor(out=wii[:], in0=wi[:], in1=sn[:], op=ALU.mult), reads=[wi, sn], writes=[wii])
                            kb.op("dve", lambda e: e.tensor_tensor(out=xre[:], in0=wri[:], in1=wii[:], op=ALU.subtract), reads=[wri, wii], writes=[xre])
                            kb.op("pool", lambda e: e.tensor_tensor(out=wri[:], in0=wi[:], in1=cs[:], op=ALU.mult), reads=[wi, cs], writes=[wri])
                            kb.op("dve", lambda e: e.tensor_tensor(out=wii[:], in0=wr[:], in1=sn[:], op=ALU.mult), reads=[wr, sn], writes=[wii])
                            kb.op("pool", lambda e: e.tensor_tensor(out=xim[:], in0=wri[:], in1=wii[:], op=ALU.add), reads=[wri, wii], writes=[xim])
                            for tg in range(HS // 512):
                                tk = slice(t0 + 512 * tg, t0 + 512 * tg + 512)
                                lk = slice(512 * tg, 512 * tg + 512)
                                py = psY[tg % 2]
                                c0 = CTz[0][:, q, :] if rb == 96 else CT[0][:, q, rb:rb + 32]
                                c1 = CTz[1][:, q, :] if rb == 96 else CT[1][:, q, rb:rb + 32]
                                kb.op("pe", lambda e: e.matmul(py[rows, :], c0, xre[:, lk], start=True, stop=False), reads=[CT[0], CTz[0], xre], writes=[py])
                                kb.op("pe", lambda e: e.matmul(py[rows, :], c1, xim[:, lk], start=False, stop=True), reads=[CT[1], CTz[1], xim], writes=[py])
                                kb.op("dve", lambda e: e.tensor_tensor(out=yT[rows, q, tk], in0=yT[rows, q, tk], in1=py[rows, :], op=ALU.add), reads=[yT, py], writes=[yT])
                        return (s1, s2)
                    steps.append(mk())
            pipeline(steps, 2)
            kb.barrier()
        dbg = p.d.get("dbg_s5y")
        if dbg is not None:
            kb.dma("sp", dbg.h.rearrange("(q p) t -> p q t", p=128), yT[:], reads=[yT], writes=[dbg])
        with ExitStack() as es2:
            sb2 = lambda n, shp, dt=F32: p.sb(es2, "s5g_" + n, shp, dt)
            gT = sb2("gT", [128, 4, S], BF16)
            Wg = sb2("Wg", [128, 4, 512], BF16)
            kb.dma("sp", Wg[:], wglu.h.rearrange("(k p) n -> p k n", p=128), reads=[wglu], writes=[Wg])
            bg = sb2("bg", [128, 4])
            kb.dma("sp", bg[:], b_glu[l].rearrange("(q p) -> p q", p=128), writes=[bg], allow_slow_non_contiguous=True)
            gn = sb2("gn", [128, 4])
            kb.dma("sp", gn[:], gain[l, 0].rearrange("(q p) -> p q", p=128), writes=[gn], allow_slow_non_contiguous=True)
            ones = load_const_bf16(p, es2, "c_ones", (128, 128))
            sig = [sb2("sig%d" % i, [128, 512]) for i in range(2)]
            sq = sb2("sq", [128, 4, 512], BF16)
            rs = sb2("rs", [128, 512])
            yo = [sb2("yo%d" % i, [128, 512], BF16) for i in range(2)]
            pG = [p.ps(es2, "s5_pG%d" % i, [128, 512]) for i in range(2)]
            pN = p.ps(es2, "s5_pN", [128, 512])
            for q in range(4):
                kb.op("act", lambda e: e.activation(out=yT[:, q, :], in_=yT[:, q, :], func=AF.Gelu_apprx_tanh), reads=[yT], writes=[yT])
                kb.op("dve", lambda e: e.tensor_copy(out=gT[:, q, :], in_=yT[:, q, :]), reads=[yT], writes=[gT])
            cnt = 0
            for tg in range(4):
                tk = slice(512 * tg, 512 * tg + 512)
                for nq in range(4):
                    pg, sg = pG[cnt % 2], sig[cnt % 2]
                    cnt += 1
                    for kq in range(4):
                        kb.op("pe", lambda e: e.matmul(pg[:], Wg[:, kq, 128 * nq:128 * nq + 128], gT[:, kq, tk], start=(kq == 0), stop=(kq == 3)), reads=[Wg, gT], writes=[pg])
                    kb.op("act", lambda e: e.activation(out=sg[:], in_=pg[:], func=AF.Sigmoid, bias=bg[:, nq:nq + 1]), reads=[pg, bg], writes=[sg])
                    kb.op("dve", lambda e: e.tensor_tensor(out=yT[:, nq, tk], in0=yT[:, nq, tk], in1=sg[:], op=ALU.mult), reads=[yT, sg], writes=[yT])
                    kb.op("act", lambda e: e.activation(out=sq[:, nq, :], in_=yT[:, nq, tk], func=AF.Square), reads=[yT], writes=[sq])
                for nq in range(4):
                    kb.op("pe", lambda e: e.matmul(pN[:], ones[:], sq[:, nq, :], start=(nq == 0), stop=(nq == 3)), reads=[ones, sq], writes=[pN])
                kb.op("act", lambda e: e.activation(out=rs[:], in_=pN[:], func=AF.Sqrt, bias=1e-6, scale=1.0 / 512), reads=[pN], writes=[rs])
                kb.op("dve", lambda e: e.reciprocal(out=rs[:], in_=rs[:]), reads=[rs], writes=[rs])
                for nq in range(4):
                    y_ = yo[cnt % 2]
                    cnt += 1
                    kb.op("dve", lambda e: e.scalar_tensor_tensor(out=y_[:], in0=yT[:, nq, tk], scalar=gn[:, nq:nq + 1], in1=rs[:], op0=ALU.mult, op1=ALU.mult), reads=[yT, gn, rs], writes=[y_])
                    kb.dma("pool", ysT[128 * nq:128 * nq + 128, tk], y_[:], reads=[y_], writes=[ysT])
            dbg = p.d.get("dbg_s5ya")
            if dbg is not None:
                kb.dma("sp", dbg.h.rearrange("(q p) t -> p q t", p=128), yT[:], reads=[yT], writes=[dbg])
            kb.barrier()


def resid_ln(p, lnw, hold, pfs, tt, out_dram=None, out_res=None):
    kb = p.kb
    for cg in range(4):
        kb.op("dve", lambda e: e.scalar_tensor_tensor(out=hold[:, 512 * cg:512 * cg + 512], in0=hold[:, 512 * cg:512 * cg + 512], scalar=ALPHA, in1=pfs[cg][:], op0=ALU.mult, op1=ALU.add), reads=[hold, pfs[cg]], writes=[hold])
    emit_ln(p, lnw, hold, tt, 1e-5, out_dram=out_dram, out_res=out_res)


def phase_outproj(p, l):
    kb = p.kb
    W = p.wb[("w_out", l)]
    ysT = p.dram("ysT", (D, S), BF16)
    g = p.dram("ln1_g", IN_SHAPES["ln1_g"], F32, kind="ExternalInput")
    b = p.dram("ln1_b", IN_SHAPES["ln1_b"], F32, kind="ExternalInput")
    with ExitStack() as es:
        Wt = p.sb(es, "op_W", [128, KC, D], BF16)
        for c in range(0, KC, 4):
            kb.dma("sp", Wt[:, c:c + 4, :], W[c * 128:(c + 4) * 128, :].rearrange("(c p) n -> p c n", p=128), reads=[W], writes=[Wt])
        lnw = ln_tiles(p, es, "op_ln")
        ln_load_gb(p, lnw, g[l], b[l])
        yst = [p.sb(es, "op_ys%d" % i, [128, KC, 128], BF16) for i in range(2)]
        hold = [p.sb(es, "op_h%d" % i, [128, D], F32) for i in range(2)]
        pf = [p.ps(es, "op_pf%d" % i, [128, 512]) for i in range(4)]
        hold.append(p.sb(es, "op_h2", [128, D], F32))
        hold.append(p.sb(es, "op_h3", [128, D], F32))
        yst.append(p.sb(es, "op_ys2", [128, KC, 128], BF16))
        steps = []
        for tt in range(NT):
            def mk(tt=tt):
                ys, ho = yst[tt % 3], hold[tt % 4]

                def s0():
                    kb.dma("sp", ys[:], ysT[:, tt * 128:(tt + 1) * 128].rearrange("(c p) t -> p c t", p=128), reads=[ysT], writes=[ys])
                    kb.dma("sp", ho[:], p.h[tt * 128:(tt + 1) * 128, :], reads=[p.hres[tt]], writes=[ho])

                def s1():
                    for cg in range(4):
                        for k in range(KC):
                            kb.op("pe", lambda e: e.matmul(pf[cg][:], ys[:, k, :], Wt[:, k, 512 * cg:512 * cg + 512], start=(k == 0), stop=(k == KC - 1)), reads=[ys, Wt], writes=[pf[cg]])
                        kb.op("dve", lambda e: e.scalar_tensor_tensor(out=ho[:, 512 * cg:512 * cg + 512], in0=ho[:, 512 * cg:512 * cg + 512], scalar=ALPHA, in1=pf[cg][:], op0=ALU.mult, op1=ALU.add), reads=[ho, pf[cg]], writes=[ho])
                    emit_ln_a(p, lnw, ho, tt, 1e-5)

                def s2():
                    emit_ln_b(p, lnw, tt)
                return (s0, s1, s2)
            steps.append(mk())
        pipeline(steps, 3)
        kb.barrier()


def phase_xattn(p, l):
    kb = p.kb
    wq, wkv, wo = p.wb[("xa_wq", l)], p.wb[("xa_wkv", l)], p.wb[("xa_wo", l)]
    mem = p.dram("mem", (MEM, D), F32, kind="ExternalInput")
    g = p.dram("ln2_g", IN_SHAPES["ln2_g"], F32, kind="ExternalInput")
    b = p.dram("ln2_b", IN_SHAPES["ln2_b"], F32, kind="ExternalInput")
    with ExitStack() as es0:
        o_all = p.sb(es0, "xa_o", [128, NT, 512], BF16)
        with ExitStack() as es:
            hT = load_hT(p, es)
            Wq = p.sb(es, "xa_Wq", [128, KC, 512], BF16)
            Wkv = p.sb(es, "xa_Wkv", [128, KC, 1024], BF16)
            kb.dma("sp", Wq[:], wq.h.rearrange("(c p) n -> p c n", p=128), reads=[wq], writes=[Wq])
            kb.dma("sp", Wkv[:], wkv.h.rearrange("(c p) n -> p c n", p=128), reads=[wkv], writes=[Wkv])
            memT = p.sb(es, "xa_memT", [128, KC, MEM], BF16)
            kT = p.sb(es, "xa_kT", [128, 4, MEM], BF16)
            Vx = p.sb(es, "xa_V", [128, 2, 4, 129], BF16)
            kb.op("pool", lambda e: e.memset(Vx[:, :, :, 128:129], 1.0), writes=[Vx])
            pA = [p.ps(es, "xa_pA%d" % i, [128, 512]) for i in range(2)]
            pS = [p.ps(es, "xa_pS%d" % i, [128, 512]) for i in range(2)]
            pO = [p.ps(es, "xa_pO%d" % i, [128, 512]) for i in range(2)]
            pTb = p.ps(es, "xa_pTb", [128, 2048], BF16)
            with ExitStack() as es2:
                mf = p.sb(es2, "xa_mf", [128, D], F32)
                mb_ = p.sb(es2, "xa_mb", [128, D], BF16)
                for mt in range(2):
                    kb.dma("sp", mf[:], mem[mt * 128:(mt + 1) * 128, :], writes=[mf])
                    kb.op("act", lambda e: e.activation(out=mb_[:], in_=mf[:], func=AF.Copy), reads=[mf], writes=[mb_])
                    for c in range(KC):
                        kb.op("pe", lambda e: e.transpose(pTb[:, 128 * c:128 * c + 128], mb_[:, 128 * c:128 * c + 128], p.ident[:]), reads=[mb_, p.ident], writes=[pTb])
                    kb.op("dve", lambda e: e.tensor_copy(out=memT[:, :, 128 * mt:128 * mt + 128], in_=pTb[:].rearrange("p (c t) -> p c t", c=KC)), reads=[pTb], writes=[memT])
            for h in range(4):
                pa = pA[h % 2]
                for k in range(KC):
                    kb.op("pe", lambda e: e.matmul(pa[:, 0:MEM], Wkv[:, k, 128 * h:128 * h + 128], memT[:, k, :], start=(k == 0), stop=(k == KC - 1)), reads=[Wkv, memT], writes=[pa])
                kb.op("act", lambda e: e.activation(out=kT[:, h, :], in_=pa[:, 0:MEM], func=AF.Copy), reads=[pa], writes=[kT])
            for mt in range(2):
                pa = pA[mt % 2]
                for k in range(KC):
                    kb.op("pe", lambda e: e.matmul(pa[:], memT[:, k, 128 * mt:128 * mt + 128], Wkv[:, k, 512:1024], start=(k == 0), stop=(k == KC - 1)), reads=[Wkv, memT], writes=[pa])
                kb.op("act", lambda e: e.activation(out=Vx[:, mt, :, 0:128], in_=pa[:].rearrange("p (h d) -> p h d", d=128), func=AF.Copy), reads=[pa], writes=[Vx])
            qs = [p.sb(es, "xa_qs%d" % i, [128, 512], BF16) for i in range(2)]
            Pm = [[p.sb(es, "xa_P%d_%d" % (i, m), [128, 512], BF16) for m in range(2)] for i in range(2)]
            rd = p.sb(es, "xa_rd", [128, 2], F32)
            qs.append(p.sb(es, "xa_qs2", [128, 512], BF16))
            Pm.append([p.sb(es, "xa_P2_%d" % m, [128, 512], BF16) for m in range(2)])
            rds = [rd, p.sb(es, "xa_rd2", [128, 2], F32)]
            steps = []
            for G in range(4):
                for h in range(4):
                    def mk(G=G, h=h, cnt=len(steps)):
                        pa, q_, P_ = pA[cnt % 2], qs[cnt % 3], Pm[cnt % 3]

                        def s1():
                            for k in range(KC):
                                kb.op("pe", lambda e: e.matmul(pa[:], Wq[:, k, 128 * h:128 * h + 128], hT[:, k, 512 * G:512 * G + 512], start=(k == 0), stop=(k == KC - 1)), reads=[Wq, hT], writes=[pa])
                            kb.op("act", lambda e: e.mul(q_[:], pa[:], 128 ** -0.5), reads=[pa], writes=[q_])

                        def s2():
                            for m in range(2):
                                kb.op("pe", lambda e: e.matmul(pS[m][:], kT[:, h, 128 * m:128 * m + 128], q_[:], start=True, stop=True), reads=[kT, q_], writes=[pS[m]])
                                kb.op("act", lambda e: e.activation(out=P_[m][:], in_=pS[m][:], func=AF.Exp), reads=[pS[m]], writes=[P_[m]])

                        def s3():
                            for jj in range(2):
                                po = pO[jj]
                                rd_ = rds[jj]
                                for j2 in range(2):
                                    j = 2 * jj + j2
                                    for m in range(2):
                                        kb.op("pe", lambda e: e.matmul(po[:, 129 * j2:129 * j2 + 129], P_[m][:, 128 * j:128 * j + 128], Vx[:, m, h, :], start=(m == 0), stop=(m == 1)), reads=[P_[m], Vx], writes=[po])
                                kb.op("dve", lambda e: e.reciprocal(out=rd_[:], in_=po[:, 0:258].rearrange("p (j f) -> p j f", f=129)[:, :, 128]), reads=[po], writes=[rd_])
                                for j2 in range(2):
                                    j = 2 * jj + j2
                                    kb.op("dve", lambda e: e.tensor_scalar(out=o_all[:, 4 * G + j, 128 * h:128 * h + 128], in0=po[:, 129 * j2:129 * j2 + 128], scalar1=rd_[:, j2:j2 + 1], scalar2=None, op0=ALU.mult), reads=[po, rd_], writes=[o_all])
                        return (s1, s2, s3)
                    steps.append(mk())
            pipeline(steps, 3)
            kb.barrier()
        with ExitStack() as es:
            Wo = p.sb(es, "xa_Wo", [128, 4, D], BF16)
            kb.dma("sp", Wo[:], wo.h.rearrange("(c p) n -> p c n", p=128), reads=[wo], writes=[Wo])
            lnw = ln_tiles(p, es, "xa_ln")
            ln_load_gb(p, lnw, g[l], b[l])
            hold = [p.sb(es, "xa_h%d" % i, [128, D], F32) for i in range(2)]
            pf = [p.ps(es, "xa_pf%d" % i, [128, 512]) for i in range(4)]
            pT2 = p.ps(es, "xa_pT2", [128, 1024], BF16)
            oT = [p.sb(es, "xa_oT%d" % i, [128, 4, 128], BF16) for i in range(2)]
            hold.append(p.sb(es, "xa_h2", [128, D], F32))
            hold.append(p.sb(es, "xa_h3", [128, D], F32))
            steps = []
            for tt in range(NT):
                def mk(tt=tt):
                    ho, o_ = hold[tt % 4], oT[tt % 2]

                    def s0():
                        kb.dma("sp", ho[:], p.h[tt * 128:(tt + 1) * 128, :], reads=[p.hres[tt]], writes=[ho])

                    def s1():
                        for c in range(4):
                            kb.op("pe", lambda e: e.transpose(pT2[:, 128 * c:128 * c + 128], o_all[:, tt, 128 * c:128 * c + 128], p.ident[:]), reads=[o_all, p.ident], writes=[pT2])
                        kb.op("act", lambda e: e.activation(out=o_[:], in_=pT2[:, 0:512].rearrange("p (c t) -> p c t", c=4), func=AF.Copy), reads=[pT2], writes=[o_])

                    def s2():
                        for cg in range(4):
                            for c in range(4):
                                kb.op("pe", lambda e: e.matmul(pf[cg][:], o_[:, c, :], Wo[:, c, 512 * cg:512 * cg + 512], start=(c == 0), stop=(c == 3)), reads=[o_, Wo], writes=[pf[cg]])
                            kb.op("dve", lambda e: e.scalar_tensor_tensor(out=ho[:, 512 * cg:512 * cg + 512], in0=ho[:, 512 * cg:512 * cg + 512], scalar=ALPHA, in1=pf[cg][:], op0=ALU.mult, op1=ALU.add), reads=[ho, pf[cg]], writes=[ho])
                        emit_ln_a(p, lnw, ho, tt, 1e-5)

                    def s3():
                        emit_ln_b(p, lnw, tt)
                    return (s0, s1, s2, s3)
                steps.append(mk())
            pipeline(steps, 4)
            kb.barrier()


def phase_ffn(p, l, final_out=None):
    kb = p.kb
    wg, wu, wd = p.wb[("ffn_w_gate", l)], p.wb[("ffn_w_up", l)], p.wb[("ffn_w_down", l)]
    g = p.dram("ln3_g", IN_SHAPES["ln3_g"], F32, kind="ExternalInput")
    b = p.dram("ln3_b", IN_SHAPES["ln3_b"], F32, kind="ExternalInput")
    NCH = FFN // 128
    with ExitStack() as es:
        lnw = ln_tiles(p, es, "ff_ln")
        ln_load_gb(p, lnw, g[l], b[l])
        hTg = p.sb(es, "ff_hT", [128, KC, 512], BF16)
        a_t = p.sb(es, "ff_a", [128, NCH, 512], BF16)
        Wg_t = [p.sb(es, "ff_Wg%d" % i, [128, KC, 256], BF16) for i in range(2)]
        Wu_t = [p.sb(es, "ff_Wu%d" % i, [128, KC, 256], BF16) for i in range(2)]
        Wd_t = [p.sb(es, "ff_Wd%d" % i, [128, 11, 512], BF16) for i in range(2)]
        s_t = [p.sb(es, "ff_s%d" % i, [128, 512], F32) for i in range(2)]
        v4 = p.sb(es, "ff_v4", [128, 4, D], F32)
        pg = [p.ps(es, "ff_pg%d" % i, [128, 512]) for i in range(2)]
        pu = [p.ps(es, "ff_pu%d" % i, [128, 512]) for i in range(2)]
        pf = [p.ps(es, "ff_pf%d" % i, [128, 512]) for i in range(2)]
        lnw["hbs"] = lnw["hbs"] + [p.sb(es, "ff_hb%d" % i, [128, D], BF16) for i in range(2, 4)]
        lnw["hTs"] = lnw["hTs"] + [p.sb(es, "ff_hTs%d" % i, [128, KC, 128], BF16) for i in range(2, 4)]
        wl = 0
        dl = 0
        cnt = 0
        accs = [pf[0], pf[1], pg[0], pg[1]]
        pending = []

        class _W:
            def __init__(self, j):
                self.j = j

            def __getitem__(self, k):
                if k == "hbs":
                    return [lnw["hbs"][self.j]] * 2
                if k == "hTs":
                    return [lnw["hTs"][self.j]] * 2
                return lnw[k]

        for tg in range(4):
            kb.dma("sp", hTg[:], p.hTd[:, 512 * tg:512 * tg + 512].rearrange("(c p) t -> p c t", p=128), reads=[p.hTd], writes=[hTg])
            for c2 in range(NCH // 2):
                Wg_, Wu_ = Wg_t[wl % 2], Wu_t[wl % 2]
                wl += 1
                kb.dma("sp", Wg_[:], wg[:, 256 * c2:256 * c2 + 256].rearrange("(c p) n -> p c n", p=128), reads=[wg], writes=[Wg_])
                kb.dma("sp", Wu_[:], wu[:, 256 * c2:256 * c2 + 256].rearrange("(c p) n -> p c n", p=128), reads=[wu], writes=[Wu_])
                for ci in range(2):
                    ch = 2 * c2 + ci
                    pg_, pu_, st_ = pg[cnt % 2], pu[cnt % 2], s_t[cnt % 2]
                    cnt += 1
                    for k in range(KC):
                        kb.op("pe", lambda e: e.matmul(pg_[:], Wg_[:, k, 128 * ci:128 * ci + 128], hTg[:, k, :], start=(k == 0), stop=(k == KC - 1)), reads=[Wg_, hTg], writes=[pg_])
                    for k in range(KC):
                        kb.op("pe", lambda e: e.matmul(pu_[:], Wu_[:, k, 128 * ci:128 * ci + 128], hTg[:, k, :], start=(k == 0), stop=(k == KC - 1)), reads=[Wu_, hTg], writes=[pu_])
                    kb.op("act", lambda e: e.activation(out=st_[:], in_=pg_[:], func=AF.Silu), reads=[pg_], writes=[st_])
                    kb.op("dve", lambda e: e.tensor_tensor(out=a_t[:, ch, :], in0=st_[:], in1=pu_[:], op=ALU.mult), reads=[st_, pu_], writes=[a_t])
                if c2 == 1:
                    for f in pending:
                        f()
                    pending = []
                    for j in range(4):
                        tt = 4 * tg + j
                        kb.dma("sp", v4[:, j, :], p.h[tt * 128:(tt + 1) * 128, :], reads=[p.hres[tt]], writes=[v4])
            for cg in range(4):
                for pc in range(4):
                    Wd_ = Wd_t[dl % 2]
                    dl += 1
                    kb.dma("sp", Wd_[:], wd[pc * 11 * 128:(pc + 1) * 11 * 128, 512 * cg:512 * cg + 512].rearrange("(c p) n -> p c n", p=128), reads=[wd], writes=[Wd_])
                    for j in range(4):
                        for c in range(11):
                            ch = pc * 11 + c
                            kb.op("pe", lambda e: e.matmul(accs[j][:], a_t[:, ch, 128 * j:128 * j + 128], Wd_[:, c, :], start=(ch == 0), stop=(ch == NCH - 1)), reads=[a_t, Wd_], writes=[accs[j]])
                for j in range(4):
                    kb.op("dve", lambda e: e.scalar_tensor_tensor(out=v4[:, j, 512 * cg:512 * cg + 512], in0=v4[:, j, 512 * cg:512 * cg + 512], scalar=ALPHA, in1=accs[j][:], op0=ALU.mult, op1=ALU.add), reads=[v4, accs[j]], writes=[v4])
            fin = final_out is not None
            for j in range(4):
                tt = 4 * tg + j
                emit_ln_a(p, _W(j), _V4(v4, j), tt, 1e-5, out_dram=(final_out.h if fin else None), out_res=final_out, write_hT=(not fin), h_store=(not fin))
                if not fin:
                    pending.append(lambda j=j, tt=tt: emit_ln_b(p, _W(j), tt))
        for f in pending:
            f()
        kb.barrier()


class _V4:
    def __init__(self, t, j):
        self.t, self.j, self.r = t, j, t.r

    def __getitem__(self, k):
        if isinstance(k, tuple):
            return self.t.h[(k[0], self.j) + tuple(k[1:])]
        return self.t.h[k, self.j]


def build_full(depth=DEPTH):
    p = Prog(ext_out=["out"])
    setup_globals(p)
    out = p.dram("out", (S, D), F32)
    convert_weights(p, names=["w_in"], layers=[0])
    phase_ln0(p)
    for l in range(depth):
        convert_weights(p, names=["s5_w_glu", "nsa_cmp_w1", "nsa_cmp_w2"], layers=[l])
        phase_inproj(p, l)
        phase_s5(p, l, after_setup=lambda: convert_weights(p, names=["w_out", "xa_wq", "xa_wkv", "xa_wo", "ffn_w_gate"], layers=[l]))
        phase_nsa(p, l, after_setup=lambda: convert_weights(p, names=["ffn_w_up"] , layers=[l]))

        def cv3():
            convert_weights(p, names=["ffn_w_down"], layers=[l])
            if l + 1 < depth:
                convert_weights(p, names=["w_in"], layers=[l + 1])
        phase_sb(p, l, after_setup=cv3)
        phase_dil(p, l)
        phase_outproj(p, l)
        phase_xattn(p, l)
        phase_ffn(p, l, final_out=(out if l == depth - 1 else None))
    p.kb.finish([out.r])
    p.kb.close()
    return p


def kernel(**inputs):
    p = build_full()
    consts = host_consts()
    n = 8
    in_maps = []
    for b in range(n):
        m = {}
        for name, kind in p.kinds.items():
            if kind != "ExternalInput":
                continue
            if name in consts:
                m[name] = consts[name]
            elif name in ("x", "mem"):
                m[name] = np.ascontiguousarray(np.asarray(inputs[name])[b], dtype=np.float32)
            else:
                m[name] = np.ascontiguousarray(np.asarray(inputs[name]), dtype=np.float32)
        in_maps.append(m)
    res = run_bass_kernel_spmd(p.nc, in_maps, core_ids=list(range(n)))
    return np.stack([np.asarray(res.results[b]["out"], dtype=np.float32) for b in range(n)], axis=0)
```

```python
import math
from contextlib import ExitStack

import numpy as np
import ml_dtypes
import concourse.bass as bass
import concourse.mybir as mybir
from concourse.bass_utils import run_bass_kernel_spmd

F32 = mybir.dt.float32
BF16 = mybir.dt.bfloat16
AF = mybir.ActivationFunctionType
ALU = mybir.AluOpType
AX = mybir.AxisListType


class Res:
    __slots__ = ("name", "w", "r", "excl")

    def __init__(self, name=""):
        self.name = name
        self.excl = False
        self.w = None
        self.r = []


class KB:
    N_DMA_SEMS = 24

    def __init__(self, nc):
        self.nc = nc
        self.es = ExitStack()
        self.eng = {"pe": nc.tensor, "dve": nc.vector, "act": nc.scalar, "pool": nc.gpsimd, "sp": nc.sync}
        self.sem = {}
        self.cnt = {}
        self.semh = {}
        for e in self.eng:
            h = self.es.enter_context(nc.semaphore("s_" + e))
            self.sem[e] = h
            self.cnt[e] = 0
            self.semh[("c", e)] = h
        self.dsem = {}
        self.dval = {}
        self.dnext = {}
        for q in ("sp", "pool", "act"):
            self.dsem[q] = []
            for i in range(self.N_DMA_SEMS):
                h = self.es.enter_context(nc.semaphore("d_%s_%d" % (q, i)))
                self.dsem[q].append(h)
                self.semh[("d", q, i)] = h
                self.dval[(q, i)] = 0
            self.dnext[q] = 0
        self.known = {e: {} for e in self.eng}
        self.ninstr = 0
        for h in self.semh.values():
            nc.gpsimd.sem_clear(h)
        nc.all_engine_barrier()

    def _wait(self, e, ev):
        if ev is None:
            return
        key, val = ev
        if key == ("c", e) and e == "pe":
            return
        k = self.known[e]
        if k.get(key, 0) >= val:
            return
        self.eng[e].wait_ge(self.semh[key], val)
        k[key] = val

    def _deps(self, e, reads, writes):
        for r in reads:
            self._wait(e, r.w)
        for w in writes:
            self._wait(e, w.w)
            for ev in w.r:
                self._wait(e, ev)

    def _commit(self, ev, reads, writes):
        for r in reads:
            if r in writes:
                continue
            r.r.append(ev)
            if len(r.r) > 12:
                d = {}
                for k, v in r.r:
                    d[k] = max(d.get(k, 0), v)
                r.r = list(d.items())
        for w in writes:
            w.w = ev
            w.r = []

    def op(self, e, fn, reads=(), writes=()):
        self._deps(e, reads, writes)
        ins = fn(self.eng[e])
        self.cnt[e] += 1
        ins.then_inc(self.sem[e], 1)
        ev = (("c", e), self.cnt[e])
        self._commit(ev, reads, writes)
        self.ninstr += 1
        return ev

    def dma(self, q, out, in_, reads=(), writes=(), **kw):
        i = self.dnext[q]
        self.dnext[q] = (i + 1) % len(self.dsem[q])
        key = ("d", q, i)
        if self.dval[(q, i)] > 0:
            self._wait(q, (key, self.dval[(q, i)]))
        self._deps(q, reads, writes)
        kw.setdefault("allow_slow_non_contiguous", True)
        ins = self.eng[q].dma_start(out=out, in_=in_, **kw)
        self.dval[(q, i)] += 16
        ins.then_inc(self.semh[key], 16)
        ev = (key, self.dval[(q, i)])
        self._commit(ev, reads, writes)
        self.ninstr += 1
        return ev

    def finish(self, final_res):
        for r in final_res:
            self._wait("sp", r.w)
        for q in self.dsem:
            for i in range(len(self.dsem[q])):
                if self.dval[(q, i)] > 0:
                    self._wait("sp", (("d", q, i), self.dval[(q, i)]))
        for e in ("pe", "dve", "act", "pool"):
            if self.cnt[e] > 0:
                self._wait("sp", (("c", e), self.cnt[e]))
        self.nc.all_engine_barrier()
        for h in self.semh.values():
            self.nc.gpsimd.sem_clear(h)
        self.nc.all_engine_barrier()

    def close(self):
        self.es.close()

    def barrier(self):
        evs = []
        for e in ("pe", "dve", "act", "pool", "sp"):
            if self.cnt[e] > 0:
                evs.append((("c", e), self.cnt[e]))
        for q in self.dsem:
            for i in range(len(self.dsem[q])):
                if self.dval[(q, i)] > 0:
                    evs.append((("d", q, i), self.dval[(q, i)]))
        for e in ("pe", "dve", "act", "pool", "sp"):
            for ev in evs:
                if ev[0] == ("c", e):
                    continue
                self._wait(e, ev)


def _res(x):
    return x.r if hasattr(x, "r") and not isinstance(x, Res) else x


_orig_op = KB.op
_orig_dma = KB.dma


def _op(self, e, fn, reads=(), writes=()):
    reads = [_res(x) for x in reads]
    writes = [_res(x) for x in writes]
    for r in reads:
        if r.excl and r not in writes:
            writes.append(r)
    return _orig_op(self, e, fn, reads, writes)


def _dma(self, q, out, in_, reads=(), writes=(), **kw):
    return _orig_dma(self, q, out, in_, [_res(x) for x in reads], [_res(x) for x in writes], **kw)


KB.op = _op
KB.dma = _dma


def pipeline(steps, nst):
    n = len(steps)
    for i in range(n + nst - 1):
        for s_ in range(nst):
            k = i - s_
            if 0 <= k < n and steps[k][s_] is not None:
                steps[k][s_]()


def pipeline_multi(groups):
    st = []
    for steps, nst in groups:
        st.append([steps, nst, 0, len(steps) + nst - 1])
    while True:
        best = None
        for g in st:
            if g[2] < g[3]:
                frac = g[2] / float(g[3])
                if best is None or frac < best[0]:
                    best = (frac, g)
        if best is None:
            break
        g = best[1]
        steps, nst, i = g[0], g[1], g[2]
        for s_ in range(nst):
            k = i - s_
            if 0 <= k < len(steps) and steps[k][s_] is not None:
                steps[k][s_]()
        g[2] += 1


def drive(gen):
    for steps, nst in gen:
        pipeline(steps, nst)


def drive_merged(gens):
    items = [next(g) for g in gens]
    pipeline_multi(items)
    for g in reversed(gens):
        for steps, nst in g:
            pipeline(steps, nst)


class TL:
    def __init__(self, h, name=""):
        self.h = h
        self.r = Res(name)

    def __getitem__(self, k):
        return self.h[k]


D = 2048
S = 2048
NT = S // 128
KC = D // 128
DEPTH = 2
MEM = 256
FFN = 5632
ALPHA = (2 * DEPTH) ** 0.25
IN_W = 4888
O_U, O_NQ, O_NKV, O_GATE, O_SB, O_DIL = 0, 512, 1024, 1792, 1816, 3352
FM_SRC = [128 * i for i in range(14)] + [1816 + 128 * j for j in range(8)] + [3352 + 128 * j for j in range(8)]
FM_ROPE = set([4, 5, 6, 7, 10, 12] + list(range(22, 30)))
FM_U, FM_NQ, FM_KCMP, FM_VCMP, FM_KSLC, FM_KWIN, FM_SBQ, FM_SBK, FM_DQ, FM_DK = 0, 4, 8, 9, 10, 12, 14, 18, 22, 26
TM_SRC = [(1408, 128), (1664, 128), (2840, 512), (4376, 512), (1792, 24)]
TM_VSLC, TM_VWIN, TM_SBV, TM_DV = 0, 128, 256, 768
TM_W = 1280

WNAMES = ["w_in", "s5_w_glu", "nsa_cmp_w1", "nsa_cmp_w2", "w_out", "xa_wq", "xa_wkv", "xa_wo",
          "ffn_w_gate", "ffn_w_up", "ffn_w_down"]
IN_SHAPES = {
    "x": (S, D), "mem": (MEM, D), "ln_in_g": (D,), "ln_in_b": (D,), "w_in": (DEPTH, D, IN_W),
    "s5_lambda_re": (DEPTH, 32, 64), "s5_lambda_im": (DEPTH, 32, 64), "s5_log_dt": (DEPTH, 32),
    "s5_b_re": (DEPTH, 32, 64, 16), "s5_b_im": (DEPTH, 32, 64, 16), "s5_c_re": (DEPTH, 32, 16, 64),
    "s5_c_im": (DEPTH, 32, 16, 64), "s5_d": (DEPTH, 32, 16), "s5_w_glu": (DEPTH, 512, 512),
    "s5_b_glu": (DEPTH, 512), "nsa_cmp_pe": (DEPTH, 2, 32, 64), "nsa_cmp_w1": (DEPTH, 2, 2048, 128),
    "nsa_cmp_w2": (DEPTH, 2, 128, 64), "mix_norm_g": (DEPTH, 4, 512), "w_out": (DEPTH, D, D),
    "ln1_g": (DEPTH, D), "ln1_b": (DEPTH, D), "xa_wq": (DEPTH, D, 512), "xa_wkv": (DEPTH, D, 1024),
    "xa_wo": (DEPTH, 512, D), "ln2_g": (DEPTH, D), "ln2_b": (DEPTH, D), "ffn_w_gate": (DEPTH, D, FFN),
    "ffn_w_up": (DEPTH, D, FFN), "ffn_w_down": (DEPTH, FFN, D), "ln3_g": (DEPTH, D), "ln3_b": (DEPTH, D),
}


def host_consts():
    c = {}
    c["c_ident"] = np.eye(128, dtype=np.float32)
    half = 32
    inv = (10000.0 ** (-np.arange(half, dtype=np.float32) / half)).astype(np.float32)
    pos = np.arange(S, dtype=np.float32)
    ang = pos[None, :] * inv[:, None]
    cos = np.cos(ang).astype(np.float32)
    sin = np.sin(ang).astype(np.float32)
    cos64 = np.concatenate([cos, cos], 0)
    sin64 = np.concatenate([-sin, sin], 0)
    c["c_cos"] = np.concatenate([cos64, cos64], 0)
    c["c_sin"] = np.concatenate([sin64, sin64], 0)
    sw = np.zeros((128, 128), np.float32)
    for m in range(128):
        k = m + 32 if (m % 64) < 32 else m - 32
        sw[k, m] = 1.0
    c["c_swap"] = sw
    k = np.arange(128)[:, None, None]
    i = np.arange(4)[None, :, None]
    q = np.arange(512)[None, None, :]
    c["c_maskS"] = ((128 * i + k) < q).astype(np.float32)
    c["c_maskC"] = ((128 * i + k) <= q).astype(np.float32)
    c["c_maskCn"] = 1.0 - c["c_maskC"]
    jj = np.arange(128)[:, None]
    kk = np.arange(128)[None, :]
    c["c_negU"] = -(jj >= kk).astype(np.float32)
    c["c_negOnes"] = -np.ones((128, 128), np.float32)
    c["c_ones"] = np.ones((128, 128), np.float32)
    c["c_maskD"] = np.stack([(jj >= kk), (jj <= kk)], 1).astype(np.float32)
    cc = np.arange(128)[:, None]
    tt_ = np.arange(S)[None, :]
    c["c_maskCmp"] = ((16 * cc + 31 <= tt_) & (cc < 127)).astype(np.float32)
    ci = np.arange(128)[:, None] * 16
    sj = np.arange(32)[None, :] * 64
    ov = np.clip(np.minimum(ci + 32, sj + 64) - np.maximum(ci, sj), 0, None) / 32.0
    ov[127] = 0
    c["c_overlap"] = ov.astype(np.float32)
    t = np.arange(S)[:, None]
    n = np.arange(32)[None, :]
    cur = t // 64
    invalid = n * 64 > t
    forced = ((n == 0) | (n == cur) | (n == cur - 1)) & ~invalid
    c["c_selMul"] = (~(invalid | forced)).astype(np.float32)
    c["c_selAdd"] = np.where(forced, 1e4, np.where(invalid, -1e4, 0.0)).astype(np.float32)
    eb = np.zeros((32, 16, 128), np.float32)
    for kb_ in range(16):
        for k in range(128):
            eb[2 * kb_ + k // 64, kb_, k] = 32768.0
    c["c_Ebig"] = eb
    c["c_iota"] = np.tile(np.arange(S, dtype=np.float32)[None, :], (128, 1))
    pp = np.arange(128)
    c["c_mask2"] = np.stack([((pp // 16) % 2 == 0), ((pp // 16) % 2 == 1)], 1).astype(np.float32)
    m3 = ((pp[:, None] // 64) == ((pp[None, :] // 16) % 2)).astype(np.float32)
    c["c_mask2z"] = c["c_mask2"] * (pp[:, None] >= 96)
    c["c_mask3"] = m3
    c["c_mask3n"] = -m3
    c["c_cos_cmp"] = np.ascontiguousarray(c["c_cos"][:, 31::16][:, :127])
    c["c_sin_cmp"] = np.ascontiguousarray(c["c_sin"][:, 31::16][:, :127])
    return c


class Prog:
    def __init__(self, ext_in=(), ext_out=()):
        self.nc = bass.Bass("TRN2", target_bir_lowering=False)
        self.kb = KB(self.nc)
        self.ext_in = set(ext_in)
        self.ext_out = set(ext_out)
        self.d = {}
        self.kinds = {}
        self.uid = 0
        self.bgq = []
        self.es = self.kb.es

    def dram(self, name, shape, dtype, kind=None):
        if name in self.d:
            return self.d[name]
        if kind is None:
            kind = "ExternalInput" if name in self.ext_in else ("ExternalOutput" if name in self.ext_out else "Internal")
        t = TL(self.nc.dram_tensor(name, list(shape), dtype, kind=kind).ap(), name)
        self.d[name] = t
        self.kinds[name] = kind
        return t

    def bg(self, n=1):
        for _ in range(n):
            if self.bgq:
                self.bgq.pop(0)()

    def sb(self, es, name, shape, dtype):
        self.uid += 1
        name = "%s_u%d" % (name, self.uid)
        return TL(es.enter_context(self.nc.sbuf_tensor(name, list(shape), dtype)), name)

    def ps(self, es, name, shape, dtype=F32):
        self.uid += 1
        name = "%s_u%d" % (name, self.uid)
        t = TL(es.enter_context(self.nc.psum_tensor(name, list(shape), dtype)), name)
        t.r.excl = True
        return t


def setup_globals(p):
    kb = p.kb
    p.hTd = p.dram("hT_stream", (D, S), BF16)
    p.ident = p.sb(p.es, "ident", [128, 128], BF16)
    cid = p.dram("c_ident", (128, 128), F32, kind="ExternalInput")
    kb.dma("pool", p.ident[:], cid[:, :], writes=[p.ident])
    p.h = p.dram("h_stream", (S, D), F32)
    p.hres = [Res("h%d" % i) for i in range(NT)]


def ln_tiles(p, es, pfx):
    w = {}
    w["st"] = p.sb(es, pfx + "st", [128, 24], F32)
    w["mv"] = p.sb(es, pfx + "mv", [128, 2], F32)
    w["hbs"] = [p.sb(es, pfx + "hb%d" % i, [128, D], BF16) for i in range(2)]
    w["pT"] = p.ps(es, pfx + "pT", [128, D], BF16)
    w["g"] = p.sb(es, pfx + "g", [128, D], F32)
    w["b"] = p.sb(es, pfx + "b", [128, D], F32)
    w["hTs"] = [p.sb(es, pfx + "hTs%d" % i, [128, KC, 128], BF16) for i in range(2)]
    return w


def ln_load_gb(p, w, g_ap, b_ap):
    p.kb.dma("sp", w["g"][:], g_ap.partition_broadcast(128), writes=[w["g"]])
    p.kb.dma("sp", w["b"][:], b_ap.partition_broadcast(128), writes=[w["b"]])


def emit_ln_a(p, w, v, tt, eps, out_dram=None, out_res=None, write_hT=True, h_store=True):
    kb = p.kb
    st, mv, g, b = w["st"], w["mv"], w["g"], w["b"]
    hb = w["hbs"][tt % 2]
    for c in range(4):
        kb.op("dve", lambda e: e.bn_stats(out=st[:, 6 * c:6 * c + 6], in_=v[:, 512 * c:512 * c + 512]), reads=[v], writes=[st])
    kb.op("dve", lambda e: e.bn_aggr(out=mv[:], in_=st[:]), reads=[st], writes=[mv])
    kb.op("act", lambda e: e.activation(out=mv[:, 1:2], in_=mv[:, 1:2], func=AF.Sqrt, bias=eps), reads=[mv], writes=[mv])
    kb.op("dve", lambda e: e.reciprocal(out=mv[:, 1:2], in_=mv[:, 1:2]), reads=[mv], writes=[mv])
    kb.op("dve", lambda e: e.tensor_scalar(out=v[:], in0=v[:], scalar1=mv[:, 0:1], scalar2=mv[:, 1:2], op0=ALU.subtract, op1=ALU.mult), reads=[v, mv], writes=[v])
    kb.op("pool", lambda e: e.tensor_tensor(out=v[:], in0=v[:], in1=g[:], op=ALU.mult), reads=[v, g], writes=[v])
    kb.op("pool", lambda e: e.tensor_tensor(out=v[:], in0=v[:], in1=b[:], op=ALU.add), reads=[v, b], writes=[v])
    if h_store:
        kb.dma("pool", p.h[tt * 128:(tt + 1) * 128, :], v[:], reads=[v], writes=[p.hres[tt]])
    if out_dram is not None:
        kb.dma("pool", out_dram[tt * 128:(tt + 1) * 128, :], v[:], reads=[v], writes=[out_res])
    if write_hT:
        kb.op("act", lambda e: e.activation(out=hb[:], in_=v[:], func=AF.Copy), reads=[v], writes=[hb])


def emit_ln_b(p, w, tt):
    kb = p.kb
    hb, pT = w["hbs"][tt % 2], w["pT"]
    for c in range(KC):
        kb.op("pe", lambda e: e.transpose(pT[:, 128 * c:128 * c + 128], hb[:, 128 * c:128 * c + 128], p.ident[:]), reads=[hb, p.ident], writes=[pT])
    hs = w["hTs"][tt % 2]
    kb.op("act", lambda e: e.activation(out=hs[:], in_=pT[:].rearrange("p (c t) -> p c t", c=KC), func=AF.Copy), reads=[pT], writes=[hs])
    kb.dma("act", p.hTd[:, tt * 128:(tt + 1) * 128].rearrange("(c p) t -> p c t", p=128), hs[:], reads=[hs], writes=[p.hTd])


def emit_ln(p, w, v, tt, eps, out_dram=None, out_res=None, write_hT=True, h_store=True):
    emit_ln_a(p, w, v, tt, eps, out_dram, out_res, write_hT, h_store)
    if write_hT:
        emit_ln_b(p, w, tt)


def load_hT(p, es, name="hT"):
    t = p.sb(es, name, [128, KC, S], BF16)
    for c in range(0, KC, 4):
        p.kb.dma("sp", t[:, c:c + 4, :], p.hTd[c * 128:(c + 4) * 128, :].rearrange("(c p) t -> p c t", p=128), reads=[p.hTd], writes=[t])
    return t


def convert_weights(p, names=WNAMES, layers=range(DEPTH), defer=False):
    kb = p.kb
    if not hasattr(p, "wb"):
        p.wb = {}
    for l in layers:
        for n in names:
            shp = IN_SHAPES[n][1:]
            src = p.dram(n, IN_SHAPES[n], F32, kind="ExternalInput")
            dst = p.dram("bf_%s_%d" % (n, l), shp, BF16)
            p.wb[(n, l)] = dst
            s_l = src[l]
            d_l = dst.h
            if len(shp) == 3:
                s_l = s_l.rearrange("a b c -> (a b) c")
                d_l = d_l.rearrange("a b c -> (a b) c")
            rows, cols = s_l.shape
            step = max(1, (1024 * 1024) // cols)
            r0 = 0
            while r0 < rows:
                r1 = min(rows, r0 + step)
                if defer:
                    p.bgq.append(lambda d_=d_l[r0:r1, :], s_=s_l[r0:r1, :], dst=dst: kb.dma("pool", d_, s_, writes=[dst]))
                else:
                    kb.dma("pool", d_l[r0:r1, :], s_l[r0:r1, :], writes=[dst])
                r0 = r1


def phase_ln0(p):
    kb = p.kb
    x = p.dram("x", (S, D), F32, kind="ExternalInput")
    g = p.dram("ln_in_g", (D,), F32, kind="ExternalInput")
    b = p.dram("ln_in_b", (D,), F32, kind="ExternalInput")
    with ExitStack() as es:
        w = ln_tiles(p, es, "l0")
        ln_load_gb(p, w, g.h, b.h)
        vs = [p.sb(es, "l0v%d" % i, [128, D], F32) for i in range(2)]
        for tt in range(NT):
            v = vs[tt % 2]
            kb.dma("sp", v[:], x[tt * 128:(tt + 1) * 128, :], writes=[v])
            emit_ln(p, w, v, tt, 1e-5)
        kb.barrier()


def phase_inproj(p, l, do_fm=True, do_tm=True, max_groups=99):
    kb = p.kb
    wb = p.wb[("w_in", l)]
    fm = p.dram("fm", (30 * 128, S), BF16)
    tm = p.dram("tm", (S, TM_W), BF16)
    gates = p.dram("gates", (S, 24), F32)
    ccos = p.dram("c_cos", (128, S), F32, kind="ExternalInput")
    csin = p.dram("c_sin", (128, S), F32, kind="ExternalInput")
    cswap = p.dram("c_swap", (128, 128), F32, kind="ExternalInput")
    with ExitStack() as es:
        p.hT = load_hT(p, es)
        cos = p.sb(es, "a_cos", [128, S], F32)
        sin = p.sb(es, "a_sin", [128, S], F32)
        swp = p.sb(es, "a_swp", [128, 128], BF16)
        kb.dma("sp", cos[:], ccos[:, :], writes=[cos])
        kb.dma("sp", sin[:], csin[:, :], writes=[sin])
        kb.dma("pool", swp[:], cswap[:, :], writes=[swp])
        wts = [p.sb(es, "a_w%d" % i, [128, KC, 512], BF16) for i in range(2)]
        pacc = [p.ps(es, "a_pacc%d" % i, [128, 512]) for i in range(2)]
        psw = [p.ps(es, "a_psw%d" % i, [128, 512]) for i in range(2)]
        xb = [p.sb(es, "a_xb%d" % i, [128, 512], BF16) for i in range(2)]
        t1 = [p.sb(es, "a_t1%d" % i, [128, 512], F32) for i in range(2)]
        t2 = [p.sb(es, "a_t2%d" % i, [128, 512], F32) for i in range(2)]
        outs = [p.sb(es, "a_out%d" % i, [128, S], BF16) for i in range(2)]
        groups = []
        i = 0
        while i < len(FM_SRC):
            j = i
            while j + 1 < len(FM_SRC) and j + 1 - i < 4 and FM_SRC[j + 1] == FM_SRC[j] + 128:
                j += 1
            groups.append((i, j - i + 1))
            i = j + 1
        steps = []
        for gi, (c0, n) in enumerate(groups if do_fm else []):
            if gi >= max_groups:
                break
            for ci in range(n):
                for tg in range(4):
                    def mk(gi=gi, c0=c0, n=n, ci=ci, tg=tg, cnt=len(steps)):
                        wt = wts[gi % 2]
                        src0 = FM_SRC[c0]
                        ch = c0 + ci
                        ot = outs[ch % 2]
                        pa = pacc[cnt % 2]
                        osl = ot[:, 512 * tg:512 * tg + 512]
                        rope = ch in FM_ROPE
                        x_b, ps2, a1, a2 = xb[cnt % 2], psw[cnt % 2], t1[cnt % 2], t2[cnt % 2]

                        def fin():
                            if tg == 3:
                                kb.dma("pool", fm[ch * 128:(ch + 1) * 128, :], ot[:], reads=[ot], writes=[fm])

                        def s1():
                            if ci == 0 and tg == 0:
                                kb.dma("sp", wt[:, :, 0:128 * n], wb[:, src0:src0 + 128 * n].rearrange("(c p) n -> p c n", p=128), reads=[wb], writes=[wt])
                            for k in range(KC):
                                kb.op("pe", lambda e: e.matmul(pa[:], wt[:, k, 128 * ci:128 * ci + 128], p.hT[:, k, 512 * tg:512 * tg + 512], start=(k == 0), stop=(k == KC - 1)), reads=[wt, p.hT], writes=[pa])
                            if rope:
                                kb.op("act", lambda e: e.activation(out=x_b[:], in_=pa[:], func=AF.Copy), reads=[pa], writes=[x_b])
                            else:
                                if cnt % 2 == 0:
                                    kb.op("act", lambda e: e.activation(out=osl, in_=pa[:], func=AF.Copy), reads=[pa], writes=[ot])
                                else:
                                    kb.op("dve", lambda e: e.tensor_copy(out=osl, in_=pa[:]), reads=[pa], writes=[ot])
                                fin()

                        def s2():
                            kb.op("pe", lambda e: e.matmul(ps2[:], swp[:], x_b[:], start=True, stop=True), reads=[swp, x_b], writes=[ps2])
                            kb.op("dve", lambda e: e.tensor_tensor(out=a1[:], in0=pa[:], in1=cos[:, 512 * tg:512 * tg + 512], op=ALU.mult), reads=[pa, cos], writes=[a1])
                            kb.op("dve", lambda e: e.tensor_tensor(out=a2[:], in0=ps2[:], in1=sin[:, 512 * tg:512 * tg + 512], op=ALU.mult), reads=[ps2, sin], writes=[a2])
                            kb.op("dve", lambda e: e.tensor_tensor(out=osl, in0=a1[:], in1=a2[:], op=ALU.add), reads=[a1, a2], writes=[ot])
                            fin()
                        return (s1, s2 if rope else None)
                    steps.append(mk())
        pipeline(steps, 2)
        kb.barrier()
    with ExitStack() as es:
        p.hT = load_hT(p, es)
        wv = p.sb(es, "a_wv", [128, KC, TM_W + 24], BF16)
        off = 0
        for (s0, wd) in TM_SRC:
            kb.dma("sp", wv[:, :, off:off + wd], wb[:, s0:s0 + wd].rearrange("(c p) n -> p c n", p=128), reads=[wb], writes=[wv])
            off += wd
        pt = [[p.ps(es, "a_pt%d_%d" % (i, j), [128, 512]) for j in range(4)] for i in range(2)]
        ot = [p.sb(es, "a_ot%d" % i, [128, TM_W], BF16) for i in range(2)]
        gt = [p.sb(es, "a_gt%d" % i, [128, 24], F32) for i in range(2)]
        segs = [(0, 256, 0), (256, 512, 1), (768, 512, 2)]
        for tt in range(NT if do_tm else 0):
            pp = pt[tt % 2]
            o = ot[tt % 2]
            g_ = gt[tt % 2]
            for k in range(KC):
                lhsT = p.hT[:, k, 128 * tt:128 * tt + 128]
                for (c0, wd, pi) in segs:
                    kb.op("pe", lambda e: e.matmul(pp[pi][:, 0:wd], lhsT, wv[:, k, c0:c0 + wd], start=(k == 0), stop=(k == KC - 1)), reads=[wv, p.hT], writes=[pp[pi]])
                kb.op("pe", lambda e: e.matmul(pp[3][:, 0:24], lhsT, wv[:, k, TM_W:TM_W + 24], start=(k == 0), stop=(k == KC - 1)), reads=[wv, p.hT], writes=[pp[3]])
            kb.op("act", lambda e: e.activation(out=o[:, 0:256], in_=pp[0][:, 0:256], func=AF.Copy), reads=[pp[0]], writes=[o])
            kb.op("act", lambda e: e.activation(out=g_[:], in_=pp[3][:, 0:24], func=AF.Sigmoid), reads=[pp[3]], writes=[g_])
            kb.op("dve", lambda e: e.tensor_copy(out=o[:, 256:768], in_=pp[1][:]), reads=[pp[1]], writes=[o])
            kb.op("dve", lambda e: e.tensor_copy(out=o[:, 768:1280], in_=pp[2][:]), reads=[pp[2]], writes=[o])
            kb.dma("pool", tm[tt * 128:(tt + 1) * 128, :], o[:], reads=[o], writes=[tm])
            kb.dma("pool", gates[tt * 128:(tt + 1) * 128, :], g_[:], reads=[g_], writes=[gates])
        kb.barrier()


def go_tiles(p, es, pfx, l, grp):
    w = {}
    w["ss"] = p.sb(es, pfx + "ss", [128, 2], F32)
    w["junk"] = p.sb(es, pfx + "junk", [128, 512], F32)
    w["yb"] = p.sb(es, pfx + "yb", [128, 512], BF16)
    w["pt"] = p.ps(es, pfx + "pt", [128, 1024], BF16)
    w["yt"] = [p.sb(es, pfx + "yt%d" % i, [128, 4, 128], BF16) for i in range(2)]
    w["gain"] = p.sb(es, pfx + "gain", [128, 512], F32)
    g = p.dram("mix_norm_g", IN_SHAPES["mix_norm_g"], F32, kind="ExternalInput")
    p.kb.dma("sp", w["gain"][:], g[l, grp, :].partition_broadcast(128), writes=[w["gain"]])
    w["cnt"] = 0
    return w


def emit_group_out(p, w, y_ap, y_tl, grp, tt):
    kb = p.kb
    ysT = p.dram("ysT", (D, S), BF16)
    ss, junk, yb, pt, gain = w["ss"], w["junk"], w["yb"], w["pt"], w["gain"]
    yt = w["yt"][w["cnt"] % 2]
    w["cnt"] += 1
    kb.op("act", lambda e: e.activation(out=junk[:], in_=y_ap, func=AF.Square, accum_out=ss[:, 0:1]), reads=[y_tl], writes=[junk, ss])
    kb.op("act", lambda e: e.activation(out=ss[:, 1:2], in_=ss[:, 0:1], func=AF.Sqrt, bias=1e-6, scale=1.0 / 512), reads=[ss], writes=[ss])
    kb.op("dve", lambda e: e.reciprocal(out=ss[:, 1:2], in_=ss[:, 1:2]), reads=[ss], writes=[ss])
    kb.op("dve", lambda e: e.scalar_tensor_tensor(out=yb[:], in0=y_ap, scalar=ss[:, 1:2], in1=gain[:], op0=ALU.mult, op1=ALU.mult), reads=[y_tl, ss, gain], writes=[yb])
    for c in range(4):
        kb.op("pe", lambda e: e.transpose(pt[:, 128 * c:128 * c + 128], yb[:, 128 * c:128 * c + 128], p.ident[:]), reads=[yb, p.ident], writes=[pt])
    kb.op("act", lambda e: e.activation(out=yt[:], in_=pt[:, 0:512].rearrange("p (c t) -> p c t", c=4), func=AF.Copy), reads=[pt], writes=[yt])
    kb.dma("act", ysT[grp * 512:(grp + 1) * 512, tt * 128:(tt + 1) * 128].rearrange("(c p) t -> p c t", p=128), yt[:], reads=[yt], writes=[ysT])


def load_const_bf16(p, es, name, shape):
    src = p.dram(name, shape, F32, kind="ExternalInput")
    t = p.sb(es, "k_" + name, list(shape), BF16)
    p.kb.dma("pool", t[:], src.h, writes=[t])
    return t


def gen_sb(p, l, heads=range(8), groups=range(4), npB=2, npO=2):
    kb = p.kb
    fm = p.dram("fm", (30 * 128, S), BF16)
    tm = p.dram("tm", (S, TM_W), BF16)
    with ExitStack() as es:
        maskS = load_const_bf16(p, es, "c_maskS", (128, 4, 512))
        negU = load_const_bf16(p, es, "c_negU", (128, 128))
        negO = load_const_bf16(p, es, "c_negOnes", (128, 128))
        V = p.sb(es, "sb_V", [128, NT, 512], BF16)
        kb.dma("sp", V[:], tm[:, TM_SBV:TM_SBV + 512].rearrange("(t p) f -> p t f", p=128), reads=[tm], writes=[V])
        yc = p.sb(es, "sb_yc", [128, NT, 512], F32)
        kb.op("pool", lambda e: e.memset(yc[:], 0.0), writes=[yc])
        qTs = [p.sb(es, "sb_q%d" % i, [128, S], BF16) for i in range(2)]
        kTs = [p.sb(es, "sb_k%d" % i, [128, S], BF16) for i in range(2)]
        psA = [p.ps(es, "sb_pA%d" % i, [128, 512]) for i in range(2)]
        psB = [p.ps(es, "sb_pB%d" % i, [128, 512]) for i in range(npB)]
        psO = [p.ps(es, "sb_pO%d" % i, [128, 512]) for i in range(npO)]
        e_t = [p.sb(es, "sb_e%d" % i, [128, 512], F32) for i in range(2)]
        sp_t = [p.sb(es, "sb_sp%d" % i, [128, 512], BF16) for i in range(2)]
        w_t = [p.sb(es, "sb_w%d" % i, [128, 512], BF16) for i in range(2)]
        Ssum = [p.sb(es, "sb_S%d" % i, [128, 512], BF16) for i in range(2)]
        sp3 = sp_t + [p.sb(es, "sb_sp2", [128, 512], BF16)]
        w3 = w_t + [p.sb(es, "sb_w2", [128, 512], BF16)]
        steps = []
        hg = 0
        loaded = None
        for h in heads:
            hp = h // 2
            base = 64 * (h % 2)
            qT, kT = qTs[hp % 2], kTs[hp % 2]
            need_load = loaded != hp
            loaded = hp
            for G in groups:
                Ss = Ssum[hg % 2]
                hg += 1
                for kb_ in range(4 * G + 3, -1, -1):
                    def mk(h=h, hp=hp, base=base, qT=qT, kT=kT, G=G, kb_=kb_, Ss=Ss, idx=len(steps), load=need_load):
                        i = kb_ - 4 * G
                        diag = i >= 0
                        first = kb_ == 4 * G + 3
                        pa, pb, po = psA[idx % 2], psB[idx % npB], psO[idx % npO]
                        et, st_, wt = e_t[idx % 2], sp3[idx % 3], w3[idx % 3]
                        qs = qT[base:base + 64, 512 * G:512 * G + 512]
                        ks = kT[base:base + 64, 128 * kb_:128 * kb_ + 128]

                        def s1():
                            if idx % 12 == 3:
                                p.bg()
                            if load:
                                kb.dma("sp", qT[:], fm[(FM_SBQ + hp) * 128:(FM_SBQ + hp + 1) * 128, :], reads=[fm], writes=[qT])
                                kb.dma("sp", kT[:], fm[(FM_SBK + hp) * 128:(FM_SBK + hp + 1) * 128, :], reads=[fm], writes=[kT])
                                kb.op("act", lambda e: e.mul(qT[:], qT[:], 0.125), reads=[qT], writes=[qT])
                            kb.op("pe", lambda e: e.matmul(pa[:], ks, qs, start=True, stop=True), reads=[kT, qT], writes=[pa])
                            kb.op("act", lambda e: e.activation(out=et[:], in_=pa[:], func=AF.Exp), reads=[pa], writes=[et])
                            kb.op("act", lambda e: e.activation(out=st_[:], in_=et[:], func=AF.Ln, bias=1.0), reads=[et], writes=[st_])
                            if diag:
                                kb.op("pool", lambda e: e.tensor_tensor(out=st_[:], in0=st_[:], in1=maskS[:, i, :], op=ALU.mult), reads=[st_, maskS], writes=[st_])

                        def s2():
                            kb.op("pe", lambda e: e.matmul(pb[:], ks, qs, start=True, stop=False), reads=[kT, qT], writes=[pb])
                            kb.op("pe", lambda e: e.matmul(pb[:], negU[:], st_[:], start=False, stop=first), reads=[negU, st_], writes=[pb])
                            if not first:
                                kb.op("pe", lambda e: e.matmul(pb[:], negO[:], Ss[:], start=False, stop=True), reads=[negO, Ss], writes=[pb])
                            if first:
                                kb.op("dve", lambda e: e.tensor_copy(out=Ss[:], in_=st_[:]), reads=[st_], writes=[Ss])
                            elif kb_ > 0:
                                kb.op("dve", lambda e: e.tensor_tensor(out=Ss[:], in0=Ss[:], in1=st_[:], op=ALU.add), reads=[Ss, st_], writes=[Ss])
                            kb.op("act", lambda e: e.activation(out=wt[:], in_=pb[:], func=AF.Exp), reads=[pb], writes=[wt])
                            if diag:
                                kb.op("pool", lambda e: e.tensor_tensor(out=wt[:], in0=wt[:], in1=maskS[:, i, :], op=ALU.mult), reads=[wt, maskS], writes=[wt])

                        def s3():
                            j0 = max(i, 0)
                            for j in range(j0, 4):
                                kb.op("pe", lambda e: e.matmul(po[:, 64 * j:64 * j + 64], wt[:, 128 * j:128 * j + 128], V[:, kb_, 64 * h:64 * h + 64], start=True, stop=True), reads=[wt, V], writes=[po])
                            ysl = yc[:, 4 * G + j0:4 * G + 4, 64 * h:64 * h + 64]
                            kb.op("dve", lambda e: e.tensor_tensor(out=ysl, in0=ysl, in1=po[:, 64 * j0:256].rearrange("p (j d) -> p j d", d=64), op=ALU.add), reads=[yc, po], writes=[yc])
                        return (s1, s2, s3)
                    steps.append(mk())
                    need_load = False
        yield (steps, 3)
        p.bg(len(p.bgq))
        dbg = p.d.get("dbg_yc")
        if dbg is not None:
            kb.dma("sp", dbg.h.rearrange("(t p) f -> p t f", p=128), yc[:], reads=[yc], writes=[dbg])
        gw = go_tiles(p, es, "sbgo", l, 2)
        for tt in range(NT):
            emit_group_out(p, gw, yc[:, tt, :], yc, 2, tt)
        kb.barrier()


DIL_CFG = ((1, 2048), (4, 512), (16, 128))


def phase_dil(p, l, branches=range(3), heads=range(8), max_pairs=999):
    kb = p.kb
    fm = p.dram("fm", (30 * 128, S), BF16)
    tm = p.dram("tm", (S, TM_W), BF16)
    dacc = [p.dram("dil_acc%d" % c, (S, 520), F32) for c in range(3)]
    with ExitStack() as es:
        maskD = load_const_bf16(p, es, "c_maskD", (128, 2, 128))
        qT = p.sb(es, "dl_q", [128, 4, S], BF16)
        kT = p.sb(es, "dl_k", [128, 4, S], BF16)
        kb.dma("sp", qT[:], fm[FM_DQ * 128:(FM_DQ + 4) * 128, :].rearrange("(c p) t -> p c t", p=128), reads=[fm], writes=[qT])
        kb.dma("sp", kT[:], fm[FM_DK * 128:(FM_DK + 4) * 128, :].rearrange("(c p) t -> p c t", p=128), reads=[fm], writes=[kT])
        kb.op("act", lambda e: e.mul(qT[:], qT[:], 0.125), reads=[qT], writes=[qT])
        Vp = [p.sb(es, "dl_v%d" % i, [128, 8, 65], BF16) for i in range(3)]
        for v in Vp:
            kb.op("pool", lambda e: e.memset(v[:, :, 64:65], 1.0), writes=[v])
        psS = [p.ps(es, "dl_pS%d" % i, [128, 512]) for i in range(3)]
        psO = [p.ps(es, "dl_pO%d" % i, [128, 512]) for i in range(2)]
        pt = [p.sb(es, "dl_pt%d" % i, [128, 256], BF16) for i in range(3)]
        Oall = [p.sb(es, "dl_O%d" % i, [128, 520], F32) for i in range(2)]
        vcnt = 0
        npair = 0
        steps = []
        for c in branches:
            dil, L = DIL_CFG[c]
            vsrc = tm[:, TM_DV:TM_DV + 512].rearrange("(l r) f -> r l f", r=dil)
            dst = dacc[c].h.rearrange("(l r) f -> r l f", r=dil)
            for r in range(dil):
                vt = {}
                for b in range(L // 128):
                    if npair >= max_pairs:
                        break
                    npair += 1
                    kbs = [x for x in (b - 1, b) if x >= 0]
                    loads = []
                    for x in kbs:
                        if x not in vt:
                            vt[x] = Vp[vcnt % 3]
                            vcnt += 1
                            loads.append((vt[x], vsrc[r, 128 * x:128 * x + 128, :].rearrange("p (h d) -> p h d", d=64)))
                    oa = Oall[npair % 2]
                    vts = [vt[x] for x in kbs]
                    hl = list(heads)
                    for h in hl:
                        def mk(c=c, dil=dil, r=r, b=b, kbs=kbs, vts=vts, oa=oa, h=h, idx=len(steps), loads=(loads if h == hl[0] else []), lasth=(h == hl[-1]), dst=dst):
                            hp, base = h // 2, 64 * (h % 2)
                            qv = qT[base:base + 64, hp, :].rearrange("p (l r) -> p r l", r=dil)[:, r, 128 * b:128 * b + 128]
                            kvw = kT[base:base + 64, hp, :].rearrange("p (l r) -> p r l", r=dil)
                            ps, pp = psS[idx % 3], pt[idx % 3]
                            n = len(kbs)
                            po = psO[(h // 4) % 2]
                            o0 = 65 * (h % 4)

                            def s1():
                                for (t, src) in loads:
                                    kb.dma("sp", t[:, :, 0:64], src, reads=[tm], writes=[t])
                                for ix, x in enumerate(kbs):
                                    kb.op("pe", lambda e: e.matmul(ps[:, 128 * ix:128 * ix + 128], kvw[:, r, 128 * x:128 * x + 128], qv, start=True, stop=True), reads=[kT, qT], writes=[ps])
                                kb.op("act", lambda e: e.activation(out=pp[:, 0:128 * n], in_=ps[:, 0:128 * n], func=AF.Exp), reads=[ps], writes=[pp])
                                m0 = 2 - n
                                kb.op("pool", lambda e: e.tensor_tensor(out=pp[:, 0:128 * n], in0=pp[:, 0:128 * n], in1=maskD[:, m0:2, :].rearrange("p a q -> p (a q)"), op=ALU.mult), reads=[pp, maskD], writes=[pp])

                            def s2():
                                for ix, x in enumerate(kbs):
                                    kb.op("pe", lambda e: e.matmul(po[:, o0:o0 + 65], pp[:, 128 * ix:128 * ix + 128], vts[ix][:, h, :], start=(ix == 0), stop=(ix == n - 1)), reads=[pp, vts[ix]], writes=[po])
                                if h % 4 == 3 or lasth:
                                    hh = h // 4
                                    kb.op("dve", lambda e: e.tensor_copy(out=oa[:, 260 * hh:260 * hh + 260], in_=po[:, 0:260]), reads=[po], writes=[oa])
                                if lasth:
                                    kb.dma("pool", dst[r, 128 * b:128 * b + 128, :], oa[:], reads=[oa], writes=[dacc[c]])
                            return (s1, s2)
                        steps.append(mk())
                    for x in list(vt):
                        if x < b:
                            del vt[x]
        pipeline(steps, 2)
        kb.barrier()
    with ExitStack() as es:
        gw = go_tiles(p, es, "dlgo", l, 3)
        acc = [[p.sb(es, "dl_a%d_%d" % (i, c), [128, 8, 65], F32) for c in range(3)] for i in range(2)]
        rd = p.sb(es, "dl_rd", [128, 8], F32)
        yd = [p.sb(es, "dl_y%d" % i, [128, 512], F32) for i in range(2)]
        for tt in range(NT):
            a = acc[tt % 2]
            y = yd[tt % 2]
            for c in range(3):
                kb.dma("sp", a[c][:], dacc[c][128 * tt:128 * tt + 128, :].rearrange("p (h d) -> p h d", d=65), reads=[dacc[c]], writes=[a[c]])
            kb.op("dve", lambda e: e.tensor_tensor(out=a[0][:], in0=a[0][:], in1=a[1][:], op=ALU.add), reads=[a[0], a[1]], writes=[a[0]])
            kb.op("dve", lambda e: e.tensor_tensor(out=a[0][:], in0=a[0][:], in1=a[2][:], op=ALU.add), reads=[a[0], a[2]], writes=[a[0]])
            kb.op("dve", lambda e: e.reciprocal(out=rd[:], in_=a[0][:, :, 64]), reads=[a[0]], writes=[rd])
            kb.op("dve", lambda e: e.tensor_tensor(out=y[:].rearrange("p (h d) -> p h d", d=64), in0=a[0][:, :, 0:64], in1=rd[:].unsqueeze(2).to_broadcast([128, 8, 64]), op=ALU.mult), reads=[a[0], rd], writes=[y])
            dbg = p.d.get("dbg_yd")
            if dbg is not None:
                kb.dma("sp", dbg[128 * tt:128 * tt + 128, :], y[:], reads=[y], writes=[dbg])
            emit_group_out(p, gw, y[:], y, 3, tt)
        kb.barrier()


def phase_nsa(p, l, heads=range(8), do_slc=True, do_win=True, after_setup=None):
    kb = p.kb
    fm = p.dram("fm", (30 * 128, S), BF16)
    tm = p.dram("tm", (S, TM_W), BF16)
    gates = p.dram("gates", (S, 24), F32)
    w1 = p.wb[("nsa_cmp_w1", l)]
    w2 = p.wb[("nsa_cmp_w2", l)]
    pe = p.dram("nsa_cmp_pe", IN_SHAPES["nsa_cmp_pe"], F32, kind="ExternalInput")
    with ExitStack() as es:
        swp = load_const_bf16(p, es, "c_swap", (128, 128))
        maskC = load_const_bf16(p, es, "c_maskC", (128, 4, 512))
        maskCn = load_const_bf16(p, es, "c_maskCn", (128, 4, 512))
        maskCmp = load_const_bf16(p, es, "c_maskCmp", (128, S))
        Ebig = load_const_bf16(p, es, "c_Ebig", (32, 16, 128))
        ccos = p.sb(es, "ns_cos", [128, 127], F32)
        csin = p.sb(es, "ns_sin", [128, 127], F32)
        kb.dma("sp", ccos[:], p.dram("c_cos_cmp", (128, 127), F32, kind="ExternalInput")[:, :], writes=[ccos])
        kb.dma("sp", csin[:], p.dram("c_sin_cmp", (128, 127), F32, kind="ExternalInput")[:, :], writes=[csin])
        selMul = p.sb(es, "ns_selMul", [128, NT, 32], F32)
        selAdd = p.sb(es, "ns_selAdd", [128, NT, 32], F32)
        kb.dma("sp", selMul[:], p.dram("c_selMul", (S, 32), F32, kind="ExternalInput").h.rearrange("(t p) n -> p t n", p=128), writes=[selMul])
        kb.dma("sp", selAdd[:], p.dram("c_selAdd", (S, 32), F32, kind="ExternalInput").h.rearrange("(t p) n -> p t n", p=128), writes=[selAdd])
        qT = p.sb(es, "ns_q", [128, 4, S], BF16)
        kb.dma("sp", qT[:], fm[FM_NQ * 128:(FM_NQ + 4) * 128, :].rearrange("(c p) t -> p c t", p=128), reads=[fm], writes=[qT])
        kb.op("act", lambda e: e.mul(qT[:], qT[:], 0.125), reads=[qT], writes=[qT])
        gt = p.sb(es, "ns_gt", [128, NT, 24], F32)
        kb.dma("sp", gt[:], gates.h.rearrange("(t p) f -> p t f", p=128), reads=[gates], writes=[gt])
        yb = p.sb(es, "ns_yb", [128, NT, 512], F32)
        kb.op("pool", lambda e: e.memset(yb[:], 0.0), writes=[yb])
        impacc = p.sb(es, "ns_imp", [128, NT, 2, 32], F32)
        kb.op("pool", lambda e: e.memset(impacc[:], 0.0), writes=[impacc])
        kdup = {}
        for nm, ch in (("slc", FM_KSLC), ("win", FM_KWIN)):
            for g in range(2):
                t = p.sb(es, "ns_k%s%d" % (nm, g), [128, S], BF16)
                for half in range(2):
                    kb.dma("sp", t[64 * half:64 * half + 64, :], fm[ch * 128 + 64 * g:ch * 128 + 64 * g + 64, :], reads=[fm], writes=[t])
                kdup[(nm, g)] = t
        Vs = {}
        for nm, off in (("slc", TM_VSLC), ("win", TM_VWIN)):
            t = p.sb(es, "ns_v" + nm, [128, NT, 2, 65], BF16)
            kb.op("pool", lambda e: e.memset(t[:, :, :, 64:65], 1.0), writes=[t])
            for g in range(2):
                kb.dma("sp", t[:, :, g, 0:64], tm[:, off + 64 * g:off + 64 * g + 64].rearrange("(t p) d -> p t d", p=128), reads=[tm], writes=[t])
            Vs[nm] = t
        pS = [p.ps(es, "ns_pS%d" % i, [128, 512]) for i in range(2)]
        pO = [p.ps(es, "ns_pO%d" % i, [128, 512]) for i in range(2)]
        pX = [p.ps(es, "ns_pX%d" % i, [128, 512]) for i in range(2)]
        pXb = p.ps(es, "ns_pXb", [128, 1024], BF16)
        kcT = [p.sb(es, "ns_kc%d" % g, [128, 128], BF16) for g in range(2)]
        Rg = [p.sb(es, "ns_R%d" % g, [128, 97], BF16) for g in range(2)]
        ovl = load_const_bf16(p, es, "c_overlap", (128, 32))
        with ExitStack() as es2:
            tT = p.sb(es2, "ns_tT", [128, S], BF16)
            W1 = p.sb(es2, "ns_W1", [128, 32, 128], BF16)
            W2d = p.sb(es2, "ns_W2d", [128, 2, 64], BF16)
            pet = p.sb(es2, "ns_pe", [128, 32], F32)
            xl = [p.sb(es2, "ns_xl%d" % i, [128, 128], BF16) for i in range(4)]
            h1 = p.sb(es2, "ns_h1", [128, 128], BF16)
            xb = p.sb(es2, "ns_xb", [128, 128], BF16)
            a1 = p.sb(es2, "ns_a1", [128, 128], F32)
            a2 = p.sb(es2, "ns_a2", [128, 128], F32)
            xc = 0
            for j in range(2):
                kb.dma("sp", tT[:], fm[(FM_KCMP + j) * 128:(FM_KCMP + j + 1) * 128, :], reads=[fm], writes=[tT])
                for half in range(2):
                    kb.dma("sp", W1[64 * half:64 * half + 64, :, :], w1[j].rearrange("(l d) n -> d l n", d=64), reads=[w1], writes=[W1])
                    kb.dma("sp", pet[64 * half:64 * half + 64, :], pe[l, j].rearrange("l d -> d l"), writes=[pet], allow_slow_non_contiguous=True)
                    kb.dma("sp", W2d[:, half, :], w2[j], reads=[w2], writes=[W2d])
                for g in range(2):
                    base = 64 * g
                    ph = pX[0]
                    for ll in range(32):
                        x_ = xl[xc % 4]
                        xc += 1
                        src = tT[base:base + 64, :].rearrange("p (i s) -> p s i", s=16)
                        sh, lo = ll // 16, ll % 16
                        kb.op("dve", lambda e: e.tensor_scalar(out=x_[base:base + 64, 0:127], in0=src[:, lo, sh:sh + 127], scalar1=pet[base:base + 64, ll:ll + 1], scalar2=None, op0=ALU.add), reads=[tT, pet], writes=[x_])
                        kb.op("pe", lambda e: e.matmul(ph[:, 0:127], W1[base:base + 64, ll, :], x_[base:base + 64, 0:127], start=(ll == 0), stop=(ll == 31)), reads=[W1, x_], writes=[ph])
                    kb.op("act", lambda e: e.activation(out=h1[:, 0:127], in_=ph[:, 0:127], func=AF.Gelu_apprx_tanh), reads=[ph], writes=[h1])
                    if j == 0:
                        pk = pX[1]
                        kb.op("pe", lambda e: e.matmul(pk[:, 0:127], W2d[:].rearrange("p a d -> p (a d)"), h1[:, 0:127], start=True, stop=True), reads=[W2d, h1], writes=[pk])
                        kb.op("act", lambda e: e.activation(out=xb[:, 0:127], in_=pk[:, 0:127], func=AF.Copy), reads=[pk], writes=[xb])
                        kb.op("dve", lambda e: e.tensor_tensor(out=a1[:, 0:127], in0=pk[:, 0:127], in1=ccos[:], op=ALU.mult), reads=[pk, ccos], writes=[a1])
                        pk2 = pO[0]
                        kb.op("pe", lambda e: e.matmul(pk2[:, 0:127], swp[:], xb[:, 0:127], start=True, stop=True), reads=[swp, xb], writes=[pk2])
                        kb.op("dve", lambda e: e.tensor_tensor(out=a2[:, 0:127], in0=pk2[:, 0:127], in1=csin[:], op=ALU.mult), reads=[pk2, csin], writes=[a2])
                        kb.op("dve", lambda e: e.tensor_tensor(out=kcT[g][:, 0:127], in0=a1[:, 0:127], in1=a2[:, 0:127], op=ALU.add), reads=[a1, a2], writes=[kcT[g]])
                    else:
                        pv = pX[1]
                        kb.op("pe", lambda e: e.matmul(pv[0:127, 0:64], h1[:, 0:127], W2d[:, 0, :], start=True, stop=True), reads=[W2d, h1], writes=[pv])
                        kb.op("pool", lambda e: e.memset(Rg[g][:, 64:65], 1.0), writes=[Rg[g]])
                        kb.op("act", lambda e: e.activation(out=Rg[g][0:127, 0:64], in_=pv[0:127, 0:64], func=AF.Copy), reads=[pv], writes=[Rg[g]])
                        kb.op("dve", lambda e: e.tensor_copy(out=Rg[g][:, 65:97], in_=ovl[:]), reads=[ovl], writes=[Rg[g]])
        pc = [p.sb(es, "ns_pc%d" % i, [128, 512], BF16) for i in range(2)]
        rd = [p.sb(es, "ns_rd%d" % i, [128, 4], F32) for i in range(2)]
        steps = []
        for h in heads:
            for G in range(4):
                def mk(h=h, G=G, idx=len(steps)):
                    g, hp, base = h // 4, h // 2, 64 * (h % 2)
                    ps, po, pc_, rd_ = pS[idx % 2], pO[idx % 2], pc[idx % 2], rd[idx % 2]

                    def s1():
                        kb.op("pe", lambda e: e.matmul(ps[0:127, :], kcT[g][base:base + 64, 0:127], qT[base:base + 64, hp, 512 * G:512 * G + 512], start=True, stop=True), reads=[kcT[g], qT], writes=[ps])
                        kb.op("act", lambda e: e.activation(out=pc_[0:127, :], in_=ps[0:127, :], func=AF.Exp), reads=[ps], writes=[pc_])
                        kb.op("pool", lambda e: e.tensor_tensor(out=pc_[0:127, :], in0=pc_[0:127, :], in1=maskCmp[0:127, 512 * G:512 * G + 512], op=ALU.mult), reads=[pc_, maskCmp], writes=[pc_])

                    def s2():
                        for j in range(4):
                            kb.op("pe", lambda e: e.matmul(po[:, 97 * j:97 * j + 97], pc_[0:127, 128 * j:128 * j + 128], Rg[g][0:127, :], start=True, stop=True), reads=[pc_, Rg[g]], writes=[po])
                        pov = po[:, 0:388].rearrange("p (j f) -> p j f", f=97)
                        kb.op("dve", lambda e: e.tensor_scalar(out=rd_[:], in0=pov[:, :, 64], scalar1=1e-30, scalar2=None, op0=ALU.max), reads=[po], writes=[rd_])
                        kb.op("dve", lambda e: e.reciprocal(out=rd_[:], in_=rd_[:]), reads=[rd_], writes=[rd_])
                        for j in range(4):
                            tt = 4 * G + j
                            kb.op("dve", lambda e: e.tensor_scalar(out=yb[:, tt, 64 * h:64 * h + 64], in0=po[:, 97 * j:97 * j + 64], scalar1=rd_[:, j:j + 1], scalar2=gt[:, tt, 3 * h:3 * h + 1], op0=ALU.mult, op1=ALU.mult), reads=[po, rd_, gt], writes=[yb])
                            kb.op("dve", lambda e: e.scalar_tensor_tensor(out=impacc[:, tt, g, :], in0=po[:, 97 * j + 65:97 * j + 97], scalar=rd_[:, j:j + 1], in1=impacc[:, tt, g, :], op0=ALU.mult, op1=ALU.add), reads=[po, rd_, impacc], writes=[impacc])
                    return (s1, s2)
                steps.append(mk())
        pipeline(steps, 2)
        dbg = p.d.get("dbg_imp")
        if dbg is not None:
            kb.dma("sp", dbg.h.rearrange("(t p) g n -> p t g n", p=128), impacc[:], reads=[impacc], writes=[dbg])
        selT = [p.sb(es, "ns_selT%d" % g, [32, S], BF16) for g in range(2)]
        m8 = p.sb(es, "ns_m8", [128, 8], F32)
        selb = p.sb(es, "ns_selb", [128, 4, 32], BF16)
        for g in range(2):
            kb.op("dve", lambda e: e.tensor_tensor(out=impacc[:, :, g, :], in0=impacc[:, :, g, :], in1=selMul[:], op=ALU.mult), reads=[impacc, selMul], writes=[impacc])
            kb.op("dve", lambda e: e.tensor_tensor(out=impacc[:, :, g, :], in0=impacc[:, :, g, :], in1=selAdd[:], op=ALU.add), reads=[impacc, selAdd], writes=[impacc])
            for G in range(4):
                for j in range(4):
                    tt = 4 * G + j
                    kb.op("dve", lambda e: e.max(out=m8[:], in_=impacc[:, tt, g, :]), reads=[impacc], writes=[m8])
                    kb.op("dve", lambda e: e.tensor_scalar(out=selb[:, j, :], in0=impacc[:, tt, g, :], scalar1=m8[:, 7:8], scalar2=1.0, op0=ALU.is_ge, op1=ALU.subtract), reads=[impacc, m8], writes=[selb])
                for j in range(4):
                    kb.op("pe", lambda e: e.transpose(pXb[0:32, 128 * j:128 * j + 128], selb[:, j, :], p.ident[:]), reads=[selb, p.ident], writes=[pXb])
                kb.op("act", lambda e: e.activation(out=selT[g][:, 512 * G:512 * G + 512], in_=pXb[0:32, 0:512], func=AF.Copy), reads=[pXb], writes=[selT[g]])
        Pt = [p.sb(es, "ns_P%d" % i, [128, 512], BF16) for i in range(2)]
        Oacc = [p.sb(es, "ns_Oa%d" % i, [128, 4, 65], F32) for i in range(2)]
        rdg = [p.sb(es, "ns_rdg%d" % i, [128, 4], F32) for i in range(2)]
        Pt = Pt + [p.sb(es, "ns_P2", [128, 512], BF16)]
        oc = 0
        branches = ([("slc", 1)] if do_slc else []) + ([("win", 2)] if do_win else [])
        steps = []
        for h in heads:
            for (nm, gi) in branches:
                for G in range(4):
                    oa, rg_ = Oacc[oc % 2], rdg[oc % 2]
                    oc += 1
                    kbs = list(range(0, 4 * G + 4) if nm == "slc" else range(max(0, 4 * G - 4), 4 * G + 4))
                    for kb_ in kbs:
                        def mk(h=h, nm=nm, gi=gi, G=G, kb_=kb_, oa=oa, rg_=rg_, idx=len(steps), firstk=(kb_ == kbs[0]), lastk=(kb_ == kbs[-1])):
                            g, hp, base = h // 4, h // 2, 64 * (h % 2)
                            kd, V = kdup[(nm, g)], Vs[nm]
                            qs = qT[base:base + 64, hp, 512 * G:512 * G + 512]
                            i = kb_ - 4 * G
                            ps, po, P_ = pS[idx % 2], pO[idx % 2], Pt[idx % 3]
                            ks = kd[base:base + 64, 128 * kb_:128 * kb_ + 128]
                            if i >= 0:
                                js = range(i, 4)
                            elif nm == "win":
                                js = range(0, i + 5)
                            else:
                                js = range(4)

                            def s1():
                                if idx % 28 == 5:
                                    p.bg()
                                if firstk:
                                    kb.op("pool", lambda e: e.memset(oa[:], 0.0), writes=[oa])
                                if nm == "slc":
                                    kb.op("pe", lambda e: e.matmul(ps[:], ks, qs, start=True, stop=False), reads=[kd, qT], writes=[ps])
                                    kb.op("pe", lambda e: e.matmul(ps[:], Ebig[:, kb_, :], selT[g][:, 512 * G:512 * G + 512], start=False, stop=True), reads=[Ebig, selT[g]], writes=[ps])
                                else:
                                    kb.op("pe", lambda e: e.matmul(ps[:], ks, qs, start=True, stop=True), reads=[kd, qT], writes=[ps])
                                kb.op("act", lambda e: e.activation(out=P_[:], in_=ps[:], func=AF.Exp), reads=[ps], writes=[P_])
                                if i >= 0:
                                    kb.op("pool", lambda e: e.tensor_tensor(out=P_[:], in0=P_[:], in1=maskC[:, i, :], op=ALU.mult), reads=[P_, maskC], writes=[P_])
                                elif nm == "win":
                                    kb.op("pool", lambda e: e.tensor_tensor(out=P_[:], in0=P_[:], in1=maskCn[:, i + 4, :], op=ALU.mult), reads=[P_, maskCn], writes=[P_])

                            def s2():
                                for j in js:
                                    kb.op("pe", lambda e: e.matmul(po[:, 65 * j:65 * j + 65], P_[:, 128 * j:128 * j + 128], V[:, kb_, g, :], start=True, stop=True), reads=[P_, V], writes=[po])
                                j0, j1 = js[0], js[-1] + 1
                                kb.op("dve", lambda e: e.tensor_tensor(out=oa[:, j0:j1, :], in0=oa[:, j0:j1, :], in1=po[:, 65 * j0:65 * j1].rearrange("p (j f) -> p j f", f=65), op=ALU.add), reads=[oa, po], writes=[oa])
                                if lastk:
                                    kb.op("dve", lambda e: e.reciprocal(out=rg_[:], in_=oa[:, :, 64]), reads=[oa], writes=[rg_])
                                    kb.op("dve", lambda e: e.tensor_tensor(out=rg_[:], in0=rg_[:], in1=gt[:, 4 * G:4 * G + 4, 3 * h + gi], op=ALU.mult), reads=[rg_, gt], writes=[rg_])
                                    for j in range(4):
                                        tt = 4 * G + j
                                        kb.op("dve", lambda e: e.scalar_tensor_tensor(out=yb[:, tt, 64 * h:64 * h + 64], in0=oa[:, j, 0:64], scalar=rg_[:, j:j + 1], in1=yb[:, tt, 64 * h:64 * h + 64], op0=ALU.mult, op1=ALU.add), reads=[oa, rg_, yb], writes=[yb])
                            return (s1, s2)
                        steps.append(mk())
        pipeline(steps, 2)
        dbg = p.d.get("dbg_yb")
        if dbg is not None:
            kb.dma("sp", dbg.h.rearrange("(t p) f -> p t f", p=128), yb[:], reads=[yb], writes=[dbg])
        gw = go_tiles(p, es, "nsgo", l, 1)
        for tt in range(NT):
            emit_group_out(p, gw, yb[:, tt, :], yb, 1, tt)
        kb.barrier()


def gen_s5(p, l, chunks=range(16), HS=S // 2, nps=2):
    kb = p.kb
    PI = math.pi
    fm = p.dram("fm", (30 * 128, S), BF16)
    ysT = p.dram("ysT", (D, S), BF16)
    inp = lambda n: p.dram(n, IN_SHAPES[n], F32, kind="ExternalInput")
    lam_re, lam_im, log_dt = inp("s5_lambda_re"), inp("s5_lambda_im"), inp("s5_log_dt")
    b_re, b_im, c_re, c_im = inp("s5_b_re"), inp("s5_b_im"), inp("s5_c_re"), inp("s5_c_im")
    d_skip, b_glu, gain = inp("s5_d"), inp("s5_b_glu"), inp("mix_norm_g")
    wglu = p.wb[("s5_w_glu", l)]
    with ExitStack() as es:
        uT = p.sb(es, "s5_uT", [128, 4, S], BF16)
        kb.dma("sp", uT[:], fm[FM_U * 128:(FM_U + 4) * 128, :].rearrange("(c p) t -> p c t", p=128), reads=[fm], writes=[uT])
        yT = p.sb(es, "s5_yT", [128, 4, S], F32)
        dT = p.sb(es, "s5_dT", [128, 4], F32)
        kb.dma("sp", dT[:], d_skip[l].rearrange("(q j) c -> (j c) q", j=8), writes=[dT], allow_slow_non_contiguous=True)
        BT = [p.sb(es, "s5_BT%d" % i, [128, 4, 2, 64], BF16) for i in range(2)]
        BTz = [p.sb(es, "s5_BTz%d" % i, [128, 4, 2, 64], BF16) for i in range(2)]
        CT = [p.sb(es, "s5_CT%d" % i, [128, 4, 128], BF16) for i in range(2)]
        CTz = [p.sb(es, "s5_CTz%d" % i, [128, 4, 64], BF16) for i in range(2)]
        for q in range(4):
            kb.op("dve", lambda e: e.tensor_scalar(out=yT[:, q, :], in0=uT[:, q, :], scalar1=dT[:, q:q + 1], scalar2=None, op0=ALU.mult), reads=[uT, dT], writes=[yT])
        r_p = p.sb(es, "s5_rp", [128, 16], F32)
        th_p = p.sb(es, "s5_thp", [128, 16], F32)
        with ExitStack() as es2:
            sb2 = lambda n, shp, dt=F32: p.sb(es2, "s5p_" + n, shp, dt)
            pX = p.ps(es2, "s5p_pX", [128, 1024], BF16)
            mask2 = sb2("mask2", [128, 2])
            kb.dma("sp", mask2[:], p.dram("c_mask2", (128, 2), F32, kind="ExternalInput")[:, :], writes=[mask2])
            mask2z = sb2("mask2z", [128, 2])
            kb.dma("sp", mask2z[:], p.dram("c_mask2z", (128, 2), F32, kind="ExternalInput")[:, :], writes=[mask2z])
            mask3 = [sb2("mask3%d" % i, [128, 128]) for i in range(2)]
            kb.dma("sp", mask3[0][:], p.dram("c_mask3", (128, 128), F32, kind="ExternalInput")[:, :], writes=[mask3[0]])
            kb.dma("sp", mask3[1][:], p.dram("c_mask3n", (128, 128), F32, kind="ExternalInput")[:, :], writes=[mask3[1]])

            def prep(P_, G_, lr_src, li_src, dt_loads, pfx):
                t = {k: sb2(pfx + k, [P_, G_]) for k in ("lr", "li", "dt", "mag", "ang", "tmp")}
                kb.dma("sp", t["lr"][:], lr_src, writes=[t["lr"]], allow_slow_non_contiguous=True)
                kb.dma("sp", t["li"][:], li_src, writes=[t["li"]], allow_slow_non_contiguous=True)
                for (dst_sl, src) in dt_loads:
                    kb.dma("sp", t["dt"][dst_sl, :], src, writes=[t["dt"]])
                kb.op("dve", lambda e: e.tensor_scalar(out=t["lr"][:], in0=t["lr"][:], scalar1=-1e-4, scalar2=None, op0=ALU.min), reads=[t["lr"]], writes=[t["lr"]])
                kb.op("act", lambda e: e.activation(out=t["dt"][:], in_=t["dt"][:], func=AF.Exp), reads=[t["dt"]], writes=[t["dt"]])
                kb.op("dve", lambda e: e.tensor_tensor(out=t["tmp"][:], in0=t["lr"][:], in1=t["dt"][:], op=ALU.mult), reads=[t["lr"], t["dt"]], writes=[t["tmp"]])
                kb.op("act", lambda e: e.activation(out=t["mag"][:], in_=t["tmp"][:], func=AF.Exp), reads=[t["tmp"]], writes=[t["mag"]])
                kb.op("dve", lambda e: e.tensor_tensor(out=t["ang"][:], in0=t["li"][:], in1=t["dt"][:], op=ALU.mult), reads=[t["li"], t["dt"]], writes=[t["ang"]])
                return t

            tp = prep(128, 16,
                      lam_re[l].rearrange("g s -> (g s)").rearrange("(c q) -> q c", q=128),
                      lam_im[l].rearrange("g s -> (g s)").rearrange("(c q) -> q c", q=128),
                      [(slice(64 * two, 64 * two + 64), log_dt[l].rearrange("(c two) -> two c", two=2)[two].partition_broadcast(64)) for two in range(2)], "p")
            kb.op("dve", lambda e: e.tensor_copy(out=r_p[:], in_=tp["mag"][:]), reads=[tp["mag"]], writes=[r_p])
            kb.op("dve", lambda e: e.tensor_copy(out=th_p[:], in_=tp["ang"][:]), reads=[tp["ang"]], writes=[th_p])
            ts = prep(64, 32, lam_re[l].rearrange("g s -> s g"), lam_im[l].rearrange("g s -> s g"),
                      [(slice(0, 64), log_dt[l].partition_broadcast(64))], "s")
            sn, cs = sb2("sn", [64, 32]), sb2("cs", [64, 32])
            ki0 = sb2("ki0", [64, 32], mybir.dt.int32)
            kb.op("dve", lambda e: e.tensor_scalar(out=ki0[:], in0=ts["ang"][:], scalar1=1.0 / (2 * PI), scalar2=None, op0=ALU.mult), reads=[ts["ang"]], writes=[ki0])
            kb.op("dve", lambda e: e.tensor_copy(out=sn[:], in_=ki0[:]), reads=[ki0], writes=[sn])
            kb.op("dve", lambda e: e.scalar_tensor_tensor(out=sn[:], in0=sn[:], scalar=-2 * PI, in1=ts["ang"][:], op0=ALU.mult, op1=ALU.add), reads=[sn, ts["ang"]], writes=[sn])
            kb.op("dve", lambda e: e.tensor_scalar(out=sn[:], in0=sn[:], scalar1=-PI, scalar2=PI, op0=ALU.max, op1=ALU.min), reads=[sn], writes=[sn])
            kb.op("act", lambda e: e.activation(out=cs[:], in_=sn[:], func=AF.Abs), reads=[sn], writes=[cs])
            kb.op("act", lambda e: e.activation(out=sn[:], in_=sn[:], func=AF.Sin), reads=[sn], writes=[sn])
            kb.op("act", lambda e: e.activation(out=cs[:], in_=cs[:], func=AF.Sin, scale=-1.0, bias=0.5 * PI), reads=[cs], writes=[cs])
            are, aim, den, zre, zim, t1, t2 = [sb2(n, [64, 32]) for n in ("are", "aim", "den", "zre", "zim", "t1", "t2")]
            tt_ = lambda o, a, b, op: kb.op("dve", lambda e: e.tensor_tensor(out=o[:], in0=a[:], in1=b[:], op=op), reads=[a, b], writes=[o])
            tt_(are, ts["mag"], cs, ALU.mult)
            tt_(aim, ts["mag"], sn, ALU.mult)
            kb.op("dve", lambda e: e.tensor_scalar(out=are[:], in0=are[:], scalar1=-1.0, scalar2=None, op0=ALU.add), reads=[are], writes=[are])
            tt_(t1, ts["lr"], ts["lr"], ALU.mult)
            tt_(t2, ts["li"], ts["li"], ALU.mult)
            tt_(den, t1, t2, ALU.add)
            kb.op("dve", lambda e: e.reciprocal(out=den[:], in_=den[:]), reads=[den], writes=[den])
            tt_(t1, are, ts["lr"], ALU.mult)
            tt_(t2, aim, ts["li"], ALU.mult)
            tt_(zre, t1, t2, ALU.add)
            tt_(zre, zre, den, ALU.mult)
            tt_(t1, aim, ts["lr"], ALU.mult)
            tt_(t2, are, ts["li"], ALU.mult)
            tt_(zim, t1, t2, ALU.subtract)
            tt_(zim, zim, den, ALU.mult)
            Bs = [sb2("Bs%d" % i, [64, 32, 16]) for i in range(2)]
            kb.dma("sp", Bs[0][:], b_re[l].rearrange("g s c -> s g c"), writes=[Bs[0]])
            kb.dma("sp", Bs[1][:], b_im[l].rearrange("g s c -> s g c"), writes=[Bs[1]])
            m1, m2 = sb2("m1", [64, 32, 16]), sb2("m2", [64, 32, 16])
            bb = [sb2("bb%d" % i, [64, 32, 16], BF16) for i in range(2)]
            zb = lambda z: z[:].unsqueeze(2).to_broadcast([64, 32, 16])
            kb.op("dve", lambda e: e.tensor_tensor(out=m1[:], in0=Bs[0][:], in1=zb(zre), op=ALU.mult), reads=[Bs[0], zre], writes=[m1])
            kb.op("dve", lambda e: e.tensor_tensor(out=m2[:], in0=Bs[1][:], in1=zb(zim), op=ALU.mult), reads=[Bs[1], zim], writes=[m2])
            kb.op("dve", lambda e: e.tensor_tensor(out=bb[0][:], in0=m1[:], in1=m2[:], op=ALU.subtract), reads=[m1, m2], writes=[bb[0]])
            kb.op("dve", lambda e: e.tensor_tensor(out=m1[:], in0=Bs[1][:], in1=zb(zre), op=ALU.mult), reads=[Bs[1], zre], writes=[m1])
            kb.op("dve", lambda e: e.tensor_tensor(out=m2[:], in0=Bs[0][:], in1=zb(zim), op=ALU.mult), reads=[Bs[0], zim], writes=[m2])
            kb.op("dve", lambda e: e.tensor_tensor(out=bb[1][:], in0=m1[:], in1=m2[:], op=ALU.add), reads=[m1, m2], writes=[bb[1]])
            for ri in range(2):
                for q in range(4):
                    kb.op("pe", lambda e: e.transpose(pX[:, 0:64], bb[ri][:, 8 * q:8 * q + 8, :].rearrange("s j c -> s (j c)"), p.ident[0:64, 0:64]), reads=[bb[ri], p.ident], writes=[pX])
                    for two in range(2):
                        kb.op("dve", lambda e: e.tensor_scalar(out=BT[ri][:, q, two, :], in0=pX[:, 0:64], scalar1=mask2[:, two:two + 1], scalar2=None, op0=ALU.mult), reads=[pX, mask2], writes=[BT[ri]])
                        kb.op("dve", lambda e: e.tensor_scalar(out=BTz[ri][:, q, two, :], in0=pX[:, 0:64], scalar1=mask2z[:, two:two + 1], scalar2=None, op0=ALU.mult), reads=[pX, mask2z], writes=[BTz[ri]])
            Cn = sb2("Cn", [128, 4, 64])
            Cd = sb2("Cd", [128, 4, 2, 64], BF16)
            for ri, csrc in enumerate((c_re, c_im)):
                kb.dma("sp", Cn[:], csrc[l].rearrange("(q j) c s -> (j c) q s", j=8), writes=[Cn])
                for two in range(2):
                    kb.op("dve", lambda e: e.tensor_copy(out=Cd[:, :, two, :], in_=Cn[:]), reads=[Cn], writes=[Cd])
                for q in range(4):
                    kb.op("pe", lambda e: e.transpose(pX[:, 0:128], Cd[:, q, :, :].rearrange("p a s -> p (a s)"), p.ident[:]), reads=[Cd, p.ident], writes=[pX])
                    kb.op("dve", lambda e: e.tensor_tensor(out=CT[ri][:, q, :], in0=pX[:, 0:128], in1=mask3[ri][:], op=ALU.mult), reads=[pX, mask3[ri]], writes=[CT[ri]])
                    kb.op("pool", lambda e: e.memset(CTz[ri][:, q, 0:32], 0.0), writes=[CTz[ri]])
                    kb.op("dve", lambda e: e.tensor_copy(out=CTz[ri][:, q, 32:64], in_=CT[ri][:, q, 96:128]), reads=[CT[ri]], writes=[CTz[ri]])
            kb.barrier()
        with ExitStack() as es2:
            sb2 = lambda n, shp, dt=F32: p.sb(es2, "s5m_" + n, shp, dt)
            iota = sb2("iota", [128, S])
            kb.dma("sp", iota[:], p.dram("c_iota", (128, S), F32, kind="ExternalInput")[:, :], writes=[iota])
            nparts = S // HS
            wri, wii, wi, wr = [sb2(n, [128, HS]) for n in ("wri", "wii", "wi", "wr")]
            phs = [sb2("ph%d" % i, [128, HS]) for i in range(2)]
            sns = [sb2("sn%d" % i, [128, HS]) for i in range(2)]
            css = [sb2("cs%d" % i, [128, HS]) for i in range(2)]
            xre, xim = sb2("xre", [128, HS], BF16), sb2("xim", [128, HS], BF16)
            ki = sb2("ki", [128, HS], mybir.dt.int32)
            tmp = [sb2("t%d" % i, [128, 512]) for i in range(4)]
            car = sb2("car", [128, 2])
            psBr = [p.ps(es2, "s5_pBr%d" % i, [128, 512]) for i in range(nps)]
            psBi = [p.ps(es2, "s5_pBi%d" % i, [128, 512]) for i in range(nps)]
            psY = [p.ps(es2, "s5_pY%d" % i, [128, 512]) for i in range(nps)]
            steps = []
            for ch in chunks:
                for half in range(nparts):
                    def mk(ch=ch, half=half, idx=len(steps)):
                        q, rb = ch // 4, 32 * (ch % 4)
                        rows = slice(rb, rb + 32) if rb < 96 else slice(64, 128)
                        t0 = half * HS
                        ph, sn, cs = phs[idx % 2], sns[idx % 2], css[idx % 2]

                        def s1():
                            if idx % (nparts) == 1:
                                p.bg()
                            kb.op("dve", lambda e: e.tensor_scalar(out=ph[:], in0=iota[:, t0:t0 + HS], scalar1=th_p[:, ch:ch + 1], scalar2=None, op0=ALU.mult), reads=[iota, th_p], writes=[ph])
                            kb.op("dve", lambda e: e.tensor_scalar(out=ki[:], in0=ph[:], scalar1=1.0 / (2 * PI), scalar2=None, op0=ALU.mult), reads=[ph], writes=[ki])
                            kb.op("dve", lambda e: e.tensor_copy(out=sn[:], in_=ki[:]), reads=[ki], writes=[sn])
                            kb.op("dve", lambda e: e.scalar_tensor_tensor(out=sn[:], in0=sn[:], scalar=-2 * PI, in1=ph[:], op0=ALU.mult, op1=ALU.add), reads=[sn, ph], writes=[sn])
                            kb.op("dve", lambda e: e.tensor_scalar(out=sn[:], in0=sn[:], scalar1=-PI, scalar2=PI, op0=ALU.max, op1=ALU.min), reads=[sn], writes=[sn])
                            kb.op("act", lambda e: e.activation(out=cs[:], in_=sn[:], func=AF.Abs), reads=[sn], writes=[cs])
                            kb.op("act", lambda e: e.activation(out=sn[:], in_=sn[:], func=AF.Sin), reads=[sn], writes=[sn])
                            kb.op("act", lambda e: e.activation(out=cs[:], in_=cs[:], func=AF.Sin, scale=-1.0, bias=0.5 * PI), reads=[cs], writes=[cs])

                        def s2():
                            for tg in range(HS // 512):
                                tk = slice(t0 + 512 * tg, t0 + 512 * tg + 512)
                                lk = slice(512 * tg, 512 * tg + 512)
                                pr, pi_ = psBr[tg % nps], psBi[tg % nps]
                                Bm = BTz if rb == 96 else BT
                                kb.op("pe", lambda e: e.matmul(pr[:], Bm[0][rows, q, :, :].rearrange("p a s -> p (a s)"), uT[rows, q, tk], start=True, stop=True), reads=[Bm[0], uT], writes=[pr])
                                kb.op("pe", lambda e: e.matmul(pi_[:], Bm[1][rows, q, :, :].rearrange("p a s -> p (a s)"), uT[rows, q, tk], start=True, stop=True), reads=[Bm[1], uT], writes=[pi_])
                                a, b, c, d = tmp
                                kb.op("dve", lambda e: e.tensor_tensor(out=a[:], in0=pr[:], in1=cs[:, lk], op=ALU.mult), reads=[pr, cs], writes=[a])
                                kb.op("dve", lambda e: e.tensor_tensor(out=b[:], in0=pi_[:], in1=sn[:, lk], op=ALU.mult), reads=[pi_, sn], writes=[b])
                                kb.op("pool", lambda e: e.tensor_tensor(out=wri[:, lk], in0=a[:], in1=b[:], op=ALU.add), reads=[a, b], writes=[wri])
                                kb.op("dve", lambda e: e.tensor_tensor(out=c[:], in0=pi_[:], in1=cs[:, lk], op=ALU.mult), reads=[pi_, cs], writes=[c])
                                kb.op("dve", lambda e: e.tensor_tensor(out=d[:], in0=pr[:], in1=sn[:, lk], op=ALU.mult), reads=[pr, sn], writes=[d])
                                kb.op("pool", lambda e: e.tensor_tensor(out=wii[:, lk], in0=c[:], in1=d[:], op=ALU.subtract), reads=[c, d], writes=[wii])
                            rbc = r_p[:, ch:ch + 1].to_broadcast([128, HS])
                            ini_r = 0.0 if half == 0 else car[:, 0:1]
                            ini_i = 0.0 if half == 0 else car[:, 1:2]
                            kb.op("dve", lambda e: e.tensor_tensor_scan(out=wr[:], data0=rbc, data1=wri[:], initial=ini_r, op0=ALU.mult, op1=ALU.add), reads=[r_p, wri, car], writes=[wr])
                            kb.op("dve", lambda e: e.tensor_tensor_scan(out=wi[:], data0=rbc, data1=wii[:], initial=ini_i, op0=ALU.mult, op1=ALU.add), reads=[r_p, wii, car], writes=[wi])
                            if half < nparts - 1:
                                kb.op("act", lambda e: e.activation(out=car[:, 0:1], in_=wr[:, HS - 1:HS], func=AF.Copy), reads=[wr], writes=[car])
                                kb.op("act", lambda e: e.activation(out=car[:, 1:2], in_=wi[:, HS - 1:HS], func=AF.Copy), reads=[wi], writes=[car])
                            kb.op("dve", lambda e: e.tensor_tensor(out=wri[:], in0=wr[:], in1=cs[:], op=ALU.mult), reads=[wr, cs], writes=[wri])
                            kb.op("pool", lambda e: e.tensor_tensor(out=wii[:], in0=wi[:], in1=sn[:], op=ALU.mult), reads=[wi, sn], writes=[wii])
                            kb.op("dve", lambda e: e.tensor_tensor(out=xre[:], in0=wri[:], in1=wii[:], op=ALU.subtract), reads=[wri, wii], writes=[xre])
                            kb.op("pool", lambda e: e.tensor_tensor(out=wri[:], in0=wi[:], in1=cs[:], op=ALU.mult), reads=[wi, cs], writes=[wri])
                            kb.op("dve", lambda e: e.tensor_tensor(out=wii[:], in0=wr[:], in1=sn[:], op=ALU.mult), reads=[wr, sn], writes=[wii])
                            kb.op("pool", lambda e: e.tensor_tensor(out=xim[:], in0=wri[:], in1=wii[:], op=ALU.add), reads=[wri, wii], writes=[xim])
                            for tg in range(HS // 512):
                                tk = slice(t0 + 512 * tg, t0 + 512 * tg + 512)
                                lk = slice(512 * tg, 512 * tg + 512)
                                py = psY[tg % nps]
                                c0 = CTz[0][:, q, :] if rb == 96 else CT[0][:, q, rb:rb + 32]
                                c1 = CTz[1][:, q, :] if rb == 96 else CT[1][:, q, rb:rb + 32]
                                kb.op("pe", lambda e: e.matmul(py[rows, :], c0, xre[:, lk], start=True, stop=False), reads=[CT[0], CTz[0], xre], writes=[py])
                                kb.op("pe", lambda e: e.matmul(py[rows, :], c1, xim[:, lk], start=False, stop=True), reads=[CT[1], CTz[1], xim], writes=[py])
                                kb.op("dve", lambda e: e.tensor_tensor(out=yT[rows, q, tk], in0=yT[rows, q, tk], in1=py[rows, :], op=ALU.add), reads=[yT, py], writes=[yT])
                        return (s1, s2)
                    steps.append(mk())
            yield (steps, 2)
            kb.barrier()
        dbg = p.d.get("dbg_s5y")
        if dbg is not None:
            kb.dma("sp", dbg.h.rearrange("(q p) t -> p q t", p=128), yT[:], reads=[yT], writes=[dbg])
        with ExitStack() as es2:
            sb2 = lambda n, shp, dt=F32: p.sb(es2, "s5g_" + n, shp, dt)
            gT = sb2("gT", [128, 4, S], BF16)
            Wg = sb2("Wg", [128, 4, 512], BF16)
            kb.dma("sp", Wg[:], wglu.h.rearrange("(k p) n -> p k n", p=128), reads=[wglu], writes=[Wg])
            bg = sb2("bg", [128, 4])
            kb.dma("sp", bg[:], b_glu[l].rearrange("(q p) -> p q", p=128), writes=[bg], allow_slow_non_contiguous=True)
            gn = sb2("gn", [128, 4])
            kb.dma("sp", gn[:], gain[l, 0].rearrange("(q p) -> p q", p=128), writes=[gn], allow_slow_non_contiguous=True)
            ones = load_const_bf16(p, es2, "c_ones", (128, 128))
            sig = [sb2("sig%d" % i, [128, 512]) for i in range(2)]
            sq = sb2("sq", [128, 4, 512], BF16)
            rs = sb2("rs", [128, 512])
            yo = [sb2("yo%d" % i, [128, 512], BF16) for i in range(2)]
            pG = [p.ps(es2, "s5_pG%d" % i, [128, 512]) for i in range(2)]
            pN = p.ps(es2, "s5_pN", [128, 512])
            for q in range(4):
                kb.op("act", lambda e: e.activation(out=yT[:, q, :], in_=yT[:, q, :], func=AF.Gelu_apprx_tanh), reads=[yT], writes=[yT])
                kb.op("dve", lambda e: e.tensor_copy(out=gT[:, q, :], in_=yT[:, q, :]), reads=[yT], writes=[gT])
            cnt = 0
            for tg in range(4):
                tk = slice(512 * tg, 512 * tg + 512)
                for nq in range(4):
                    pg, sg = pG[cnt % 2], sig[cnt % 2]
                    cnt += 1
                    for kq in range(4):
                        kb.op("pe", lambda e: e.matmul(pg[:], Wg[:, kq, 128 * nq:128 * nq + 128], gT[:, kq, tk], start=(kq == 0), stop=(kq == 3)), reads=[Wg, gT], writes=[pg])
                    kb.op("act", lambda e: e.activation(out=sg[:], in_=pg[:], func=AF.Sigmoid, bias=bg[:, nq:nq + 1]), reads=[pg, bg], writes=[sg])
                    kb.op("dve", lambda e: e.tensor_tensor(out=yT[:, nq, tk], in0=yT[:, nq, tk], in1=sg[:], op=ALU.mult), reads=[yT, sg], writes=[yT])
                    kb.op("act", lambda e: e.activation(out=sq[:, nq, :], in_=yT[:, nq, tk], func=AF.Square), reads=[yT], writes=[sq])
                for nq in range(4):
                    kb.op("pe", lambda e: e.matmul(pN[:], ones[:], sq[:, nq, :], start=(nq == 0), stop=(nq == 3)), reads=[ones, sq], writes=[pN])
                kb.op("act", lambda e: e.activation(out=rs[:], in_=pN[:], func=AF.Sqrt, bias=1e-6, scale=1.0 / 512), reads=[pN], writes=[rs])
                kb.op("dve", lambda e: e.reciprocal(out=rs[:], in_=rs[:]), reads=[rs], writes=[rs])
                for nq in range(4):
                    y_ = yo[cnt % 2]
                    cnt += 1
                    kb.op("dve", lambda e: e.scalar_tensor_tensor(out=y_[:], in0=yT[:, nq, tk], scalar=gn[:, nq:nq + 1], in1=rs[:], op0=ALU.mult, op1=ALU.mult), reads=[yT, gn, rs], writes=[y_])
                    kb.dma("pool", ysT[128 * nq:128 * nq + 128, tk], y_[:], reads=[y_], writes=[ysT])
            dbg = p.d.get("dbg_s5ya")
            if dbg is not None:
                kb.dma("sp", dbg.h.rearrange("(q p) t -> p q t", p=128), yT[:], reads=[yT], writes=[dbg])
            kb.barrier()


def phase_s5(p, l, **kw):
    drive(gen_s5(p, l, **kw))


def phase_sb(p, l, **kw):
    drive(gen_sb(p, l, **kw))


def resid_ln(p, lnw, hold, pfs, tt, out_dram=None, out_res=None):
    kb = p.kb
    for cg in range(4):
        kb.op("dve", lambda e: e.scalar_tensor_tensor(out=hold[:, 512 * cg:512 * cg + 512], in0=hold[:, 512 * cg:512 * cg + 512], scalar=ALPHA, in1=pfs[cg][:], op0=ALU.mult, op1=ALU.add), reads=[hold, pfs[cg]], writes=[hold])
    emit_ln(p, lnw, hold, tt, 1e-5, out_dram=out_dram, out_res=out_res)


def phase_outproj(p, l):
    kb = p.kb
    W = p.wb[("w_out", l)]
    ysT = p.dram("ysT", (D, S), BF16)
    g = p.dram("ln1_g", IN_SHAPES["ln1_g"], F32, kind="ExternalInput")
    b = p.dram("ln1_b", IN_SHAPES["ln1_b"], F32, kind="ExternalInput")
    with ExitStack() as es:
        Wt = p.sb(es, "op_W", [128, KC, D], BF16)
        for c in range(0, KC, 4):
            kb.dma("sp", Wt[:, c:c + 4, :], W[c * 128:(c + 4) * 128, :].rearrange("(c p) n -> p c n", p=128), reads=[W], writes=[Wt])
        lnw = ln_tiles(p, es, "op_ln")
        ln_load_gb(p, lnw, g[l], b[l])
        yst = [p.sb(es, "op_ys%d" % i, [128, KC, 128], BF16) for i in range(2)]
        hold = [p.sb(es, "op_h%d" % i, [128, D], F32) for i in range(2)]
        pf = [p.ps(es, "op_pf%d" % i, [128, 512]) for i in range(4)]
        hold.append(p.sb(es, "op_h2", [128, D], F32))
        hold.append(p.sb(es, "op_h3", [128, D], F32))
        yst.append(p.sb(es, "op_ys2", [128, KC, 128], BF16))
        steps = []
        for tt in range(NT):
            def mk(tt=tt):
                ys, ho = yst[tt % 3], hold[tt % 4]

                def s0():
                    kb.dma("sp", ys[:], ysT[:, tt * 128:(tt + 1) * 128].rearrange("(c p) t -> p c t", p=128), reads=[ysT], writes=[ys])
                    kb.dma("sp", ho[:], p.h[tt * 128:(tt + 1) * 128, :], reads=[p.hres[tt]], writes=[ho])

                def s1():
                    for cg in range(4):
                        for k in range(KC):
                            kb.op("pe", lambda e: e.matmul(pf[cg][:], ys[:, k, :], Wt[:, k, 512 * cg:512 * cg + 512], start=(k == 0), stop=(k == KC - 1)), reads=[ys, Wt], writes=[pf[cg]])
                        kb.op("dve", lambda e: e.scalar_tensor_tensor(out=ho[:, 512 * cg:512 * cg + 512], in0=ho[:, 512 * cg:512 * cg + 512], scalar=ALPHA, in1=pf[cg][:], op0=ALU.mult, op1=ALU.add), reads=[ho, pf[cg]], writes=[ho])
                    emit_ln_a(p, lnw, ho, tt, 1e-5)

                def s2():
                    emit_ln_b(p, lnw, tt)
                return (s0, s1, s2)
            steps.append(mk())
        pipeline(steps, 3)
        kb.barrier()


def phase_xattn(p, l):
    kb = p.kb
    wq, wkv, wo = p.wb[("xa_wq", l)], p.wb[("xa_wkv", l)], p.wb[("xa_wo", l)]
    mem = p.dram("mem", (MEM, D), F32, kind="ExternalInput")
    g = p.dram("ln2_g", IN_SHAPES["ln2_g"], F32, kind="ExternalInput")
    b = p.dram("ln2_b", IN_SHAPES["ln2_b"], F32, kind="ExternalInput")
    with ExitStack() as es0:
        o_all = p.sb(es0, "xa_o", [128, NT, 512], BF16)
        with ExitStack() as es:
            hT = load_hT(p, es)
            Wq = p.sb(es, "xa_Wq", [128, KC, 512], BF16)
            Wkv = p.sb(es, "xa_Wkv", [128, KC, 1024], BF16)
            kb.dma("sp", Wq[:], wq.h.rearrange("(c p) n -> p c n", p=128), reads=[wq], writes=[Wq])
            kb.dma("sp", Wkv[:], wkv.h.rearrange("(c p) n -> p c n", p=128), reads=[wkv], writes=[Wkv])
            memT = p.sb(es, "xa_memT", [128, KC, MEM], BF16)
            kT = p.sb(es, "xa_kT", [128, 4, MEM], BF16)
            Vx = p.sb(es, "xa_V", [128, 2, 4, 129], BF16)
            kb.op("pool", lambda e: e.memset(Vx[:, :, :, 128:129], 1.0), writes=[Vx])
            pA = [p.ps(es, "xa_pA%d" % i, [128, 512]) for i in range(2)]
            pS = [p.ps(es, "xa_pS%d" % i, [128, 512]) for i in range(2)]
            pO = [p.ps(es, "xa_pO%d" % i, [128, 512]) for i in range(2)]
            pTb = p.ps(es, "xa_pTb", [128, 2048], BF16)
            with ExitStack() as es2:
                mf = p.sb(es2, "xa_mf", [128, D], F32)
                mb_ = p.sb(es2, "xa_mb", [128, D], BF16)
                for mt in range(2):
                    kb.dma("sp", mf[:], mem[mt * 128:(mt + 1) * 128, :], writes=[mf])
                    kb.op("act", lambda e: e.activation(out=mb_[:], in_=mf[:], func=AF.Copy), reads=[mf], writes=[mb_])
                    for c in range(KC):
                        kb.op("pe", lambda e: e.transpose(pTb[:, 128 * c:128 * c + 128], mb_[:, 128 * c:128 * c + 128], p.ident[:]), reads=[mb_, p.ident], writes=[pTb])
                    kb.op("dve", lambda e: e.tensor_copy(out=memT[:, :, 128 * mt:128 * mt + 128], in_=pTb[:].rearrange("p (c t) -> p c t", c=KC)), reads=[pTb], writes=[memT])
            for h in range(4):
                pa = pA[h % 2]
                for k in range(KC):
                    kb.op("pe", lambda e: e.matmul(pa[:, 0:MEM], Wkv[:, k, 128 * h:128 * h + 128], memT[:, k, :], start=(k == 0), stop=(k == KC - 1)), reads=[Wkv, memT], writes=[pa])
                kb.op("act", lambda e: e.activation(out=kT[:, h, :], in_=pa[:, 0:MEM], func=AF.Copy), reads=[pa], writes=[kT])
            for mt in range(2):
                pa = pA[mt % 2]
                for k in range(KC):
                    kb.op("pe", lambda e: e.matmul(pa[:], memT[:, k, 128 * mt:128 * mt + 128], Wkv[:, k, 512:1024], start=(k == 0), stop=(k == KC - 1)), reads=[Wkv, memT], writes=[pa])
                kb.op("act", lambda e: e.activation(out=Vx[:, mt, :, 0:128], in_=pa[:].rearrange("p (h d) -> p h d", d=128), func=AF.Copy), reads=[pa], writes=[Vx])
            qs = [p.sb(es, "xa_qs%d" % i, [128, 512], BF16) for i in range(2)]
            Pm = [[p.sb(es, "xa_P%d_%d" % (i, m), [128, 512], BF16) for m in range(2)] for i in range(2)]
            rd = p.sb(es, "xa_rd", [128, 2], F32)
            qs.append(p.sb(es, "xa_qs2", [128, 512], BF16))
            Pm.append([p.sb(es, "xa_P2_%d" % m, [128, 512], BF16) for m in range(2)])
            rds = [rd, p.sb(es, "xa_rd2", [128, 2], F32)]
            steps = []
            for G in range(4):
                for h in range(4):
                    def mk(G=G, h=h, cnt=len(steps)):
                        pa, q_, P_ = pA[cnt % 2], qs[cnt % 3], Pm[cnt % 3]

                        def s1():
                            for k in range(KC):
                                kb.op("pe", lambda e: e.matmul(pa[:], Wq[:, k, 128 * h:128 * h + 128], hT[:, k, 512 * G:512 * G + 512], start=(k == 0), stop=(k == KC - 1)), reads=[Wq, hT], writes=[pa])
                            kb.op("act", lambda e: e.mul(q_[:], pa[:], 128 ** -0.5), reads=[pa], writes=[q_])

                        def s2():
                            for m in range(2):
                                kb.op("pe", lambda e: e.matmul(pS[m][:], kT[:, h, 128 * m:128 * m + 128], q_[:], start=True, stop=True), reads=[kT, q_], writes=[pS[m]])
                                kb.op("act", lambda e: e.activation(out=P_[m][:], in_=pS[m][:], func=AF.Exp), reads=[pS[m]], writes=[P_[m]])

                        def s3():
                            for jj in range(2):
                                po = pO[jj]
                                rd_ = rds[jj]
                                for j2 in range(2):
                                    j = 2 * jj + j2
                                    for m in range(2):
                                        kb.op("pe", lambda e: e.matmul(po[:, 129 * j2:129 * j2 + 129], P_[m][:, 128 * j:128 * j + 128], Vx[:, m, h, :], start=(m == 0), stop=(m == 1)), reads=[P_[m], Vx], writes=[po])
                                kb.op("dve", lambda e: e.reciprocal(out=rd_[:], in_=po[:, 0:258].rearrange("p (j f) -> p j f", f=129)[:, :, 128]), reads=[po], writes=[rd_])
                                for j2 in range(2):
                                    j = 2 * jj + j2
                                    kb.op("dve", lambda e: e.tensor_scalar(out=o_all[:, 4 * G + j, 128 * h:128 * h + 128], in0=po[:, 129 * j2:129 * j2 + 128], scalar1=rd_[:, j2:j2 + 1], scalar2=None, op0=ALU.mult), reads=[po, rd_], writes=[o_all])
                        return (s1, s2, s3)
                    steps.append(mk())
            pipeline(steps, 3)
            kb.barrier()
        with ExitStack() as es:
            Wo = p.sb(es, "xa_Wo", [128, 4, D], BF16)
            kb.dma("sp", Wo[:], wo.h.rearrange("(c p) n -> p c n", p=128), reads=[wo], writes=[Wo])
            lnw = ln_tiles(p, es, "xa_ln")
            ln_load_gb(p, lnw, g[l], b[l])
            hold = [p.sb(es, "xa_h%d" % i, [128, D], F32) for i in range(2)]
            pf = [p.ps(es, "xa_pf%d" % i, [128, 512]) for i in range(4)]
            pT2 = p.ps(es, "xa_pT2", [128, 1024], BF16)
            oT = [p.sb(es, "xa_oT%d" % i, [128, 4, 128], BF16) for i in range(2)]
            hold.append(p.sb(es, "xa_h2", [128, D], F32))
            hold.append(p.sb(es, "xa_h3", [128, D], F32))
            steps = []
            for tt in range(NT):
                def mk(tt=tt):
                    ho, o_ = hold[tt % 4], oT[tt % 2]

                    def s0():
                        kb.dma("sp", ho[:], p.h[tt * 128:(tt + 1) * 128, :], reads=[p.hres[tt]], writes=[ho])

                    def s1():
                        for c in range(4):
                            kb.op("pe", lambda e: e.transpose(pT2[:, 128 * c:128 * c + 128], o_all[:, tt, 128 * c:128 * c + 128], p.ident[:]), reads=[o_all, p.ident], writes=[pT2])
                        kb.op("act", lambda e: e.activation(out=o_[:], in_=pT2[:, 0:512].rearrange("p (c t) -> p c t", c=4), func=AF.Copy), reads=[pT2], writes=[o_])

                    def s2():
                        for cg in range(4):
                            for c in range(4):
                                kb.op("pe", lambda e: e.matmul(pf[cg][:], o_[:, c, :], Wo[:, c, 512 * cg:512 * cg + 512], start=(c == 0), stop=(c == 3)), reads=[o_, Wo], writes=[pf[cg]])
                            kb.op("dve", lambda e: e.scalar_tensor_tensor(out=ho[:, 512 * cg:512 * cg + 512], in0=ho[:, 512 * cg:512 * cg + 512], scalar=ALPHA, in1=pf[cg][:], op0=ALU.mult, op1=ALU.add), reads=[ho, pf[cg]], writes=[ho])
                        emit_ln_a(p, lnw, ho, tt, 1e-5)

                    def s3():
                        emit_ln_b(p, lnw, tt)
                    return (s0, s1, s2, s3)
                steps.append(mk())
            pipeline(steps, 4)
            kb.barrier()


def phase_ffn(p, l, final_out=None):
    kb = p.kb
    wg, wu, wd = p.wb[("ffn_w_gate", l)], p.wb[("ffn_w_up", l)], p.wb[("ffn_w_down", l)]
    g = p.dram("ln3_g", IN_SHAPES["ln3_g"], F32, kind="ExternalInput")
    b = p.dram("ln3_b", IN_SHAPES["ln3_b"], F32, kind="ExternalInput")
    NCH = FFN // 128
    with ExitStack() as es:
        lnw = ln_tiles(p, es, "ff_ln")
        ln_load_gb(p, lnw, g[l], b[l])
        hTg = p.sb(es, "ff_hT", [128, KC, 512], BF16)
        a_t = p.sb(es, "ff_a", [128, NCH, 512], BF16)
        Wg_t = [p.sb(es, "ff_Wg%d" % i, [128, KC, 256], BF16) for i in range(2)]
        Wu_t = [p.sb(es, "ff_Wu%d" % i, [128, KC, 256], BF16) for i in range(2)]
        Wd_t = [p.sb(es, "ff_Wd%d" % i, [128, 11, 512], BF16) for i in range(2)]
        s_t = [p.sb(es, "ff_s%d" % i, [128, 512], F32) for i in range(2)]
        v4 = p.sb(es, "ff_v4", [128, 4, D], F32)
        pg = [p.ps(es, "ff_pg%d" % i, [128, 512]) for i in range(2)]
        pu = [p.ps(es, "ff_pu%d" % i, [128, 512]) for i in range(2)]
        pf = [p.ps(es, "ff_pf%d" % i, [128, 512]) for i in range(2)]
        lnw["hbs"] = lnw["hbs"] + [p.sb(es, "ff_hb%d" % i, [128, D], BF16) for i in range(2, 4)]
        lnw["hTs"] = lnw["hTs"] + [p.sb(es, "ff_hTs%d" % i, [128, KC, 128], BF16) for i in range(2, 4)]
        wl = 0
        dl = 0
        cnt = 0
        accs = [pf[0], pf[1], pg[0], pg[1]]
        pending = []

        class _W:
            def __init__(self, j):
                self.j = j

            def __getitem__(self, k):
                if k == "hbs":
                    return [lnw["hbs"][self.j]] * 2
                if k == "hTs":
                    return [lnw["hTs"][self.j]] * 2
                return lnw[k]

        for tg in range(4):
            kb.dma("sp", hTg[:], p.hTd[:, 512 * tg:512 * tg + 512].rearrange("(c p) t -> p c t", p=128), reads=[p.hTd], writes=[hTg])
            for c2 in range(NCH // 2):
                Wg_, Wu_ = Wg_t[wl % 2], Wu_t[wl % 2]
                wl += 1
                kb.dma("sp", Wg_[:], wg[:, 256 * c2:256 * c2 + 256].rearrange("(c p) n -> p c n", p=128), reads=[wg], writes=[Wg_])
                kb.dma("sp", Wu_[:], wu[:, 256 * c2:256 * c2 + 256].rearrange("(c p) n -> p c n", p=128), reads=[wu], writes=[Wu_])
                for ci in range(2):
                    ch = 2 * c2 + ci
                    pg_, pu_, st_ = pg[cnt % 2], pu[cnt % 2], s_t[cnt % 2]
                    cnt += 1
                    for k in range(KC):
                        kb.op("pe", lambda e: e.matmul(pg_[:], Wg_[:, k, 128 * ci:128 * ci + 128], hTg[:, k, :], start=(k == 0), stop=(k == KC - 1)), reads=[Wg_, hTg], writes=[pg_])
                    for k in range(KC):
                        kb.op("pe", lambda e: e.matmul(pu_[:], Wu_[:, k, 128 * ci:128 * ci + 128], hTg[:, k, :], start=(k == 0), stop=(k == KC - 1)), reads=[Wu_, hTg], writes=[pu_])
                    kb.op("act", lambda e: e.activation(out=st_[:], in_=pg_[:], func=AF.Silu), reads=[pg_], writes=[st_])
                    kb.op("dve", lambda e: e.tensor_tensor(out=a_t[:, ch, :], in0=st_[:], in1=pu_[:], op=ALU.mult), reads=[st_, pu_], writes=[a_t])
                if c2 == 1:
                    for f in pending:
                        f()
                    pending = []
                    for j in range(4):
                        tt = 4 * tg + j
                        kb.dma("sp", v4[:, j, :], p.h[tt * 128:(tt + 1) * 128, :], reads=[p.hres[tt]], writes=[v4])
            for cg in range(4):
                for pc in range(4):
                    Wd_ = Wd_t[dl % 2]
                    dl += 1
                    kb.dma("sp", Wd_[:], wd[pc * 11 * 128:(pc + 1) * 11 * 128, 512 * cg:512 * cg + 512].rearrange("(c p) n -> p c n", p=128), reads=[wd], writes=[Wd_])
                    for j in range(4):
                        for c in range(11):
                            ch = pc * 11 + c
                            kb.op("pe", lambda e: e.matmul(accs[j][:], a_t[:, ch, 128 * j:128 * j + 128], Wd_[:, c, :], start=(ch == 0), stop=(ch == NCH - 1)), reads=[a_t, Wd_], writes=[accs[j]])
                for j in range(4):
                    kb.op("dve", lambda e: e.scalar_tensor_tensor(out=v4[:, j, 512 * cg:512 * cg + 512], in0=v4[:, j, 512 * cg:512 * cg + 512], scalar=ALPHA, in1=accs[j][:], op0=ALU.mult, op1=ALU.add), reads=[v4, accs[j]], writes=[v4])
            fin = final_out is not None
            for j in range(4):
                tt = 4 * tg + j
                emit_ln_a(p, _W(j), _V4(v4, j), tt, 1e-5, out_dram=(final_out.h if fin else None), out_res=final_out, write_hT=(not fin), h_store=(not fin))
                if not fin:
                    pending.append(lambda j=j, tt=tt: emit_ln_b(p, _W(j), tt))
        for f in pending:
            f()
        kb.barrier()


class _V4:
    def __init__(self, t, j):
        self.t, self.j, self.r = t, j, t.r

    def __getitem__(self, k):
        if isinstance(k, tuple):
            return self.t.h[(k[0], self.j) + tuple(k[1:])]
        return self.t.h[k, self.j]


def build_full(depth=DEPTH):
    p = Prog(ext_out=["out"])
    setup_globals(p)
    out = p.dram("out", (S, D), F32)
    convert_weights(p, names=["w_in"], layers=[0])
    phase_ln0(p)
    for l in range(depth):
        convert_weights(p, names=["s5_w_glu", "nsa_cmp_w1", "nsa_cmp_w2"], layers=[l])
        phase_inproj(p, l)
        convert_weights(p, names=["w_out", "xa_wq", "xa_wkv", "xa_wo", "ffn_w_gate", "ffn_w_up", "ffn_w_down"], layers=[l], defer=True)
        if l + 1 < depth:
            convert_weights(p, names=["w_in"], layers=[l + 1], defer=True)
        phase_nsa(p, l)
        drive_merged([gen_s5(p, l, HS=S // 4, nps=1), gen_sb(p, l, npB=1, npO=1)])
        phase_dil(p, l)
        phase_outproj(p, l)
        phase_xattn(p, l)
        phase_ffn(p, l, final_out=(out if l == depth - 1 else None))
    p.kb.finish([out.r])
    p.kb.close()
    return p


def kernel(**inputs):
    p = build_full()
    consts = host_consts()
    n = 8
    in_maps = []
    for b in range(n):
        m = {}
        for name, kind in p.kinds.items():
            if kind != "ExternalInput":
                continue
            if name in consts:
                m[name] = consts[name]
            elif name in ("x", "mem"):
                m[name] = np.ascontiguousarray(np.asarray(inputs[name])[b], dtype=np.float32)
            else:
                m[name] = np.ascontiguousarray(np.asarray(inputs[name]), dtype=np.float32)
        in_maps.append(m)
    res = run_bass_kernel_spmd(p.nc, in_maps, core_ids=list(range(n)))
    return np.stack([np.asarray(res.results[b]["out"], dtype=np.float32) for b in range(n)], axis=0)
```

```python
import math
from contextlib import ExitStack

import numpy as np
import ml_dtypes
import concourse.bass as bass
import concourse.mybir as mybir
from concourse.bass_utils import run_bass_kernel_spmd

F32 = mybir.dt.float32
BF16 = mybir.dt.bfloat16
AF = mybir.ActivationFunctionType
ALU = mybir.AluOpType
AX = mybir.AxisListType


class Res:
    __slots__ = ("name", "w", "r", "excl")

    def __init__(self, name=""):
        self.name = name
        self.excl = False
        self.w = None
        self.r = []


class KB:
    N_DMA_SEMS = 24

    def __init__(self, nc):
        self.nc = nc
        self.es = ExitStack()
        self.eng = {"pe": nc.tensor, "dve": nc.vector, "act": nc.scalar, "pool": nc.gpsimd, "sp": nc.sync}
        self.sem = {}
        self.cnt = {}
        self.semh = {}
        for e in self.eng:
            h = self.es.enter_context(nc.semaphore("s_" + e))
            self.sem[e] = h
            self.cnt[e] = 0
            self.semh[("c", e)] = h
        self.dsem = {}
        self.dval = {}
        self.dnext = {}
        for q in ("sp", "pool", "act"):
            self.dsem[q] = []
            for i in range(self.N_DMA_SEMS):
                h = self.es.enter_context(nc.semaphore("d_%s_%d" % (q, i)))
                self.dsem[q].append(h)
                self.semh[("d", q, i)] = h
                self.dval[(q, i)] = 0
            self.dnext[q] = 0
        self.known = {e: {} for e in self.eng}
        self.ninstr = 0
        for h in self.semh.values():
            nc.gpsimd.sem_clear(h)
        nc.all_engine_barrier()

    def _wait(self, e, ev):
        if ev is None:
            return
        key, val = ev
        if key == ("c", e) and e == "pe":
            return
        k = self.known[e]
        if k.get(key, 0) >= val:
            return
        self.eng[e].wait_ge(self.semh[key], val)
        k[key] = val

    def _deps(self, e, reads, writes):
        for r in reads:
            self._wait(e, r.w)
        for w in writes:
            self._wait(e, w.w)
            for ev in w.r:
                self._wait(e, ev)

    def _commit(self, ev, reads, writes):
        for r in reads:
            if r in writes:
                continue
            r.r.append(ev)
            if len(r.r) > 12:
                d = {}
                for k, v in r.r:
                    d[k] = max(d.get(k, 0), v)
                r.r = list(d.items())
        for w in writes:
            w.w = ev
            w.r = []

    def op(self, e, fn, reads=(), writes=()):
        self._deps(e, reads, writes)
        ins = fn(self.eng[e])
        self.cnt[e] += 1
        ins.then_inc(self.sem[e], 1)
        ev = (("c", e), self.cnt[e])
        self._commit(ev, reads, writes)
        self.ninstr += 1
        return ev

    def dma(self, q, out, in_, reads=(), writes=(), **kw):
        i = self.dnext[q]
        self.dnext[q] = (i + 1) % len(self.dsem[q])
        key = ("d", q, i)
        if self.dval[(q, i)] > 0:
            self._wait(q, (key, self.dval[(q, i)]))
        self._deps(q, reads, writes)
        kw.setdefault("allow_slow_non_contiguous", True)
        ins = self.eng[q].dma_start(out=out, in_=in_, **kw)
        self.dval[(q, i)] += 16
        ins.then_inc(self.semh[key], 16)
        ev = (key, self.dval[(q, i)])
        self._commit(ev, reads, writes)
        self.ninstr += 1
        return ev

    def finish(self, final_res):
        for r in final_res:
            self._wait("sp", r.w)
        for q in self.dsem:
            for i in range(len(self.dsem[q])):
                if self.dval[(q, i)] > 0:
                    self._wait("sp", (("d", q, i), self.dval[(q, i)]))
        for e in ("pe", "dve", "act", "pool"):
            if self.cnt[e] > 0:
                self._wait("sp", (("c", e), self.cnt[e]))
        self.nc.all_engine_barrier()
        for h in self.semh.values():
            self.nc.gpsimd.sem_clear(h)
        self.nc.all_engine_barrier()

    def close(self):
        self.es.close()

    def barrier(self):
        evs = []
        for e in ("pe", "dve", "act", "pool", "sp"):
            if self.cnt[e] > 0:
                evs.append((("c", e), self.cnt[e]))
        for q in self.dsem:
            for i in range(len(self.dsem[q])):
                if self.dval[(q, i)] > 0:
                    evs.append((("d", q, i), self.dval[(q, i)]))
        for e in ("pe", "dve", "act", "pool", "sp"):
            for ev in evs:
                if ev[0] == ("c", e):
                    continue
                self._wait(e, ev)


def _res(x):
    return x.r if hasattr(x, "r") and not isinstance(x, Res) else x


_orig_op = KB.op
_orig_dma = KB.dma


def _op(self, e, fn, reads=(), writes=()):
    reads = [_res(x) for x in reads]
    writes = [_res(x) for x in writes]
    for r in reads:
        if r.excl and r not in writes:
            writes.append(r)
    return _orig_op(self, e, fn, reads, writes)


def _dma(self, q, out, in_, reads=(), writes=(), **kw):
    return _orig_dma(self, q, out, in_, [_res(x) for x in reads], [_res(x) for x in writes], **kw)


KB.op = _op
KB.dma = _dma


def pipeline(steps, nst):
    n = len(steps)
    for i in range(n + nst - 1):
        for s_ in range(nst):
            k = i - s_
            if 0 <= k < n and steps[k][s_] is not None:
                steps[k][s_]()


def pipeline_multi(groups):
    st = []
    for steps, nst in groups:
        st.append([steps, nst, 0, len(steps) + nst - 1])
    while True:
        best = None
        for g in st:
            if g[2] < g[3]:
                frac = g[2] / float(g[3])
                if best is None or frac < best[0]:
                    best = (frac, g)
        if best is None:
            break
        g = best[1]
        steps, nst, i = g[0], g[1], g[2]
        for s_ in range(nst):
            k = i - s_
            if 0 <= k < len(steps) and steps[k][s_] is not None:
                steps[k][s_]()
        g[2] += 1


def drive(gen):
    for steps, nst in gen:
        pipeline(steps, nst)


def drive_merged(gens):
    items = [next(g) for g in gens]
    pipeline_multi(items)
    for g in reversed(gens):
        for steps, nst in g:
            pipeline(steps, nst)


class TL:
    def __init__(self, h, name=""):
        self.h = h
        self.r = Res(name)

    def __getitem__(self, k):
        return self.h[k]


D = 2048
S = 2048
NT = S // 128
KC = D // 128
DEPTH = 2
MEM = 256
FFN = 5632
ALPHA = (2 * DEPTH) ** 0.25
IN_W = 4888
O_U, O_NQ, O_NKV, O_GATE, O_SB, O_DIL = 0, 512, 1024, 1792, 1816, 3352
FM_SRC = [128 * i for i in range(14)] + [1816 + 128 * j for j in range(8)] + [3352 + 128 * j for j in range(8)]
FM_ROPE = set([4, 5, 6, 7, 10, 12] + list(range(22, 30)))
FM_U, FM_NQ, FM_KCMP, FM_VCMP, FM_KSLC, FM_KWIN, FM_SBQ, FM_SBK, FM_DQ, FM_DK = 0, 4, 8, 9, 10, 12, 14, 18, 22, 26
TM_SRC = [(1408, 128), (1664, 128), (2840, 512), (4376, 512), (1792, 24)]
TM_VSLC, TM_VWIN, TM_SBV, TM_DV = 0, 128, 256, 768
TM_W = 1280

WNAMES = ["w_in", "s5_w_glu", "nsa_cmp_w1", "nsa_cmp_w2", "w_out", "xa_wq", "xa_wkv", "xa_wo",
          "ffn_w_gate", "ffn_w_up", "ffn_w_down"]
IN_SHAPES = {
    "x": (S, D), "mem": (MEM, D), "ln_in_g": (D,), "ln_in_b": (D,), "w_in": (DEPTH, D, IN_W),
    "s5_lambda_re": (DEPTH, 32, 64), "s5_lambda_im": (DEPTH, 32, 64), "s5_log_dt": (DEPTH, 32),
    "s5_b_re": (DEPTH, 32, 64, 16), "s5_b_im": (DEPTH, 32, 64, 16), "s5_c_re": (DEPTH, 32, 16, 64),
    "s5_c_im": (DEPTH, 32, 16, 64), "s5_d": (DEPTH, 32, 16), "s5_w_glu": (DEPTH, 512, 512),
    "s5_b_glu": (DEPTH, 512), "nsa_cmp_pe": (DEPTH, 2, 32, 64), "nsa_cmp_w1": (DEPTH, 2, 2048, 128),
    "nsa_cmp_w2": (DEPTH, 2, 128, 64), "mix_norm_g": (DEPTH, 4, 512), "w_out": (DEPTH, D, D),
    "ln1_g": (DEPTH, D), "ln1_b": (DEPTH, D), "xa_wq": (DEPTH, D, 512), "xa_wkv": (DEPTH, D, 1024),
    "xa_wo": (DEPTH, 512, D), "ln2_g": (DEPTH, D), "ln2_b": (DEPTH, D), "ffn_w_gate": (DEPTH, D, FFN),
    "ffn_w_up": (DEPTH, D, FFN), "ffn_w_down": (DEPTH, FFN, D), "ln3_g": (DEPTH, D), "ln3_b": (DEPTH, D),
}


def host_consts():
    c = {}
    c["c_ident"] = np.eye(128, dtype=np.float32)
    half = 32
    inv = (10000.0 ** (-np.arange(half, dtype=np.float32) / half)).astype(np.float32)
    pos = np.arange(S, dtype=np.float32)
    ang = pos[None, :] * inv[:, None]
    cos = np.cos(ang).astype(np.float32)
    sin = np.sin(ang).astype(np.float32)
    cos64 = np.concatenate([cos, cos], 0)
    sin64 = np.concatenate([-sin, sin], 0)
    c["c_cos"] = np.concatenate([cos64, cos64], 0)
    c["c_sin"] = np.concatenate([sin64, sin64], 0)
    sw = np.zeros((128, 128), np.float32)
    for m in range(128):
        k = m + 32 if (m % 64) < 32 else m - 32
        sw[k, m] = 1.0
    c["c_swap"] = sw
    k = np.arange(128)[:, None, None]
    i = np.arange(4)[None, :, None]
    q = np.arange(512)[None, None, :]
    c["c_maskS"] = ((128 * i + k) < q).astype(np.float32)
    c["c_maskC"] = ((128 * i + k) <= q).astype(np.float32)
    c["c_maskCn"] = 1.0 - c["c_maskC"]
    jj = np.arange(128)[:, None]
    kk = np.arange(128)[None, :]
    c["c_negU"] = -(jj >= kk).astype(np.float32)
    c["c_negOnes"] = -np.ones((128, 128), np.float32)
    c["c_ones"] = np.ones((128, 128), np.float32)
    c["c_maskD"] = np.stack([(jj >= kk), (jj <= kk)], 1).astype(np.float32)
    cc = np.arange(128)[:, None]
    tt_ = np.arange(S)[None, :]
    c["c_maskCmp"] = ((16 * cc + 31 <= tt_) & (cc < 127)).astype(np.float32)
    ci = np.arange(128)[:, None] * 16
    sj = np.arange(32)[None, :] * 64
    ov = np.clip(np.minimum(ci + 32, sj + 64) - np.maximum(ci, sj), 0, None) / 32.0
    ov[127] = 0
    c["c_overlap"] = ov.astype(np.float32)
    t = np.arange(S)[:, None]
    n = np.arange(32)[None, :]
    cur = t // 64
    invalid = n * 64 > t
    forced = ((n == 0) | (n == cur) | (n == cur - 1)) & ~invalid
    c["c_selMul"] = (~(invalid | forced)).astype(np.float32)
    c["c_selAdd"] = np.where(forced, 1e4, np.where(invalid, -1e4, 0.0)).astype(np.float32)
    eb = np.zeros((32, 16, 128), np.float32)
    for kb_ in range(16):
        for k in range(128):
            eb[2 * kb_ + k // 64, kb_, k] = 32768.0
    c["c_Ebig"] = eb
    c["c_iota"] = np.tile(np.arange(S, dtype=np.float32)[None, :], (128, 1))
    pp = np.arange(128)
    c["c_mask2"] = np.stack([((pp // 16) % 2 == 0), ((pp // 16) % 2 == 1)], 1).astype(np.float32)
    m3 = ((pp[:, None] // 64) == ((pp[None, :] // 16) % 2)).astype(np.float32)
    c["c_mask2z"] = c["c_mask2"] * (pp[:, None] >= 96)
    c["c_mask3"] = m3
    c["c_mask3n"] = -m3
    c["c_cos_cmp"] = np.ascontiguousarray(c["c_cos"][:, 31::16][:, :127])
    c["c_sin_cmp"] = np.ascontiguousarray(c["c_sin"][:, 31::16][:, :127])
    return c


class Prog:
    def __init__(self, ext_in=(), ext_out=()):
        self.nc = bass.Bass("TRN2", target_bir_lowering=False)
        self.kb = KB(self.nc)
        self.ext_in = set(ext_in)
        self.ext_out = set(ext_out)
        self.d = {}
        self.kinds = {}
        self.uid = 0
        self.bgq = []
        self.es = self.kb.es

    def dram(self, name, shape, dtype, kind=None):
        if name in self.d:
            return self.d[name]
        if kind is None:
            kind = "ExternalInput" if name in self.ext_in else ("ExternalOutput" if name in self.ext_out else "Internal")
        t = TL(self.nc.dram_tensor(name, list(shape), dtype, kind=kind).ap(), name)
        self.d[name] = t
        self.kinds[name] = kind
        return t

    def bg(self, n=1):
        for _ in range(n):
            if self.bgq:
                self.bgq.pop(0)()

    def sb(self, es, name, shape, dtype):
        self.uid += 1
        name = "%s_u%d" % (name, self.uid)
        return TL(es.enter_context(self.nc.sbuf_tensor(name, list(shape), dtype)), name)

    def ps(self, es, name, shape, dtype=F32):
        self.uid += 1
        name = "%s_u%d" % (name, self.uid)
        t = TL(es.enter_context(self.nc.psum_tensor(name, list(shape), dtype)), name)
        t.r.excl = True
        return t


def setup_globals(p):
    kb = p.kb
    p.hTd = p.dram("hT_stream", (D, S), BF16)
    p.ident = p.sb(p.es, "ident", [128, 128], BF16)
    cid = p.dram("c_ident", (128, 128), F32, kind="ExternalInput")
    kb.dma("pool", p.ident[:], cid[:, :], writes=[p.ident])
    p.h = p.dram("h_stream", (S, D), F32)
    p.hres = [Res("h%d" % i) for i in range(NT)]


def ln_tiles(p, es, pfx):
    w = {}
    w["st"] = p.sb(es, pfx + "st", [128, 24], F32)
    w["mv"] = p.sb(es, pfx + "mv", [128, 2], F32)
    w["hbs"] = [p.sb(es, pfx + "hb%d" % i, [128, D], BF16) for i in range(2)]
    w["pT"] = p.ps(es, pfx + "pT", [128, D], BF16)
    w["g"] = p.sb(es, pfx + "g", [128, D], F32)
    w["b"] = p.sb(es, pfx + "b", [128, D], F32)
    w["hTs"] = [p.sb(es, pfx + "hTs%d" % i, [128, KC, 128], BF16) for i in range(2)]
    return w


def ln_load_gb(p, w, g_ap, b_ap):
    p.kb.dma("sp", w["g"][:], g_ap.partition_broadcast(128), writes=[w["g"]])
    p.kb.dma("sp", w["b"][:], b_ap.partition_broadcast(128), writes=[w["b"]])


def emit_ln_a(p, w, v, tt, eps, out_dram=None, out_res=None, write_hT=True, h_store=True):
    kb = p.kb
    st, mv, g, b = w["st"], w["mv"], w["g"], w["b"]
    hb = w["hbs"][tt % 2]
    for c in range(4):
        kb.op("dve", lambda e: e.bn_stats(out=st[:, 6 * c:6 * c + 6], in_=v[:, 512 * c:512 * c + 512]), reads=[v], writes=[st])
    kb.op("dve", lambda e: e.bn_aggr(out=mv[:], in_=st[:]), reads=[st], writes=[mv])
    kb.op("act", lambda e: e.activation(out=mv[:, 1:2], in_=mv[:, 1:2], func=AF.Sqrt, bias=eps), reads=[mv], writes=[mv])
    kb.op("dve", lambda e: e.reciprocal(out=mv[:, 1:2], in_=mv[:, 1:2]), reads=[mv], writes=[mv])
    kb.op("dve", lambda e: e.tensor_scalar(out=v[:], in0=v[:], scalar1=mv[:, 0:1], scalar2=mv[:, 1:2], op0=ALU.subtract, op1=ALU.mult), reads=[v, mv], writes=[v])
    kb.op("pool", lambda e: e.tensor_tensor(out=v[:], in0=v[:], in1=g[:], op=ALU.mult), reads=[v, g], writes=[v])
    kb.op("pool", lambda e: e.tensor_tensor(out=v[:], in0=v[:], in1=b[:], op=ALU.add), reads=[v, b], writes=[v])
    if h_store:
        kb.dma("pool", p.h[tt * 128:(tt + 1) * 128, :], v[:], reads=[v], writes=[p.hres[tt]])
    if out_dram is not None:
        kb.dma("pool", out_dram[tt * 128:(tt + 1) * 128, :], v[:], reads=[v], writes=[out_res])
    if write_hT:
        kb.op("act", lambda e: e.activation(out=hb[:], in_=v[:], func=AF.Copy), reads=[v], writes=[hb])


def emit_ln_b(p, w, tt):
    kb = p.kb
    hb, pT = w["hbs"][tt % 2], w["pT"]
    for c in range(KC):
        kb.op("pe", lambda e: e.transpose(pT[:, 128 * c:128 * c + 128], hb[:, 128 * c:128 * c + 128], p.ident[:]), reads=[hb, p.ident], writes=[pT])
    hs = w["hTs"][tt % 2]
    kb.op("act", lambda e: e.activation(out=hs[:], in_=pT[:].rearrange("p (c t) -> p c t", c=KC), func=AF.Copy), reads=[pT], writes=[hs])
    kb.dma("act", p.hTd[:, tt * 128:(tt + 1) * 128].rearrange("(c p) t -> p c t", p=128), hs[:], reads=[hs], writes=[p.hTd])


def emit_ln(p, w, v, tt, eps, out_dram=None, out_res=None, write_hT=True, h_store=True):
    emit_ln_a(p, w, v, tt, eps, out_dram, out_res, write_hT, h_store)
    if write_hT:
        emit_ln_b(p, w, tt)


def load_hT(p, es, name="hT"):
    t = p.sb(es, name, [128, KC, S], BF16)
    for c in range(0, KC, 4):
        p.kb.dma("sp", t[:, c:c + 4, :], p.hTd[c * 128:(c + 4) * 128, :].rearrange("(c p) t -> p c t", p=128), reads=[p.hTd], writes=[t])
    return t


def convert_weights(p, names=WNAMES, layers=range(DEPTH), defer=False):
    kb = p.kb
    if not hasattr(p, "wb"):
        p.wb = {}
    for l in layers:
        for n in names:
            shp = IN_SHAPES[n][1:]
            src = p.dram(n, IN_SHAPES[n], F32, kind="ExternalInput")
            dst = p.dram("bf_%s_%d" % (n, l), shp, BF16)
            p.wb[(n, l)] = dst
            s_l = src[l]
            d_l = dst.h
            if len(shp) == 3:
                s_l = s_l.rearrange("a b c -> (a b) c")
                d_l = d_l.rearrange("a b c -> (a b) c")
            rows, cols = s_l.shape
            step = max(1, (1024 * 1024) // cols)
            r0 = 0
            while r0 < rows:
                r1 = min(rows, r0 + step)
                if defer:
                    p.bgq.append(lambda d_=d_l[r0:r1, :], s_=s_l[r0:r1, :], dst=dst: kb.dma("pool", d_, s_, writes=[dst]))
                else:
                    kb.dma("pool", d_l[r0:r1, :], s_l[r0:r1, :], writes=[dst])
                r0 = r1


def phase_ln0(p):
    kb = p.kb
    x = p.dram("x", (S, D), F32, kind="ExternalInput")
    g = p.dram("ln_in_g", (D,), F32, kind="ExternalInput")
    b = p.dram("ln_in_b", (D,), F32, kind="ExternalInput")
    with ExitStack() as es:
        w = ln_tiles(p, es, "l0")
        ln_load_gb(p, w, g.h, b.h)
        vs = [p.sb(es, "l0v%d" % i, [128, D], F32) for i in range(2)]
        for tt in range(NT):
            v = vs[tt % 2]
            kb.dma("sp", v[:], x[tt * 128:(tt + 1) * 128, :], writes=[v])
            p.bg()
            emit_ln(p, w, v, tt, 1e-5)
        p.bg(len(p.bgq))
        kb.barrier()


def phase_inproj(p, l, do_fm=True, do_tm=True, max_groups=99):
    kb = p.kb
    wb = p.wb[("w_in", l)]
    fm = p.dram("fm", (30 * 128, S), BF16)
    tm = p.dram("tm", (S, TM_W), BF16)
    gates = p.dram("gates", (S, 24), F32)
    ccos = p.dram("c_cos", (128, S), F32, kind="ExternalInput")
    csin = p.dram("c_sin", (128, S), F32, kind="ExternalInput")
    cswap = p.dram("c_swap", (128, 128), F32, kind="ExternalInput")
    with ExitStack() as es:
        p.hT = load_hT(p, es)
        cos = p.sb(es, "a_cos", [128, S], F32)
        sin = p.sb(es, "a_sin", [128, S], F32)
        swp = p.sb(es, "a_swp", [128, 128], BF16)
        kb.dma("sp", cos[:], ccos[:, :], writes=[cos])
        kb.dma("sp", sin[:], csin[:, :], writes=[sin])
        kb.dma("pool", swp[:], cswap[:, :], writes=[swp])
        wts = [p.sb(es, "a_w%d" % i, [128, KC, 512], BF16) for i in range(2)]
        pacc = [p.ps(es, "a_pacc%d" % i, [128, 512]) for i in range(2)]
        psw = [p.ps(es, "a_psw%d" % i, [128, 512]) for i in range(2)]
        xb = [p.sb(es, "a_xb%d" % i, [128, 512], BF16) for i in range(2)]
        t1 = [p.sb(es, "a_t1%d" % i, [128, 512], F32) for i in range(2)]
        t2 = [p.sb(es, "a_t2%d" % i, [128, 512], F32) for i in range(2)]
        outs = [p.sb(es, "a_out%d" % i, [128, S], BF16) for i in range(2)]
        groups = []
        i = 0
        while i < len(FM_SRC):
            j = i
            while j + 1 < len(FM_SRC) and j + 1 - i < 4 and FM_SRC[j + 1] == FM_SRC[j] + 128:
                j += 1
            groups.append((i, j - i + 1))
            i = j + 1
        steps = []
        for gi, (c0, n) in enumerate(groups if do_fm else []):
            if gi >= max_groups:
                break
            for ci in range(n):
                for tg in range(4):
                    def mk(gi=gi, c0=c0, n=n, ci=ci, tg=tg, cnt=len(steps)):
                        wt = wts[gi % 2]
                        src0 = FM_SRC[c0]
                        ch = c0 + ci
                        ot = outs[ch % 2]
                        pa = pacc[cnt % 2]
                        osl = ot[:, 512 * tg:512 * tg + 512]
                        rope = ch in FM_ROPE
                        x_b, ps2, a1, a2 = xb[cnt % 2], psw[cnt % 2], t1[cnt % 2], t2[cnt % 2]

                        def fin():
                            if tg == 3:
                                kb.dma("pool", fm[ch * 128:(ch + 1) * 128, :], ot[:], reads=[ot], writes=[fm])

                        def s1():
                            if ci == 0 and tg == 0:
                                kb.dma("sp", wt[:, :, 0:128 * n], wb[:, src0:src0 + 128 * n].rearrange("(c p) n -> p c n", p=128), reads=[wb], writes=[wt])
                            for k in range(KC):
                                kb.op("pe", lambda e: e.matmul(pa[:], wt[:, k, 128 * ci:128 * ci + 128], p.hT[:, k, 512 * tg:512 * tg + 512], start=(k == 0), stop=(k == KC - 1)), reads=[wt, p.hT], writes=[pa])
                            if rope:
                                kb.op("act", lambda e: e.activation(out=x_b[:], in_=pa[:], func=AF.Copy), reads=[pa], writes=[x_b])
                            else:
                                if cnt % 2 == 0:
                                    kb.op("act", lambda e: e.activation(out=osl, in_=pa[:], func=AF.Copy), reads=[pa], writes=[ot])
                                else:
                                    kb.op("dve", lambda e: e.tensor_copy(out=osl, in_=pa[:]), reads=[pa], writes=[ot])
                                fin()

                        def s2():
                            kb.op("pe", lambda e: e.matmul(ps2[:], swp[:], x_b[:], start=True, stop=True), reads=[swp, x_b], writes=[ps2])
                            kb.op("dve", lambda e: e.tensor_tensor(out=a1[:], in0=pa[:], in1=cos[:, 512 * tg:512 * tg + 512], op=ALU.mult), reads=[pa, cos], writes=[a1])
                            kb.op("dve", lambda e: e.tensor_tensor(out=a2[:], in0=ps2[:], in1=sin[:, 512 * tg:512 * tg + 512], op=ALU.mult), reads=[ps2, sin], writes=[a2])
                            kb.op("dve", lambda e: e.tensor_tensor(out=osl, in0=a1[:], in1=a2[:], op=ALU.add), reads=[a1, a2], writes=[ot])
                            fin()
                        return (s1, s2 if rope else None)
                    steps.append(mk())
        pipeline(steps, 2)
        kb.barrier()
    with ExitStack() as es:
        p.hT = load_hT(p, es)
        wv = p.sb(es, "a_wv", [128, KC, TM_W + 24], BF16)
        off = 0
        for (s0, wd) in TM_SRC:
            kb.dma("sp", wv[:, :, off:off + wd], wb[:, s0:s0 + wd].rearrange("(c p) n -> p c n", p=128), reads=[wb], writes=[wv])
            off += wd
        pt = [[p.ps(es, "a_pt%d_%d" % (i, j), [128, 512]) for j in range(4)] for i in range(2)]
        ot = [p.sb(es, "a_ot%d" % i, [128, TM_W], BF16) for i in range(2)]
        gt = [p.sb(es, "a_gt%d" % i, [128, 24], F32) for i in range(2)]
        segs = [(0, 256, 0), (256, 512, 1), (768, 512, 2)]
        for tt in range(NT if do_tm else 0):
            pp = pt[tt % 2]
            o = ot[tt % 2]
            g_ = gt[tt % 2]
            for k in range(KC):
                lhsT = p.hT[:, k, 128 * tt:128 * tt + 128]
                for (c0, wd, pi) in segs:
                    kb.op("pe", lambda e: e.matmul(pp[pi][:, 0:wd], lhsT, wv[:, k, c0:c0 + wd], start=(k == 0), stop=(k == KC - 1)), reads=[wv, p.hT], writes=[pp[pi]])
                kb.op("pe", lambda e: e.matmul(pp[3][:, 0:24], lhsT, wv[:, k, TM_W:TM_W + 24], start=(k == 0), stop=(k == KC - 1)), reads=[wv, p.hT], writes=[pp[3]])
            kb.op("act", lambda e: e.activation(out=o[:, 0:256], in_=pp[0][:, 0:256], func=AF.Copy), reads=[pp[0]], writes=[o])
            kb.op("act", lambda e: e.activation(out=g_[:], in_=pp[3][:, 0:24], func=AF.Sigmoid), reads=[pp[3]], writes=[g_])
            kb.op("dve", lambda e: e.tensor_copy(out=o[:, 256:768], in_=pp[1][:]), reads=[pp[1]], writes=[o])
            kb.op("dve", lambda e: e.tensor_copy(out=o[:, 768:1280], in_=pp[2][:]), reads=[pp[2]], writes=[o])
            kb.dma("pool", tm[tt * 128:(tt + 1) * 128, :], o[:], reads=[o], writes=[tm])
            kb.dma("pool", gates[tt * 128:(tt + 1) * 128, :], g_[:], reads=[g_], writes=[gates])
        kb.barrier()


def go_tiles(p, es, pfx, l, grp):
    w = {}
    w["ss"] = p.sb(es, pfx + "ss", [128, 2], F32)
    w["junk"] = p.sb(es, pfx + "junk", [128, 512], F32)
    w["yb"] = p.sb(es, pfx + "yb", [128, 512], BF16)
    w["pt"] = p.ps(es, pfx + "pt", [128, 1024], BF16)
    w["yt"] = [p.sb(es, pfx + "yt%d" % i, [128, 4, 128], BF16) for i in range(2)]
    w["gain"] = p.sb(es, pfx + "gain", [128, 512], F32)
    g = p.dram("mix_norm_g", IN_SHAPES["mix_norm_g"], F32, kind="ExternalInput")
    p.kb.dma("sp", w["gain"][:], g[l, grp, :].partition_broadcast(128), writes=[w["gain"]])
    w["cnt"] = 0
    return w


def emit_group_out(p, w, y_ap, y_tl, grp, tt):
    kb = p.kb
    ysT = p.dram("ysT", (D, S), BF16)
    ss, junk, yb, pt, gain = w["ss"], w["junk"], w["yb"], w["pt"], w["gain"]
    yt = w["yt"][w["cnt"] % 2]
    w["cnt"] += 1
    kb.op("act", lambda e: e.activation(out=junk[:], in_=y_ap, func=AF.Square, accum_out=ss[:, 0:1]), reads=[y_tl], writes=[junk, ss])
    kb.op("act", lambda e: e.activation(out=ss[:, 1:2], in_=ss[:, 0:1], func=AF.Sqrt, bias=1e-6, scale=1.0 / 512), reads=[ss], writes=[ss])
    kb.op("dve", lambda e: e.reciprocal(out=ss[:, 1:2], in_=ss[:, 1:2]), reads=[ss], writes=[ss])
    kb.op("dve", lambda e: e.scalar_tensor_tensor(out=yb[:], in0=y_ap, scalar=ss[:, 1:2], in1=gain[:], op0=ALU.mult, op1=ALU.mult), reads=[y_tl, ss, gain], writes=[yb])
    for c in range(4):
        kb.op("pe", lambda e: e.transpose(pt[:, 128 * c:128 * c + 128], yb[:, 128 * c:128 * c + 128], p.ident[:]), reads=[yb, p.ident], writes=[pt])
    kb.op("act", lambda e: e.activation(out=yt[:], in_=pt[:, 0:512].rearrange("p (c t) -> p c t", c=4), func=AF.Copy), reads=[pt], writes=[yt])
    kb.dma("act", ysT[grp * 512:(grp + 1) * 512, tt * 128:(tt + 1) * 128].rearrange("(c p) t -> p c t", p=128), yt[:], reads=[yt], writes=[ysT])


def load_const_bf16(p, es, name, shape):
    src = p.dram(name, shape, F32, kind="ExternalInput")
    t = p.sb(es, "k_" + name, list(shape), BF16)
    p.kb.dma("pool", t[:], src.h, writes=[t])
    return t


def gen_sb(p, l, heads=range(8), groups=range(4), npB=2, npO=2):
    kb = p.kb
    fm = p.dram("fm", (30 * 128, S), BF16)
    tm = p.dram("tm", (S, TM_W), BF16)
    with ExitStack() as es:
        maskS = load_const_bf16(p, es, "c_maskS", (128, 4, 512))
        negU = load_const_bf16(p, es, "c_negU", (128, 128))
        negO = load_const_bf16(p, es, "c_negOnes", (128, 128))
        V = p.sb(es, "sb_V", [128, NT, 512], BF16)
        kb.dma("sp", V[:], tm[:, TM_SBV:TM_SBV + 512].rearrange("(t p) f -> p t f", p=128), reads=[tm], writes=[V])
        yc = p.sb(es, "sb_yc", [128, NT, 512], F32)
        kb.op("pool", lambda e: e.memset(yc[:], 0.0), writes=[yc])
        qTs = [p.sb(es, "sb_q%d" % i, [128, S], BF16) for i in range(2)]
        kTs = [p.sb(es, "sb_k%d" % i, [128, S], BF16) for i in range(2)]
        psA = [p.ps(es, "sb_pA%d" % i, [128, 512]) for i in range(2)]
        psB = [p.ps(es, "sb_pB%d" % i, [128, 512]) for i in range(npB)]
        psO = [p.ps(es, "sb_pO%d" % i, [128, 512]) for i in range(npO)]
        e_t = [p.sb(es, "sb_e%d" % i, [128, 512], F32) for i in range(2)]
        sp_t = [p.sb(es, "sb_sp%d" % i, [128, 512], BF16) for i in range(2)]
        w_t = [p.sb(es, "sb_w%d" % i, [128, 512], BF16) for i in range(2)]
        Ssum = [p.sb(es, "sb_S%d" % i, [128, 512], BF16) for i in range(2)]
        sp3 = sp_t + [p.sb(es, "sb_sp2", [128, 512], BF16)]
        w3 = w_t + [p.sb(es, "sb_w2", [128, 512], BF16)]
        steps = []
        hg = 0
        loaded = None
        for h in heads:
            hp = h // 2
            base = 64 * (h % 2)
            qT, kT = qTs[hp % 2], kTs[hp % 2]
            need_load = loaded != hp
            loaded = hp
            for G in groups:
                Ss = Ssum[hg % 2]
                hg += 1
                for kb_ in range(4 * G + 3, -1, -1):
                    def mk(h=h, hp=hp, base=base, qT=qT, kT=kT, G=G, kb_=kb_, Ss=Ss, idx=len(steps), load=need_load):
                        i = kb_ - 4 * G
                        diag = i >= 0
                        first = kb_ == 4 * G + 3
                        pa, pb, po = psA[idx % 2], psB[idx % npB], psO[idx % npO]
                        et, st_, wt = e_t[idx % 2], sp3[idx % 3], w3[idx % 3]
                        qs = qT[base:base + 64, 512 * G:512 * G + 512]
                        ks = kT[base:base + 64, 128 * kb_:128 * kb_ + 128]

                        def s1():
                            if idx % 12 == 3:
                                p.bg()
                            if load:
                                kb.dma("sp", qT[:], fm[(FM_SBQ + hp) * 128:(FM_SBQ + hp + 1) * 128, :], reads=[fm], writes=[qT])
                                kb.dma("sp", kT[:], fm[(FM_SBK + hp) * 128:(FM_SBK + hp + 1) * 128, :], reads=[fm], writes=[kT])
                                kb.op("act", lambda e: e.mul(qT[:], qT[:], 0.125), reads=[qT], writes=[qT])
                            kb.op("pe", lambda e: e.matmul(pa[:], ks, qs, start=True, stop=True), reads=[kT, qT], writes=[pa])
                            kb.op("act", lambda e: e.activation(out=et[:], in_=pa[:], func=AF.Exp), reads=[pa], writes=[et])
                            kb.op("act", lambda e: e.activation(out=st_[:], in_=et[:], func=AF.Ln, bias=1.0), reads=[et], writes=[st_])
                            if diag:
                                kb.op("pool", lambda e: e.tensor_tensor(out=st_[:], in0=st_[:], in1=maskS[:, i, :], op=ALU.mult), reads=[st_, maskS], writes=[st_])

                        def s2():
                            kb.op("pe", lambda e: e.matmul(pb[:], ks, qs, start=True, stop=False), reads=[kT, qT], writes=[pb])
                            kb.op("pe", lambda e: e.matmul(pb[:], negU[:], st_[:], start=False, stop=first), reads=[negU, st_], writes=[pb])
                            if not first:
                                kb.op("pe", lambda e: e.matmul(pb[:], negO[:], Ss[:], start=False, stop=True), reads=[negO, Ss], writes=[pb])
                            if first:
                                kb.op("dve", lambda e: e.tensor_copy(out=Ss[:], in_=st_[:]), reads=[st_], writes=[Ss])
                            elif kb_ > 0:
                                kb.op("dve", lambda e: e.tensor_tensor(out=Ss[:], in0=Ss[:], in1=st_[:], op=ALU.add), reads=[Ss, st_], writes=[Ss])
                            kb.op("act", lambda e: e.activation(out=wt[:], in_=pb[:], func=AF.Exp), reads=[pb], writes=[wt])
                            if diag:
                                kb.op("pool", lambda e: e.tensor_tensor(out=wt[:], in0=wt[:], in1=maskS[:, i, :], op=ALU.mult), reads=[wt, maskS], writes=[wt])

                        def s3():
                            j0 = max(i, 0)
                            for j in range(j0, 4):
                                kb.op("pe", lambda e: e.matmul(po[:, 64 * j:64 * j + 64], wt[:, 128 * j:128 * j + 128], V[:, kb_, 64 * h:64 * h + 64], start=True, stop=True), reads=[wt, V], writes=[po])
                            ysl = yc[:, 4 * G + j0:4 * G + 4, 64 * h:64 * h + 64]
                            kb.op("dve", lambda e: e.tensor_tensor(out=ysl, in0=ysl, in1=po[:, 64 * j0:256].rearrange("p (j d) -> p j d", d=64), op=ALU.add), reads=[yc, po], writes=[yc])
                        return (s1, s2, s3)
                    steps.append(mk())
                    need_load = False
        yield (steps, 3)
        p.bg(len(p.bgq))
        dbg = p.d.get("dbg_yc")
        if dbg is not None:
            kb.dma("sp", dbg.h.rearrange("(t p) f -> p t f", p=128), yc[:], reads=[yc], writes=[dbg])
        gw = go_tiles(p, es, "sbgo", l, 2)
        for tt in range(NT):
            emit_group_out(p, gw, yc[:, tt, :], yc, 2, tt)
        kb.barrier()


DIL_CFG = ((1, 2048), (4, 512), (16, 128))


def phase_dil(p, l, branches=range(3), heads=range(8), max_pairs=999):
    kb = p.kb
    fm = p.dram("fm", (30 * 128, S), BF16)
    tm = p.dram("tm", (S, TM_W), BF16)
    dacc = [p.dram("dil_acc%d" % c, (S, 520), F32) for c in range(3)]
    with ExitStack() as es:
        maskD = load_const_bf16(p, es, "c_maskD", (128, 2, 128))
        qT = p.sb(es, "dl_q", [128, 4, S], BF16)
        kT = p.sb(es, "dl_k", [128, 4, S], BF16)
        kb.dma("sp", qT[:], fm[FM_DQ * 128:(FM_DQ + 4) * 128, :].rearrange("(c p) t -> p c t", p=128), reads=[fm], writes=[qT])
        kb.dma("sp", kT[:], fm[FM_DK * 128:(FM_DK + 4) * 128, :].rearrange("(c p) t -> p c t", p=128), reads=[fm], writes=[kT])
        kb.op("act", lambda e: e.mul(qT[:], qT[:], 0.125), reads=[qT], writes=[qT])
        Vp = [p.sb(es, "dl_v%d" % i, [128, 8, 65], BF16) for i in range(3)]
        for v in Vp:
            kb.op("pool", lambda e: e.memset(v[:, :, 64:65], 1.0), writes=[v])
        psS = [p.ps(es, "dl_pS%d" % i, [128, 512]) for i in range(3)]
        psO = [p.ps(es, "dl_pO%d" % i, [128, 512]) for i in range(2)]
        pt = [p.sb(es, "dl_pt%d" % i, [128, 256], BF16) for i in range(3)]
        Oall = [p.sb(es, "dl_O%d" % i, [128, 520], F32) for i in range(2)]
        vcnt = 0
        npair = 0
        steps = []
        for c in branches:
            dil, L = DIL_CFG[c]
            vsrc = tm[:, TM_DV:TM_DV + 512].rearrange("(l r) f -> r l f", r=dil)
            dst = dacc[c].h.rearrange("(l r) f -> r l f", r=dil)
            for r in range(dil):
                vt = {}
                for b in range(L // 128):
                    if npair >= max_pairs:
                        break
                    npair += 1
                    kbs = [x for x in (b - 1, b) if x >= 0]
                    loads = []
                    for x in kbs:
                        if x not in vt:
                            vt[x] = Vp[vcnt % 3]
                            vcnt += 1
                            loads.append((vt[x], vsrc[r, 128 * x:128 * x + 128, :].rearrange("p (h d) -> p h d", d=64)))
                    oa = Oall[npair % 2]
                    vts = [vt[x] for x in kbs]
                    hl = list(heads)
                    for h in hl:
                        def mk(c=c, dil=dil, r=r, b=b, kbs=kbs, vts=vts, oa=oa, h=h, idx=len(steps), loads=(loads if h == hl[0] else []), lasth=(h == hl[-1]), dst=dst):
                            hp, base = h // 2, 64 * (h % 2)
                            qv = qT[base:base + 64, hp, :].rearrange("p (l r) -> p r l", r=dil)[:, r, 128 * b:128 * b + 128]
                            kvw = kT[base:base + 64, hp, :].rearrange("p (l r) -> p r l", r=dil)
                            ps, pp = psS[idx % 3], pt[idx % 3]
                            n = len(kbs)
                            po = psO[(h // 4) % 2]
                            o0 = 65 * (h % 4)

                            def s1():
                                for (t, src) in loads:
                                    kb.dma("sp", t[:, :, 0:64], src, reads=[tm], writes=[t])
                                for ix, x in enumerate(kbs):
                                    kb.op("pe", lambda e: e.matmul(ps[:, 128 * ix:128 * ix + 128], kvw[:, r, 128 * x:128 * x + 128], qv, start=True, stop=True), reads=[kT, qT], writes=[ps])
                                kb.op("act", lambda e: e.activation(out=pp[:, 0:128 * n], in_=ps[:, 0:128 * n], func=AF.Exp), reads=[ps], writes=[pp])
                                m0 = 2 - n
                                kb.op("pool", lambda e: e.tensor_tensor(out=pp[:, 0:128 * n], in0=pp[:, 0:128 * n], in1=maskD[:, m0:2, :].rearrange("p a q -> p (a q)"), op=ALU.mult), reads=[pp, maskD], writes=[pp])

                            def s2():
                                for ix, x in enumerate(kbs):
                                    kb.op("pe", lambda e: e.matmul(po[:, o0:o0 + 65], pp[:, 128 * ix:128 * ix + 128], vts[ix][:, h, :], start=(ix == 0), stop=(ix == n - 1)), reads=[pp, vts[ix]], writes=[po])
                                if h % 4 == 3 or lasth:
                                    hh = h // 4
                                    kb.op("dve", lambda e: e.tensor_copy(out=oa[:, 260 * hh:260 * hh + 260], in_=po[:, 0:260]), reads=[po], writes=[oa])
                                if lasth:
                                    kb.dma("pool", dst[r, 128 * b:128 * b + 128, :], oa[:], reads=[oa], writes=[dacc[c]])
                            return (s1, s2)
                        steps.append(mk())
                    for x in list(vt):
                        if x < b:
                            del vt[x]
        pipeline(steps, 2)
        kb.barrier()
    with ExitStack() as es:
        gw = go_tiles(p, es, "dlgo", l, 3)
        acc = [[p.sb(es, "dl_a%d_%d" % (i, c), [128, 8, 65], F32) for c in range(3)] for i in range(2)]
        rd = p.sb(es, "dl_rd", [128, 8], F32)
        yd = [p.sb(es, "dl_y%d" % i, [128, 512], F32) for i in range(2)]
        for tt in range(NT):
            a = acc[tt % 2]
            y = yd[tt % 2]
            for c in range(3):
                kb.dma("sp", a[c][:], dacc[c][128 * tt:128 * tt + 128, :].rearrange("p (h d) -> p h d", d=65), reads=[dacc[c]], writes=[a[c]])
            kb.op("dve", lambda e: e.tensor_tensor(out=a[0][:], in0=a[0][:], in1=a[1][:], op=ALU.add), reads=[a[0], a[1]], writes=[a[0]])
            kb.op("dve", lambda e: e.tensor_tensor(out=a[0][:], in0=a[0][:], in1=a[2][:], op=ALU.add), reads=[a[0], a[2]], writes=[a[0]])
            kb.op("dve", lambda e: e.reciprocal(out=rd[:], in_=a[0][:, :, 64]), reads=[a[0]], writes=[rd])
            kb.op("dve", lambda e: e.tensor_tensor(out=y[:].rearrange("p (h d) -> p h d", d=64), in0=a[0][:, :, 0:64], in1=rd[:].unsqueeze(2).to_broadcast([128, 8, 64]), op=ALU.mult), reads=[a[0], rd], writes=[y])
            dbg = p.d.get("dbg_yd")
            if dbg is not None:
                kb.dma("sp", dbg[128 * tt:128 * tt + 128, :], y[:], reads=[y], writes=[dbg])
            emit_group_out(p, gw, y[:], y, 3, tt)
        kb.barrier()


def phase_nsa(p, l, heads=range(8), do_slc=True, do_win=True, after_setup=None):
    kb = p.kb
    fm = p.dram("fm", (30 * 128, S), BF16)
    tm = p.dram("tm", (S, TM_W), BF16)
    gates = p.dram("gates", (S, 24), F32)
    w1 = p.wb[("nsa_cmp_w1", l)]
    w2 = p.wb[("nsa_cmp_w2", l)]
    pe = p.dram("nsa_cmp_pe", IN_SHAPES["nsa_cmp_pe"], F32, kind="ExternalInput")
    with ExitStack() as es:
        swp = load_const_bf16(p, es, "c_swap", (128, 128))
        maskC = load_const_bf16(p, es, "c_maskC", (128, 4, 512))
        maskCn = load_const_bf16(p, es, "c_maskCn", (128, 4, 512))
        maskCmp = load_const_bf16(p, es, "c_maskCmp", (128, S))
        Ebig = load_const_bf16(p, es, "c_Ebig", (32, 16, 128))
        ccos = p.sb(es, "ns_cos", [128, 127], F32)
        csin = p.sb(es, "ns_sin", [128, 127], F32)
        kb.dma("sp", ccos[:], p.dram("c_cos_cmp", (128, 127), F32, kind="ExternalInput")[:, :], writes=[ccos])
        kb.dma("sp", csin[:], p.dram("c_sin_cmp", (128, 127), F32, kind="ExternalInput")[:, :], writes=[csin])
        selMul = p.sb(es, "ns_selMul", [128, NT, 32], F32)
        selAdd = p.sb(es, "ns_selAdd", [128, NT, 32], F32)
        kb.dma("sp", selMul[:], p.dram("c_selMul", (S, 32), F32, kind="ExternalInput").h.rearrange("(t p) n -> p t n", p=128), writes=[selMul])
        kb.dma("sp", selAdd[:], p.dram("c_selAdd", (S, 32), F32, kind="ExternalInput").h.rearrange("(t p) n -> p t n", p=128), writes=[selAdd])
        qT = p.sb(es, "ns_q", [128, 4, S], BF16)
        kb.dma("sp", qT[:], fm[FM_NQ * 128:(FM_NQ + 4) * 128, :].rearrange("(c p) t -> p c t", p=128), reads=[fm], writes=[qT])
        kb.op("act", lambda e: e.mul(qT[:], qT[:], 0.125), reads=[qT], writes=[qT])
        gt = p.sb(es, "ns_gt", [128, NT, 24], F32)
        kb.dma("sp", gt[:], gates.h.rearrange("(t p) f -> p t f", p=128), reads=[gates], writes=[gt])
        yb = p.sb(es, "ns_yb", [128, NT, 512], F32)
        kb.op("pool", lambda e: e.memset(yb[:], 0.0), writes=[yb])
        impacc = p.sb(es, "ns_imp", [128, NT, 2, 32], F32)
        kb.op("pool", lambda e: e.memset(impacc[:], 0.0), writes=[impacc])
        kdup = {}
        for nm, ch in (("slc", FM_KSLC), ("win", FM_KWIN)):
            for g in range(2):
                t = p.sb(es, "ns_k%s%d" % (nm, g), [128, S], BF16)
                for half in range(2):
                    kb.dma("sp", t[64 * half:64 * half + 64, :], fm[ch * 128 + 64 * g:ch * 128 + 64 * g + 64, :], reads=[fm], writes=[t])
                kdup[(nm, g)] = t
        Vs = {}
        for nm, off in (("slc", TM_VSLC), ("win", TM_VWIN)):
            t = p.sb(es, "ns_v" + nm, [128, NT, 2, 65], BF16)
            kb.op("pool", lambda e: e.memset(t[:, :, :, 64:65], 1.0), writes=[t])
            for g in range(2):
                kb.dma("sp", t[:, :, g, 0:64], tm[:, off + 64 * g:off + 64 * g + 64].rearrange("(t p) d -> p t d", p=128), reads=[tm], writes=[t])
            Vs[nm] = t
        pS = [p.ps(es, "ns_pS%d" % i, [128, 512]) for i in range(2)]
        pO = [p.ps(es, "ns_pO%d" % i, [128, 512]) for i in range(2)]
        pX = [p.ps(es, "ns_pX%d" % i, [128, 512]) for i in range(2)]
        pXb = p.ps(es, "ns_pXb", [128, 1024], BF16)
        kcT = [p.sb(es, "ns_kc%d" % g, [128, 128], BF16) for g in range(2)]
        Rg = [p.sb(es, "ns_R%d" % g, [128, 97], BF16) for g in range(2)]
        ovl = load_const_bf16(p, es, "c_overlap", (128, 32))
        with ExitStack() as es2:
            tT = p.sb(es2, "ns_tT", [128, S], BF16)
            W1 = p.sb(es2, "ns_W1", [128, 32, 128], BF16)
            W2d = p.sb(es2, "ns_W2d", [128, 2, 64], BF16)
            pet = p.sb(es2, "ns_pe", [128, 32], F32)
            xl = [p.sb(es2, "ns_xl%d" % i, [128, 128], BF16) for i in range(4)]
            h1 = p.sb(es2, "ns_h1", [128, 128], BF16)
            xb = p.sb(es2, "ns_xb", [128, 128], BF16)
            a1 = p.sb(es2, "ns_a1", [128, 128], F32)
            a2 = p.sb(es2, "ns_a2", [128, 128], F32)
            xc = 0
            for j in range(2):
                kb.dma("sp", tT[:], fm[(FM_KCMP + j) * 128:(FM_KCMP + j + 1) * 128, :], reads=[fm], writes=[tT])
                for half in range(2):
                    kb.dma("sp", W1[64 * half:64 * half + 64, :, :], w1[j].rearrange("(l d) n -> d l n", d=64), reads=[w1], writes=[W1])
                    kb.dma("sp", pet[64 * half:64 * half + 64, :], pe[l, j].rearrange("l d -> d l"), writes=[pet], allow_slow_non_contiguous=True)
                    kb.dma("sp", W2d[:, half, :], w2[j], reads=[w2], writes=[W2d])
                for g in range(2):
                    base = 64 * g
                    ph = pX[0]
                    for ll in range(32):
                        x_ = xl[xc % 4]
                        xc += 1
                        src = tT[base:base + 64, :].rearrange("p (i s) -> p s i", s=16)
                        sh, lo = ll // 16, ll % 16
                        kb.op("dve", lambda e: e.tensor_scalar(out=x_[base:base + 64, 0:127], in0=src[:, lo, sh:sh + 127], scalar1=pet[base:base + 64, ll:ll + 1], scalar2=None, op0=ALU.add), reads=[tT, pet], writes=[x_])
                        kb.op("pe", lambda e: e.matmul(ph[:, 0:127], W1[base:base + 64, ll, :], x_[base:base + 64, 0:127], start=(ll == 0), stop=(ll == 31)), reads=[W1, x_], writes=[ph])
                    kb.op("act", lambda e: e.activation(out=h1[:, 0:127], in_=ph[:, 0:127], func=AF.Gelu_apprx_tanh), reads=[ph], writes=[h1])
                    if j == 0:
                        pk = pX[1]
                        kb.op("pe", lambda e: e.matmul(pk[:, 0:127], W2d[:].rearrange("p a d -> p (a d)"), h1[:, 0:127], start=True, stop=True), reads=[W2d, h1], writes=[pk])
                        kb.op("act", lambda e: e.activation(out=xb[:, 0:127], in_=pk[:, 0:127], func=AF.Copy), reads=[pk], writes=[xb])
                        kb.op("dve", lambda e: e.tensor_tensor(out=a1[:, 0:127], in0=pk[:, 0:127], in1=ccos[:], op=ALU.mult), reads=[pk, ccos], writes=[a1])
                        pk2 = pO[0]
                        kb.op("pe", lambda e: e.matmul(pk2[:, 0:127], swp[:], xb[:, 0:127], start=True, stop=True), reads=[swp, xb], writes=[pk2])
                        kb.op("dve", lambda e: e.tensor_tensor(out=a2[:, 0:127], in0=pk2[:, 0:127], in1=csin[:], op=ALU.mult), reads=[pk2, csin], writes=[a2])
                        kb.op("dve", lambda e: e.tensor_tensor(out=kcT[g][:, 0:127], in0=a1[:, 0:127], in1=a2[:, 0:127], op=ALU.add), reads=[a1, a2], writes=[kcT[g]])
                    else:
                        pv = pX[1]
                        kb.op("pe", lambda e: e.matmul(pv[0:127, 0:64], h1[:, 0:127], W2d[:, 0, :], start=True, stop=True), reads=[W2d, h1], writes=[pv])
                        kb.op("pool", lambda e: e.memset(Rg[g][:, 64:65], 1.0), writes=[Rg[g]])
                        kb.op("act", lambda e: e.activation(out=Rg[g][0:127, 0:64], in_=pv[0:127, 0:64], func=AF.Copy), reads=[pv], writes=[Rg[g]])
                        kb.op("dve", lambda e: e.tensor_copy(out=Rg[g][:, 65:97], in_=ovl[:]), reads=[ovl], writes=[Rg[g]])
        pc = [p.sb(es, "ns_pc%d" % i, [128, 512], BF16) for i in range(2)]
        rd = [p.sb(es, "ns_rd%d" % i, [128, 4], F32) for i in range(2)]
        steps = []
        for h in heads:
            for G in range(4):
                def mk(h=h, G=G, idx=len(steps)):
                    g, hp, base = h // 4, h // 2, 64 * (h % 2)
                    ps, po, pc_, rd_ = pS[idx % 2], pO[idx % 2], pc[idx % 2], rd[idx % 2]

                    def s1():
                        kb.op("pe", lambda e: e.matmul(ps[0:127, :], kcT[g][base:base + 64, 0:127], qT[base:base + 64, hp, 512 * G:512 * G + 512], start=True, stop=True), reads=[kcT[g], qT], writes=[ps])
                        kb.op("act", lambda e: e.activation(out=pc_[0:127, :], in_=ps[0:127, :], func=AF.Exp), reads=[ps], writes=[pc_])
                        kb.op("pool", lambda e: e.tensor_tensor(out=pc_[0:127, :], in0=pc_[0:127, :], in1=maskCmp[0:127, 512 * G:512 * G + 512], op=ALU.mult), reads=[pc_, maskCmp], writes=[pc_])

                    def s2():
                        for j in range(4):
                            kb.op("pe", lambda e: e.matmul(po[:, 97 * j:97 * j + 97], pc_[0:127, 128 * j:128 * j + 128], Rg[g][0:127, :], start=True, stop=True), reads=[pc_, Rg[g]], writes=[po])
                        pov = po[:, 0:388].rearrange("p (j f) -> p j f", f=97)
                        kb.op("dve", lambda e: e.tensor_scalar(out=rd_[:], in0=pov[:, :, 64], scalar1=1e-30, scalar2=None, op0=ALU.max), reads=[po], writes=[rd_])
                        kb.op("dve", lambda e: e.reciprocal(out=rd_[:], in_=rd_[:]), reads=[rd_], writes=[rd_])
                        for j in range(4):
                            tt = 4 * G + j
                            kb.op("dve", lambda e: e.tensor_scalar(out=yb[:, tt, 64 * h:64 * h + 64], in0=po[:, 97 * j:97 * j + 64], scalar1=rd_[:, j:j + 1], scalar2=gt[:, tt, 3 * h:3 * h + 1], op0=ALU.mult, op1=ALU.mult), reads=[po, rd_, gt], writes=[yb])
                            kb.op("dve", lambda e: e.scalar_tensor_tensor(out=impacc[:, tt, g, :], in0=po[:, 97 * j + 65:97 * j + 97], scalar=rd_[:, j:j + 1], in1=impacc[:, tt, g, :], op0=ALU.mult, op1=ALU.add), reads=[po, rd_, impacc], writes=[impacc])
                    return (s1, s2)
                steps.append(mk())
        pipeline(steps, 2)
        dbg = p.d.get("dbg_imp")
        if dbg is not None:
            kb.dma("sp", dbg.h.rearrange("(t p) g n -> p t g n", p=128), impacc[:], reads=[impacc], writes=[dbg])
        selT = [p.sb(es, "ns_selT%d" % g, [32, S], BF16) for g in range(2)]
        m8 = p.sb(es, "ns_m8", [128, 8], F32)
        selb = p.sb(es, "ns_selb", [128, 4, 32], BF16)
        for g in range(2):
            kb.op("dve", lambda e: e.tensor_tensor(out=impacc[:, :, g, :], in0=impacc[:, :, g, :], in1=selMul[:], op=ALU.mult), reads=[impacc, selMul], writes=[impacc])
            kb.op("dve", lambda e: e.tensor_tensor(out=impacc[:, :, g, :], in0=impacc[:, :, g, :], in1=selAdd[:], op=ALU.add), reads=[impacc, selAdd], writes=[impacc])
            for G in range(4):
                for j in range(4):
                    tt = 4 * G + j
                    kb.op("dve", lambda e: e.max(out=m8[:], in_=impacc[:, tt, g, :]), reads=[impacc], writes=[m8])
                    kb.op("dve", lambda e: e.tensor_scalar(out=selb[:, j, :], in0=impacc[:, tt, g, :], scalar1=m8[:, 7:8], scalar2=1.0, op0=ALU.is_ge, op1=ALU.subtract), reads=[impacc, m8], writes=[selb])
                for j in range(4):
                    kb.op("pe", lambda e: e.transpose(pXb[0:32, 128 * j:128 * j + 128], selb[:, j, :], p.ident[:]), reads=[selb, p.ident], writes=[pXb])
                kb.op("act", lambda e: e.activation(out=selT[g][:, 512 * G:512 * G + 512], in_=pXb[0:32, 0:512], func=AF.Copy), reads=[pXb], writes=[selT[g]])
        Pt = [p.sb(es, "ns_P%d" % i, [128, 512], BF16) for i in range(2)]
        Oacc = [p.sb(es, "ns_Oa%d" % i, [128, 4, 65], F32) for i in range(2)]
        rdg = [p.sb(es, "ns_rdg%d" % i, [128, 4], F32) for i in range(2)]
        Pt = Pt + [p.sb(es, "ns_P2", [128, 512], BF16)]
        oc = 0
        branches = ([("slc", 1)] if do_slc else []) + ([("win", 2)] if do_win else [])
        steps = []
        for h in heads:
            for (nm, gi) in branches:
                for G in range(4):
                    oa, rg_ = Oacc[oc % 2], rdg[oc % 2]
                    oc += 1
                    kbs = list(range(0, 4 * G + 4) if nm == "slc" else range(max(0, 4 * G - 4), 4 * G + 4))
                    for kb_ in kbs:
                        def mk(h=h, nm=nm, gi=gi, G=G, kb_=kb_, oa=oa, rg_=rg_, idx=len(steps), firstk=(kb_ == kbs[0]), lastk=(kb_ == kbs[-1])):
                            g, hp, base = h // 4, h // 2, 64 * (h % 2)
                            kd, V = kdup[(nm, g)], Vs[nm]
                            qs = qT[base:base + 64, hp, 512 * G:512 * G + 512]
                            i = kb_ - 4 * G
                            ps, po, P_ = pS[idx % 2], pO[idx % 2], Pt[idx % 3]
                            ks = kd[base:base + 64, 128 * kb_:128 * kb_ + 128]
                            if i >= 0:
                                js = range(i, 4)
                            elif nm == "win":
                                js = range(0, i + 5)
                            else:
                                js = range(4)

                            def s1():
                                if idx % 28 == 5:
                                    p.bg()
                                if firstk:
                                    kb.op("pool", lambda e: e.memset(oa[:], 0.0), writes=[oa])
                                if nm == "slc":
                                    kb.op("pe", lambda e: e.matmul(ps[:], ks, qs, start=True, stop=False), reads=[kd, qT], writes=[ps])
                                    kb.op("pe", lambda e: e.matmul(ps[:], Ebig[:, kb_, :], selT[g][:, 512 * G:512 * G + 512], start=False, stop=True), reads=[Ebig, selT[g]], writes=[ps])
                                else:
                                    kb.op("pe", lambda e: e.matmul(ps[:], ks, qs, start=True, stop=True), reads=[kd, qT], writes=[ps])
                                kb.op("act", lambda e: e.activation(out=P_[:], in_=ps[:], func=AF.Exp), reads=[ps], writes=[P_])
                                if i >= 0:
                                    kb.op("pool", lambda e: e.tensor_tensor(out=P_[:], in0=P_[:], in1=maskC[:, i, :], op=ALU.mult), reads=[P_, maskC], writes=[P_])
                                elif nm == "win":
                                    kb.op("pool", lambda e: e.tensor_tensor(out=P_[:], in0=P_[:], in1=maskCn[:, i + 4, :], op=ALU.mult), reads=[P_, maskCn], writes=[P_])

                            def s2():
                                for j in js:
                                    kb.op("pe", lambda e: e.matmul(po[:, 65 * j:65 * j + 65], P_[:, 128 * j:128 * j + 128], V[:, kb_, g, :], start=True, stop=True), reads=[P_, V], writes=[po])
                                j0, j1 = js[0], js[-1] + 1
                                kb.op("dve", lambda e: e.tensor_tensor(out=oa[:, j0:j1, :], in0=oa[:, j0:j1, :], in1=po[:, 65 * j0:65 * j1].rearrange("p (j f) -> p j f", f=65), op=ALU.add), reads=[oa, po], writes=[oa])
                                if lastk:
                                    kb.op("dve", lambda e: e.reciprocal(out=rg_[:], in_=oa[:, :, 64]), reads=[oa], writes=[rg_])
                                    kb.op("dve", lambda e: e.tensor_tensor(out=rg_[:], in0=rg_[:], in1=gt[:, 4 * G:4 * G + 4, 3 * h + gi], op=ALU.mult), reads=[rg_, gt], writes=[rg_])
                                    for j in range(4):
                                        tt = 4 * G + j
                                        kb.op("dve", lambda e: e.scalar_tensor_tensor(out=yb[:, tt, 64 * h:64 * h + 64], in0=oa[:, j, 0:64], scalar=rg_[:, j:j + 1], in1=yb[:, tt, 64 * h:64 * h + 64], op0=ALU.mult, op1=ALU.add), reads=[oa, rg_, yb], writes=[yb])
                            return (s1, s2)
                        steps.append(mk())
        pipeline(steps, 2)
        dbg = p.d.get("dbg_yb")
        if dbg is not None:
            kb.dma("sp", dbg.h.rearrange("(t p) f -> p t f", p=128), yb[:], reads=[yb], writes=[dbg])
        gw = go_tiles(p, es, "nsgo", l, 1)
        for tt in range(NT):
            emit_group_out(p, gw, yb[:, tt, :], yb, 1, tt)
        kb.barrier()


def gen_s5(p, l, chunks=range(16), HS=S // 2, nps=2):
    kb = p.kb
    PI = math.pi
    fm = p.dram("fm", (30 * 128, S), BF16)
    ysT = p.dram("ysT", (D, S), BF16)
    inp = lambda n: p.dram(n, IN_SHAPES[n], F32, kind="ExternalInput")
    lam_re, lam_im, log_dt = inp("s5_lambda_re"), inp("s5_lambda_im"), inp("s5_log_dt")
    b_re, b_im, c_re, c_im = inp("s5_b_re"), inp("s5_b_im"), inp("s5_c_re"), inp("s5_c_im")
    d_skip, b_glu, gain = inp("s5_d"), inp("s5_b_glu"), inp("mix_norm_g")
    wglu = p.wb[("s5_w_glu", l)]
    with ExitStack() as es:
        uT = p.sb(es, "s5_uT", [128, 4, S], BF16)
        kb.dma("sp", uT[:], fm[FM_U * 128:(FM_U + 4) * 128, :].rearrange("(c p) t -> p c t", p=128), reads=[fm], writes=[uT])
        yT = p.sb(es, "s5_yT", [128, 4, S], F32)
        dT = p.sb(es, "s5_dT", [128, 4], F32)
        kb.dma("sp", dT[:], d_skip[l].rearrange("(q j) c -> (j c) q", j=8), writes=[dT], allow_slow_non_contiguous=True)
        BT = [p.sb(es, "s5_BT%d" % i, [128, 4, 2, 64], BF16) for i in range(2)]
        BTz = [p.sb(es, "s5_BTz%d" % i, [128, 4, 2, 64], BF16) for i in range(2)]
        CT = [p.sb(es, "s5_CT%d" % i, [128, 4, 128], BF16) for i in range(2)]
        CTz = [p.sb(es, "s5_CTz%d" % i, [128, 4, 64], BF16) for i in range(2)]
        for q in range(4):
            kb.op("dve", lambda e: e.tensor_scalar(out=yT[:, q, :], in0=uT[:, q, :], scalar1=dT[:, q:q + 1], scalar2=None, op0=ALU.mult), reads=[uT, dT], writes=[yT])
        r_p = p.sb(es, "s5_rp", [128, 16], F32)
        th_p = p.sb(es, "s5_thp", [128, 16], F32)
        with ExitStack() as es2:
            sb2 = lambda n, shp, dt=F32: p.sb(es2, "s5p_" + n, shp, dt)
            pX = p.ps(es2, "s5p_pX", [128, 1024], BF16)
            mask2 = sb2("mask2", [128, 2])
            kb.dma("sp", mask2[:], p.dram("c_mask2", (128, 2), F32, kind="ExternalInput")[:, :], writes=[mask2])
            mask2z = sb2("mask2z", [128, 2])
            kb.dma("sp", mask2z[:], p.dram("c_mask2z", (128, 2), F32, kind="ExternalInput")[:, :], writes=[mask2z])
            mask3 = [sb2("mask3%d" % i, [128, 128]) for i in range(2)]
            kb.dma("sp", mask3[0][:], p.dram("c_mask3", (128, 128), F32, kind="ExternalInput")[:, :], writes=[mask3[0]])
            kb.dma("sp", mask3[1][:], p.dram("c_mask3n", (128, 128), F32, kind="ExternalInput")[:, :], writes=[mask3[1]])

            def prep(P_, G_, lr_src, li_src, dt_loads, pfx):
                t = {k: sb2(pfx + k, [P_, G_]) for k in ("lr", "li", "dt", "mag", "ang", "tmp")}
                kb.dma("sp", t["lr"][:], lr_src, writes=[t["lr"]], allow_slow_non_contiguous=True)
                kb.dma("sp", t["li"][:], li_src, writes=[t["li"]], allow_slow_non_contiguous=True)
                for (dst_sl, src) in dt_loads:
                    kb.dma("sp", t["dt"][dst_sl, :], src, writes=[t["dt"]])
                kb.op("dve", lambda e: e.tensor_scalar(out=t["lr"][:], in0=t["lr"][:], scalar1=-1e-4, scalar2=None, op0=ALU.min), reads=[t["lr"]], writes=[t["lr"]])
                kb.op("act", lambda e: e.activation(out=t["dt"][:], in_=t["dt"][:], func=AF.Exp), reads=[t["dt"]], writes=[t["dt"]])
                kb.op("dve", lambda e: e.tensor_tensor(out=t["tmp"][:], in0=t["lr"][:], in1=t["dt"][:], op=ALU.mult), reads=[t["lr"], t["dt"]], writes=[t["tmp"]])
                kb.op("act", lambda e: e.activation(out=t["mag"][:], in_=t["tmp"][:], func=AF.Exp), reads=[t["tmp"]], writes=[t["mag"]])
                kb.op("dve", lambda e: e.tensor_tensor(out=t["ang"][:], in0=t["li"][:], in1=t["dt"][:], op=ALU.mult), reads=[t["li"], t["dt"]], writes=[t["ang"]])
                return t

            tp = prep(128, 16,
                      lam_re[l].rearrange("g s -> (g s)").rearrange("(c q) -> q c", q=128),
                      lam_im[l].rearrange("g s -> (g s)").rearrange("(c q) -> q c", q=128),
                      [(slice(64 * two, 64 * two + 64), log_dt[l].rearrange("(c two) -> two c", two=2)[two].partition_broadcast(64)) for two in range(2)], "p")
            kb.op("dve", lambda e: e.tensor_copy(out=r_p[:], in_=tp["mag"][:]), reads=[tp["mag"]], writes=[r_p])
            kb.op("dve", lambda e: e.tensor_copy(out=th_p[:], in_=tp["ang"][:]), reads=[tp["ang"]], writes=[th_p])
            ts = prep(64, 32, lam_re[l].rearrange("g s -> s g"), lam_im[l].rearrange("g s -> s g"),
                      [(slice(0, 64), log_dt[l].partition_broadcast(64))], "s")
            sn, cs = sb2("sn", [64, 32]), sb2("cs", [64, 32])
            ki0 = sb2("ki0", [64, 32], mybir.dt.int32)
            kb.op("dve", lambda e: e.tensor_scalar(out=ki0[:], in0=ts["ang"][:], scalar1=1.0 / (2 * PI), scalar2=None, op0=ALU.mult), reads=[ts["ang"]], writes=[ki0])
            kb.op("dve", lambda e: e.tensor_copy(out=sn[:], in_=ki0[:]), reads=[ki0], writes=[sn])
            kb.op("dve", lambda e: e.scalar_tensor_tensor(out=sn[:], in0=sn[:], scalar=-2 * PI, in1=ts["ang"][:], op0=ALU.mult, op1=ALU.add), reads=[sn, ts["ang"]], writes=[sn])
            kb.op("dve", lambda e: e.tensor_scalar(out=sn[:], in0=sn[:], scalar1=-PI, scalar2=PI, op0=ALU.max, op1=ALU.min), reads=[sn], writes=[sn])
            kb.op("act", lambda e: e.activation(out=cs[:], in_=sn[:], func=AF.Abs), reads=[sn], writes=[cs])
            kb.op("act", lambda e: e.activation(out=sn[:], in_=sn[:], func=AF.Sin), reads=[sn], writes=[sn])
            kb.op("act", lambda e: e.activation(out=cs[:], in_=cs[:], func=AF.Sin, scale=-1.0, bias=0.5 * PI), reads=[cs], writes=[cs])
            are, aim, den, zre, zim, t1, t2 = [sb2(n, [64, 32]) for n in ("are", "aim", "den", "zre", "zim", "t1", "t2")]
            tt_ = lambda o, a, b, op: kb.op("dve", lambda e: e.tensor_tensor(out=o[:], in0=a[:], in1=b[:], op=op), reads=[a, b], writes=[o])
            tt_(are, ts["mag"], cs, ALU.mult)
            tt_(aim, ts["mag"], sn, ALU.mult)
            kb.op("dve", lambda e: e.tensor_scalar(out=are[:], in0=are[:], scalar1=-1.0, scalar2=None, op0=ALU.add), reads=[are], writes=[are])
            tt_(t1, ts["lr"], ts["lr"], ALU.mult)
            tt_(t2, ts["li"], ts["li"], ALU.mult)
            tt_(den, t1, t2, ALU.add)
            kb.op("dve", lambda e: e.reciprocal(out=den[:], in_=den[:]), reads=[den], writes=[den])
            tt_(t1, are, ts["lr"], ALU.mult)
            tt_(t2, aim, ts["li"], ALU.mult)
            tt_(zre, t1, t2, ALU.add)
            tt_(zre, zre, den, ALU.mult)
            tt_(t1, aim, ts["lr"], ALU.mult)
            tt_(t2, are, ts["li"], ALU.mult)
            tt_(zim, t1, t2, ALU.subtract)
            tt_(zim, zim, den, ALU.mult)
            Bs = [sb2("Bs%d" % i, [64, 32, 16]) for i in range(2)]
            kb.dma("sp", Bs[0][:], b_re[l].rearrange("g s c -> s g c"), writes=[Bs[0]])
            kb.dma("sp", Bs[1][:], b_im[l].rearrange("g s c -> s g c"), writes=[Bs[1]])
            m1, m2 = sb2("m1", [64, 32, 16]), sb2("m2", [64, 32, 16])
            bb = [sb2("bb%d" % i, [64, 32, 16], BF16) for i in range(2)]
            zb = lambda z: z[:].unsqueeze(2).to_broadcast([64, 32, 16])
            kb.op("dve", lambda e: e.tensor_tensor(out=m1[:], in0=Bs[0][:], in1=zb(zre), op=ALU.mult), reads=[Bs[0], zre], writes=[m1])
            kb.op("dve", lambda e: e.tensor_tensor(out=m2[:], in0=Bs[1][:], in1=zb(zim), op=ALU.mult), reads=[Bs[1], zim], writes=[m2])
            kb.op("dve", lambda e: e.tensor_tensor(out=bb[0][:], in0=m1[:], in1=m2[:], op=ALU.subtract), reads=[m1, m2], writes=[bb[0]])
            kb.op("dve", lambda e: e.tensor_tensor(out=m1[:], in0=Bs[1][:], in1=zb(zre), op=ALU.mult), reads=[Bs[1], zre], writes=[m1])
            kb.op("dve", lambda e: e.tensor_tensor(out=m2[:], in0=Bs[0][:], in1=zb(zim), op=ALU.mult), reads=[Bs[0], zim], writes=[m2])
            kb.op("dve", lambda e: e.tensor_tensor(out=bb[1][:], in0=m1[:], in1=m2[:], op=ALU.add), reads=[m1, m2], writes=[bb[1]])
            for ri in range(2):
                for q in range(4):
                    kb.op("pe", lambda e: e.transpose(pX[:, 0:64], bb[ri][:, 8 * q:8 * q + 8, :].rearrange("s j c -> s (j c)"), p.ident[0:64, 0:64]), reads=[bb[ri], p.ident], writes=[pX])
                    for two in range(2):
                        kb.op("dve", lambda e: e.tensor_scalar(out=BT[ri][:, q, two, :], in0=pX[:, 0:64], scalar1=mask2[:, two:two + 1], scalar2=None, op0=ALU.mult), reads=[pX, mask2], writes=[BT[ri]])
                        kb.op("dve", lambda e: e.tensor_scalar(out=BTz[ri][:, q, two, :], in0=pX[:, 0:64], scalar1=mask2z[:, two:two + 1], scalar2=None, op0=ALU.mult), reads=[pX, mask2z], writes=[BTz[ri]])
            Cn = sb2("Cn", [128, 4, 64])
            Cd = sb2("Cd", [128, 4, 2, 64], BF16)
            for ri, csrc in enumerate((c_re, c_im)):
                kb.dma("sp", Cn[:], csrc[l].rearrange("(q j) c s -> (j c) q s", j=8), writes=[Cn])
                for two in range(2):
                    kb.op("dve", lambda e: e.tensor_copy(out=Cd[:, :, two, :], in_=Cn[:]), reads=[Cn], writes=[Cd])
                for q in range(4):
                    kb.op("pe", lambda e: e.transpose(pX[:, 0:128], Cd[:, q, :, :].rearrange("p a s -> p (a s)"), p.ident[:]), reads=[Cd, p.ident], writes=[pX])
                    kb.op("dve", lambda e: e.tensor_tensor(out=CT[ri][:, q, :], in0=pX[:, 0:128], in1=mask3[ri][:], op=ALU.mult), reads=[pX, mask3[ri]], writes=[CT[ri]])
                    kb.op("pool", lambda e: e.memset(CTz[ri][:, q, 0:32], 0.0), writes=[CTz[ri]])
                    kb.op("dve", lambda e: e.tensor_copy(out=CTz[ri][:, q, 32:64], in_=CT[ri][:, q, 96:128]), reads=[CT[ri]], writes=[CTz[ri]])
            kb.barrier()
        with ExitStack() as es2:
            sb2 = lambda n, shp, dt=F32: p.sb(es2, "s5m_" + n, shp, dt)
            iota = sb2("iota", [128, S])
            kb.dma("sp", iota[:], p.dram("c_iota", (128, S), F32, kind="ExternalInput")[:, :], writes=[iota])
            nparts = S // HS
            wri, wii, wi, wr = [sb2(n, [128, HS]) for n in ("wri", "wii", "wi", "wr")]
            phs = [sb2("ph%d" % i, [128, HS]) for i in range(2)]
            sns = [sb2("sn%d" % i, [128, HS]) for i in range(2)]
            css = [sb2("cs%d" % i, [128, HS]) for i in range(2)]
            xre, xim = sb2("xre", [128, HS], BF16), sb2("xim", [128, HS], BF16)
            ki = sb2("ki", [128, HS], mybir.dt.int32)
            tmp = [sb2("t%d" % i, [128, 512]) for i in range(4)]
            car = sb2("car", [128, 2])
            psBr = [p.ps(es2, "s5_pBr%d" % i, [128, 512]) for i in range(nps)]
            psBi = [p.ps(es2, "s5_pBi%d" % i, [128, 512]) for i in range(nps)]
            psY = [p.ps(es2, "s5_pY%d" % i, [128, 512]) for i in range(nps)]
            steps = []
            for ch in chunks:
                for half in range(nparts):
                    def mk(ch=ch, half=half, idx=len(steps)):
                        q, rb = ch // 4, 32 * (ch % 4)
                        rows = slice(rb, rb + 32) if rb < 96 else slice(64, 128)
                        t0 = half * HS
                        ph, sn, cs = phs[idx % 2], sns[idx % 2], css[idx % 2]

                        def s1():
                            if idx % (nparts) == 1:
                                p.bg()
                            kb.op("dve", lambda e: e.tensor_scalar(out=ph[:], in0=iota[:, t0:t0 + HS], scalar1=th_p[:, ch:ch + 1], scalar2=None, op0=ALU.mult), reads=[iota, th_p], writes=[ph])
                            kb.op("dve", lambda e: e.tensor_scalar(out=ki[:], in0=ph[:], scalar1=1.0 / (2 * PI), scalar2=None, op0=ALU.mult), reads=[ph], writes=[ki])
                            kb.op("dve", lambda e: e.tensor_copy(out=sn[:], in_=ki[:]), reads=[ki], writes=[sn])
                            kb.op("dve", lambda e: e.scalar_tensor_tensor(out=sn[:], in0=sn[:], scalar=-2 * PI, in1=ph[:], op0=ALU.mult, op1=ALU.add), reads=[sn, ph], writes=[sn])
                            kb.op("dve", lambda e: e.tensor_scalar(out=sn[:], in0=sn[:], scalar1=-PI, scalar2=PI, op0=ALU.max, op1=ALU.min), reads=[sn], writes=[sn])
                            kb.op("act", lambda e: e.activation(out=cs[:], in_=sn[:], func=AF.Abs), reads=[sn], writes=[cs])
                            kb.op("act", lambda e: e.activation(out=sn[:], in_=sn[:], func=AF.Sin), reads=[sn], writes=[sn])
                            kb.op("act", lambda e: e.activation(out=cs[:], in_=cs[:], func=AF.Sin, scale=-1.0, bias=0.5 * PI), reads=[cs], writes=[cs])

                        def s2():
                            for tg in range(HS // 512):
                                tk = slice(t0 + 512 * tg, t0 + 512 * tg + 512)
                                lk = slice(512 * tg, 512 * tg + 512)
                                pr, pi_ = psBr[tg % nps], psBi[tg % nps]
                                Bm = BTz if rb == 96 else BT
                                kb.op("pe", lambda e: e.matmul(pr[:], Bm[0][rows, q, :, :].rearrange("p a s -> p (a s)"), uT[rows, q, tk], start=True, stop=True), reads=[Bm[0], uT], writes=[pr])
                                kb.op("pe", lambda e: e.matmul(pi_[:], Bm[1][rows, q, :, :].rearrange("p a s -> p (a s)"), uT[rows, q, tk], start=True, stop=True), reads=[Bm[1], uT], writes=[pi_])
                                a, b, c, d = tmp
                                kb.op("dve", lambda e: e.tensor_tensor(out=a[:], in0=pr[:], in1=cs[:, lk], op=ALU.mult), reads=[pr, cs], writes=[a])
                                kb.op("dve", lambda e: e.tensor_tensor(out=b[:], in0=pi_[:], in1=sn[:, lk], op=ALU.mult), reads=[pi_, sn], writes=[b])
                                kb.op("pool", lambda e: e.tensor_tensor(out=wri[:, lk], in0=a[:], in1=b[:], op=ALU.add), reads=[a, b], writes=[wri])
                                kb.op("dve", lambda e: e.tensor_tensor(out=c[:], in0=pi_[:], in1=cs[:, lk], op=ALU.mult), reads=[pi_, cs], writes=[c])
                                kb.op("dve", lambda e: e.tensor_tensor(out=d[:], in0=pr[:], in1=sn[:, lk], op=ALU.mult), reads=[pr, sn], writes=[d])
                                kb.op("pool", lambda e: e.tensor_tensor(out=wii[:, lk], in0=c[:], in1=d[:], op=ALU.subtract), reads=[c, d], writes=[wii])
                            rbc = r_p[:, ch:ch + 1].to_broadcast([128, HS])
                            ini_r = 0.0 if half == 0 else car[:, 0:1]
                            ini_i = 0.0 if half == 0 else car[:, 1:2]
                            kb.op("dve", lambda e: e.tensor_tensor_scan(out=wr[:], data0=rbc, data1=wri[:], initial=ini_r, op0=ALU.mult, op1=ALU.add), reads=[r_p, wri, car], writes=[wr])
                            kb.op("dve", lambda e: e.tensor_tensor_scan(out=wi[:], data0=rbc, data1=wii[:], initial=ini_i, op0=ALU.mult, op1=ALU.add), reads=[r_p, wii, car], writes=[wi])
                            if half < nparts - 1:
                                kb.op("act", lambda e: e.activation(out=car[:, 0:1], in_=wr[:, HS - 1:HS], func=AF.Copy), reads=[wr], writes=[car])
                                kb.op("act", lambda e: e.activation(out=car[:, 1:2], in_=wi[:, HS - 1:HS], func=AF.Copy), reads=[wi], writes=[car])
                            kb.op("dve", lambda e: e.tensor_tensor(out=wri[:], in0=wr[:], in1=cs[:], op=ALU.mult), reads=[wr, cs], writes=[wri])
                            kb.op("pool", lambda e: e.tensor_tensor(out=wii[:], in0=wi[:], in1=sn[:], op=ALU.mult), reads=[wi, sn], writes=[wii])
                            kb.op("dve", lambda e: e.tensor_tensor(out=xre[:], in0=wri[:], in1=wii[:], op=ALU.subtract), reads=[wri, wii], writes=[xre])
                            kb.op("pool", lambda e: e.tensor_tensor(out=wri[:], in0=wi[:], in1=cs[:], op=ALU.mult), reads=[wi, cs], writes=[wri])
                            kb.op("dve", lambda e: e.tensor_tensor(out=wii[:], in0=wr[:], in1=sn[:], op=ALU.mult), reads=[wr, sn], writes=[wii])
                            kb.op("pool", lambda e: e.tensor_tensor(out=xim[:], in0=wri[:], in1=wii[:], op=ALU.add), reads=[wri, wii], writes=[xim])
                            for tg in range(HS // 512):
                                tk = slice(t0 + 512 * tg, t0 + 512 * tg + 512)
                                lk = slice(512 * tg, 512 * tg + 512)
                                py = psY[tg % nps]
                                c0 = CTz[0][:, q, :] if rb == 96 else CT[0][:, q, rb:rb + 32]
                                c1 = CTz[1][:, q, :] if rb == 96 else CT[1][:, q, rb:rb + 32]
                                kb.op("pe", lambda e: e.matmul(py[rows, :], c0, xre[:, lk], start=True, stop=False), reads=[CT[0], CTz[0], xre], writes=[py])
                                kb.op("pe", lambda e: e.matmul(py[rows, :], c1, xim[:, lk], start=False, stop=True), reads=[CT[1], CTz[1], xim], writes=[py])
                                kb.op("dve", lambda e: e.tensor_tensor(out=yT[rows, q, tk], in0=yT[rows, q, tk], in1=py[rows, :], op=ALU.add), reads=[yT, py], writes=[yT])
                        return (s1, s2)
                    steps.append(mk())
            yield (steps, 2)
            kb.barrier()
        dbg = p.d.get("dbg_s5y")
        if dbg is not None:
            kb.dma("sp", dbg.h.rearrange("(q p) t -> p q t", p=128), yT[:], reads=[yT], writes=[dbg])
        with ExitStack() as es2:
            sb2 = lambda n, shp, dt=F32: p.sb(es2, "s5g_" + n, shp, dt)
            gT = sb2("gT", [128, 4, S], BF16)
            Wg = sb2("Wg", [128, 4, 512], BF16)
            kb.dma("sp", Wg[:], wglu.h.rearrange("(k p) n -> p k n", p=128), reads=[wglu], writes=[Wg])
            bg = sb2("bg", [128, 4])
            kb.dma("sp", bg[:], b_glu[l].rearrange("(q p) -> p q", p=128), writes=[bg], allow_slow_non_contiguous=True)
            gn = sb2("gn", [128, 4])
            kb.dma("sp", gn[:], gain[l, 0].rearrange("(q p) -> p q", p=128), writes=[gn], allow_slow_non_contiguous=True)
            ones = load_const_bf16(p, es2, "c_ones", (128, 128))
            sig = [sb2("sig%d" % i, [128, 512]) for i in range(2)]
            sq = sb2("sq", [128, 4, 512], BF16)
            rs = sb2("rs", [128, 512])
            yo = [sb2("yo%d" % i, [128, 512], BF16) for i in range(2)]
            pG = [p.ps(es2, "s5_pG%d" % i, [128, 512]) for i in range(2)]
            pN = p.ps(es2, "s5_pN", [128, 512])
            for q in range(4):
                kb.op("act", lambda e: e.activation(out=yT[:, q, :], in_=yT[:, q, :], func=AF.Gelu_apprx_tanh), reads=[yT], writes=[yT])
                kb.op("dve", lambda e: e.tensor_copy(out=gT[:, q, :], in_=yT[:, q, :]), reads=[yT], writes=[gT])
            cnt = 0
            for tg in range(4):
                tk = slice(512 * tg, 512 * tg + 512)
                for nq in range(4):
                    pg, sg = pG[cnt % 2], sig[cnt % 2]
                    cnt += 1
                    for kq in range(4):
                        kb.op("pe", lambda e: e.matmul(pg[:], Wg[:, kq, 128 * nq:128 * nq + 128], gT[:, kq, tk], start=(kq == 0), stop=(kq == 3)), reads=[Wg, gT], writes=[pg])
                    kb.op("act", lambda e: e.activation(out=sg[:], in_=pg[:], func=AF.Sigmoid, bias=bg[:, nq:nq + 1]), reads=[pg, bg], writes=[sg])
                    kb.op("dve", lambda e: e.tensor_tensor(out=yT[:, nq, tk], in0=yT[:, nq, tk], in1=sg[:], op=ALU.mult), reads=[yT, sg], writes=[yT])
                    kb.op("act", lambda e: e.activation(out=sq[:, nq, :], in_=yT[:, nq, tk], func=AF.Square), reads=[yT], writes=[sq])
                for nq in range(4):
                    kb.op("pe", lambda e: e.matmul(pN[:], ones[:], sq[:, nq, :], start=(nq == 0), stop=(nq == 3)), reads=[ones, sq], writes=[pN])
                kb.op("act", lambda e: e.activation(out=rs[:], in_=pN[:], func=AF.Sqrt, bias=1e-6, scale=1.0 / 512), reads=[pN], writes=[rs])
                kb.op("dve", lambda e: e.reciprocal(out=rs[:], in_=rs[:]), reads=[rs], writes=[rs])
                for nq in range(4):
                    y_ = yo[cnt % 2]
                    cnt += 1
                    kb.op("dve", lambda e: e.scalar_tensor_tensor(out=y_[:], in0=yT[:, nq, tk], scalar=gn[:, nq:nq + 1], in1=rs[:], op0=ALU.mult, op1=ALU.mult), reads=[yT, gn, rs], writes=[y_])
                    kb.dma("pool", ysT[128 * nq:128 * nq + 128, tk], y_[:], reads=[y_], writes=[ysT])
            dbg = p.d.get("dbg_s5ya")
            if dbg is not None:
                kb.dma("sp", dbg.h.rearrange("(q p) t -> p q t", p=128), yT[:], reads=[yT], writes=[dbg])
            kb.barrier()


def phase_s5(p, l, **kw):
    drive(gen_s5(p, l, **kw))


def phase_sb(p, l, **kw):
    drive(gen_sb(p, l, **kw))


def resid_ln(p, lnw, hold, pfs, tt, out_dram=None, out_res=None):
    kb = p.kb
    for cg in range(4):
        kb.op("dve", lambda e: e.scalar_tensor_tensor(out=hold[:, 512 * cg:512 * cg + 512], in0=hold[:, 512 * cg:512 * cg + 512], scalar=ALPHA, in1=pfs[cg][:], op0=ALU.mult, op1=ALU.add), reads=[hold, pfs[cg]], writes=[hold])
    emit_ln(p, lnw, hold, tt, 1e-5, out_dram=out_dram, out_res=out_res)


def phase_outproj(p, l):
    kb = p.kb
    W = p.wb[("w_out", l)]
    ysT = p.dram("ysT", (D, S), BF16)
    g = p.dram("ln1_g", IN_SHAPES["ln1_g"], F32, kind="ExternalInput")
    b = p.dram("ln1_b", IN_SHAPES["ln1_b"], F32, kind="ExternalInput")
    with ExitStack() as es:
        Wt = p.sb(es, "op_W", [128, KC, D], BF16)
        for c in range(0, KC, 4):
            kb.dma("sp", Wt[:, c:c + 4, :], W[c * 128:(c + 4) * 128, :].rearrange("(c p) n -> p c n", p=128), reads=[W], writes=[Wt])
        lnw = ln_tiles(p, es, "op_ln")
        ln_load_gb(p, lnw, g[l], b[l])
        yst = [p.sb(es, "op_ys%d" % i, [128, KC, 128], BF16) for i in range(2)]
        hold = [p.sb(es, "op_h%d" % i, [128, D], F32) for i in range(2)]
        pf = [p.ps(es, "op_pf%d" % i, [128, 512]) for i in range(4)]
        hold.append(p.sb(es, "op_h2", [128, D], F32))
        hold.append(p.sb(es, "op_h3", [128, D], F32))
        yst.append(p.sb(es, "op_ys2", [128, KC, 128], BF16))
        steps = []
        for tt in range(NT):
            def mk(tt=tt):
                ys, ho = yst[tt % 3], hold[tt % 4]

                def s0():
                    kb.dma("sp", ys[:], ysT[:, tt * 128:(tt + 1) * 128].rearrange("(c p) t -> p c t", p=128), reads=[ysT], writes=[ys])
                    kb.dma("sp", ho[:], p.h[tt * 128:(tt + 1) * 128, :], reads=[p.hres[tt]], writes=[ho])

                def s1():
                    for cg in range(4):
                        for k in range(KC):
                            kb.op("pe", lambda e: e.matmul(pf[cg][:], ys[:, k, :], Wt[:, k, 512 * cg:512 * cg + 512], start=(k == 0), stop=(k == KC - 1)), reads=[ys, Wt], writes=[pf[cg]])
                        kb.op("dve", lambda e: e.scalar_tensor_tensor(out=ho[:, 512 * cg:512 * cg + 512], in0=ho[:, 512 * cg:512 * cg + 512], scalar=ALPHA, in1=pf[cg][:], op0=ALU.mult, op1=ALU.add), reads=[ho, pf[cg]], writes=[ho])
                    emit_ln_a(p, lnw, ho, tt, 1e-5)

                def s2():
                    emit_ln_b(p, lnw, tt)
                return (s0, s1, s2)
            steps.append(mk())
        pipeline(steps, 3)
        kb.barrier()


def phase_xattn(p, l):
    kb = p.kb
    wq, wkv, wo = p.wb[("xa_wq", l)], p.wb[("xa_wkv", l)], p.wb[("xa_wo", l)]
    mem = p.dram("mem", (MEM, D), F32, kind="ExternalInput")
    g = p.dram("ln2_g", IN_SHAPES["ln2_g"], F32, kind="ExternalInput")
    b = p.dram("ln2_b", IN_SHAPES["ln2_b"], F32, kind="ExternalInput")
    with ExitStack() as es0:
        o_all = p.sb(es0, "xa_o", [128, NT, 512], BF16)
        with ExitStack() as es:
            hT = load_hT(p, es)
            Wq = p.sb(es, "xa_Wq", [128, KC, 512], BF16)
            Wkv = p.sb(es, "xa_Wkv", [128, KC, 1024], BF16)
            kb.dma("sp", Wq[:], wq.h.rearrange("(c p) n -> p c n", p=128), reads=[wq], writes=[Wq])
            kb.dma("sp", Wkv[:], wkv.h.rearrange("(c p) n -> p c n", p=128), reads=[wkv], writes=[Wkv])
            memT = p.sb(es, "xa_memT", [128, KC, MEM], BF16)
            kT = p.sb(es, "xa_kT", [128, 4, MEM], BF16)
            Vx = p.sb(es, "xa_V", [128, 2, 4, 129], BF16)
            kb.op("pool", lambda e: e.memset(Vx[:, :, :, 128:129], 1.0), writes=[Vx])
            pA = [p.ps(es, "xa_pA%d" % i, [128, 512]) for i in range(2)]
            pS = [p.ps(es, "xa_pS%d" % i, [128, 512]) for i in range(2)]
            pO = [p.ps(es, "xa_pO%d" % i, [128, 512]) for i in range(2)]
            pTb = p.ps(es, "xa_pTb", [128, 2048], BF16)
            with ExitStack() as es2:
                mf = p.sb(es2, "xa_mf", [128, D], F32)
                mb_ = p.sb(es2, "xa_mb", [128, D], BF16)
                for mt in range(2):
                    kb.dma("sp", mf[:], mem[mt * 128:(mt + 1) * 128, :], writes=[mf])
                    kb.op("act", lambda e: e.activation(out=mb_[:], in_=mf[:], func=AF.Copy), reads=[mf], writes=[mb_])
                    for c in range(KC):
                        kb.op("pe", lambda e: e.transpose(pTb[:, 128 * c:128 * c + 128], mb_[:, 128 * c:128 * c + 128], p.ident[:]), reads=[mb_, p.ident], writes=[pTb])
                    kb.op("dve", lambda e: e.tensor_copy(out=memT[:, :, 128 * mt:128 * mt + 128], in_=pTb[:].rearrange("p (c t) -> p c t", c=KC)), reads=[pTb], writes=[memT])
            for h in range(4):
                pa = pA[h % 2]
                for k in range(KC):
                    kb.op("pe", lambda e: e.matmul(pa[:, 0:MEM], Wkv[:, k, 128 * h:128 * h + 128], memT[:, k, :], start=(k == 0), stop=(k == KC - 1)), reads=[Wkv, memT], writes=[pa])
                kb.op("act", lambda e: e.activation(out=kT[:, h, :], in_=pa[:, 0:MEM], func=AF.Copy), reads=[pa], writes=[kT])
            for mt in range(2):
                pa = pA[mt % 2]
                for k in range(KC):
                    kb.op("pe", lambda e: e.matmul(pa[:], memT[:, k, 128 * mt:128 * mt + 128], Wkv[:, k, 512:1024], start=(k == 0), stop=(k == KC - 1)), reads=[Wkv, memT], writes=[pa])
                kb.op("act", lambda e: e.activation(out=Vx[:, mt, :, 0:128], in_=pa[:].rearrange("p (h d) -> p h d", d=128), func=AF.Copy), reads=[pa], writes=[Vx])
            qs = [p.sb(es, "xa_qs%d" % i, [128, 512], BF16) for i in range(2)]
            Pm = [[p.sb(es, "xa_P%d_%d" % (i, m), [128, 512], BF16) for m in range(2)] for i in range(2)]
            rd = p.sb(es, "xa_rd", [128, 2], F32)
            qs.append(p.sb(es, "xa_qs2", [128, 512], BF16))
            Pm.append([p.sb(es, "xa_P2_%d" % m, [128, 512], BF16) for m in range(2)])
            rds = [rd, p.sb(es, "xa_rd2", [128, 2], F32)]
            steps = []
            for G in range(4):
                for h in range(4):
                    def mk(G=G, h=h, cnt=len(steps)):
                        pa, q_, P_ = pA[cnt % 2], qs[cnt % 3], Pm[cnt % 3]

                        def s1():
                            for k in range(KC):
                                kb.op("pe", lambda e: e.matmul(pa[:], Wq[:, k, 128 * h:128 * h + 128], hT[:, k, 512 * G:512 * G + 512], start=(k == 0), stop=(k == KC - 1)), reads=[Wq, hT], writes=[pa])
                            kb.op("act", lambda e: e.mul(q_[:], pa[:], 128 ** -0.5), reads=[pa], writes=[q_])

                        def s2():
                            for m in range(2):
                                kb.op("pe", lambda e: e.matmul(pS[m][:], kT[:, h, 128 * m:128 * m + 128], q_[:], start=True, stop=True), reads=[kT, q_], writes=[pS[m]])
                                kb.op("act", lambda e: e.activation(out=P_[m][:], in_=pS[m][:], func=AF.Exp), reads=[pS[m]], writes=[P_[m]])

                        def s3():
                            for jj in range(2):
                                po = pO[jj]
                                rd_ = rds[jj]
                                for j2 in range(2):
                                    j = 2 * jj + j2
                                    for m in range(2):
                                        kb.op("pe", lambda e: e.matmul(po[:, 129 * j2:129 * j2 + 129], P_[m][:, 128 * j:128 * j + 128], Vx[:, m, h, :], start=(m == 0), stop=(m == 1)), reads=[P_[m], Vx], writes=[po])
                                kb.op("dve", lambda e: e.reciprocal(out=rd_[:], in_=po[:, 0:258].rearrange("p (j f) -> p j f", f=129)[:, :, 128]), reads=[po], writes=[rd_])
                                for j2 in range(2):
                                    j = 2 * jj + j2
                                    kb.op("dve", lambda e: e.tensor_scalar(out=o_all[:, 4 * G + j, 128 * h:128 * h + 128], in0=po[:, 129 * j2:129 * j2 + 128], scalar1=rd_[:, j2:j2 + 1], scalar2=None, op0=ALU.mult), reads=[po, rd_], writes=[o_all])
                        return (s1, s2, s3)
                    steps.append(mk())
            pipeline(steps, 3)
            kb.barrier()
        with ExitStack() as es:
            Wo = p.sb(es, "xa_Wo", [128, 4, D], BF16)
            kb.dma("sp", Wo[:], wo.h.rearrange("(c p) n -> p c n", p=128), reads=[wo], writes=[Wo])
            lnw = ln_tiles(p, es, "xa_ln")
            ln_load_gb(p, lnw, g[l], b[l])
            hold = [p.sb(es, "xa_h%d" % i, [128, D], F32) for i in range(2)]
            pf = [p.ps(es, "xa_pf%d" % i, [128, 512]) for i in range(4)]
            pT2 = p.ps(es, "xa_pT2", [128, 1024], BF16)
            oT = [p.sb(es, "xa_oT%d" % i, [128, 4, 128], BF16) for i in range(2)]
            hold.append(p.sb(es, "xa_h2", [128, D], F32))
            hold.append(p.sb(es, "xa_h3", [128, D], F32))
            steps = []
            for tt in range(NT):
                def mk(tt=tt):
                    ho, o_ = hold[tt % 4], oT[tt % 2]

                    def s0():
                        kb.dma("sp", ho[:], p.h[tt * 128:(tt + 1) * 128, :], reads=[p.hres[tt]], writes=[ho])

                    def s1():
                        for c in range(4):
                            kb.op("pe", lambda e: e.transpose(pT2[:, 128 * c:128 * c + 128], o_all[:, tt, 128 * c:128 * c + 128], p.ident[:]), reads=[o_all, p.ident], writes=[pT2])
                        kb.op("act", lambda e: e.activation(out=o_[:], in_=pT2[:, 0:512].rearrange("p (c t) -> p c t", c=4), func=AF.Copy), reads=[pT2], writes=[o_])

                    def s2():
                        for cg in range(4):
                            for c in range(4):
                                kb.op("pe", lambda e: e.matmul(pf[cg][:], o_[:, c, :], Wo[:, c, 512 * cg:512 * cg + 512], start=(c == 0), stop=(c == 3)), reads=[o_, Wo], writes=[pf[cg]])
                            kb.op("dve", lambda e: e.scalar_tensor_tensor(out=ho[:, 512 * cg:512 * cg + 512], in0=ho[:, 512 * cg:512 * cg + 512], scalar=ALPHA, in1=pf[cg][:], op0=ALU.mult, op1=ALU.add), reads=[ho, pf[cg]], writes=[ho])
                        emit_ln_a(p, lnw, ho, tt, 1e-5)

                    def s3():
                        emit_ln_b(p, lnw, tt)
                    return (s0, s1, s2, s3)
                steps.append(mk())
            pipeline(steps, 4)
            kb.barrier()


def phase_ffn(p, l, final_out=None):
    kb = p.kb
    wg, wu, wd = p.wb[("ffn_w_gate", l)], p.wb[("ffn_w_up", l)], p.wb[("ffn_w_down", l)]
    g = p.dram("ln3_g", IN_SHAPES["ln3_g"], F32, kind="ExternalInput")
    b = p.dram("ln3_b", IN_SHAPES["ln3_b"], F32, kind="ExternalInput")
    NCH = FFN // 128
    with ExitStack() as es:
        lnw = ln_tiles(p, es, "ff_ln")
        ln_load_gb(p, lnw, g[l], b[l])
        hTg = p.sb(es, "ff_hT", [128, KC, 512], BF16)
        a_t = p.sb(es, "ff_a", [128, NCH, 512], BF16)
        Wg_t = [p.sb(es, "ff_Wg%d" % i, [128, KC, 256], BF16) for i in range(2)]
        Wu_t = [p.sb(es, "ff_Wu%d" % i, [128, KC, 256], BF16) for i in range(2)]
        Wd_t = [p.sb(es, "ff_Wd%d" % i, [128, 11, 512], BF16) for i in range(2)]
        s_t = [p.sb(es, "ff_s%d" % i, [128, 512], F32) for i in range(2)]
        v4 = p.sb(es, "ff_v4", [128, 4, D], F32)
        pg = [p.ps(es, "ff_pg%d" % i, [128, 512]) for i in range(2)]
        pu = [p.ps(es, "ff_pu%d" % i, [128, 512]) for i in range(2)]
        pf = [p.ps(es, "ff_pf%d" % i, [128, 512]) for i in range(2)]
        lnw["hbs"] = lnw["hbs"] + [p.sb(es, "ff_hb%d" % i, [128, D], BF16) for i in range(2, 4)]
        lnw["hTs"] = lnw["hTs"] + [p.sb(es, "ff_hTs%d" % i, [128, KC, 128], BF16) for i in range(2, 4)]
        wl = 0
        dl = 0
        cnt = 0
        accs = [pf[0], pf[1], pg[0], pg[1]]
        pending = []

        class _W:
            def __init__(self, j):
                self.j = j

            def __getitem__(self, k):
                if k == "hbs":
                    return [lnw["hbs"][self.j]] * 2
                if k == "hTs":
                    return [lnw["hTs"][self.j]] * 2
                return lnw[k]

        for tg in range(4):
            kb.dma("sp", hTg[:], p.hTd[:, 512 * tg:512 * tg + 512].rearrange("(c p) t -> p c t", p=128), reads=[p.hTd], writes=[hTg])
            for c2 in range(NCH // 2):
                Wg_, Wu_ = Wg_t[wl % 2], Wu_t[wl % 2]
                wl += 1
                kb.dma("sp", Wg_[:], wg[:, 256 * c2:256 * c2 + 256].rearrange("(c p) n -> p c n", p=128), reads=[wg], writes=[Wg_])
                kb.dma("sp", Wu_[:], wu[:, 256 * c2:256 * c2 + 256].rearrange("(c p) n -> p c n", p=128), reads=[wu], writes=[Wu_])
                for ci in range(2):
                    ch = 2 * c2 + ci
                    pg_, pu_, st_ = pg[cnt % 2], pu[cnt % 2], s_t[cnt % 2]
                    cnt += 1
                    for k in range(KC):
                        kb.op("pe", lambda e: e.matmul(pg_[:], Wg_[:, k, 128 * ci:128 * ci + 128], hTg[:, k, :], start=(k == 0), stop=(k == KC - 1)), reads=[Wg_, hTg], writes=[pg_])
                    for k in range(KC):
                        kb.op("pe", lambda e: e.matmul(pu_[:], Wu_[:, k, 128 * ci:128 * ci + 128], hTg[:, k, :], start=(k == 0), stop=(k == KC - 1)), reads=[Wu_, hTg], writes=[pu_])
                    kb.op("act", lambda e: e.activation(out=st_[:], in_=pg_[:], func=AF.Silu), reads=[pg_], writes=[st_])
                    kb.op("dve", lambda e: e.tensor_tensor(out=a_t[:, ch, :], in0=st_[:], in1=pu_[:], op=ALU.mult), reads=[st_, pu_], writes=[a_t])
                if c2 == 1:
                    for f in pending:
                        f()
                    pending = []
                    for j in range(4):
                        tt = 4 * tg + j
                        kb.dma("sp", v4[:, j, :], p.h[tt * 128:(tt + 1) * 128, :], reads=[p.hres[tt]], writes=[v4])
            for cg in range(4):
                for pc in range(4):
                    Wd_ = Wd_t[dl % 2]
                    dl += 1
                    kb.dma("sp", Wd_[:], wd[pc * 11 * 128:(pc + 1) * 11 * 128, 512 * cg:512 * cg + 512].rearrange("(c p) n -> p c n", p=128), reads=[wd], writes=[Wd_])
                    for j in range(4):
                        for c in range(11):
                            ch = pc * 11 + c
                            kb.op("pe", lambda e: e.matmul(accs[j][:], a_t[:, ch, 128 * j:128 * j + 128], Wd_[:, c, :], start=(ch == 0), stop=(ch == NCH - 1)), reads=[a_t, Wd_], writes=[accs[j]])
                for j in range(4):
                    kb.op("dve", lambda e: e.scalar_tensor_tensor(out=v4[:, j, 512 * cg:512 * cg + 512], in0=v4[:, j, 512 * cg:512 * cg + 512], scalar=ALPHA, in1=accs[j][:], op0=ALU.mult, op1=ALU.add), reads=[v4, accs[j]], writes=[v4])
            fin = final_out is not None
            for j in range(4):
                tt = 4 * tg + j
                emit_ln_a(p, _W(j), _V4(v4, j), tt, 1e-5, out_dram=(final_out.h if fin else None), out_res=final_out, write_hT=(not fin), h_store=(not fin))
                if not fin:
                    pending.append(lambda j=j, tt=tt: emit_ln_b(p, _W(j), tt))
        for f in pending:
            f()
        kb.barrier()


class _V4:
    def __init__(self, t, j):
        self.t, self.j, self.r = t, j, t.r

    def __getitem__(self, k):
        if isinstance(k, tuple):
            return self.t.h[(k[0], self.j) + tuple(k[1:])]
        return self.t.h[k, self.j]


def build_full(depth=DEPTH):
    p = Prog(ext_out=["out"])
    setup_globals(p)
    out = p.dram("out", (S, D), F32)
    convert_weights(p, names=["w_in"], layers=[0], defer=True)
    phase_ln0(p)
    for l in range(depth):
        convert_weights(p, names=["s5_w_glu", "nsa_cmp_w1", "nsa_cmp_w2"], layers=[l])
        phase_inproj(p, l)
        convert_weights(p, names=["w_out", "xa_wq", "xa_wkv", "xa_wo", "ffn_w_gate", "ffn_w_up", "ffn_w_down"], layers=[l], defer=True)
        if l + 1 < depth:
            convert_weights(p, names=["w_in"], layers=[l + 1], defer=True)
        phase_s5(p, l)
        phase_nsa(p, l)
        phase_sb(p, l)
        phase_dil(p, l)
        phase_outproj(p, l)
        phase_xattn(p, l)
        phase_ffn(p, l, final_out=(out if l == depth - 1 else None))
    p.kb.finish([out.r])
    p.kb.close()
    return p


def kernel(**inputs):
    p = build_full()
    consts = host_consts()
    n = 8
    in_maps = []
    for b in range(n):
        m = {}
        for name, kind in p.kinds.items():
            if kind != "ExternalInput":
                continue
            if name in consts:
                m[name] = consts[name]
            elif name in ("x", "mem"):
                m[name] = np.ascontiguousarray(np.asarray(inputs[name])[b], dtype=np.float32)
            else:
                m[name] = np.ascontiguousarray(np.asarray(inputs[name]), dtype=np.float32)
        in_maps.append(m)
    res = run_bass_kernel_spmd(p.nc, in_maps, core_ids=list(range(n)))
    return np.stack([np.asarray(res.results[b]["out"], dtype=np.float32) for b in range(n)], axis=0)
```
